# Optimizing a Trainium2 kernel written in Bass

```python
import math
import jax, jax.numpy as jnp
from jax import lax
import numpy as np

D_MODEL = 2048
BATCH = 16
SEQ = 256
DEPTH = 4
DEC_BATCH = 8
DEC_SEQ = 4096
PAST_LEN = 256

GRID_W = 64
N_MIXERS = 3
N_A = (DEPTH + 2) // N_MIXERS
N_B = (DEPTH + 1) // N_MIXERS
N_C = DEPTH // N_MIXERS
D_FF = -(-(8 * D_MODEL) // (3 * 256)) * 256
NORM_EPS = 1e-6

RW_HEAD = 64
RW_H = D_MODEL // RW_HEAD
RW_DECAY_LORA = max(32, int(round(1.8 * math.sqrt(D_MODEL) / 32)) * 32)
RW_AAA_LORA = RW_DECAY_LORA
RW_GATE_LORA = max(32, int(round(0.6 * D_MODEL ** 0.8 / 32)) * 32)
RW_LN_EPS = 64e-5

HG_K = 128
HG_H = D_MODEL // HG_K
HG_V = D_MODEL // HG_H
HG_CHUNK = 32

ML_H = 16
ML_NOPE = 128
ML_ROPE = 64
ML_V = 128
ML_Q_LORA = 512
ML_KV_LORA = 512
ML_SCALE = 1.0 / math.sqrt(ML_NOPE + ML_ROPE)
Q_BLOCK = 128
ROPE_BASE = 10000.0

F32 = jnp.float32

kernel_name = 'hybrid_rwkv7_hgrn2_mla_diffusion_step'


def _rmsnorm(x, w):
    xf = x.astype(F32)
    y = xf * lax.rsqrt(jnp.mean(xf * xf, axis=-1, keepdims=True) + NORM_EPS)
    return y.astype(x.dtype) * w


def _adaln(cond, w, b):
    m = jax.nn.silu(cond) @ w + b
    return jnp.split(m[:, None, :], 6, axis=-1)


def _swiglu(h, w_in, w_out):
    a, b = jnp.split(h @ w_in, 2, axis=-1)
    return (jax.nn.silu(a) * b) @ w_out


def _centred_shift(x):
    xp = jnp.pad(x, ((0, 0), (1, 1), (0, 0)))
    return 0.5 * (xp[:, :-2] + xp[:, 2:])


def _rev(ts):
    return tuple(jnp.flip(t, axis=1) for t in ts)


def _wkv7_scan(r, w, k, v, kk, kka, s0):
    def step(S, inp):
        r_t, w_t, k_t, v_t, kk_t, kka_t = inp
        sa = jnp.einsum('bhvk,bhk->bhv', S, kk_t)
        S = (S * w_t[:, :, None, :] - sa[..., None] * kka_t[:, :, None, :]
             + v_t[..., None] * k_t[:, :, None, :])
        return S, jnp.einsum('bhvk,bhk->bhv', S, r_t)
    xs = tuple(jnp.swapaxes(t, 0, 1) for t in (r, w, k, v, kk, kka))
    s_fin, ys = lax.scan(step, s0, xs)
    return jnp.swapaxes(ys, 0, 1), s_fin


def _rwkv7_mix(h, p, s0_f, s0_b):
    B, T, D = h.shape
    heads = lambda t: t.reshape(B, T, RW_H, RW_HEAD)
    xx = _centred_shift(h) - h
    xr, xw, xk, xv, xa, xg = (h + xx * p['mu'][i] for i in range(6))
    r = heads(xr @ p['wr'])
    k = xk @ p['wk']
    v = heads(xv @ p['wv'])
    g = jax.nn.sigmoid(xg @ p['g1']) @ p['g2']
    kk = heads(k * p['kk']).astype(F32)
    kk = kk * lax.rsqrt(jnp.sum(kk * kk, axis=-1, keepdims=True) + 1e-12)
    ys, bonuses, finals = [], [], []
    for d, s0 in enumerate((s0_f, s0_b)):
        wlog = -jax.nn.softplus(-(p['w0'][d] + jnp.tanh(xw @ p['w1'][d]) @ p['w2'][d]).astype(F32)) - 0.5
        decay = jnp.exp(-jnp.exp(wlog))
        a = jax.nn.sigmoid(p['a0'][d] + (xa @ p['a1'][d]) @ p['a2'][d])
        kd = heads(k * (1 + (a - 1) * p['ka']))
        ins = tuple(t.astype(F32) for t in (r, heads(decay), kd, v, kk, kk * heads(a).astype(F32)))
        if d == 1:
            ins = _rev(ins)
        y, s_fin = _wkv7_scan(*ins, s0.astype(F32))
        if d == 1:
            y = jnp.flip(y, axis=1)
        ys.append(y)
        bonuses.append(jnp.sum(r * kd * p['rk'][d], axis=-1, keepdims=True) * v)
        finals.append(s_fin)
    y = ys[0] + ys[1]
    mu = jnp.mean(y, axis=-1, keepdims=True)
    var = jnp.mean(jnp.square(y - mu), axis=-1, keepdims=True)
    yn = ((y - mu) * lax.rsqrt(var + RW_LN_EPS)).astype(h.dtype).reshape(B, T, D)
    yn = yn * p['lnx_w'] + p['lnx_b'] + (bonuses[0] + bonuses[1]).reshape(B, T, D)
    return (yn * g) @ p['wo'], finals[0], finals[1]


def _hgrn2_chunk_scan(q, k, gl, i, s0):
    B, T, H, K = q.shape
    V = i.shape[-1]
    C = HG_CHUNK
    n = T // C
    chunks = lambda t: t.reshape(B, n, C, H, t.shape[-1]).transpose(1, 0, 3, 2, 4)
    causal = jnp.tril(jnp.ones((C, C), dtype=bool))[:, :, None]

    def step(S, inp):
        qc, kc, gc, ic = inp
        G = jnp.cumsum(gc, axis=2)
        o = jnp.einsum('bhtk,bhkv->bhtv', qc * jnp.exp(G), S)
        dec = jnp.exp(jnp.where(causal, G[:, :, :, None, :] - G[:, :, None, :, :], -jnp.inf))
        A = jnp.einsum('bhtk,bhsk,bhtsk->bhts', qc, kc, dec)
        o = o + jnp.einsum('bhts,bhsv->bhtv', A, ic)
        G_end = G[:, :, -1:, :]
        S = (jnp.exp(G_end[:, :, 0, :, None]) * S
             + jnp.einsum('bhsk,bhsv->bhkv', kc * jnp.exp(G_end - G), ic))
        return S, o

    s_fin, o = lax.scan(step, s0, tuple(chunks(t) for t in (q, k, gl, i)))
    return o.transpose(1, 0, 3, 2, 4).reshape(B, T, H, V), s_fin


def _hgrn2_mix(h, p, lb, s0_f, s0_b):
    B, T, D = h.shape
    hk, hv = HG_H * HG_K, HG_H * HG_V
    q, f_fw, f_bw, iv, g = jnp.split(h @ p['w_in'], [hk, 2 * hk, 3 * hk, 3 * hk + hv], axis=-1)
    to_k = lambda t: t.reshape(B, T, HG_H, HG_K).astype(F32)
    to_v = lambda t: t.reshape(B, T, HG_H, HG_V).astype(F32)
    q = to_k(jax.nn.silu(q))
    iv = to_v(iv)
    outs, finals = [], []
    for d, (z, s0) in enumerate(((f_fw, s0_f), (f_bw, s0_b))):
        f = lb + (1 - lb) * jax.nn.sigmoid(to_k(z))
        ins = (q, 1 - f, jnp.log(f), iv)
        if d == 1:
            ins = _rev(ins)
        o, s_fin = _hgrn2_chunk_scan(*ins, s0.astype(F32))
        if d == 1:
            o = jnp.flip(o, axis=1)
        outs.append(o)
        finals.append(s_fin)
    o = outs[0] + outs[1]
    o = o * lax.rsqrt(jnp.mean(o * o, axis=-1, keepdims=True) + NORM_EPS) * p['norm_w'] * jax.nn.silu(to_v(g))
    return o.reshape(B, T, D).astype(h.dtype) @ p['wo'], finals[0], finals[1]


def _mla_project(h, p):
    B, T, _ = h.shape
    cq, ckv, kr = jnp.split(h @ p['w_down'], [ML_Q_LORA, ML_Q_LORA + ML_KV_LORA], axis=-1)
    q = (_rmsnorm(cq, p['qnorm_w']) @ p['w_uq']).reshape(B, T, ML_H, ML_NOPE + ML_ROPE)
    return q[..., :ML_NOPE], q[..., ML_NOPE:], _rmsnorm(ckv, p['kvnorm_w']), kr


def _mla_expand(ckv, p):
    B, L, _ = ckv.shape
    kv = (ckv @ p['w_ukv']).reshape(B, L, ML_H, ML_NOPE + ML_V)
    return kv[..., :ML_NOPE], kv[..., ML_NOPE:]


def _mla_attend(q_nope, q_rope, k_nope, k_rope, v):
    B, T, H, _ = q_nope.shape
    nb = T // Q_BLOCK
    blocks = lambda t: jnp.swapaxes(t.reshape(B, nb, Q_BLOCK, *t.shape[2:]), 0, 1)

    def one_block(qs):
        qn, qr = qs
        s = jnp.einsum('bqhd,bkhd->bhqk', qn, k_nope) + jnp.einsum('bqhd,bkd->bhqk', qr, k_rope)
        pr = jax.nn.softmax(s.astype(F32) * ML_SCALE, axis=-1).astype(v.dtype)
        return jnp.einsum('bhqk,bkhd->bqhd', pr, v)

    o = lax.map(one_block, (blocks(q_nope), blocks(q_rope)))
    return jnp.swapaxes(o, 0, 1).reshape(B, T, H * ML_V)


def _axial_angles(T):
    rows = T // GRID_W
    r = jnp.broadcast_to(jnp.arange(rows, dtype=F32)[:, None], (rows, GRID_W)).reshape(-1)
    col = jnp.broadcast_to(jnp.arange(GRID_W, dtype=F32)[None, :], (rows, GRID_W)).reshape(-1)
    nf = ML_ROPE // 4
    inv = ROPE_BASE ** (-jnp.arange(nf, dtype=F32) / nf)
    return r[:, None] * inv, col[:, None] * inv


def _rotate(x, ang):
    x1, x2 = jnp.split(x, 2, axis=-1)
    cs, sn = jnp.cos(ang), jnp.sin(ang)
    return jnp.concatenate([x1 * cs - x2 * sn, x2 * cs + x1 * sn], axis=-1)


def _axial_rope(x, ang_r, ang_c):
    xr, xc = jnp.split(x, 2, axis=-1)
    return jnp.concatenate([_rotate(xr, ang_r), _rotate(xc, ang_c)], axis=-1).astype(x.dtype)


def _mla_context(h, p):
    qn, qr, ckv, kr = _mla_project(h, p)
    kn, v = _mla_expand(ckv, p)
    return _mla_attend(qn, qr, kn, kr, v) @ p['wo'], ckv, kr


def _mla_latent(h, p, ckv_ctx, kr_ctx):
    qn, qr, ckv, kr = _mla_project(h, p)
    ang_r, ang_c = _axial_angles(h.shape[1])
    qr = _axial_rope(qr, ang_r[:, None, :], ang_c[:, None, :])
    kr = _axial_rope(kr, ang_r, ang_c)
    kn, v = _mla_expand(jnp.concatenate([ckv, ckv_ctx.astype(ckv.dtype)], axis=1), p)
    kr_all = jnp.concatenate([kr, kr_ctx.astype(kr.dtype)], axis=1)
    return _mla_attend(qn, qr, kn, kr_all, v) @ p['wo']


def setup_inputs(seed: int = 0) -> dict:
    key = jax.random.key(seed)
    ks = iter(jax.random.split(key, 64))
    nrm = lambda shape, scale=1.0: scale * jax.random.normal(next(ks), shape, F32)
    gain = lambda shape: 1.0 + nrm(shape, 0.02)
    D = D_MODEL
    LD, LA, LG = RW_DECAY_LORA, RW_AAA_LORA, RW_GATE_LORA
    inp = {}
    inp['x_prompt'] = nrm((BATCH, SEQ, D))
    inp['x_sample'] = nrm((DEC_BATCH, DEC_SEQ, D))
    inp['state_rwkv'] = nrm((DEC_BATCH, N_A, 2, RW_H, RW_HEAD, RW_HEAD), 0.5)
    inp['state_hgrn'] = nrm((DEC_BATCH, N_B, 2, HG_H, HG_K, HG_V), 0.5)
    inp['cache_ckv'] = nrm((DEC_BATCH, N_C, PAST_LEN, ML_KV_LORA))
    inp['cache_krope'] = nrm((DEC_BATCH, N_C, PAST_LEN, ML_ROPE))
    inp['c'] = nrm((DEC_BATCH, D))
    inp['c_ctx'] = nrm((D,))
    inp['ada_w'] = nrm((DEPTH, D, 6 * D), 0.5 * D ** -0.5)
    inp['ada_b'] = nrm((DEPTH, 6 * D), 0.02)
    inp['norm1_w'] = gain((DEPTH, D))
    inp['norm2_w'] = gain((DEPTH, D))
    inp['ffn_w_in'] = nrm((DEPTH, D, 2 * D_FF), D ** -0.5)
    inp['ffn_w_out'] = nrm((DEPTH, D_FF, D), D_FF ** -0.5)
    inp['final_norm_w'] = gain((D,))
    inp['rw_mu'] = jax.random.uniform(next(ks), (N_A, 6, D), F32)
    inp['rw_wr'] = nrm((N_A, D, D), D ** -0.5)
    inp['rw_wk'] = nrm((N_A, D, D), D ** -0.5)
    inp['rw_wv'] = nrm((N_A, D, D), D ** -0.5)
    inp['rw_wo'] = nrm((N_A, D, D), D ** -0.5)
    inp['rw_w0'] = nrm((N_A, 2, D), 0.3)
    inp['rw_w1'] = nrm((N_A, 2, D, LD), D ** -0.5)
    inp['rw_w2'] = nrm((N_A, 2, LD, D), 0.1 * LD ** -0.5)
    inp['rw_a0'] = nrm((N_A, 2, D), 0.3)
    inp['rw_a1'] = nrm((N_A, 2, D, LA), D ** -0.5)
    inp['rw_a2'] = nrm((N_A, 2, LA, D), 0.1 * LA ** -0.5)
    inp['rw_g1'] = nrm((N_A, D, LG), D ** -0.5)
    inp['rw_g2'] = nrm((N_A, LG, D), LG ** -0.5)
    inp['rw_kk'] = 0.85 + nrm((N_A, D), 0.02)
    inp['rw_ka'] = gain((N_A, D))
    inp['rw_rk'] = nrm((N_A, 2, RW_H, RW_HEAD), 0.1)
    inp['rw_lnx_w'] = gain((N_A, D))
    inp['rw_lnx_b'] = nrm((N_A, D), 0.02)
    inp['hg_w_in'] = nrm((N_B, D, 3 * HG_H * HG_K + 2 * HG_H * HG_V), D ** -0.5)
    inp['hg_lb'] = nrm((DEPTH, HG_H * HG_K), 0.5)
    inp['hg_norm_w'] = gain((N_B, HG_V))
    inp['hg_wo'] = nrm((N_B, HG_H * HG_V, D), (HG_H * HG_V) ** -0.5)
    inp['ml_w_down'] = nrm((N_C, D, ML_Q_LORA + ML_KV_LORA + ML_ROPE), D ** -0.5)
    inp['ml_qnorm_w'] = gain((N_C, ML_Q_LORA))
    inp['ml_kvnorm_w'] = gain((N_C, ML_KV_LORA))
    inp['ml_w_uq'] = nrm((N_C, ML_Q_LORA, ML_H * (ML_NOPE + ML_ROPE)), ML_Q_LORA ** -0.5)
    inp['ml_w_ukv'] = nrm((N_C, ML_KV_LORA, ML_H * (ML_NOPE + ML_V)), ML_KV_LORA ** -0.5)
    inp['ml_wo'] = nrm((N_C, ML_H * ML_V, D), (ML_H * ML_V) ** -0.5)
    return inp


def reference(x_prompt, x_sample, state_rwkv, state_hgrn, cache_ckv, cache_krope, c, c_ctx,
              ada_w, ada_b, norm1_w, norm2_w, ffn_w_in, ffn_w_out, final_norm_w,
              rw_mu, rw_wr, rw_wk, rw_wv, rw_wo, rw_w0, rw_w1, rw_w2, rw_a0, rw_a1, rw_a2,
              rw_g1, rw_g2, rw_kk, rw_ka, rw_rk, rw_lnx_w, rw_lnx_b,
              hg_w_in, hg_lb, hg_norm_w, hg_wo,
              ml_w_down, ml_qnorm_w, ml_kvnorm_w, ml_w_uq, ml_w_ukv, ml_wo):
    n_ctx = x_prompt.shape[0]
    lb_table = jnp.cumsum(jax.nn.softmax(hg_lb.astype(F32), axis=0), axis=0)
    lb_table = lb_table - lb_table[0]
    x_c, x_s = x_prompt, x_sample
    cond_ctx = c_ctx[None, :]
    new_rwkv, new_hgrn, new_ckv, new_krope = [], [], [], []
    for l in range(DEPTH):
        kind, j = l % N_MIXERS, l // N_MIXERS
        sh1c, sc1c, g1c, sh2c, sc2c, g2c = _adaln(cond_ctx, ada_w[l], ada_b[l])
        sh1s, sc1s, g1s, sh2s, sc2s, g2s = _adaln(c, ada_w[l], ada_b[l])
        h_c = _rmsnorm(x_c, norm1_w[l]) * (1 + sc1c) + sh1c
        h_s = _rmsnorm(x_s, norm1_w[l]) * (1 + sc1s) + sh1s
        if kind == 0:
            p = {'mu': rw_mu[j], 'wr': rw_wr[j], 'wk': rw_wk[j], 'wv': rw_wv[j], 'wo': rw_wo[j],
                 'w0': rw_w0[j], 'w1': rw_w1[j], 'w2': rw_w2[j], 'a0': rw_a0[j], 'a1': rw_a1[j],
                 'a2': rw_a2[j], 'g1': rw_g1[j], 'g2': rw_g2[j], 'kk': rw_kk[j], 'ka': rw_ka[j],
                 'rk': rw_rk[j], 'lnx_w': rw_lnx_w[j], 'lnx_b': rw_lnx_b[j]}
            zero = jnp.zeros((n_ctx, RW_H, RW_HEAD, RW_HEAD), F32)
            o_c, s_f, s_b = _rwkv7_mix(h_c, p, zero, zero)
            o_s, _, _ = _rwkv7_mix(h_s, p, state_rwkv[:, j, 0], state_rwkv[:, j, 1])
            new_rwkv.append(jnp.stack([s_f, s_b], axis=1))
        elif kind == 1:
            p = {'w_in': hg_w_in[j], 'norm_w': hg_norm_w[j], 'wo': hg_wo[j]}
            lb = lb_table[l].reshape(HG_H, HG_K)
            zero = jnp.zeros((n_ctx, HG_H, HG_K, HG_V), F32)
            o_c, s_f, s_b = _hgrn2_mix(h_c, p, lb, zero, zero)
            o_s, _, _ = _hgrn2_mix(h_s, p, lb, state_hgrn[:, j, 0], state_hgrn[:, j, 1])
            new_hgrn.append(jnp.stack([s_f, s_b], axis=1))
        else:
            p = {'w_down': ml_w_down[j], 'qnorm_w': ml_qnorm_w[j], 'kvnorm_w': ml_kvnorm_w[j],
                 'w_uq': ml_w_uq[j], 'w_ukv': ml_w_ukv[j], 'wo': ml_wo[j]}
            o_c, ckv_c, kr_c = _mla_context(h_c, p)
            o_s = _mla_latent(h_s, p, cache_ckv[:, j], cache_krope[:, j])
            new_ckv.append(ckv_c)
            new_krope.append(kr_c)
        x_c = x_c + g1c * o_c
        x_s = x_s + g1s * o_s
        x_c = x_c + g2c * _swiglu(_rmsnorm(x_c, norm2_w[l]) * (1 + sc2c) + sh2c, ffn_w_in[l], ffn_w_out[l])
        x_s = x_s + g2s * _swiglu(_rmsnorm(x_s, norm2_w[l]) * (1 + sc2s) + sh2s, ffn_w_in[l], ffn_w_out[l])
    y_prompt = _rmsnorm(x_c, final_norm_w)
    y_sample = _rmsnorm(x_s, final_norm_w)
    new_state_rwkv = jnp.stack(new_rwkv, axis=1)
    new_state_hgrn = jnp.stack(new_hgrn, axis=1)
    new_cache_ckv = jnp.stack(new_ckv, axis=1)
    new_cache_krope = jnp.stack(new_krope, axis=1)
    return (y_prompt, y_sample, new_state_rwkv, new_state_hgrn, new_cache_ckv, new_cache_krope)
```

```python
import math
import numpy as np
import concourse.bass as bass
import concourse.mybir as mybir
from concourse.bass_utils import run_bass_kernel_spmd

F32 = mybir.dt.float32
BF16 = mybir.dt.bfloat16
AF = mybir.ActivationFunctionType
ALU = mybir.AluOpType
AX = mybir.AxisListType

D = 2048
KC = 16
DFF = 5632
NS = 4096
NP = 256
R = NS + 2 * NP
DEPTH = 4
EPS = 1e-6


class Buf:
    __slots__ = ("ap", "lw", "rd", "name", "psum")

    def __init__(self, ap, name="", psum=False):
        self.ap = ap
        self.lw = None
        self.rd = []
        self.name = name
        self.psum = psum


class FW:
    def __init__(self, nc, ndma=64):
        self.nc = nc
        self.eng = {"pe": nc.tensor, "act": nc.scalar, "dve": nc.vector, "pool": nc.gpsimd, "sp": nc.sync}
        self.sem = {}
        self.cnt = {}
        for k in self.eng:
            self.sem[k] = nc.alloc_semaphore("s_" + k)
            self.cnt[k] = 0
        self.ndma = ndma
        self.dsem = [nc.alloc_semaphore("d%d" % i) for i in range(ndma)]
        self.dcnt = [0] * ndma
        self.dslots = {"sp": list(range(0, ndma // 2)), "pool": list(range(ndma // 2, 3 * ndma // 4)),
                       "act": list(range(3 * ndma // 4, ndma))}
        self.dnext = {"sp": 0, "pool": 0, "act": 0}
        self.waited = {k: {} for k in self.eng}
        self.n_inst = 0
        self.n_wait = 0

    def _semobj(self, key):
        return self.sem[key] if isinstance(key, str) else self.dsem[key]

    def _wait(self, e, deps, skip_self):
        w = self.waited[e]
        need = {}
        for d in deps:
            if d is None:
                continue
            k, v = d
            if skip_self and k == e:
                continue
            if w.get(k, 0) >= v:
                continue
            if need.get(k, 0) < v:
                need[k] = v
        for k, v in need.items():
            self.eng[e].wait_ge(self._semobj(k), v)
            w[k] = v
            self.n_wait += 1

    @staticmethod
    def _deps(reads, writes, e=None):
        deps = []
        for b in reads:
            deps.append(b.lw)
            if b.psum:
                deps.extend(ev for ev in b.rd if ev[0] != e)
        for b in writes:
            deps.append(b.lw)
            deps.extend(b.rd)
        return deps

    @staticmethod
    def _commit(ev, reads, writes):
        for b in reads:
            b.rd.append(ev)
            if len(b.rd) > 48:
                m = {}
                for k, v in b.rd:
                    if m.get(k, 0) < v:
                        m[k] = v
                b.rd = list(m.items())
        for b in writes:
            b.lw = ev
            b.rd = []

    def op(self, e, fn, reads=(), writes=(), rt=None):
        self._wait(e, self._deps(reads, writes, e), skip_self=(e == "pe"))
        if e == "pe":
            last = getattr(self, "last_rt", None)
            if rt is not None and last is not None and rt != last and self.cnt["pe"] > 0:
                self.eng[e].wait_ge(self.sem["pe"], self.cnt["pe"])
                self.n_wait += 1
            self.last_rt = rt
        ins = fn(self.eng[e])
        self.cnt[e] += 1
        ins.then_inc(self.sem[e], 1)
        self._commit((e, self.cnt[e]), reads, writes)
        self.n_inst += 1
        return ins

    def dma(self, e, out, in_, reads=(), writes=(), **kw):
        self._wait(e, self._deps(reads, writes), skip_self=False)
        sl = self.dslots[e]
        i = sl[self.dnext[e]]
        self.dnext[e] = (self.dnext[e] + 1) % len(sl)
        if self.dcnt[i] > 0:
            self._wait(e, [(i, self.dcnt[i])], skip_self=False)
        ins = self.eng[e].dma_start(out=out, in_=in_, **kw)
        self.dcnt[i] += 16
        ins.then_inc(self.dsem[i], 16)
        self._commit((i, self.dcnt[i]), reads, writes)
        self.n_inst += 1
        return ins

    def barrier(self):
        evs = [(k, c) for k, c in self.cnt.items() if c > 0]
        evs += [(i, c) for i, c in enumerate(self.dcnt) if c > 0]
        for e in self.eng:
            self._wait(e, evs, skip_self=True)


class Rot:
    def __init__(self, bufs):
        self.bufs = bufs
        self.i = 0

    def next(self):
        b = self.bufs[self.i]
        self.i = (self.i + 1) % len(self.bufs)
        return b


class WStream:
    def __init__(self, rot, loaders):
        self.rot = rot
        self.loaders = loaders
        self.depth = len(rot.bufs)
        self.issued = []
        self.i = 0

    def _issue(self):
        k = len(self.issued)
        if k < len(self.loaders):
            b = self.rot.next()
            self.issued.append((b, self.loaders[k](b)))

    def next(self):
        while len(self.issued) < min(len(self.loaders), self.i + self.depth - 1) or len(self.issued) <= self.i:
            self._issue()
        r = self.issued[self.i]
        self.issued[self.i] = None
        self.i += 1
        return r


class Builder:
    def __init__(self, plan, TB=512, debug_out=()):
        self.plan = plan
        self.TB = TB
        self.stop = None
        self.nc = bass.Bass("TRN2", target_bir_lowering=False)
        self.fw = FW(self.nc)
        self.din = {}
        self.dout = {}
        self.stack = []
        self.debug_out = debug_out

    def inp(self, name, shape):
        ap = self.nc.dram_tensor(name, list(shape), F32, kind="ExternalInput").ap()
        self.din[name] = Buf(ap, name)
        return self.din[name]

    def outp(self, name, shape):
        ap = self.nc.dram_tensor(name, list(shape), F32, kind="ExternalOutput").ap()
        self.dout[name] = Buf(ap, name)
        return self.dout[name]

    def scratch(self, name, shape, dtype=F32):
        kind = "ExternalOutput" if name in self.debug_out else "Internal"
        ap = self.nc.dram_tensor(name, list(shape), dtype, kind=kind).ap()
        return Buf(ap, name)

    def phase_begin(self):
        self.fw.barrier()
        self.stack.append([])

    def phase_end(self):
        self.fw.barrier()
        guards = self.stack.pop()
        for g in reversed(guards):
            g.__exit__(None, None, None)

    def sb(self, name, shape, dtype=F32):
        self.uid = getattr(self, "uid", 0) + 1
        name = "%s_u%d" % (name, self.uid)
        g = self.nc.sbuf_tensor(name, list(shape), dtype)
        t = g.__enter__()
        self.stack[-1].append(g)
        return Buf(t, name)

    def ps(self, name, shape, dtype=F32):
        self.uid = getattr(self, "uid", 0) + 1
        name = "%s_u%d" % (name, self.uid)
        g = self.nc.psum_tensor(name, list(shape), dtype)
        t = g.__enter__()
        self.stack[-1].append(g)
        return Buf(t, name, psum=True)

    def sbrot(self, name, n, shape, dtype=F32):
        return Rot([self.sb("%s%d" % (name, i), shape, dtype) for i in range(n)])

    def psrot(self, name, n, shape, dtype=F32):
        return Rot([self.ps("%s%d" % (name, i), shape, dtype) for i in range(n)])

    def colvecs(self, rows, out, ps, stage, ident):
        fw = self.fw
        n = len(rows)
        assert n * 16 <= 128
        for i, (b, ap) in enumerate(rows):
            fw.dma("sp", stage.ap[i * 16:(i + 1) * 16, :], ap.rearrange("(kc p) -> kc p", p=128),
                   reads=[b], writes=[stage])
        fw.op("pe", lambda e: e.transpose(ps.ap[:, 0:n * 16], stage.ap[0:n * 16, :], ident.ap[0:n * 16, 0:n * 16]),
              reads=[stage, ident], writes=[ps])
        fw.op("dve", lambda e: e.tensor_copy(out=out.ap[:, 0:n * 16], in_=ps.ap[:, 0:n * 16]), reads=[ps], writes=[out])

    def group_of(self, row):
        return 0 if row < NS else 1

    def blocks(self, TB=None):
        TB = TB or self.TB
        out = []
        t = 0
        while t < NS:
            out.append((t, min(TB, NS - t)))
            t += TB
        t = NS
        while t < R:
            out.append((t, min(TB, R - t)))
            t += TB
        return out

    def consts_begin(self):
        nc, fw = self.nc, self.fw
        self.stack.append([])
        self.identf = self.sb("identf", [128, 128], F32)
        self.identb = self.sb("identb", [128, 128], BF16)
        for t in (self.identf, self.identb):
            fw.op("pool", lambda e: e.memset(t.ap[:], 1.0), writes=[t])
            fw.op("pool", lambda e: e.affine_select(out=t.ap[:], in_=t.ap[:], pattern=[[-1, 128]],
                                                    compare_op=ALU.is_equal, fill=0.0, base=0, channel_multiplier=1),
                  reads=[t], writes=[t])

    def adaln(self):
        fw = self.fw
        self.phase_begin()
        cond = self.din["cond"]
        adaw, adab = self.din["ada_w"], self.din["ada_b"]
        stage = self.sb("ad_stage", [32, 128])
        pst = self.ps("ad_pst", [128, 32])
        scT = self.sb("ad_scT", [128, 32], BF16)
        fw.dma("sp", stage.ap[:], cond.ap.rearrange("j (kc p) -> (j kc) p", p=128), reads=[cond], writes=[stage])
        fw.op("pe", lambda e: e.transpose(pst.ap[:], stage.ap[:], self.identf.ap[0:32, 0:32]),
              reads=[stage, self.identf], writes=[pst])
        fw.op("act", lambda e: e.activation(out=scT.ap[:], in_=pst.ap[:], func=AF.Silu), reads=[pst], writes=[scT])
        scv = scT.ap[:].rearrange("p (j kc) -> p j kc", j=2)
        wrot = self.sbrot("ad_w", 3, [128, KC, 512], BF16)
        prot = self.psrot("ad_ps", 2, [2, 512])
        brow = self.sb("ad_b", [2, 6 * D])
        mout = self.sbrot("ad_m", 2, [2, 6 * D])
        for l in range(DEPTH):
            fw.dma("sp", brow.ap[:], adab.ap[l:l + 1, :].partition_broadcast(2), reads=[adab], writes=[brow])
            mo = mout.next()
            for cb in range(6 * D // 512):
                w = wrot.next()
                fw.dma("pool", w.ap[:], adaw.ap[l, :, cb * 512:(cb + 1) * 512].rearrange("(kc p) n -> p kc n", p=128),
                       reads=[adaw], writes=[w])
                p = prot.next()
                for kc in range(KC):
                    fw.op("pe", lambda e: e.matmul(p.ap[:], lhsT=scv[:, :, kc], rhs=w.ap[:, kc, :],
                                                   start=(kc == 0), stop=(kc == KC - 1)),
                          reads=[scT, w], writes=[p])
                fw.op("dve", lambda e: e.tensor_tensor(out=mo.ap[:, cb * 512:(cb + 1) * 512], in0=p.ap[:],
                                                       in1=brow.ap[:, cb * 512:(cb + 1) * 512], op=ALU.add),
                      reads=[p, brow], writes=[mo])
            fw.dma("sp", self.mrow.ap[l], mo.ap[:], reads=[mo], writes=[self.mrow])
        self.phase_end()

    def prep_setup(self, l, which):
        fw = self.fw
        nw = self.din["norm1_w" if which == 0 else "norm2_w"]
        sh_s, sc_s = (0, 1) if which == 0 else (3, 4)
        m = self.mrow
        stage = self.sb("pp_stage", [128, 128])
        pst = self.ps("pp_pst", [128, 128])
        raw = self.sb("pp_raw", [128, 128])
        rows = [(nw, nw.ap[l]),
                (m, m.ap[l, 0, sc_s * D:(sc_s + 1) * D]), (m, m.ap[l, 1, sc_s * D:(sc_s + 1) * D]),
                (m, m.ap[l, 0, sh_s * D:(sh_s + 1) * D]), (m, m.ap[l, 1, sh_s * D:(sh_s + 1) * D])]
        self.colvecs(rows, raw, pst, stage, self.identf)
        modc = self.sb("pp_modc", [128, 64])
        for g in range(2):
            fw.op("dve", lambda e: e.scalar_tensor_tensor(out=modc.ap[:, g * 16:(g + 1) * 16],
                                                          in0=raw.ap[:, (1 + g) * 16:(2 + g) * 16], scalar=1.0,
                                                          in1=raw.ap[:, 0:16], op0=ALU.add, op1=ALU.mult),
                  reads=[raw], writes=[modc])
            fw.op("dve", lambda e: e.tensor_copy(out=modc.ap[:, (2 + g) * 16:(3 + g) * 16],
                                                 in_=raw.ap[:, (3 + g) * 16:(4 + g) * 16]),
                  reads=[raw], writes=[modc])
        self.modc = modc
        self.pp_x = self.sbrot("pp_x", 2, [128, D])
        self.pp_xn = self.sbrot("pp_xn", 2, [128, D], BF16)
        self.pp_junk = self.sb("pp_junk", [128, D], BF16)
        self.pp_st = self.sbrot("pp_st", 4, [128, 4])
        self.pp_ps = self.psrot("pp_ps", 2, [128, 8 * 128], BF16)

    def rstd_of(self, x, st, width=D, eps=EPS):
        fw = self.fw
        fw.op("act", lambda e: e.activation(out=self.pp_junk.ap[:, 0:width], in_=x.ap[:, 0:width], func=AF.Square,
                                            accum_out=st.ap[:, 0:1]),
              reads=[x], writes=[self.pp_junk, st])
        fw.op("act", lambda e: e.activation(out=st.ap[:, 1:2], in_=st.ap[:, 0:1], func=AF.Ln, scale=1.0 / width,
                                            bias=self.epsc.ap[:, 0:1] if eps == EPS else self.epsc.ap[:, 1:2]),
              reads=[st, self.epsc], writes=[st])
        fw.op("act", lambda e: e.activation(out=st.ap[:, 1:2], in_=st.ap[:, 1:2], func=AF.Exp, scale=-0.5),
              reads=[st], writes=[st])
        return st.ap[:, 1:2]

    def prep_block(self, t0, tb, hT, col0=0):
        fw = self.fw
        g = self.group_of(t0)
        modc = self.modc
        for s in range(tb // 128):
            x = self.pp_x.next()
            r0 = t0 + s * 128
            X = self.Xt[r0 // 128]
            fw.dma("sp", x.ap[:], X.ap, reads=[X], writes=[x])
            st = self.pp_st.next()
            rs = self.rstd_of(x, st)
            xn = self.pp_xn.next()
            fw.op("dve", lambda e: e.tensor_scalar(out=xn.ap[:], in0=x.ap[:], scalar1=rs, scalar2=None, op0=ALU.mult),
                  reads=[x, st], writes=[xn])
            for half in range(2):
                p = self.pp_ps.next()
                for q in range(8):
                    kc = half * 8 + q
                    fw.op("pe", lambda e: e.transpose(p.ap[:, q * 128:(q + 1) * 128], xn.ap[:, kc * 128:(kc + 1) * 128],
                                                      self.identb.ap[:]),
                          reads=[xn, self.identb], writes=[p])
                for q in range(8):
                    kc = half * 8 + q
                    dst = hT.ap[:, kc, col0 + s * 128: col0 + (s + 1) * 128]
                    src = p.ap[:, q * 128:(q + 1) * 128]
                    a_ap = modc.ap[:, g * 16 + kc: g * 16 + kc + 1]
                    b_ap = modc.ap[:, (2 + g) * 16 + kc: (2 + g) * 16 + kc + 1]
                    if half == 0:
                        fw.op("act", lambda e: e.activation(out=dst, in_=src, func=AF.Identity, scale=a_ap, bias=b_ap),
                              reads=[p, modc], writes=[hT])
                    else:
                        fw.op("dve", lambda e: e.tensor_scalar(out=dst, in0=src, scalar1=a_ap, scalar2=b_ap,
                                                               op0=ALU.mult, op1=ALU.add),
                              reads=[p, modc], writes=[hT])

    def gate_rows(self, l, sect, name):
        fw = self.fw
        g = self.sb(name, [128, 2 * D])
        for j in range(2):
            fw.dma("sp", g.ap[:, j * D:(j + 1) * D],
                   self.mrow.ap[l, j:j + 1, sect * D:(sect + 1) * D].partition_broadcast(128),
                   reads=[self.mrow], writes=[g])
        return g

    def load_w(self, wbuf, off, wsrc, src_ap, kcn, ncols):
        dst = wbuf.ap[:, off: off + kcn * ncols].rearrange("p (kc n) -> p kc n", kc=kcn)
        self.fw.dma("pool", dst, src_ap.rearrange("(kc p) n -> p kc n", p=128), reads=[wsrc], writes=[wbuf])
        return dst

    def ffn(self, l):
        fw = self.fw
        self.phase_begin()
        TB = self.TB
        w_in, w_out = self.din["ffn_w_in"], self.din["ffn_w_out"]
        self.prep_setup(l, 1)
        grow = self.gate_rows(l, 5, "ffn_g")
        hT = self.sb("ffn_hT", [128, KC, TB], BF16)
        uT = self.sb("ffn_uT", [128, DFF // 128, TB], BF16)
        CW = 256
        NK = DFF // 128
        WSZ = NK * CW
        wrot = self.sbrot("ffn_w", 3, [128, WSZ], BF16)
        psa = self.psrot("ffn_pa", 2, [128, 512])
        psb = self.psrot("ffn_pb", 2, [128, 512])
        sil = self.sbrot("ffn_sil", 2, [128, 512])
        xo = self.sbrot("ffn_xo", 3, [128, CW])
        xt = self.sbrot("ffn_xt", 3, [128, CW])
        blocks = self.blocks()
        w_in_v = w_in.ap[l].rearrange("(kc p) (two f) -> p two kc f", p=128, two=2)
        w_out_v = w_out.ap[l].rearrange("(kc p) n -> p kc n", p=128)

        def ld_in(cb):
            def f(buf):
                dst = buf.ap[:, 0:2 * KC * CW].rearrange("p (two kc n) -> p two kc n", two=2, kc=KC)
                fw.dma("pool", dst, w_in_v[:, :, :, cb * CW:(cb + 1) * CW], reads=[w_in], writes=[buf])
                return dst
            return f

        def ld_out(cb):
            def f(buf):
                dst = buf.ap[:, 0:NK * CW].rearrange("p (kc n) -> p kc n", kc=NK)
                fw.dma("pool", dst, w_out_v[:, :, cb * CW:(cb + 1) * CW], reads=[w_out], writes=[buf])
                return dst
            return f

        loaders = []
        for _ in blocks:
            loaders += [ld_in(cb) for cb in range(DFF // CW)]
            loaders += [ld_out(cb) for cb in range(D // CW)]
        ws = WStream(wrot, loaders)
        for (t0, tb) in blocks:
            g = self.group_of(t0)
            self.prep_block(t0, tb, hT)
            for cb in range(DFF // CW):
                w, wv = ws.next()
                for j in range(CW // 128):
                    fc = cb * (CW // 128) + j
                    for ts in range(0, tb, 512):
                        n = min(512, tb - ts)
                        pa, pb = psa.next(), psb.next()
                        for (pp, two) in ((pa, 0), (pb, 1)):
                            for kc in range(KC):
                                fw.op("pe", lambda e: e.matmul(pp.ap[:, 0:n], lhsT=wv[:, two, kc, j * 128:(j + 1) * 128],
                                                               rhs=hT.ap[:, kc, ts:ts + n], start=(kc == 0), stop=(kc == KC - 1)),
                                      reads=[w, hT], writes=[pp])
                        s_ = sil.next()
                        fw.op("act", lambda e: e.activation(out=s_.ap[:, 0:n], in_=pa.ap[:, 0:n], func=AF.Silu),
                              reads=[pa], writes=[s_])
                        fw.op("dve", lambda e: e.tensor_tensor(out=uT.ap[:, fc, ts:ts + n], in0=pb.ap[:, 0:n],
                                                               in1=s_.ap[:, 0:n], op=ALU.mult),
                              reads=[pb, s_], writes=[uT])
            for cb in range(D // CW):
                w, w2 = ws.next()
                for s in range(tb // 128):
                    X = self.Xt[(t0 + s * 128) // 128]
                    p = psa.next()
                    xold = xo.next()
                    fw.dma("sp", xold.ap[:], X.ap[:, cb * CW:(cb + 1) * CW], reads=[X], writes=[xold])
                    for kc in range(NK):
                        fw.op("pe", lambda e: e.matmul(p.ap[:, 0:CW], lhsT=uT.ap[:, kc, s * 128:(s + 1) * 128],
                                                       rhs=w2[:, kc, :], start=(kc == 0), stop=(kc == NK - 1)),
                              reads=[w, uT], writes=[p])
                    xn = xt.next()
                    fw.op("dve", lambda e: e.tensor_tensor(out=xn.ap[:], in0=p.ap[:, 0:CW],
                                                           in1=grow.ap[:, g * D + cb * CW: g * D + (cb + 1) * CW], op=ALU.mult),
                          reads=[p, grow], writes=[xn])
                    fw.op("dve", lambda e: e.tensor_tensor(out=xn.ap[:], in0=xn.ap[:], in1=xold.ap[:], op=ALU.add),
                          reads=[xn, xold], writes=[xn])
                    fw.dma("sp", X.ap[:, cb * CW:(cb + 1) * CW], xn.ap[:], reads=[xn], writes=[X])
        self.phase_end()

    def out_proj(self, l, AT, W_buf, W_ap, gate_sect):
        fw = self.fw
        self.phase_begin()
        TB = self.TB
        CW = 512
        grow = self.gate_rows(l, gate_sect, "op_g")
        aT = self.sbrot("op_aT", 2, [128, KC, TB], BF16)
        wrot = self.sbrot("op_w", 3, [128, KC * CW], BF16)
        psr = self.psrot("op_ps", 4, [128, 512])
        xo = self.sbrot("op_xo", 3, [128, CW])
        xt = self.sbrot("op_xt", 3, [128, CW])
        blocks = self.blocks()
        Wv = W_ap.rearrange("(kc p) n -> p kc n", p=128)

        def ld(cb):
            def f(buf):
                dst = buf.ap[:].rearrange("p (kc n) -> p kc n", kc=KC)
                fw.dma("pool", dst, Wv[:, :, cb * CW:(cb + 1) * CW], reads=[W_buf], writes=[buf])
                return dst
            return f
        loaders = []
        for _ in blocks:
            loaders += [ld(cb) for cb in range(D // CW)]
        ws = WStream(wrot, loaders)
        ATv = AT.ap.rearrange("(kc p) r -> p kc r", p=128)
        for (t0, tb) in blocks:
            g = self.group_of(t0)
            a = aT.next()
            fw.dma("sp", a.ap[:, :, 0:tb], ATv[:, :, t0:t0 + tb], reads=[AT], writes=[a])
            for cb in range(D // CW):
                w, wv = ws.next()
                for s_ in range(tb // 128):
                    X = self.Xt[(t0 + s_ * 128) // 128]
                    p = psr.next()
                    xold = xo.next()
                    fw.dma("sp", xold.ap[:], X.ap[:, cb * CW:(cb + 1) * CW], reads=[X], writes=[xold])
                    for kc in range(KC):
                        fw.op("pe", lambda e: e.matmul(p.ap[:, 0:CW], lhsT=a.ap[:, kc, s_ * 128:(s_ + 1) * 128],
                                                       rhs=wv[:, kc, :], start=(kc == 0), stop=(kc == KC - 1)),
                              reads=[w, a], writes=[p])
                    xn = xt.next()
                    fw.op("dve", lambda e: e.tensor_tensor(out=xn.ap[:], in0=p.ap[:, 0:CW],
                                                           in1=grow.ap[:, g * D + cb * CW: g * D + (cb + 1) * CW], op=ALU.mult),
                          reads=[p, grow], writes=[xn])
                    fw.op("dve", lambda e: e.tensor_tensor(out=xn.ap[:], in0=xn.ap[:], in1=xold.ap[:], op=ALU.add),
                          reads=[xn, xold], writes=[xn])
                    fw.dma("sp", X.ap[:, cb * CW:(cb + 1) * CW], xn.ap[:], reads=[xn], writes=[X])
        self.phase_end()

    def mla(self, l):
        fw = self.fw
        j = l // 3
        NKT = NS + 256 + 2 * NP
        H, DN, DR, DV = 16, 128, 64, 128
        QnT = self.scratch("ml_QnT", [H, 128, R], BF16)
        QrT = self.scratch("ml_QrT", [H, 64, R], BF16)
        KnT = self.scratch("ml_KnT", [H, 128, NKT], BF16)
        KrT = self.scratch("ml_KrT", [64, NKT], BF16)
        Vd = self.scratch("ml_V", [NKT, H * DV], BF16)
        OT = self.scratch("ml_OT", [D, R], BF16)
        w_down, w_uq, w_ukv = self.din["ml_w_down"], self.din["ml_w_uq"], self.din["ml_w_ukv"]
        ckv_out, kr_out = self.dout["ckv"], self.dout["krope"]

        self.phase_begin()
        TB = 256
        self.prep_setup(l, 0)
        Wd = self.sb("ml_Wd", [128, KC, 1088], BF16)
        Wq = self.sb("ml_Wq", [128, 4, 3072], BF16)
        Wkv = self.sb("ml_Wkv", [128, 4, 4096], BF16)
        fw.dma("pool", Wd.ap[:], w_down.ap[j].rearrange("(kc p) n -> p kc n", p=128), reads=[w_down], writes=[Wd])
        fw.dma("pool", Wq.ap[:], w_uq.ap[j].rearrange("(kc p) n -> p kc n", p=128), reads=[w_uq], writes=[Wq])
        fw.dma("pool", Wkv.ap[:], w_ukv.ap[j].rearrange("(kc p) n -> p kc n", p=128), reads=[w_ukv], writes=[Wkv])
        qnw = self.sb("ml_qnw", [128, 512])
        kvnw = self.sb("ml_kvnw", [128, 512])
        fw.dma("sp", qnw.ap[:], self.din["ml_qnorm_w"].ap[j:j + 1, :].partition_broadcast(128),
               reads=[self.din["ml_qnorm_w"]], writes=[qnw])
        fw.dma("sp", kvnw.ap[:], self.din["ml_kvnorm_w"].ap[j:j + 1, :].partition_broadcast(128),
               reads=[self.din["ml_kvnorm_w"]], writes=[kvnw])
        hT = self.sb("ml_hT", [128, KC, TB], BF16)
        cqnT = self.sb("ml_cqnT", [128, 4, TB], BF16)
        ckvnT = self.sb("ml_ckvnT", [128, 4, TB], BF16)
        krTt = self.sb("ml_krT", [64, TB], BF16)
        QnTb = self.sb("ml_QnTb", [128, H, TB], BF16)
        QrTb = self.sb("ml_QrTb", [64, H, TB], BF16)
        KnTb = self.sb("ml_KnTb", [128, H, TB], BF16)
        pA = self.psrot("ml_pA", 2, [128, 512])
        pkr = self.ps("ml_pkr", [128, 64])
        ptb = self.ps("ml_ptb", [128, 1024], BF16)
        ptf = self.ps("ml_ptf", [128, 512])
        cqn = self.sbrot("ml_cqn", 2, [128, 512], BF16)
        ckvn = self.sbrot("ml_ckvn", 2, [128, 512])
        krt = self.sbrot("ml_kr", 2, [128, 64])
        krr = self.sbrot("ml_krr", 2, [128, 64])
        tmp1 = self.sbrot("ml_t1", 2, [128, 2, 64])
        tmp2 = self.sbrot("ml_t2", 2, [128, 2, 64])
        cosr = self.sbrot("ml_cos", 2, [128, 64])
        sinr = self.sbrot("ml_sin", 2, [128, 64])
        qb16 = self.sbrot("ml_qb", 2, [128, 384], BF16)
        vtm = self.sbrot("ml_vtm", 2, [128, H * DV], BF16)
        sts = self.sbrot("ml_st", 4, [128, 4])
        rope_cos, rope_sin = self.din["rope_cos"], self.din["rope_sin"]

        def rope(dst3, src3, cs, sn, nh, t1, t2):
            csb = cs.ap[:].unsqueeze(1).to_broadcast([128, nh, 64])
            fw.op("dve", lambda e: e.tensor_tensor(out=t1.ap[:, 0:nh, :], in0=src3, in1=csb, op=ALU.mult),
                  reads=rd_src + [cs], writes=[t1])
            for rc in range(2):
                for hf in range(2):
                    o0 = rc * 32 + hf * 16
                    o1 = rc * 32 + (1 - hf) * 16
                    snb = sn.ap[:, o0:o0 + 16].unsqueeze(1).to_broadcast([128, nh, 16])
                    fw.op("dve", lambda e: e.tensor_tensor(out=t2.ap[:, 0:nh, o0:o0 + 16], in0=src3[:, :, o1:o1 + 16],
                                                           in1=snb, op=ALU.mult),
                          reads=rd_src + [sn], writes=[t2])
            fw.op("dve", lambda e: e.tensor_tensor(out=dst3, in0=t1.ap[:, 0:nh, :], in1=t2.ap[:, 0:nh, :], op=ALU.add),
                  reads=[t1, t2], writes=wr_dst)

        def kv_from_ckvnT(ncols, kcol0):
            for h in range(H):
                p = pA.next()
                for kc in range(4):
                    fw.op("pe", lambda e: e.matmul(p.ap[:, 0:ncols], lhsT=Wkv.ap[:, kc, h * 256:h * 256 + 128],
                                                   rhs=ckvnT.ap[:, kc, 0:ncols], start=(kc == 0), stop=(kc == 3)),
                          reads=[Wkv, ckvnT], writes=[p])
                eng = "act" if h % 2 == 0 else "dve"
                if eng == "act":
                    fw.op("act", lambda e: e.copy(out=KnTb.ap[:, h, 0:ncols], in_=p.ap[:, 0:ncols]), reads=[p], writes=[KnTb])
                else:
                    fw.op("dve", lambda e: e.tensor_copy(out=KnTb.ap[:, h, 0:ncols], in_=p.ap[:, 0:ncols]), reads=[p], writes=[KnTb])
            fw.dma("sp", KnT.ap[:, :, kcol0:kcol0 + ncols].rearrange("h p c -> p h c"), KnTb.ap[:, :, 0:ncols],
                   reads=[KnTb], writes=[KnT])
            Wv4 = Wkv.ap[:].rearrange("p kc (h two d) -> p kc h two d", h=H, two=2)
            for s_ in range(ncols // 128):
                v = vtm.next()
                for hg in range(4):
                    p = pA.next()
                    for kc in range(4):
                        fw.op("pe", lambda e: e.matmul(p.ap[:].rearrange("p (h d) -> p h d", h=4),
                                                       lhsT=ckvnT.ap[:, kc, s_ * 128:(s_ + 1) * 128],
                                                       rhs=Wv4[:, kc, hg * 4:(hg + 1) * 4, 1, :], start=(kc == 0), stop=(kc == 3)),
                              reads=[Wkv, ckvnT], writes=[p])
                    if hg % 2 == 0:
                        fw.op("act", lambda e: e.copy(out=v.ap[:, hg * 512:(hg + 1) * 512], in_=p.ap[:]), reads=[p], writes=[v])
                    else:
                        fw.op("dve", lambda e: e.tensor_copy(out=v.ap[:, hg * 512:(hg + 1) * 512], in_=p.ap[:]), reads=[p], writes=[v])
                fw.dma("sp", Vd.ap[kcol0 + s_ * 128: kcol0 + (s_ + 1) * 128, :], v.ap[:], reads=[v], writes=[Vd])

        def ckvn_to_T(src, s_):
            for kc in range(4):
                fw.op("pe", lambda e: e.transpose(ptf.ap[:, kc * 128:(kc + 1) * 128], src.ap[:, kc * 128:(kc + 1) * 128],
                                                  self.identf.ap[:]), reads=[src, self.identf], writes=[ptf])
            fw.op("act", lambda e: e.copy(out=ckvnT.ap[:, :, s_ * 128:(s_ + 1) * 128],
                                          in_=ptf.ap[:].rearrange("p (kc t) -> p kc t", kc=4)), reads=[ptf], writes=[ckvnT])

        def kr_to_T(src, s_):
            fw.op("pe", lambda e: e.transpose(ptf.ap[0:64, 0:128], src.ap[:, 0:64], self.identf.ap[:]),
                  reads=[src, self.identf], writes=[ptf])
            fw.op("dve", lambda e: e.tensor_copy(out=krTt.ap[:, s_ * 128:(s_ + 1) * 128], in_=ptf.ap[0:64, 0:128]),
                  reads=[ptf], writes=[krTt])

        if self.stop == "mla0":
            self.phase_end()
            return
        for (t0, tb) in self.blocks(TB):
            g = self.group_of(t0)
            kcol0 = t0 if g == 0 else t0 + 256
            self.prep_block(t0, tb, hT)
            if self.stop == "mla0b":
                break
            for s_ in range(tb // 128):
                r0 = t0 + s_ * 128
                pq, pkv = pA.next(), pA.next()
                for kc in range(KC):
                    lhs = hT.ap[:, kc, s_ * 128:(s_ + 1) * 128]
                    fw.op("pe", lambda e: e.matmul(pq.ap[:], lhsT=lhs, rhs=Wd.ap[:, kc, 0:512], start=(kc == 0), stop=(kc == KC - 1)),
                          reads=[hT, Wd], writes=[pq])
                    fw.op("pe", lambda e: e.matmul(pkv.ap[:], lhsT=lhs, rhs=Wd.ap[:, kc, 512:1024], start=(kc == 0), stop=(kc == KC - 1)),
                          reads=[hT, Wd], writes=[pkv])
                    fw.op("pe", lambda e: e.matmul(pkr.ap[:], lhsT=lhs, rhs=Wd.ap[:, kc, 1024:1088], start=(kc == 0), stop=(kc == KC - 1)),
                          reads=[hT, Wd], writes=[pkr])
                st = sts.next()
                rq = self.rstd_of(pq, st, width=512)
                cq = cqn.next()
                fw.op("dve", lambda e: e.scalar_tensor_tensor(out=cq.ap[:], in0=pq.ap[:], scalar=rq, in1=qnw.ap[:],
                                                              op0=ALU.mult, op1=ALU.mult), reads=[pq, st, qnw], writes=[cq])
                st2 = sts.next()
                rkv = self.rstd_of(pkv, st2, width=512)
                ck = ckvn.next()
                fw.op("dve", lambda e: e.scalar_tensor_tensor(out=ck.ap[:], in0=pkv.ap[:], scalar=rkv, in1=kvnw.ap[:],
                                                              op0=ALU.mult, op1=ALU.mult), reads=[pkv, st2, kvnw], writes=[ck])
                kr = krt.next()
                fw.op("act", lambda e: e.copy(out=kr.ap[:], in_=pkr.ap[:]), reads=[pkr], writes=[kr])
                if g == 1:
                    pr = r0 - NS
                    fw.dma("sp", ckv_out.ap[pr:pr + 128, :], ck.ap[:], reads=[ck], writes=[ckv_out])
                    fw.dma("sp", kr_out.ap[pr:pr + 128, :], kr.ap[:], reads=[kr], writes=[kr_out])
                    krsrc = kr
                else:
                    cs, sn = cosr.next(), sinr.next()
                    fw.dma("sp", cs.ap[:], rope_cos.ap[r0:r0 + 128, :], reads=[rope_cos], writes=[cs])
                    fw.dma("sp", sn.ap[:], rope_sin.ap[r0:r0 + 128, :], reads=[rope_sin], writes=[sn])
                    krsrc = krr.next()
                    rd_src, wr_dst = [kr], [krsrc]
                    rope(krsrc.ap[:].rearrange("p (o d) -> p o d", o=1), kr.ap[:].rearrange("p (o d) -> p o d", o=1),
                         cs, sn, 1, tmp1.next(), tmp2.next())
                if self.stop == "mla0c":
                    continue
                kr_to_T(krsrc, s_)
                ckvn_to_T(ck, s_)
                if self.stop == "mla0d":
                    continue
                for kc in range(4):
                    fw.op("pe", lambda e: e.transpose(ptb.ap[:, kc * 128:(kc + 1) * 128], cq.ap[:, kc * 128:(kc + 1) * 128],
                                                      self.identb.ap[:]), reads=[cq, self.identb], writes=[ptb])
                fw.op("act", lambda e: e.copy(out=cqnT.ap[:, :, s_ * 128:(s_ + 1) * 128],
                                              in_=ptb.ap[:, 0:512].rearrange("p (kc t) -> p kc t", kc=4)),
                      reads=[ptb], writes=[cqnT])
                if self.stop == "mla0e1":
                    continue
                for hp in range(H // 2):
                    p = pA.next()
                    for kc in range(4):
                        fw.op("pe", lambda e: e.matmul(p.ap[:, 0:384], lhsT=cqnT.ap[:, kc, s_ * 128:(s_ + 1) * 128],
                                                       rhs=Wq.ap[:, kc, hp * 384:(hp + 1) * 384], start=(kc == 0), stop=(kc == 3)),
                              reads=[cqnT, Wq], writes=[p])
                    q16 = qb16.next()
                    pv = p.ap[:, 0:384].rearrange("p (h d) -> p h d", h=2)
                    qv = q16.ap[:].rearrange("p (h d) -> p h d", h=2)
                    fw.op("act", lambda e: e.copy(out=qv[:, :, 0:128], in_=pv[:, :, 0:128]), reads=[p], writes=[q16])
                    if self.stop == "mla0e2":
                        continue
                    if g == 1 or self.stop == "mla0e4":
                        fw.op("act", lambda e: e.copy(out=qv[:, :, 128:192], in_=pv[:, :, 128:192]), reads=[p], writes=[q16])
                    else:
                        rd_src, wr_dst = [p, q16], [q16]
                        rope(qv[:, :, 128:192], pv[:, :, 128:192], cs, sn, 2, tmp1.next(), tmp2.next())
                    if self.stop == "mla0e3":
                        continue
                    for hh in range(2):
                        h = hp * 2 + hh
                        fw.op("pe", lambda e: e.transpose(ptb.ap[:, 512 + hh * 256: 512 + hh * 256 + 128], qv[:, hh, 0:128],
                                                          self.identb.ap[:]), reads=[q16, self.identb], writes=[ptb])
                        fw.op("pe", lambda e: e.transpose(ptb.ap[0:64, 512 + hh * 256 + 128: 512 + hh * 256 + 256], qv[:, hh, 128:192],
                                                          self.identb.ap[:]), reads=[q16, self.identb], writes=[ptb])
                    tv = ptb.ap[:, 512:1024].rearrange("p (h x) -> p h x", h=2)
                    fw.op("act", lambda e: e.copy(out=QnTb.ap[:, hp * 2:hp * 2 + 2, s_ * 128:(s_ + 1) * 128], in_=tv[:, :, 0:128]),
                          reads=[ptb], writes=[QnTb])
                    fw.op("act", lambda e: e.copy(out=QrTb.ap[:, hp * 2:hp * 2 + 2, s_ * 128:(s_ + 1) * 128], in_=tv[0:64, :, 128:256]),
                          reads=[ptb], writes=[QrTb])
            if self.stop in ("mla0c", "mla0d"):
                continue
            if self.stop in ("mla0e", "mla0e1", "mla0e2", "mla0e3", "mla0e4"):
                break
            fw.dma("sp", QnT.ap[:, :, t0:t0 + tb].rearrange("h p c -> p h c"), QnTb.ap[:, :, 0:tb], reads=[QnTb], writes=[QnT])
            fw.dma("sp", QrT.ap[:, :, t0:t0 + tb].rearrange("h p c -> p h c"), QrTb.ap[:, :, 0:tb], reads=[QrTb], writes=[QrT])
            fw.dma("sp", KrT.ap[:, kcol0:kcol0 + tb], krTt.ap[:, 0:tb], reads=[krTt], writes=[KrT])
            kv_from_ckvnT(tb, kcol0)
        if self.stop in ("mla0b", "mla0c", "mla0d", "mla0e", "mla0e1", "mla0e2", "mla0e3", "mla0e4"):
            self.phase_end()
            return
        cck, ckr = self.din["cache_ckv"], self.din["cache_krope"]
        for s_ in range(2):
            ck = ckvn.next()
            fw.dma("sp", ck.ap[:], cck.ap[j, s_ * 128:(s_ + 1) * 128, :], reads=[cck], writes=[ck])
            ckvn_to_T(ck, s_)
            kr = krt.next()
            fw.dma("sp", kr.ap[:], ckr.ap[j, s_ * 128:(s_ + 1) * 128, :], reads=[ckr], writes=[kr])
            kr_to_T(kr, s_)
        fw.dma("sp", KrT.ap[:, NS:NS + 256], krTt.ap[:, 0:256], reads=[krTt], writes=[KrT])
        kv_from_ckvnT(256, NS)
        self.phase_end()

        if self.stop == "mla1":
            return
        self.phase_begin()
        SCALE = 1.0 / math.sqrt(192.0)
        ones = self.sb("at_ones", [128, 128], BF16)
        fw.op("pool", lambda e: e.memset(ones.ap[:], 1.0), writes=[ones])
        NKmax = NS + 256
        Kn_r = self.sbrot("at_Kn", 2, [128, NKmax], BF16)
        V_r = self.sbrot("at_V", 2, [128, NKmax // 128, 128], BF16)
        Qn_r = self.sbrot("at_Qn", 2, [128, NS], BF16)
        Qr_r = self.sbrot("at_Qr", 2, [64, NS], BF16)
        Kr_t = self.sb("at_Kr", [64, NKmax], BF16)
        P_r = self.sbrot("at_P", 3, [128, 512], BF16)
        rd_r = self.sbrot("at_rd", 2, [128, 512])
        o_r = self.sbrot("at_o", 2, [128, 512], BF16)
        ps_s = self.psrot("at_pss", 3, [128, 512])
        ps_o = self.psrot("at_pso", 2, [128, 512])
        ps_d = self.psrot("at_psd", 2, [128, 512])
        seqs = [(0, NS, 0, NS + 256), (NS, NP, NS + 256, NP), (NS + NP, NP, NS + 256 + NP, NP)]
        for (q0, nq, k0, nk) in seqs:
            fw.dma("sp", Kr_t.ap[:, 0:nk], KrT.ap[:, k0:k0 + nk], reads=[KrT], writes=[Kr_t])
            nkt = nk // 128
            for h in range(H):
                Kn, V, Qn, Qr = Kn_r.next(), V_r.next(), Qn_r.next(), Qr_r.next()
                fw.dma("sp", Kn.ap[:, 0:nk], KnT.ap[h, :, k0:k0 + nk], reads=[KnT], writes=[Kn])
                fw.dma("sp", V.ap[:, 0:nkt, :], Vd.ap[k0:k0 + nk, h * 128:(h + 1) * 128].rearrange("(kt p) d -> p kt d", p=128),
                       reads=[Vd], writes=[V])
                fw.dma("sp", Qn.ap[:, 0:nq], QnT.ap[h, :, q0:q0 + nq], reads=[QnT], writes=[Qn])
                fw.dma("sp", Qr.ap[:, 0:nq], QrT.ap[h, :, q0:q0 + nq], reads=[QrT], writes=[Qr])
                for qb in range(0, nq, 512):
                    n = min(512, nq - qb)
                    po, pd = ps_o.next(), ps_d.next()
                    for kt in range(nkt):
                        pss = ps_s.next()
                        fw.op("pe", lambda e: e.matmul(pss.ap[:, 0:n], lhsT=Kn.ap[:, kt * 128:(kt + 1) * 128], rhs=Qn.ap[:, qb:qb + n],
                                                       start=True, stop=False), reads=[Kn, Qn], writes=[pss])
                        fw.op("pe", lambda e: e.matmul(pss.ap[:, 0:n], lhsT=Kr_t.ap[:, kt * 128:(kt + 1) * 128], rhs=Qr.ap[:, qb:qb + n],
                                                       start=False, stop=True), reads=[Kr_t, Qr], writes=[pss])
                        P = P_r.next()
                        fw.op("act", lambda e: e.activation(out=P.ap[:, 0:n], in_=pss.ap[:, 0:n], func=AF.Exp, scale=SCALE),
                              reads=[pss], writes=[P])
                        fw.op("pe", lambda e: e.matmul(po.ap[:, 0:n], lhsT=V.ap[:, kt, :], rhs=P.ap[:, 0:n],
                                                       start=(kt == 0), stop=(kt == nkt - 1)), reads=[V, P], writes=[po])
                        fw.op("pe", lambda e: e.matmul(pd.ap[:, 0:n], lhsT=ones.ap[:], rhs=P.ap[:, 0:n],
                                                       start=(kt == 0), stop=(kt == nkt - 1)), reads=[ones, P], writes=[pd])
                    rd = rd_r.next()
                    fw.op("dve", lambda e: e.reciprocal(out=rd.ap[:, 0:n], in_=pd.ap[:, 0:n]), reads=[pd], writes=[rd])
                    o = o_r.next()
                    fw.op("dve", lambda e: e.tensor_tensor(out=o.ap[:, 0:n], in0=po.ap[:, 0:n], in1=rd.ap[:, 0:n], op=ALU.mult),
                          reads=[po, rd], writes=[o])
                    fw.dma("sp", OT.ap[h * 128:(h + 1) * 128, q0 + qb:q0 + qb + n], o.ap[:, 0:n], reads=[o], writes=[OT])
        self.phase_end()
        if self.stop == "mla2":
            return
        self.out_proj(l, OT, self.din["ml_wo"], self.din["ml_wo"].ap[j], 2)

    def hgrn(self, l):
        fw = self.fw
        j = l // 3
        H, C = 16, 32
        w_in = self.din["hg_w_in"]
        Qd = [self.scratch("hg_Q%d" % d, [D, R], BF16) for d in range(2)]
        Kd = [self.scratch("hg_K%d" % d, [D, R], BF16) for d in range(2)]
        Khd = [self.scratch("hg_Kh%d" % d, [R, D], BF16) for d in range(2)]
        WCd = [self.scratch("hg_WC%d" % d, [128, H, R // C]) for d in range(2)]
        IVd = self.scratch("hg_IV", [R, D], BF16)
        GSd = self.scratch("hg_GS", [R, D], BF16)
        Od = [self.scratch("hg_O%d" % d, [R, D]) for d in range(2)]
        OT = self.scratch("hg_OT", [D, R], BF16)

        self.phase_begin()
        TB = 256
        NCH = TB // C
        self.prep_setup(l, 0)
        hlb = self.din["hg_lb"]
        stage = self.sb("hg_stage", [128, 128])
        pst = self.ps("hg_pst", [128, 128])
        lbr = self.sb("hg_lbraw", [128, 128])
        self.colvecs([(hlb, hlb.ap[i]) for i in range(DEPTH)], lbr, pst, stage, self.identf)
        fw.op("act", lambda e: e.activation(out=lbr.ap[:, 0:64], in_=lbr.ap[:, 0:64], func=AF.Exp), reads=[lbr], writes=[lbr])
        lbc = self.sb("hg_lbc", [128, 64])
        fw.op("dve", lambda e: e.tensor_tensor(out=lbc.ap[:, 32:48], in0=lbr.ap[:, 0:16], in1=lbr.ap[:, 16:32], op=ALU.add), reads=[lbr], writes=[lbc])
        fw.op("dve", lambda e: e.tensor_tensor(out=lbc.ap[:, 48:64], in0=lbr.ap[:, 32:48], in1=lbr.ap[:, 48:64], op=ALU.add), reads=[lbr], writes=[lbc])
        fw.op("dve", lambda e: e.tensor_tensor(out=lbc.ap[:, 32:48], in0=lbc.ap[:, 32:48], in1=lbc.ap[:, 48:64], op=ALU.add), reads=[lbc], writes=[lbc])
        fw.op("dve", lambda e: e.reciprocal(out=lbc.ap[:, 32:48], in_=lbc.ap[:, 32:48]), reads=[lbc], writes=[lbc])
        fw.op("dve", lambda e: e.memset(lbc.ap[:, 48:64], 0.0), reads=[lbc], writes=[lbc])
        for i in range(1, l + 1):
            fw.op("dve", lambda e: e.tensor_tensor(out=lbc.ap[:, 48:64], in0=lbc.ap[:, 48:64], in1=lbr.ap[:, i * 16:(i + 1) * 16], op=ALU.add),
                  reads=[lbc, lbr], writes=[lbc])
        fw.op("dve", lambda e: e.tensor_tensor(out=lbc.ap[:, 0:16], in0=lbc.ap[:, 48:64], in1=lbc.ap[:, 32:48], op=ALU.mult), reads=[lbc], writes=[lbc])
        fw.op("dve", lambda e: e.tensor_scalar(out=lbc.ap[:, 16:32], in0=lbc.ap[:, 0:16], scalar1=-1.0, scalar2=1.0, op0=ALU.mult, op1=ALU.add),
              reads=[lbc], writes=[lbc])
        msk = self.sb("hg_rst", [128, TB])
        fw.op("pool", lambda e: e.memset(msk.ap[:], 1.0), writes=[msk])
        fw.op("pool", lambda e: e.memset(msk.ap[:].rearrange("p (c t) -> p c t", t=C)[:, :, 0:1], 0.0), reads=[msk], writes=[msk])
        hT = self.sb("hg_hT", [128, KC, TB], BF16)
        wrot = self.sbrot("hg_w", 3, [128, KC * 512], BF16)
        hrot = Rot([[self.sb("hg_wh%d_%d" % (i, sec), [128, KC, 128], BF16) for sec in range(3)] for i in range(3)])
        p3 = [self.psrot("hg_p%d" % i, 1, [128, 512]) for i in range(3)]
        ptb = self.ps("hg_ptb", [128, 1024], BF16)
        qs_r = self.sbrot("hg_qs", 3, [128, TB])
        f_r = self.sbrot("hg_f", 4, [128, TB])
        gl_r = self.sbrot("hg_gl", 4, [128, TB])
        kk_r = self.sbrot("hg_kk", 4, [128, TB])
        F_r = self.sbrot("hg_F", 4, [128, TB])
        G_r = self.sbrot("hg_G", 4, [128, TB])
        Gr_r = self.sbrot("hg_Gr", 4, [128, TB])
        e_r = self.sbrot("hg_e", 9, [128, TB])
        o16 = self.sbrot("hg_o16", 10, [128, TB], BF16)
        kh_r = self.sbrot("hg_kh", 4, [128, TB], BF16)
        Kht = self.sb("hg_Kht", [128, TB // 128, 2, D], BF16)
        WCt = self.sb("hg_WCt", [128, 2, H, NCH])
        tmo = self.sbrot("hg_tmo", 2, [128, D], BF16)
        blocks = self.blocks(TB)
        w3 = w_in.ap[j].rearrange("(kc p) (sec f) -> p sec kc f", p=128, sec=5)
        wtm = w_in.ap[j].rearrange("(kc p) n -> p kc n", p=128)

        def ld_head(h):
            def f(grp):
                for sec in range(3):
                    fw.dma("pool", grp[sec].ap[:], w3[:, sec, :, h * 128:(h + 1) * 128], reads=[w_in], writes=[grp[sec]])
                return grp
            return f

        def ld_tm(c0):
            def f(buf):
                dst = buf.ap[:].rearrange("p (kc n) -> p kc n", kc=KC)
                fw.dma("pool", dst, wtm[:, :, c0:c0 + 512], reads=[w_in], writes=[buf])
                return dst
            return f
        lh, lt = [], []
        for _ in blocks:
            lh += [ld_head(h) for h in range(H)]
            lt += [ld_tm(3 * D + cb * 512) for cb in range(8)]
        wsH = WStream(hrot, lh)
        ws = WStream(wrot, lt)
        for (t0, tb) in blocks:
            self.prep_block(t0, tb, hT)
            nch = tb // C
            ch0 = t0 // C
            for h in range(H):
                grp, _ = wsH.next()
                pz = [p3[i].next() for i in range(3)]
                for sec in range(3):
                    for kc in range(KC):
                        fw.op("pe", lambda e: e.matmul(pz[sec].ap[:, 0:tb], lhsT=grp[sec].ap[:, kc, :], rhs=hT.ap[:, kc, 0:tb],
                                                       start=(kc == 0), stop=(kc == KC - 1)), reads=[grp[sec], hT], writes=[pz[sec]])
                qs = qs_r.next()
                fw.op("act", lambda e: e.activation(out=qs.ap[:, 0:tb], in_=pz[0].ap[:, 0:tb], func=AF.Silu), reads=[pz[0]], writes=[qs])
                for d in range(2):
                    f_, gl, kk, F, G, Gr = f_r.next(), gl_r.next(), kk_r.next(), F_r.next(), G_r.next(), Gr_r.next()
                    fw.op("act", lambda e: e.activation(out=f_.ap[:, 0:tb], in_=pz[1 + d].ap[:, 0:tb], func=AF.Sigmoid), reads=[pz[1 + d]], writes=[f_])
                    fw.op("dve", lambda e: e.tensor_scalar(out=f_.ap[:, 0:tb], in0=f_.ap[:, 0:tb], scalar1=lbc.ap[:, 16 + h:17 + h],
                                                           scalar2=lbc.ap[:, h:h + 1], op0=ALU.mult, op1=ALU.add), reads=[f_, lbc], writes=[f_])
                    fw.op("act", lambda e: e.activation(out=gl.ap[:, 0:tb], in_=f_.ap[:, 0:tb], func=AF.Ln), reads=[f_], writes=[gl])
                    fw.op("dve", lambda e: e.tensor_scalar(out=kk.ap[:, 0:tb], in0=f_.ap[:, 0:tb], scalar1=-1.0, scalar2=1.0,
                                                           op0=ALU.mult, op1=ALU.add), reads=[f_], writes=[kk])
                    fw.op("dve", lambda e: e.tensor_tensor_scan(out=F.ap[:, 0:tb], data0=msk.ap[:, 0:tb], data1=gl.ap[:, 0:tb],
                                                                initial=0.0, op0=ALU.mult, op1=ALU.add), reads=[msk, gl], writes=[F])
                    F3 = F.ap[:, 0:tb].rearrange("p (c t) -> p c t", t=C)
                    tot = F3[:, :, C - 1:C]
                    if d == 0:
                        Gt = F
                        fw.op("dve", lambda e: e.tensor_tensor(out=Gr.ap[:, 0:tb].rearrange("p (c t) -> p c t", t=C),
                                                               in0=tot.to_broadcast([128, nch, C]), in1=F3, op=ALU.subtract),
                              reads=[F], writes=[Gr])
                    else:
                        fw.op("dve", lambda e: e.tensor_tensor(out=Gr.ap[:, 0:tb], in0=F.ap[:, 0:tb], in1=gl.ap[:, 0:tb], op=ALU.subtract),
                              reads=[F, gl], writes=[Gr])
                        fw.op("dve", lambda e: e.tensor_tensor(out=G.ap[:, 0:tb].rearrange("p (c t) -> p c t", t=C),
                                                               in0=tot.to_broadcast([128, nch, C]),
                                                               in1=Gr.ap[:, 0:tb].rearrange("p (c t) -> p c t", t=C), op=ALU.subtract),
                              reads=[F, Gr], writes=[G])
                        Gt = G
                    fw.op("act", lambda e: e.activation(out=WCt.ap[:, d, h, 0:nch].unsqueeze(2), in_=tot, func=AF.Exp), reads=[F], writes=[WCt])
                    eG, enG, eR = e_r.next(), e_r.next(), e_r.next()
                    fw.op("act", lambda e: e.activation(out=eG.ap[:, 0:tb], in_=Gt.ap[:, 0:tb], func=AF.Exp), reads=[Gt], writes=[eG])
                    fw.op("act", lambda e: e.activation(out=enG.ap[:, 0:tb], in_=Gt.ap[:, 0:tb], func=AF.Exp, scale=-1.0), reads=[Gt], writes=[enG])
                    fw.op("act", lambda e: e.activation(out=eR.ap[:, 0:tb], in_=Gr.ap[:, 0:tb], func=AF.Exp), reads=[Gr], writes=[eR])
                    qt, kt, kh = o16.next(), o16.next(), kh_r.next()
                    fw.op("dve", lambda e: e.tensor_tensor(out=qt.ap[:, 0:tb], in0=qs.ap[:, 0:tb], in1=eG.ap[:, 0:tb], op=ALU.mult), reads=[qs, eG], writes=[qt])
                    fw.op("dve", lambda e: e.tensor_tensor(out=kt.ap[:, 0:tb], in0=kk.ap[:, 0:tb], in1=enG.ap[:, 0:tb], op=ALU.mult), reads=[kk, enG], writes=[kt])
                    fw.op("dve", lambda e: e.tensor_tensor(out=kh.ap[:, 0:tb], in0=kk.ap[:, 0:tb], in1=eR.ap[:, 0:tb], op=ALU.mult), reads=[kk, eR], writes=[kh])
                    fw.dma("sp", Qd[d].ap[h * 128:(h + 1) * 128, t0:t0 + tb], qt.ap[:, 0:tb], reads=[qt], writes=[Qd[d]])
                    fw.dma("sp", Kd[d].ap[h * 128:(h + 1) * 128, t0:t0 + tb], kt.ap[:, 0:tb], reads=[kt], writes=[Kd[d]])
                    for s_ in range(tb // 128):
                        fw.op("pe", lambda e: e.transpose(ptb.ap[:, (d * 2 + s_) * 128:(d * 2 + s_ + 1) * 128], kh.ap[:, s_ * 128:(s_ + 1) * 128],
                                                          self.identb.ap[:]), reads=[kh, self.identb], writes=[ptb])
                    fw.op("act", lambda e: e.copy(out=Kht.ap[:, 0:tb // 128, d, h * 128:(h + 1) * 128],
                                                  in_=ptb.ap[:, d * 256: d * 256 + tb].rearrange("p (s k) -> p s k", k=128)),
                          reads=[ptb], writes=[Kht])
            for d in range(2):
                for s_ in range(tb // 128):
                    fw.dma("sp", Khd[d].ap[t0 + s_ * 128: t0 + (s_ + 1) * 128, :], Kht.ap[:, s_, d, :], reads=[Kht], writes=[Khd[d]])
                fw.dma("sp", WCd[d].ap[:, :, ch0:ch0 + nch], WCt.ap[:, d, :, 0:nch], reads=[WCt], writes=[WCd[d]])
            for sec in range(2):
                dst_d = IVd if sec == 0 else GSd
                tiles = [tmo.next() for _ in range(tb // 128)]
                for cb in range(4):
                    w, wv = ws.next()
                    for s_ in range(tb // 128):
                        p = p3[s_ % 3].next()
                        for kc in range(KC):
                            fw.op("pe", lambda e: e.matmul(p.ap[:], lhsT=hT.ap[:, kc, s_ * 128:(s_ + 1) * 128], rhs=wv[:, kc, :],
                                                           start=(kc == 0), stop=(kc == KC - 1)), reads=[w, hT], writes=[p])
                        fw.op("act", lambda e: e.activation(out=tiles[s_].ap[:, cb * 512:(cb + 1) * 512], in_=p.ap[:],
                                                            func=(AF.Copy if sec == 0 else AF.Silu)), reads=[p], writes=[tiles[s_]])
                for s_ in range(tb // 128):
                    fw.dma("sp", dst_d.ap[t0 + s_ * 128: t0 + (s_ + 1) * 128, :], tiles[s_].ap[:], reads=[tiles[s_]], writes=[dst_d])
        self.phase_end()
        if self.stop == "hg1":
            return

        self.phase_begin()
        mk = self.sb("hg_mask", [64, 2, 64])
        fw.dma("sp", mk.ap[:], self.din["hg_mask"].ap.rearrange("d s t -> s d t"), reads=[self.din["hg_mask"]], writes=[mk])
        S = [[self.sb("hg_S%d_%d" % (d, g), [128, 4, 128]) for g in range(4)] for d in range(2)]
        Sb = [[self.sb("hg_Sb%d_%d" % (d, g), [128, 4, 128], BF16) for g in range(4)] for d in range(2)]
        qt_r = self.sbrot("hs_q", 2, [128, H, 128], BF16)
        kt_r = self.sbrot("hs_k", 2, [128, H, 128], BF16)
        kh_r2 = self.sbrot("hs_kh", 2, [64, 2, D], BF16)
        iv_r = self.sbrot("hs_iv", 2, [64, 2, D], BF16)
        wc_r = self.sbrot("hs_wc", 2, [128, H, 4])
        AT_r = self.sbrot("hs_AT", 2, [64, 16 * 64], BF16)
        o_r = self.sbrot("hs_o", 2, [64, 2, D])
        pa_r = self.psrot("hs_pa", 2, [128, 512])
        pu_r = self.psrot("hs_pu", 2, [128, 512])
        po = [self.ps("hs_po%d" % g, [128, 512]) for g in range(4)]
        sth, st_out = self.din["state_hgrn"], self.dout["st_hgrn"]

        def tile_step(d, r0, first_chunk_global):
            qt, kt, kh, iv, wc = qt_r.next(), kt_r.next(), kh_r2.next(), iv_r.next(), wc_r.next()
            fw.dma("sp", qt.ap[:], Qd[d].ap[:, r0:r0 + 128].rearrange("(h k) t -> k h t", k=128), reads=[Qd[d]], writes=[qt])
            fw.dma("sp", kt.ap[:], Kd[d].ap[:, r0:r0 + 128].rearrange("(h k) t -> k h t", k=128), reads=[Kd[d]], writes=[kt])
            fw.dma("sp", kh.ap[:], Khd[d].ap[r0:r0 + 128, :].rearrange("(hf p) f -> p hf f", p=64), reads=[Khd[d]], writes=[kh])
            fw.dma("sp", iv.ap[:], IVd.ap[r0:r0 + 128, :].rearrange("(hf p) f -> p hf f", p=64), reads=[IVd], writes=[iv])
            c0 = r0 // C
            fw.dma("sp", wc.ap[:], WCd[d].ap[:, :, c0:c0 + 4], reads=[WCd[d]], writes=[wc])
            o = o_r.next()
            halves = [0, 1] if d == 0 else [1, 0]
            for hf in halves:
                AT = AT_r.next()
                for g2 in range(2):
                    pa = pa_r.next()
                    for hh in range(8):
                        h = g2 * 8 + hh
                        fw.op("pe", lambda e: e.matmul(pa.ap[0:64, hh * 64:(hh + 1) * 64], lhsT=kt.ap[:, h, hf * 64:(hf + 1) * 64],
                                                       rhs=qt.ap[:, h, hf * 64:(hf + 1) * 64], start=True, stop=True),
                              reads=[kt, qt], writes=[pa])
                    fw.op("dve", lambda e: e.tensor_tensor(out=AT.ap[:, g2 * 512:(g2 + 1) * 512].rearrange("p (h t) -> p h t", h=8),
                                                           in0=pa.ap[0:64, :].rearrange("p (h t) -> p h t", h=8),
                                                           in1=mk.ap[:, d, :].unsqueeze(1).to_broadcast([64, 8, 64]), op=ALU.mult),
                          reads=[pa, mk], writes=[AT])
                for g in range(4):
                    for hh in range(4):
                        h = g * 4 + hh
                        fw.op("pe", lambda e: e.matmul(po[g].ap[0:64, hh * 128:(hh + 1) * 128], lhsT=AT.ap[:, h * 64:(h + 1) * 64],
                                                       rhs=iv.ap[:, hf, h * 128:(h + 1) * 128], start=(hh == 0), stop=False, skip_group_check=True),
                              reads=[AT, iv], writes=[po[g]])
                order = [0, 1] if d == 0 else [1, 0]
                for ci, c in enumerate(order):
                    cg = hf * 2 + c
                    for g in range(4):
                        for hh in range(4):
                            h = g * 4 + hh
                            fw.op("pe", lambda e: e.matmul(po[g].ap[c * 32:(c + 1) * 32, hh * 128:(hh + 1) * 128],
                                                           lhsT=qt.ap[:, h, hf * 64 + c * 32: hf * 64 + (c + 1) * 32],
                                                           rhs=Sb[d][g].ap[:, hh, :], start=False, stop=True, skip_group_check=True), reads=[qt, Sb[d][g]], writes=[po[g]])
                        pu = pu_r.next()
                        for hh in range(4):
                            h = g * 4 + hh
                            fw.op("pe", lambda e: e.matmul(pu.ap[:, hh * 128:(hh + 1) * 128], lhsT=kh.ap[c * 32:(c + 1) * 32, hf, h * 128:(h + 1) * 128],
                                                           rhs=iv.ap[c * 32:(c + 1) * 32, hf, h * 128:(h + 1) * 128], start=True, stop=True),
                                  reads=[kh, iv], writes=[pu])
                        for hh in range(4):
                            h = g * 4 + hh
                            fw.op("dve", lambda e: e.scalar_tensor_tensor(out=S[d][g].ap[:, hh, :], in0=S[d][g].ap[:, hh, :], scalar=wc.ap[:, h, cg:cg + 1],
                                                                          in1=pu.ap[:, hh * 128:(hh + 1) * 128], op0=ALU.mult, op1=ALU.add),
                                  reads=[S[d][g], wc, pu], writes=[S[d][g]])
                        fw.op("act", lambda e: e.copy(out=Sb[d][g].ap[:], in_=S[d][g].ap[:]), reads=[S[d][g]], writes=[Sb[d][g]])
                for g in range(4):
                    if g % 2 == 0:
                        fw.op("act", lambda e: e.copy(out=o.ap[:, hf, g * 512:(g + 1) * 512], in_=po[g].ap[0:64, :]), reads=[po[g]], writes=[o])
                    else:
                        fw.op("dve", lambda e: e.tensor_copy(out=o.ap[:, hf, g * 512:(g + 1) * 512], in_=po[g].ap[0:64, :]), reads=[po[g]], writes=[o])
            fw.dma("sp", Od[d].ap[r0:r0 + 128, :].rearrange("(hf p) f -> p hf f", p=64), o.ap[:], reads=[o], writes=[Od[d]])

        seqs = [(0, NS, None), (NS, NP, 0), (NS + NP, NP, 1)]
        for (row0, T, pidx) in seqs:
            for d in range(2):
                for g in range(4):
                    if pidx is None:
                        fw.dma("sp", S[d][g].ap[:], sth.ap[j, d, g * 4:(g + 1) * 4].rearrange("h k v -> k h v"), reads=[sth], writes=[S[d][g]])
                    else:
                        fw.op("pool", lambda e: e.memset(S[d][g].ap[:], 0.0), writes=[S[d][g]])
                    fw.op("act", lambda e: e.copy(out=Sb[d][g].ap[:], in_=S[d][g].ap[:]), reads=[S[d][g]], writes=[Sb[d][g]])
            nt = T // 128
            for i in range(nt):
                tile_step(0, row0 + i * 128, None)
                tile_step(1, row0 + (nt - 1 - i) * 128, None)
            if pidx is not None:
                for d in range(2):
                    for g in range(4):
                        fw.dma("sp", st_out.ap[pidx, d, g * 4:(g + 1) * 4].rearrange("h k v -> k h v"), S[d][g].ap[:], reads=[S[d][g]], writes=[st_out])
        self.phase_end()
        if self.stop == "hg2":
            return

        self.phase_begin()
        nwt = self.sb("hc_nw", [128, 128])
        fw.dma("sp", nwt.ap[:], self.din["hg_norm_w"].ap[j:j + 1, :].partition_broadcast(128), reads=[self.din["hg_norm_w"]], writes=[nwt])
        of_r = self.sbrot("hc_of", 2, [128, D])
        ob_r = self.sbrot("hc_ob", 2, [128, D])
        gs_r = self.sbrot("hc_gs", 2, [128, D], BF16)
        sq_r = self.sbrot("hc_sq", 1, [128, D])
        y_r = self.sbrot("hc_y", 2, [128, D], BF16)
        st_r = self.sbrot("hc_st", 2, [128, 16])
        OTt = self.sbrot("hc_OTt", 2, [128, KC, 512], BF16)
        ptr = self.psrot("hc_pt", 2, [128, 1024], BF16)
        eps128 = self.sb("hc_eps", [128, 1])
        fw.op("pool", lambda e: e.memset(eps128.ap[:], EPS), writes=[eps128])
        for b0 in range(0, R, 512):
            ot = OTt.next()
            for s_ in range(4):
                r0 = b0 + s_ * 128
                of, ob, gs = of_r.next(), ob_r.next(), gs_r.next()
                fw.dma("sp", of.ap[:], Od[0].ap[r0:r0 + 128, :], reads=[Od[0]], writes=[of])
                fw.dma("sp", ob.ap[:], Od[1].ap[r0:r0 + 128, :], reads=[Od[1]], writes=[ob])
                fw.dma("sp", gs.ap[:], GSd.ap[r0:r0 + 128, :], reads=[GSd], writes=[gs])
                fw.op("pool", lambda e: e.tensor_tensor(out=of.ap[:], in0=of.ap[:], in1=ob.ap[:], op=ALU.add), reads=[of, ob], writes=[of])
                sq = sq_r.next()
                fw.op("act", lambda e: e.activation(out=sq.ap[:], in_=of.ap[:], func=AF.Square), reads=[of], writes=[sq])
                st = st_r.next()
                fw.op("dve", lambda e: e.tensor_reduce(out=st.ap[:], in_=sq.ap[:].rearrange("p (h v) -> p h v", h=16), axis=AX.X, op=ALU.add),
                      reads=[sq], writes=[st])
                fw.op("act", lambda e: e.activation(out=st.ap[:], in_=st.ap[:], func=AF.Ln, scale=1.0 / 128.0, bias=eps128.ap[:, 0:1]), reads=[st, eps128], writes=[st])
                fw.op("act", lambda e: e.activation(out=st.ap[:], in_=st.ap[:], func=AF.Exp, scale=-0.5), reads=[st], writes=[st])
                of3 = of.ap[:].rearrange("p (h v) -> p h v", h=16)
                fw.op("dve", lambda e: e.tensor_tensor(out=of3, in0=of3, in1=st.ap[:].unsqueeze(2).to_broadcast([128, 16, 128]), op=ALU.mult),
                      reads=[of, st], writes=[of])
                fw.op("pool", lambda e: e.tensor_tensor(out=of3, in0=of3, in1=nwt.ap[:].unsqueeze(1).to_broadcast([128, 16, 128]), op=ALU.mult),
                      reads=[of, nwt], writes=[of])
                y = y_r.next()
                fw.op("dve", lambda e: e.tensor_tensor(out=y.ap[:], in0=of.ap[:], in1=gs.ap[:], op=ALU.mult), reads=[of, gs], writes=[y])
                for half in range(2):
                    pt = ptr.next()
                    for q in range(8):
                        kc = half * 8 + q
                        fw.op("pe", lambda e: e.transpose(pt.ap[:, q * 128:(q + 1) * 128], y.ap[:, kc * 128:(kc + 1) * 128], self.identb.ap[:]),
                              reads=[y, self.identb], writes=[pt])
                    fw.op("act", lambda e: e.copy(out=ot.ap[:, half * 8:(half + 1) * 8, s_ * 128:(s_ + 1) * 128],
                                                  in_=pt.ap[:].rearrange("p (q t) -> p q t", q=8)), reads=[pt], writes=[ot])
            fw.dma("sp", OT.ap[:, b0:b0 + 512].rearrange("(kc p) t -> p kc t", p=128), ot.ap[:], reads=[ot], writes=[OT])
        self.phase_end()
        if self.stop == "hg3":
            return
        self.out_proj(l, OT, self.din["hg_wo"], self.din["hg_wo"].ap[j], 2)

    def rwkv(self, l):
        fw = self.fw
        ja = l // 3
        H, C, LD, LG = 32, 64, 96, 256
        din = self.din
        HTd = self.scratch("rw_HT%d" % l, [D, R + 2], BF16)
        Atd = [self.scratch("rw_At%d_%d" % (l, d), [D, R], BF16) for d in range(2)]
        Ktd = [self.scratch("rw_Kt%d_%d" % (l, d), [D, R], BF16) for d in range(2)]
        Btd = [self.scratch("rw_Bt%d_%d" % (l, d), [D, R], BF16) for d in range(2)]
        Rtd = [self.scratch("rw_Rt%d_%d" % (l, d), [D, R], BF16) for d in range(2)]
        Kmd = [self.scratch("rw_Km%d_%d" % (l, d), [R, D], BF16) for d in range(2)]
        Bmd = [self.scratch("rw_Bm%d_%d" % (l, d), [R, D], BF16) for d in range(2)]
        WCd = [self.scratch("rw_WC%d_%d" % (l, d), [128, KC, R // C]) for d in range(2)]
        Vd = self.scratch("rw_V%d" % l, [R, D], BF16)
        Gd = self.scratch("rw_G%d" % l, [R, D], BF16)
        CSd = self.scratch("rw_CS%d" % l, [R, 64])
        Yd = [self.scratch("rw_Y%d_%d" % (l, d), [R, D]) for d in range(2)]
        OT = self.scratch("rw_OT%d" % l, [D, R], BF16)

        self.phase_begin()
        TB = 256
        self.prep_setup(l, 0)
        hTr = self.sbrot("r0_hT", 2, [128, KC, TB], BF16)
        for (t0, tb) in self.blocks(TB):
            hT = hTr.next()
            self.prep_block(t0, tb, hT)
            fw.dma("sp", HTd.ap[:, 1 + t0:1 + t0 + tb].rearrange("(kc p) t -> p kc t", p=128), hT.ap[:, :, 0:tb], reads=[hT], writes=[HTd])
        self.phase_end()

        self.phase_begin()
        stage = self.sb("r1_stage", [128, 128])
        pst = self.ps("r1_pst", [128, 128])
        cva = self.sb("r1_cva", [128, 128])
        cvb = self.sb("r1_cvb", [128, 128])
        self.colvecs([(din["rw_mu"], din["rw_mu"].ap[ja, i]) for i in range(6)] + [(din["rw_kk"], din["rw_kk"].ap[ja]), (din["rw_ka"], din["rw_ka"].ap[ja])],
                     cva, pst, stage, self.identf)
        rkf = din["rw_rk"].ap.rearrange("j d h n -> j d (h n)")
        self.colvecs([(din["rw_w0"], din["rw_w0"].ap[ja, 0]), (din["rw_w0"], din["rw_w0"].ap[ja, 1]),
                      (din["rw_a0"], din["rw_a0"].ap[ja, 0]), (din["rw_a0"], din["rw_a0"].ap[ja, 1]),
                      (din["rw_rk"], rkf[ja, 0]), (din["rw_rk"], rkf[ja, 1])], cvb, pst, stage, self.identf)
        fw.op("dve", lambda e: e.tensor_scalar(out=cvb.ap[:, 96:112], in0=cva.ap[:, 112:128], scalar1=-1.0, scalar2=1.0, op0=ALU.mult, op1=ALU.add),
              reads=[cva], writes=[cvb])
        MU = lambda i, kc: cva.ap[:, i * 16 + kc:i * 16 + kc + 1]
        KKW = lambda kc: cva.ap[:, 96 + kc:97 + kc]
        KA = lambda kc: cva.ap[:, 112 + kc:113 + kc]
        W0 = lambda d, kc: cvb.ap[:, d * 16 + kc:d * 16 + kc + 1]
        A0 = lambda d, kc: cvb.ap[:, 32 + d * 16 + kc:33 + d * 16 + kc]
        RK = lambda d, kc: cvb.ap[:, 64 + d * 16 + kc:65 + d * 16 + kc]
        OMKA = lambda kc: cvb.ap[:, 96 + kc:97 + kc]
        W1 = self.sb("r1_W1", [128, 2, KC, LD], BF16)
        A1 = self.sb("r1_A1", [128, 2, KC, LD], BF16)
        W2 = self.sb("r1_W2", [LD, 2, D], BF16)
        A2 = self.sb("r1_A2", [LD, 2, D], BF16)
        G1 = self.sb("r1_G1", [128, KC, LG], BF16)
        G2 = self.sb("r1_G2", [128, 2, D], BF16)
        for d in range(2):
            fw.dma("pool", W1.ap[:, d], din["rw_w1"].ap[ja, d].rearrange("(kc p) n -> p kc n", p=128), reads=[din["rw_w1"]], writes=[W1])
            fw.dma("pool", A1.ap[:, d], din["rw_a1"].ap[ja, d].rearrange("(kc p) n -> p kc n", p=128), reads=[din["rw_a1"]], writes=[A1])
            fw.dma("pool", W2.ap[:, d], din["rw_w2"].ap[ja, d], reads=[din["rw_w2"]], writes=[W2])
            fw.dma("pool", A2.ap[:, d], din["rw_a2"].ap[ja, d], reads=[din["rw_a2"]], writes=[A2])
        fw.dma("pool", G1.ap[:], din["rw_g1"].ap[ja].rearrange("(kc p) n -> p kc n", p=128), reads=[din["rw_g1"]], writes=[G1])
        fw.dma("pool", G2.ap[:], din["rw_g2"].ap[ja].rearrange("(kc p) n -> p kc n", p=128), reads=[din["rw_g2"]], writes=[G2])
        bones = self.sb("r1_bones", [128, 128], BF16)
        fw.op("pool", lambda e: e.memset(bones.ap[:], 0.0), writes=[bones])
        fw.op("pool", lambda e: e.memset(bones.ap[0:64, 0:64], 1.0), reads=[bones], writes=[bones])
        fw.op("pool", lambda e: e.memset(bones.ap[64:128, 64:128], 1.0), reads=[bones], writes=[bones])
        Eh = self.sb("r1_E", [128, KC, 32], BF16)
        fw.op("pool", lambda e: e.memset(Eh.ap[:], 0.0), writes=[Eh])
        for kc in range(KC):
            fw.op("pool", lambda e: e.memset(Eh.ap[0:64, kc, 2 * kc:2 * kc + 1], 1.0), reads=[Eh], writes=[Eh])
            fw.op("pool", lambda e: e.memset(Eh.ap[64:128, kc, 2 * kc + 1:2 * kc + 2], 1.0), reads=[Eh], writes=[Eh])
        msk = self.sb("r1_rst", [128, TB])
        fw.op("pool", lambda e: e.memset(msk.ap[:], 1.0), writes=[msk])
        fw.op("pool", lambda e: e.memset(msk.ap[:].rearrange("p (c t) -> p c t", t=C)[:, :, 0:1], 0.0), reads=[msk], writes=[msk])
        e12 = self.sb("r1_e12", [128, 1])
        fw.op("pool", lambda e: e.memset(e12.ap[:], 1e-12), writes=[e12])

        hTh = self.sb("r1_hTh", [128, KC, TB + 2], BF16)
        xx = self.sb("r1_xx", [128, KC, TB], BF16)
        xmr = self.sbrot("r1_xm", 2, [128, KC, TB], BF16)
        wrot = self.sbrot("r1_w", 2, [128, KC * 512], BF16)
        NCH = TB // C
        r_f = self.sb("r1_r", [128, KC, TB], BF16)
        k_f = self.sb("r1_k", [128, KC, TB], BF16)
        kkn = self.sb("r1_kkn", [128, KC, TB], BF16)
        a_r = self.sbrot("r1_a", 4, [128, TB])
        sg_r = self.sbrot("r1_sg", 4, [128, TB])
        whT = self.sb("r1_wh", [LD, 4, TB], BF16)
        ghT = self.sb("r1_gh", [128, 2, TB], BF16)
        vtm = self.sbrot("r1_vtm", 2, [128, D], BF16)
        pp = self.psrot("r1_pp", 4, [128, 512])
        ptb = self.psrot("r1_ptb", 2, [128, 1024], BF16)
        tr = self.sbrot("r1_t", 24, [128, TB])
        ob = self.sbrot("r1_ob", 20, [128, TB], BF16)
        Kmr = self.sbrot("r1_Kmt", 4, [128, 2, TB // 128, 128], BF16)
        WCt = self.sb("r1_WCt", [128, 2, KC, NCH])
        Zr = self.sbrot("r1_Z", 4, [128, TB], BF16)
        pcs = self.ps("r1_pcs", [128, 2 * 64])
        cst = self.sbrot("r1_cs", 2, [128, 2 * 64])
        blocks = self.blocks(TB)

        def ld_sq(Wb, cb):
            def f(buf):
                dst = buf.ap[:].rearrange("p (kc n) -> p kc n", kc=KC)
                fw.dma("pool", dst, Wb.ap[ja].rearrange("(kc p) n -> p kc n", p=128)[:, :, cb * 512:(cb + 1) * 512], reads=[Wb], writes=[buf])
                return dst
            return f
        loaders = []
        for _ in blocks:
            for nm in ("rw_wr", "rw_wk", "rw_wv"):
                loaders += [ld_sq(din[nm], cb) for cb in range(4)]
        ws = WStream(wrot, loaders)

        def mix(i):
            xm = xmr.next()
            for kc in range(KC):
                fw.op("dve", lambda e: e.scalar_tensor_tensor(out=xm.ap[:, kc, 0:tb], in0=xx.ap[:, kc, 0:tb], scalar=MU(i, kc),
                                                            in1=hTh.ap[:, kc, 1:1 + tb], op0=ALU.mult, op1=ALU.add),
                      reads=[xx, hTh, cva], writes=[xm])
            return xm

        for (t0, tb) in blocks:
            seq0 = 0 if t0 < NS else (NS if t0 < NS + NP else NS + NP)
            seq1 = NS if t0 < NS else (NS + NP if t0 < NS + NP else R)
            lo = t0 - 1 if t0 > seq0 else t0
            hi = t0 + tb + 1 if t0 + tb < seq1 else t0 + tb
            fw.dma("sp", hTh.ap[:, :, (lo - t0 + 1):(hi - t0 + 1)], HTd.ap[:, lo + 1:hi + 1].rearrange("(kc p) t -> p kc t", p=128),
                   reads=[HTd], writes=[hTh])
            if lo == t0:
                fw.op("pool", lambda e: e.memset(hTh.ap[:, :, 0:1], 0.0), reads=[hTh], writes=[hTh])
            if hi == t0 + tb:
                fw.op("pool", lambda e: e.memset(hTh.ap[:, :, tb + 1:tb + 2], 0.0), reads=[hTh], writes=[hTh])
            fw.op("dve", lambda e: e.tensor_tensor(out=xx.ap[:, :, 0:tb], in0=hTh.ap[:, :, 0:tb], in1=hTh.ap[:, :, 2:2 + tb], op=ALU.add),
                  reads=[hTh], writes=[xx])
            fw.op("dve", lambda e: e.scalar_tensor_tensor(out=xx.ap[:, :, 0:tb], in0=xx.ap[:, :, 0:tb], scalar=0.5, in1=hTh.ap[:, :, 1:1 + tb],
                                                          op0=ALU.mult, op1=ALU.subtract), reads=[xx, hTh], writes=[xx])
            nch = tb // C
            ch0 = t0 // C
            for (mi, dstf) in ((0, r_f), (2, k_f)):
                xm = mix(mi)
                for cb in range(4):
                    w, wv = ws.next()
                    for jj in range(4):
                        fc = cb * 4 + jj
                        p = pp.next()
                        for kc in range(KC):
                            fw.op("pe", lambda e: e.matmul(p.ap[:, 0:tb], lhsT=wv[:, kc, jj * 128:(jj + 1) * 128], rhs=xm.ap[:, kc, 0:tb],
                                                           start=(kc == 0), stop=(kc == KC - 1)), reads=[w, xm], writes=[p])
                        fw.op("act", lambda e: e.copy(out=dstf.ap[:, fc, 0:tb], in_=p.ap[:, 0:tb]), reads=[p], writes=[dstf])
            xm = mix(3)
            vts = [vtm.next() for _ in range(tb // 128)]
            for cb in range(4):
                w, wv = ws.next()
                for s_ in range(tb // 128):
                    p = pp.next()
                    for kc in range(KC):
                        fw.op("pe", lambda e: e.matmul(p.ap[:], lhsT=xm.ap[:, kc, s_ * 128:(s_ + 1) * 128], rhs=wv[:, kc, :],
                                                       start=(kc == 0), stop=(kc == KC - 1)), reads=[w, xm], writes=[p])
                    fw.op("act", lambda e: e.copy(out=vts[s_].ap[:, cb * 512:(cb + 1) * 512], in_=p.ap[:]), reads=[p], writes=[vts[s_]])
            for s_ in range(tb // 128):
                fw.dma("sp", Vd.ap[t0 + s_ * 128:t0 + (s_ + 1) * 128, :], vts[s_].ap[:], reads=[vts[s_]], writes=[Vd])
            xm = mix(5)
            for c2 in range(2):
                p = pp.next()
                for kc in range(KC):
                    fw.op("pe", lambda e: e.matmul(p.ap[:, 0:tb], lhsT=G1.ap[:, kc, c2 * 128:(c2 + 1) * 128], rhs=xm.ap[:, kc, 0:tb],
                                                   start=(kc == 0), stop=(kc == KC - 1)), reads=[G1, xm], writes=[p])
                fw.op("act", lambda e: e.activation(out=ghT.ap[:, c2, 0:tb], in_=p.ap[:, 0:tb], func=AF.Sigmoid), reads=[p], writes=[ghT])
            gts = [vtm.next() for _ in range(tb // 128)]
            for s_ in range(tb // 128):
                for cb in range(4):
                    p = pp.next()
                    for c2 in range(2):
                        fw.op("pe", lambda e: e.matmul(p.ap[:], lhsT=ghT.ap[:, c2, s_ * 128:(s_ + 1) * 128], rhs=G2.ap[:, c2, cb * 512:(cb + 1) * 512],
                                                       start=(c2 == 0), stop=(c2 == 1)), reads=[ghT, G2], writes=[p])
                    fw.op("act", lambda e: e.copy(out=gts[s_].ap[:, cb * 512:(cb + 1) * 512], in_=p.ap[:]), reads=[p], writes=[gts[s_]])
                fw.dma("sp", Gd.ap[t0 + s_ * 128:t0 + (s_ + 1) * 128, :], gts[s_].ap[:], reads=[gts[s_]], writes=[Gd])
            for (mi, Wl, off, fn) in ((1, W1, 0, AF.Tanh), (4, A1, 2, AF.Copy)):
                xm = mix(mi)
                for d in range(2):
                    p = pp.next()
                    for kc in range(KC):
                        fw.op("pe", lambda e: e.matmul(p.ap[0:LD, 0:tb], lhsT=Wl.ap[:, d, kc, :], rhs=xm.ap[:, kc, 0:tb],
                                                       start=(kc == 0), stop=(kc == KC - 1)), reads=[Wl, xm], writes=[p])
                    fw.op("act", lambda e: e.activation(out=whT.ap[:, off + d, 0:tb], in_=p.ap[0:LD, 0:tb], func=fn), reads=[p], writes=[whT])
            for kc in range(KC):
                t_sq = ob.next()
                fw.op("dve", lambda e: e.tensor_scalar(out=kkn.ap[:, kc, 0:tb], in0=k_f.ap[:, kc, 0:tb], scalar1=KKW(kc), scalar2=None, op0=ALU.mult),
                      reads=[k_f, cva], writes=[kkn])
                fw.op("pool", lambda e: e.tensor_tensor(out=t_sq.ap[:, 0:tb], in0=kkn.ap[:, kc, 0:tb], in1=kkn.ap[:, kc, 0:tb], op=ALU.mult),
                      reads=[kkn], writes=[t_sq])
                p = pp.next()
                fw.op("pe", lambda e: e.matmul(p.ap[:, 0:tb], lhsT=bones.ap[:], rhs=t_sq.ap[:, 0:tb], start=True, stop=True), reads=[bones, t_sq], writes=[p])
                rn = tr.next()
                fw.op("act", lambda e: e.activation(out=rn.ap[:, 0:tb], in_=p.ap[:, 0:tb], func=AF.Ln, bias=e12.ap[:, 0:1]), reads=[p, e12], writes=[rn])
                fw.op("act", lambda e: e.activation(out=rn.ap[:, 0:tb], in_=rn.ap[:, 0:tb], func=AF.Exp, scale=-0.5), reads=[rn], writes=[rn])
                fw.op("dve", lambda e: e.tensor_tensor(out=kkn.ap[:, kc, 0:tb], in0=kkn.ap[:, kc, 0:tb], in1=rn.ap[:, 0:tb], op=ALU.mult),
                      reads=[kkn, rn], writes=[kkn])
            for d in range(2):
                for kc in range(KC):
                    sgt, at_ = sg_r.next(), a_r.next()
                    p = pp.next()
                    fw.op("pe", lambda e: e.matmul(p.ap[:, 0:tb], lhsT=W2.ap[:, d, kc * 128:(kc + 1) * 128], rhs=whT.ap[:, d, 0:tb], start=True, stop=True),
                          reads=[W2, whT], writes=[p])
                    fw.op("act", lambda e: e.activation(out=sgt.ap[:, 0:tb], in_=p.ap[:, 0:tb], func=AF.Sigmoid, bias=W0(d, kc)),
                          reads=[p, cvb], writes=[sgt])
                    p = pp.next()
                    fw.op("pe", lambda e: e.matmul(p.ap[:, 0:tb], lhsT=A2.ap[:, d, kc * 128:(kc + 1) * 128], rhs=whT.ap[:, 2 + d, 0:tb], start=True, stop=True),
                          reads=[A2, whT], writes=[p])
                    fw.op("act", lambda e: e.activation(out=at_.ap[:, 0:tb], in_=p.ap[:, 0:tb], func=AF.Sigmoid, bias=A0(d, kc)),
                          reads=[p, cvb], writes=[at_])
                    a_ = at_.ap[:, 0:tb]
                    a_f = at_
                    lw, F, L, Lex = tr.next(), tr.next(), tr.next(), tr.next()
                    fw.op("pool", lambda e: e.tensor_scalar(out=lw.ap[:, 0:tb], in0=sgt.ap[:, 0:tb], scalar1=-math.exp(-0.5), scalar2=None, op0=ALU.mult),
                          reads=[sgt], writes=[lw])
                    fw.op("dve", lambda e: e.tensor_tensor_scan(out=F.ap[:, 0:tb], data0=msk.ap[:, 0:tb], data1=lw.ap[:, 0:tb], initial=0.0,
                                                                op0=ALU.mult, op1=ALU.add), reads=[msk, lw], writes=[F])
                    F3 = F.ap[:, 0:tb].rearrange("p (c t) -> p c t", t=C)
                    tot = F3[:, :, C - 1:C]
                    if d == 0:
                        Lt = F
                        fw.op("pool", lambda e: e.tensor_tensor(out=Lex.ap[:, 0:tb], in0=F.ap[:, 0:tb], in1=lw.ap[:, 0:tb], op=ALU.subtract),
                              reads=[F, lw], writes=[Lex])
                    else:
                        fw.op("dve", lambda e: e.tensor_tensor(out=Lex.ap[:, 0:tb].rearrange("p (c t) -> p c t", t=C), in0=tot.to_broadcast([128, nch, C]),
                                                               in1=F3, op=ALU.subtract), reads=[F], writes=[Lex])
                        fw.op("pool", lambda e: e.tensor_tensor(out=L.ap[:, 0:tb], in0=Lex.ap[:, 0:tb], in1=lw.ap[:, 0:tb], op=ALU.add),
                              reads=[Lex, lw], writes=[L])
                        Lt = L
                    fw.op("act", lambda e: e.activation(out=WCt.ap[:, d, kc, 0:nch].unsqueeze(2), in_=tot, func=AF.Exp), reads=[F], writes=[WCt])
                    eL, enL, eX = tr.next(), tr.next(), tr.next()
                    fw.op("act", lambda e: e.activation(out=eL.ap[:, 0:tb], in_=Lt.ap[:, 0:tb], func=AF.Exp), reads=[Lt], writes=[eL])
                    fw.op("act", lambda e: e.activation(out=enL.ap[:, 0:tb], in_=Lt.ap[:, 0:tb], func=AF.Exp, scale=-1.0), reads=[Lt], writes=[enL])
                    fw.op("act", lambda e: e.activation(out=eX.ap[:, 0:tb], in_=Lex.ap[:, 0:tb], func=AF.Exp), reads=[Lex], writes=[eX])
                    kd, bp = tr.next(), tr.next()
                    fw.op("dve", lambda e: e.tensor_scalar(out=kd.ap[:, 0:tb], in0=a_, scalar1=KA(kc), scalar2=OMKA(kc), op0=ALU.mult, op1=ALU.add),
                          reads=[a_f, cva, cvb], writes=[kd])
                    fw.op("dve", lambda e: e.tensor_tensor(out=kd.ap[:, 0:tb], in0=kd.ap[:, 0:tb], in1=k_f.ap[:, kc, 0:tb], op=ALU.mult),
                          reads=[kd, k_f], writes=[kd])
                    fw.op("dve", lambda e: e.tensor_tensor(out=bp.ap[:, 0:tb], in0=kkn.ap[:, kc, 0:tb], in1=a_, op=ALU.mult), reads=[kkn, a_f], writes=[bp])
                    at, kt, bt, rt, bn = ob.next(), ob.next(), ob.next(), ob.next(), ob.next()
                    fw.op("dve", lambda e: e.tensor_tensor(out=at.ap[:, 0:tb], in0=kkn.ap[:, kc, 0:tb], in1=eX.ap[:, 0:tb], op=ALU.mult), reads=[kkn, eX], writes=[at])
                    fw.op("dve", lambda e: e.tensor_tensor(out=kt.ap[:, 0:tb], in0=kd.ap[:, 0:tb], in1=enL.ap[:, 0:tb], op=ALU.mult), reads=[kd, enL], writes=[kt])
                    fw.op("pool", lambda e: e.tensor_tensor(out=bt.ap[:, 0:tb], in0=bp.ap[:, 0:tb], in1=enL.ap[:, 0:tb], op=ALU.mult), reads=[bp, enL], writes=[bt])
                    fw.op("dve", lambda e: e.tensor_tensor(out=rt.ap[:, 0:tb], in0=r_f.ap[:, kc, 0:tb], in1=eL.ap[:, 0:tb], op=ALU.mult), reads=[r_f, eL], writes=[rt])
                    fw.op("pool", lambda e: e.tensor_scalar(out=bn.ap[:, 0:tb], in0=bt.ap[:, 0:tb], scalar1=-1.0, scalar2=None, op0=ALU.mult), reads=[bt], writes=[bn])
                    Zt = Zr.next()
                    fw.op("dve", lambda e: e.scalar_tensor_tensor(out=Zt.ap[:, 0:tb], in0=kd.ap[:, 0:tb], scalar=RK(d, kc), in1=r_f.ap[:, kc, 0:tb],
                                                                  op0=ALU.mult, op1=ALU.mult), reads=[kd, cvb, r_f], writes=[Zt])
                    for s_ in range(tb // 128):
                        fw.op("pe", lambda e: e.matmul(pcs.ap[:, s_ * 64 + d * 32: s_ * 64 + (d + 1) * 32], lhsT=Zt.ap[:, s_ * 128:(s_ + 1) * 128], rhs=Eh.ap[:, kc, :],
                                                       start=(kc == 0 and d == 0 and s_ == 0), stop=(kc == KC - 1 and d == 1 and s_ == tb // 128 - 1),
                                                       skip_group_check=True), reads=[Zt, Eh], writes=[pcs])
                    rows = slice(kc * 128, (kc + 1) * 128)
                    fw.dma("sp", Atd[d].ap[rows, t0:t0 + tb], at.ap[:, 0:tb], reads=[at], writes=[Atd[d]])
                    fw.dma("sp", Ktd[d].ap[rows, t0:t0 + tb], kt.ap[:, 0:tb], reads=[kt], writes=[Ktd[d]])
                    fw.dma("sp", Btd[d].ap[rows, t0:t0 + tb], bt.ap[:, 0:tb], reads=[bt], writes=[Btd[d]])
                    fw.dma("sp", Rtd[d].ap[rows, t0:t0 + tb], rt.ap[:, 0:tb], reads=[rt], writes=[Rtd[d]])
                    pt = ptb.next()
                    for qi, src in enumerate((kt, bn)):
                        for s_ in range(tb // 128):
                            fw.op("pe", lambda e: e.transpose(pt.ap[:, (qi * 2 + s_) * 128:(qi * 2 + s_ + 1) * 128], src.ap[:, s_ * 128:(s_ + 1) * 128],
                                                              self.identb.ap[:]), reads=[src, self.identb], writes=[pt])
                    Kmt = Kmr.next()
                    fw.op("act", lambda e: e.copy(out=Kmt.ap[:, :, 0:tb // 128, :],
                                                  in_=pt.ap[:, 0:512].rearrange("p (q s k) -> p q s k", q=2, k=128)[:, :, 0:tb // 128, :]), reads=[pt], writes=[Kmt])
                    for qi, dst in enumerate((Kmd[d], Bmd[d])):
                        fw.dma("sp", dst.ap[t0:t0 + tb, kc * 128:(kc + 1) * 128].rearrange("(s p) f -> p s f", p=128), Kmt.ap[:, qi, 0:tb // 128, :],
                               reads=[Kmt], writes=[dst])
            for d in range(2):
                fw.dma("sp", WCd[d].ap[:, :, ch0:ch0 + nch], WCt.ap[:, d, :, 0:nch], reads=[WCt], writes=[WCd[d]])
            cs = cst.next()
            fw.op("act", lambda e: e.copy(out=cs.ap[:], in_=pcs.ap[:]), reads=[pcs], writes=[cs])
            for s_ in range(tb // 128):
                fw.dma("sp", CSd.ap[t0 + s_ * 128:t0 + (s_ + 1) * 128, :], cs.ap[:, s_ * 64:(s_ + 1) * 64], reads=[cs], writes=[CSd])
        self.phase_end()
        if self.stop == "rw1":
            return
        self.rwkv_scan(l, ja, Atd, Ktd, Btd, Rtd, Kmd, Bmd, WCd, Vd, Yd)
        if self.stop in ("rw2", "rs_a", "rs_b", "rs_0"):
            return
        self.rwkv_out(l, ja, Yd, Vd, Gd, CSd, OT)

    def rwkv_scan(self, l, ja, Atd, Ktd, Btd, Rtd, Kmd, Bmd, WCd, Vd, Yd):
        fw = self.fw
        din = self.din
        C = 64
        self.phase_begin()
        mk = self.sb("rs_mask", [128, 8, 128])
        fw.dma("sp", mk.ap[:], din["rw_mask"].ap.rearrange("m i t -> i m t"), reads=[din["rw_mask"]], writes=[mk])
        S = [self.sb("rs_S%d" % d, [128, KC, 64]) for d in range(2)]
        Sb = [self.sb("rs_Sb%d" % d, [128, KC, 64], BF16) for d in range(2)]
        fm_r = [self.sbrot("rs_fm%d" % i, 2, [128, 8, 128], BF16) for i in range(4)]
        tm_r = [self.sbrot("rs_tm%d" % i, 2, [128, 1024], BF16) for i in range(3)]
        wc_r = self.sbrot("rs_wc", 2, [128, 8, 2])
        gr = [self.sbrot("rs_g%d" % i, 2, [128, 16, 128], BF16) for i in range(5)]
        Pr = [self.sbrot("rs_P%d" % i, 2, [128, 16, 128], BF16) for i in range(2)]
        QTr = self.sbrot("rs_QT", 2, [128, 16, 128], BF16)
        Tt = self.sbrot("rs_T", 2, [128, 16, 128], BF16)
        Xb = self.sb("rs_Xb", [128, 1024], BF16)
        Ub = self.sb("rs_Ub", [128, 1024], BF16)
        tmp = self.sb("rs_tmp", [128, 4, 64])
        yt_r = self.sbrot("rs_y", 2, [128, 1024])
        stg = self.sb("rs_stg", [64, 2, 64])
        stg2 = self.sb("rs_stg2", [64, 128])
        pg = self.psrot("rs_pg", 3, [128, 512])
        px = self.ps("rs_px", [128, 512])
        pu = self.ps("rs_pu", [128, 512])
        pss = self.ps("rs_ps", [128, 512])
        py = [self.ps("rs_py%d" % i, [128, 512]) for i in range(2)]
        idb4 = self.identb.ap[:].unsqueeze(1).to_broadcast([128, 4, 128])
        sti, st_out = din["state_rwkv"], self.dout["st_rwkv"]

        def v4(t, g):
            return t.ap[:, g * 4:(g + 1) * 4, :]

        def p4(p):
            return p.ap[:].rearrange("p (h t) -> p h t", h=4)

        def tile_step(d, r0, half):
            At, Kt, Bt, Rt = [r.next() for r in fm_r]
            for t, src in ((At, Atd[d]), (Kt, Ktd[d]), (Bt, Btd[d]), (Rt, Rtd[d])):
                fw.dma("sp", t.ap[:], src.ap[half * 1024:(half + 1) * 1024, r0:r0 + 128].rearrange("(q p) t -> p q t", p=128), reads=[src], writes=[t])
            Km, Bm, V = [r.next() for r in tm_r]
            for t, src in ((Km, Kmd[d]), (Bm, Bmd[d]), (V, Vd)):
                fw.dma("sp", t.ap[:], src.ap[r0:r0 + 128, half * 1024:(half + 1) * 1024], reads=[src], writes=[t])
            wc = wc_r.next()
            c0 = r0 // C
            fw.dma("sp", wc.ap[:], WCd[d].ap[:, half * 8:(half + 1) * 8, c0:c0 + 2], reads=[WCd[d]], writes=[wc])
            N_, NT_, Nak, Mrk, MrbN = [r.next() for r in gr]
            T = Tt.next()
            for g in range(4):
                specs = ((Bt, At, N_, 0), (At, Bt, NT_, 1), (Kt, At, Nak, 0), (Kt, Rt, Mrk, 2), (Bt, Rt, MrbN, 3))
                for (L_, R_, dst, mi) in specs:
                    p = pg.next()
                    for hh in (0, 2, 1, 3):
                        h16 = g * 4 + hh
                        q, hp = h16 // 2, (h16 % 2) * 64
                        fw.op("pe", lambda e: e.matmul(p.ap[:, hh * 128:(hh + 1) * 128], lhsT=L_.ap[hp:hp + 64, q, :], rhs=R_.ap[hp:hp + 64, q, :],
                                                       start=True, stop=True), reads=[L_, R_], writes=[p], rt=hp)
                    fw.op("dve", lambda e: e.tensor_tensor(out=v4(dst, g), in0=p4(p), in1=mk.ap[:, d * 4 + mi, :].unsqueeze(1).to_broadcast([128, 4, 128]),
                                                           op=ALU.mult), reads=[p, mk], writes=[dst])
                fw.op("pool", lambda e: e.tensor_tensor(out=v4(T, g), in0=idb4, in1=v4(N_, g), op=ALU.subtract), reads=[self.identb, N_], writes=[T])
            if self.stop == "rs_a":
                return
            Pp, PTp = N_, NT_
            for lev in range(1, 6):
                Pc, PTc = Pr[0].next(), Pr[1].next()
                QT = QTr.next()
                for g in range(4):
                    if lev < 5:
                        p = pg.next()
                        for hh in range(4):
                            h16 = g * 4 + hh
                            fw.op("pe", lambda e: e.matmul(p.ap[:, hh * 128:(hh + 1) * 128], lhsT=PTp.ap[:, h16, :], rhs=Pp.ap[:, h16, :], start=True, stop=True),
                                  reads=[PTp, Pp], writes=[p])
                        fw.op("act", lambda e: e.copy(out=v4(Pc, g), in_=p4(p)), reads=[p], writes=[Pc])
                    p = pg.next()
                    for hh in range(4):
                        h16 = g * 4 + hh
                        fw.op("pe", lambda e: e.matmul(p.ap[:, hh * 128:(hh + 1) * 128], lhsT=Pp.ap[:, h16, :], rhs=PTp.ap[:, h16, :], start=True, stop=True),
                              reads=[PTp, Pp], writes=[p])
                    fw.op("dve", lambda e: e.tensor_copy(out=v4(PTc, g), in_=p4(p)), reads=[p], writes=[PTc])
                    fw.op("pool", lambda e: e.tensor_tensor(out=v4(QT, g), in0=v4(PTc, g), in1=idb4, op=ALU.add), reads=[PTc, self.identb], writes=[QT])
                Tn = Tt.next()
                for g in range(4):
                    p = pg.next()
                    for hh in range(4):
                        h16 = g * 4 + hh
                        fw.op("pe", lambda e: e.matmul(p.ap[:, hh * 128:(hh + 1) * 128], lhsT=QT.ap[:, h16, :], rhs=T.ap[:, h16, :], start=True, stop=True),
                              reads=[QT, T], writes=[p])
                    fw.op("act", lambda e: e.copy(out=v4(Tn, g), in_=p4(p)), reads=[p], writes=[Tn])
                T = Tn
                Pp, PTp = Pc, PTc
            if self.stop == "rs_b":
                return
            yt = yt_r.next()
            order = [0, 1] if d == 0 else [1, 0]
            for ci, c in enumerate(order):
                cr = slice(c * 64, (c + 1) * 64)
                for g8 in range(2):
                    hs = [(g8 * 8 + hh, (g8 * 8 + hh) // 2, ((g8 * 8 + hh) % 2) * 64) for hh in range(8)]
                    hso = [x for x in enumerate(hs) if x[1][2] != c * 64] + [x for x in enumerate(hs) if x[1][2] == c * 64]
                    for i_, (hh, (h16, q, hp)) in enumerate(hso):
                        fw.op("pe", lambda e: e.matmul(px.ap[cr, hh * 64:(hh + 1) * 64], lhsT=At.ap[hp:hp + 64, q, cr], rhs=Sb[d].ap[hp:hp + 64, half * 8 + q, :],
                                                       start=(i_ == 0), stop=False, skip_group_check=True), reads=[At, Sb[d]], writes=[px], rt=hp)
                    for hh, (h16, q, hp) in enumerate(hs):
                        fw.op("pe", lambda e: e.matmul(px.ap[cr, hh * 64:(hh + 1) * 64], lhsT=Nak.ap[cr, h16, cr], rhs=V.ap[cr, h16 * 64:(h16 + 1) * 64],
                                                       start=False, stop=True, skip_group_check=True), reads=[Nak, V], writes=[px], rt=c * 64)
                    fw.op("act", lambda e: e.copy(out=Xb.ap[cr, g8 * 512:(g8 + 1) * 512], in_=px.ap[cr, :]), reads=[px], writes=[Xb])
                    for hh, (h16, q, hp) in enumerate(hs):
                        fw.op("pe", lambda e: e.matmul(pu.ap[cr, hh * 64:(hh + 1) * 64], lhsT=T.ap[cr, h16, cr], rhs=Xb.ap[cr, h16 * 64:(h16 + 1) * 64],
                                                       start=(hh == 0), stop=True, skip_group_check=True), reads=[T, Xb], writes=[pu], rt=c * 64)
                    fw.op("dve", lambda e: e.tensor_copy(out=Ub.ap[cr, g8 * 512:(g8 + 1) * 512], in_=pu.ap[cr, :]), reads=[pu], writes=[Ub])
                    for i_, (hh, (h16, q, hp)) in enumerate(hso):
                        fw.op("pe", lambda e: e.matmul(py[g8].ap[cr, hh * 64:(hh + 1) * 64], lhsT=Rt.ap[hp:hp + 64, q, cr], rhs=Sb[d].ap[hp:hp + 64, half * 8 + q, :],
                                                       start=(i_ == 0), stop=False, skip_group_check=True), reads=[Rt, Sb[d]], writes=[py[g8]], rt=hp)
                    for hh, (h16, q, hp) in enumerate(hs):
                        fw.op("pe", lambda e: e.matmul(py[g8].ap[cr, hh * 64:(hh + 1) * 64], lhsT=Mrk.ap[cr, h16, cr], rhs=V.ap[cr, h16 * 64:(h16 + 1) * 64],
                                                       start=False, stop=False, skip_group_check=True), reads=[Mrk, V], writes=[py[g8]], rt=c * 64)
                        fw.op("pe", lambda e: e.matmul(py[g8].ap[cr, hh * 64:(hh + 1) * 64], lhsT=MrbN.ap[cr, h16, cr], rhs=Ub.ap[cr, h16 * 64:(h16 + 1) * 64],
                                                       start=False, stop=True, skip_group_check=True), reads=[MrbN, Ub], writes=[py[g8]], rt=c * 64)
                    for q4 in range(4):
                        q = g8 * 4 + q4
                        fw.op("pe", lambda e: e.matmul(pss.ap[:, q4 * 128:(q4 + 1) * 128], lhsT=Km.ap[cr, q * 128:(q + 1) * 128], rhs=V.ap[cr, q * 128:(q + 1) * 128],
                                                       start=(q4 == 0), stop=False, skip_group_check=True), reads=[Km, V], writes=[pss], rt=c * 64)
                        fw.op("pe", lambda e: e.matmul(pss.ap[:, q4 * 128:(q4 + 1) * 128], lhsT=Bm.ap[cr, q * 128:(q + 1) * 128], rhs=Ub.ap[cr, q * 128:(q + 1) * 128],
                                                       start=False, stop=True, skip_group_check=True), reads=[Bm, Ub], writes=[pss], rt=c * 64)
                    for hp in (0, 64):
                        hr = slice(hp, hp + 64)
                        pdiag = pss.ap[hr, :].rearrange("p (q x) -> p q x", q=4)[:, :, hp:hp + 64]
                        wcb = wc.ap[hr, g8 * 4:(g8 + 1) * 4, c:c + 1].to_broadcast([64, 4, 64])
                        Sv = S[d].ap[hr, half * 8 + g8 * 4: half * 8 + (g8 + 1) * 4, :]
                        fw.op("dve", lambda e: e.tensor_tensor(out=tmp.ap[hr], in0=pdiag, in1=wcb, op=ALU.mult), reads=[pss, wc], writes=[tmp])
                        fw.op("pool", lambda e: e.tensor_tensor(out=Sv, in0=Sv, in1=wcb, op=ALU.mult), reads=[S[d], wc], writes=[S[d]])
                        fw.op("pool", lambda e: e.tensor_tensor(out=Sv, in0=Sv, in1=tmp.ap[hr], op=ALU.add), reads=[S[d], tmp], writes=[S[d]])
                    fw.op("act", lambda e: e.copy(out=Sb[d].ap[:, half * 8 + g8 * 4: half * 8 + (g8 + 1) * 4, :],
                                                  in_=S[d].ap[:, half * 8 + g8 * 4: half * 8 + (g8 + 1) * 4, :]), reads=[S[d]], writes=[Sb[d]])
            for g8 in range(2):
                if g8 == 0:
                    fw.op("act", lambda e: e.copy(out=yt.ap[:, g8 * 512:(g8 + 1) * 512], in_=py[g8].ap[:]), reads=[py[g8]], writes=[yt])
                else:
                    fw.op("dve", lambda e: e.tensor_copy(out=yt.ap[:, g8 * 512:(g8 + 1) * 512], in_=py[g8].ap[:]), reads=[py[g8]], writes=[yt])
            fw.dma("sp", Yd[d].ap[r0:r0 + 128, half * 1024:(half + 1) * 1024], yt.ap[:], reads=[yt], writes=[Yd[d]])

        seqs = [(0, NS, None), (NS, NP, 0), (NS + NP, NP, 1)]
        if self.stop in ("rs_a", "rs_b", "rs_0", "rs_c"):
            seqs = seqs[0:2]
        for (row0, T_, pidx) in seqs:
            if self.stop in ("rs_a", "rs_b", "rs_0", "rs_c"):
                T_ = 256
            for d in range(2):
                if pidx is None:
                    for q in range(KC):
                        fw.dma("sp", stg.ap[:], sti.ap[ja, d, 2 * q:2 * q + 2].rearrange("hh v k -> v hh k"), reads=[sti], writes=[stg])
                        p = pg.next()
                        fw.op("pe", lambda e: e.transpose(p.ap[:, 0:64], stg.ap[:].rearrange("v hh k -> v (hh k)"), self.identf.ap[0:64, 0:64]),
                              reads=[stg, self.identf], writes=[p])
                        fw.op("act", lambda e: e.copy(out=S[d].ap[:, q, :], in_=p.ap[:, 0:64]), reads=[p], writes=[S[d]])
                else:
                    fw.op("pool", lambda e: e.memset(S[d].ap[:], 0.0), writes=[S[d]])
                fw.op("act", lambda e: e.copy(out=Sb[d].ap[:], in_=S[d].ap[:]), reads=[S[d]], writes=[Sb[d]])
            nt = T_ // 128
            for i in range(nt):
                if self.stop == "rs_0":
                    break
                for half in range(2):
                    tile_step(0, row0 + i * 128, half)
                    tile_step(1, row0 + (nt - 1 - i) * 128, half)
            if pidx is not None:
                for d in range(2):
                    for q in range(KC):
                        p = pg.next()
                        fw.op("pe", lambda e: e.transpose(p.ap[0:64, 0:128], S[d].ap[:, q, :], self.identf.ap[:]), reads=[S[d], self.identf], writes=[p])
                        fw.op("act", lambda e: e.copy(out=stg2.ap[:], in_=p.ap[0:64, 0:128]), reads=[p], writes=[stg2])
                        fw.dma("sp", st_out.ap[pidx, ja, d, 2 * q:2 * q + 2].rearrange("hh v k -> v hh k"),
                               stg2.ap[:].rearrange("v (hh k) -> v hh k", hh=2), reads=[stg2], writes=[st_out])
        self.phase_end()

    def rwkv_out(self, l, ja, Yd, Vd, Gd, CSd, OT):
        fw = self.fw
        din = self.din
        self.phase_begin()
        lw_row = self.sb("ro_lw", [128, D])
        lb_row = self.sb("ro_lb", [128, D])
        fw.dma("sp", lw_row.ap[:], din["rw_lnx_w"].ap[ja:ja + 1, :].partition_broadcast(128), reads=[din["rw_lnx_w"]], writes=[lw_row])
        fw.dma("sp", lb_row.ap[:], din["rw_lnx_b"].ap[ja:ja + 1, :].partition_broadcast(128), reads=[din["rw_lnx_b"]], writes=[lb_row])
        y0_r = self.sbrot("ro_y0", 2, [128, D])
        y1_r = self.sbrot("ro_y1", 2, [128, D])
        v_r = self.sbrot("ro_v", 2, [128, D], BF16)
        g_r = self.sbrot("ro_g", 2, [128, D], BF16)
        cs_r = self.sbrot("ro_cs", 2, [128, 64])
        st_r = self.sbrot("ro_st", 2, [128, 64])
        o_r = self.sbrot("ro_o", 2, [128, D], BF16)
        OTt = self.sbrot("ro_OTt", 2, [128, KC, 512], BF16)
        ptr = self.psrot("ro_pt", 2, [128, 1024], BF16)
        epsl = self.sb("ro_eps", [128, 1])
        fw.op("pool", lambda e: e.memset(epsl.ap[:], 64e-5), writes=[epsl])
        h3 = lambda ap_: ap_.rearrange("p (h n) -> p h n", h=32)
        for b0 in range(0, R, 512):
            ot = OTt.next()
            for s_ in range(4):
                r0 = b0 + s_ * 128
                y0, y1, v, g, cs, st = y0_r.next(), y1_r.next(), v_r.next(), g_r.next(), cs_r.next(), st_r.next()
                fw.dma("sp", y0.ap[:], Yd[0].ap[r0:r0 + 128, :], reads=[Yd[0]], writes=[y0])
                fw.dma("sp", y1.ap[:], Yd[1].ap[r0:r0 + 128, :], reads=[Yd[1]], writes=[y1])
                fw.dma("sp", v.ap[:], Vd.ap[r0:r0 + 128, :], reads=[Vd], writes=[v])
                fw.dma("sp", g.ap[:], Gd.ap[r0:r0 + 128, :], reads=[Gd], writes=[g])
                fw.dma("sp", cs.ap[:], CSd.ap[r0:r0 + 128, :], reads=[CSd], writes=[cs])
                fw.op("pool", lambda e: e.tensor_tensor(out=y0.ap[:], in0=y0.ap[:], in1=y1.ap[:], op=ALU.add), reads=[y0, y1], writes=[y0])
                fw.op("dve", lambda e: e.tensor_reduce(out=st.ap[:, 0:32], in_=h3(y0.ap[:]), axis=AX.X, op=ALU.add), reads=[y0], writes=[st])
                fw.op("dve", lambda e: e.tensor_scalar(out=st.ap[:, 0:32], in0=st.ap[:, 0:32], scalar1=1.0 / 64.0, scalar2=None, op0=ALU.mult), reads=[st], writes=[st])
                fw.op("dve", lambda e: e.tensor_tensor(out=h3(y0.ap[:]), in0=h3(y0.ap[:]), in1=st.ap[:, 0:32].unsqueeze(2).to_broadcast([128, 32, 64]),
                                                       op=ALU.subtract), reads=[y0, st], writes=[y0])
                fw.op("act", lambda e: e.activation(out=y1.ap[:], in_=y0.ap[:], func=AF.Square), reads=[y0], writes=[y1])
                fw.op("dve", lambda e: e.tensor_reduce(out=st.ap[:, 32:64], in_=h3(y1.ap[:]), axis=AX.X, op=ALU.add), reads=[y1], writes=[st])
                fw.op("act", lambda e: e.activation(out=st.ap[:, 32:64], in_=st.ap[:, 32:64], func=AF.Ln, scale=1.0 / 64.0, bias=epsl.ap[:, 0:1]), reads=[st, epsl], writes=[st])
                fw.op("act", lambda e: e.activation(out=st.ap[:, 32:64], in_=st.ap[:, 32:64], func=AF.Exp, scale=-0.5), reads=[st], writes=[st])
                fw.op("dve", lambda e: e.tensor_tensor(out=h3(y0.ap[:]), in0=h3(y0.ap[:]), in1=st.ap[:, 32:64].unsqueeze(2).to_broadcast([128, 32, 64]),
                                                       op=ALU.mult), reads=[y0, st], writes=[y0])
                fw.op("pool", lambda e: e.tensor_tensor(out=y0.ap[:], in0=y0.ap[:], in1=lw_row.ap[:], op=ALU.mult), reads=[y0, lw_row], writes=[y0])
                fw.op("pool", lambda e: e.tensor_tensor(out=y0.ap[:], in0=y0.ap[:], in1=lb_row.ap[:], op=ALU.add), reads=[y0, lb_row], writes=[y0])
                fw.op("dve", lambda e: e.tensor_tensor(out=cs.ap[:, 0:32], in0=cs.ap[:, 0:32], in1=cs.ap[:, 32:64], op=ALU.add), reads=[cs], writes=[cs])
                fw.op("dve", lambda e: e.tensor_tensor(out=h3(y1.ap[:]), in0=h3(v.ap[:]), in1=cs.ap[:, 0:32].unsqueeze(2).to_broadcast([128, 32, 64]),
                                                       op=ALU.mult), reads=[v, cs], writes=[y1])
                fw.op("pool", lambda e: e.tensor_tensor(out=y0.ap[:], in0=y0.ap[:], in1=y1.ap[:], op=ALU.add), reads=[y0, y1], writes=[y0])
                o = o_r.next()
                fw.op("dve", lambda e: e.tensor_tensor(out=o.ap[:], in0=y0.ap[:], in1=g.ap[:], op=ALU.mult), reads=[y0, g], writes=[o])
                for hf in range(2):
                    pt = ptr.next()
                    for q in range(8):
                        kc = hf * 8 + q
                        fw.op("pe", lambda e: e.transpose(pt.ap[:, q * 128:(q + 1) * 128], o.ap[:, kc * 128:(kc + 1) * 128], self.identb.ap[:]),
                              reads=[o, self.identb], writes=[pt])
                    fw.op("act", lambda e: e.copy(out=ot.ap[:, hf * 8:(hf + 1) * 8, s_ * 128:(s_ + 1) * 128],
                                                  in_=pt.ap[:].rearrange("p (q t) -> p q t", q=8)), reads=[pt], writes=[ot])
            fw.dma("sp", OT.ap[:, b0:b0 + 512].rearrange("(kc p) t -> p kc t", p=128), ot.ap[:], reads=[ot], writes=[OT])
        self.phase_end()
        if self.stop == "rw3":
            return
        self.out_proj(l, OT, din["rw_wo"], din["rw_wo"].ap[ja], 2)

    def final_norm(self):
        fw = self.fw
        self.phase_begin()
        Y = self.dout["y"]
        fnw = self.din["final_norm_w"]
        wrow = self.sb("fn_w", [128, D])
        fw.dma("sp", wrow.ap[:], fnw.ap.rearrange("(o d) -> o d", o=1).partition_broadcast(128), reads=[fnw], writes=[wrow])
        self.pp_junk = self.sb("pp_junk", [128, D], BF16)
        xr = self.sbrot("fn_x", 3, [128, D])
        yr = self.sbrot("fn_y", 3, [128, D])
        sts = self.sbrot("fn_st", 4, [128, 4])
        for s in range(R // 128):
            x = xr.next()
            fw.dma("sp", x.ap[:], self.Xt[s].ap, reads=[self.Xt[s]], writes=[x])
            st = sts.next()
            rs = self.rstd_of(x, st)
            y = yr.next()
            fw.op("dve", lambda e: e.scalar_tensor_tensor(out=y.ap[:], in0=x.ap[:], scalar=rs, in1=wrow.ap[:],
                                                          op0=ALU.mult, op1=ALU.mult),
                  reads=[x, st, wrow], writes=[y])
            fw.dma("sp", Y.ap[s * 128:(s + 1) * 128, :], y.ap[:], reads=[y], writes=[Y])
        self.phase_end()

    def declare(self):
        self.inp("xin", [R, D])
        self.inp("cond", [2, D])
        self.inp("ada_w", [DEPTH, D, 6 * D])
        self.inp("ada_b", [DEPTH, 6 * D])
        self.inp("norm1_w", [DEPTH, D])
        self.inp("norm2_w", [DEPTH, D])
        self.inp("ffn_w_in", [DEPTH, D, 2 * DFF])
        self.inp("ffn_w_out", [DEPTH, DFF, D])
        self.inp("final_norm_w", [D])
        self.inp("ml_w_down", [1, D, 1088])
        self.inp("ml_qnorm_w", [1, 512])
        self.inp("ml_kvnorm_w", [1, 512])
        self.inp("ml_w_uq", [1, 512, 3072])
        self.inp("ml_w_ukv", [1, 512, 4096])
        self.inp("ml_wo", [1, D, D])
        self.inp("cache_ckv", [1, 256, 512])
        self.inp("cache_krope", [1, 256, 64])
        self.inp("hg_w_in", [1, D, 5 * D])
        self.inp("hg_lb", [DEPTH, D])
        self.inp("hg_norm_w", [1, 128])
        self.inp("hg_wo", [1, D, D])
        self.inp("state_hgrn", [1, 2, 16, 128, 128])
        self.inp("hg_mask", [2, 64, 64])
        self.outp("st_hgrn", [2, 2, 16, 128, 128])
        for nm, shp in (("rw_mu", [2, 6, D]), ("rw_wr", [2, D, D]), ("rw_wk", [2, D, D]), ("rw_wv", [2, D, D]), ("rw_wo", [2, D, D]),
                        ("rw_w0", [2, 2, D]), ("rw_w1", [2, 2, D, 96]), ("rw_w2", [2, 2, 96, D]), ("rw_a0", [2, 2, D]),
                        ("rw_a1", [2, 2, D, 96]), ("rw_a2", [2, 2, 96, D]), ("rw_g1", [2, D, 256]), ("rw_g2", [2, 256, D]),
                        ("rw_kk", [2, D]), ("rw_ka", [2, D]), ("rw_rk", [2, 2, 32, 64]), ("rw_lnx_w", [2, D]), ("rw_lnx_b", [2, D]),
                        ("state_rwkv", [2, 2, 32, 64, 64]), ("rw_mask", [8, 128, 128])):
            self.inp(nm, shp)
        self.outp("st_rwkv", [2, 2, 2, 32, 64, 64])
        self.inp("rope_cos", [NS, 64])
        self.inp("rope_sin", [NS, 64])
        self.outp("y", [R, D])
        self.outp("ckv", [2 * NP, 512])
        self.outp("krope", [2 * NP, 64])
        Xs = self.scratch("X", [R, D])
        self.Xt = [Buf(Xs.ap[i * 128:(i + 1) * 128, :], "X%d" % i) for i in range(R // 128)]
        self.mrow = self.scratch("mrow", [DEPTH, 2, 6 * D])

    def copy_in(self):
        fw = self.fw
        xin = self.din["xin"]
        for i in range(R // 128):
            fw.dma("sp", self.Xt[i].ap, xin.ap[i * 128:(i + 1) * 128, :], reads=[xin], writes=[self.Xt[i]])

    def build(self):
        fw = self.fw
        self.declare()
        self.consts_begin()
        self.epsc = self.sb("epsc", [128, 2])
        fw.op("pool", lambda e: e.memset(self.epsc.ap[:, 0:1], EPS), writes=[self.epsc])
        fw.op("pool", lambda e: e.memset(self.epsc.ap[:, 1:2], 64e-5), writes=[self.epsc])
        self.copy_in()
        self.adaln()
        for (l, what) in self.plan:
            if what == "ffn":
                self.ffn(l)
            elif what == "mix":
                self.mixer(l)
        self.final_norm()
        fw.barrier()
        return self.nc

    def mixer(self, l):
        kind = l % 3
        if kind == 2:
            self.mla(l)
        elif kind == 1:
            self.hgrn(l)
        else:
            self.rwkv(l)


FULL_PLAN = [(l, w) for l in range(DEPTH) for w in ("mix", "ffn")]


def rope_tables():
    t = np.arange(NS)
    row = (t // 64).astype(np.float32)
    col = (t % 64).astype(np.float32)
    nf = 16
    inv = (np.float32(10000.0) ** (-np.arange(nf, dtype=np.float32) / np.float32(nf))).astype(np.float32)
    ar = row[:, None] * inv[None, :]
    ac = col[:, None] * inv[None, :]
    cs = np.concatenate([np.cos(ar), np.cos(ar), np.cos(ac), np.cos(ac)], axis=1).astype(np.float32)
    sn = np.concatenate([-np.sin(ar), np.sin(ar), -np.sin(ac), np.sin(ac)], axis=1).astype(np.float32)
    return np.ascontiguousarray(cs), np.ascontiguousarray(sn)


def hg_masks():
    i = np.arange(64)
    same = (i[:, None] // 32) == (i[None, :] // 32)
    mf = (same & (i[:, None] <= i[None, :])).astype(np.float32)
    mb = (same & (i[:, None] >= i[None, :])).astype(np.float32)
    return np.ascontiguousarray(np.stack([mf, mb], 0))


def rw_masks():
    i = np.arange(128)
    same = (i[:, None] // 64) == (i[None, :] // 64)
    out = []
    for d in range(2):
        lt = (i[:, None] < i[None, :]) if d == 0 else (i[:, None] > i[None, :])
        le = (i[:, None] <= i[None, :]) if d == 0 else (i[:, None] >= i[None, :])
        ms = (same & lt).astype(np.float32)
        mi = (same & le).astype(np.float32)
        out += [ms, ms.T.copy(), mi, -mi]
    return np.ascontiguousarray(np.stack(out, 0))


def make_in_maps(inputs, cores):
    maps = []
    for i in cores:
        m = {
            "xin": np.ascontiguousarray(np.concatenate(
                [inputs["x_sample"][i], inputs["x_prompt"][2 * i], inputs["x_prompt"][2 * i + 1]], axis=0)),
            "cond": np.ascontiguousarray(np.stack([inputs["c"][i], inputs["c_ctx"]], axis=0)),
        }
        for k in ("ada_w", "ada_b", "norm1_w", "norm2_w", "ffn_w_in", "ffn_w_out", "final_norm_w",
                  "ml_w_down", "ml_qnorm_w", "ml_kvnorm_w", "ml_w_uq", "ml_w_ukv", "ml_wo",
                  "hg_w_in", "hg_lb", "hg_norm_w", "hg_wo",
                  "rw_mu", "rw_wr", "rw_wk", "rw_wv", "rw_wo", "rw_w0", "rw_w1", "rw_w2", "rw_a0", "rw_a1", "rw_a2",
                  "rw_g1", "rw_g2", "rw_kk", "rw_ka", "rw_rk", "rw_lnx_w", "rw_lnx_b"):
            m[k] = np.ascontiguousarray(inputs[k])
        m["cache_ckv"] = np.ascontiguousarray(inputs["cache_ckv"][i])
        m["cache_krope"] = np.ascontiguousarray(inputs["cache_krope"][i])
        m["state_hgrn"] = np.ascontiguousarray(inputs["state_hgrn"][i])
        m["hg_mask"] = hg_masks()
        m["state_rwkv"] = np.ascontiguousarray(inputs["state_rwkv"][i])
        m["rw_mask"] = rw_masks()
        cs, sn = rope_tables()
        m["rope_cos"], m["rope_sin"] = cs, sn
        maps.append(m)
    return maps


def kernel(**inputs):
    inputs = {k: np.asarray(v) for k, v in inputs.items()}
    b = Builder(FULL_PLAN)
    nc = b.build()
    cores = list(range(8))
    res = run_bass_kernel_spmd(nc, make_in_maps(inputs, cores), core_ids=cores)
    rs = res.results
    f32 = np.float32
    y_sample = np.stack([np.asarray(rs[i]["y"])[0:NS] for i in cores], 0).astype(f32)
    y_prompt = np.concatenate([np.asarray(rs[i]["y"])[NS:R].reshape(2, NP, D) for i in cores], 0).astype(f32)
    st_rwkv = np.concatenate([np.asarray(rs[i]["st_rwkv"]).reshape(2, 2, 2, 32, 64, 64) for i in cores], 0).astype(f32)
    st_hgrn = np.concatenate([np.asarray(rs[i]["st_hgrn"]).reshape(2, 1, 2, 16, 128, 128) for i in cores], 0).astype(f32)
    ckv = np.concatenate([np.asarray(rs[i]["ckv"]).reshape(2, 1, NP, 512) for i in cores], 0).astype(f32)
    krope = np.concatenate([np.asarray(rs[i]["krope"]).reshape(2, 1, NP, 64) for i in cores], 0).astype(f32)
    return (y_prompt, y_sample, st_rwkv, st_hgrn, ckv, krope)
```

```python
import math
import numpy as np
import concourse.bass as bass
import concourse.mybir as mybir
from concourse.bass_utils import run_bass_kernel_spmd

F32 = mybir.dt.float32
BF16 = mybir.dt.bfloat16
AF = mybir.ActivationFunctionType
ALU = mybir.AluOpType
AX = mybir.AxisListType

D = 2048
KC = 16
DFF = 5632
NS = 4096
NP = 256
R = NS + 2 * NP
DEPTH = 4
EPS = 1e-6


class Buf:
    __slots__ = ("ap", "lw", "rd", "name", "psum")

    def __init__(self, ap, name="", psum=False):
        self.ap = ap
        self.lw = None
        self.rd = []
        self.name = name
        self.psum = psum


class FW:
    def __init__(self, nc, ndma=64):
        self.nc = nc
        self.eng = {"pe": nc.tensor, "act": nc.scalar, "dve": nc.vector, "pool": nc.gpsimd, "sp": nc.sync}
        self.sem = {}
        self.cnt = {}
        for k in self.eng:
            self.sem[k] = nc.alloc_semaphore("s_" + k)
            self.cnt[k] = 0
        self.ndma = ndma
        self.dsem = [nc.alloc_semaphore("d%d" % i) for i in range(ndma)]
        self.dcnt = [0] * ndma
        self.dslots = {"sp": list(range(0, ndma // 2)), "pool": list(range(ndma // 2, 3 * ndma // 4)),
                       "act": list(range(3 * ndma // 4, ndma))}
        self.dnext = {"sp": 0, "pool": 0, "act": 0}
        self.waited = {k: {} for k in self.eng}
        self.n_inst = 0
        self.n_wait = 0

    def _semobj(self, key):
        return self.sem[key] if isinstance(key, str) else self.dsem[key]

    def _wait(self, e, deps, skip_self):
        w = self.waited[e]
        need = {}
        for d in deps:
            if d is None:
                continue
            k, v = d
            if skip_self and k == e:
                continue
            if w.get(k, 0) >= v:
                continue
            if need.get(k, 0) < v:
                need[k] = v
        for k, v in need.items():
            self.eng[e].wait_ge(self._semobj(k), v)
            w[k] = v
            self.n_wait += 1

    @staticmethod
    def _deps(reads, writes, e=None):
        deps = []
        for b in reads:
            deps.append(b.lw)
            if b.psum:
                deps.extend(ev for ev in b.rd if ev[0] != e)
        for b in writes:
            deps.append(b.lw)
            deps.extend(b.rd)
        return deps

    @staticmethod
    def _commit(ev, reads, writes):
        for b in reads:
            b.rd.append(ev)
            if len(b.rd) > 48:
                m = {}
                for k, v in b.rd:
                    if m.get(k, 0) < v:
                        m[k] = v
                b.rd = list(m.items())
        for b in writes:
            b.lw = ev
            b.rd = []

    def op(self, e, fn, reads=(), writes=(), rt=None):
        self._wait(e, self._deps(reads, writes, e), skip_self=(e == "pe"))
        if e == "pe":
            last = getattr(self, "last_rt", None)
            if rt is not None and last is not None and rt != last and self.cnt["pe"] > 0:
                self.eng[e].wait_ge(self.sem["pe"], self.cnt["pe"])
                self.n_wait += 1
            self.last_rt = rt
        ins = fn(self.eng[e])
        self.cnt[e] += 1
        ins.then_inc(self.sem[e], 1)
        self._commit((e, self.cnt[e]), reads, writes)
        self.n_inst += 1
        return ins

    def dma(self, e, out, in_, reads=(), writes=(), **kw):
        self._wait(e, self._deps(reads, writes), skip_self=False)
        sl = self.dslots[e]
        i = sl[self.dnext[e]]
        self.dnext[e] = (self.dnext[e] + 1) % len(sl)
        if self.dcnt[i] > 0:
            self._wait(e, [(i, self.dcnt[i])], skip_self=False)
        ins = self.eng[e].dma_start(out=out, in_=in_, **kw)
        self.dcnt[i] += 16
        ins.then_inc(self.dsem[i], 16)
        self._commit((i, self.dcnt[i]), reads, writes)
        self.n_inst += 1
        return ins

    def barrier(self):
        evs = [(k, c) for k, c in self.cnt.items() if c > 0]
        evs += [(i, c) for i, c in enumerate(self.dcnt) if c > 0]
        for e in self.eng:
            self._wait(e, evs, skip_self=True)


class Rot:
    def __init__(self, bufs):
        self.bufs = bufs
        self.i = 0

    def next(self):
        b = self.bufs[self.i]
        self.i = (self.i + 1) % len(self.bufs)
        return b


class WStream:
    def __init__(self, rot, loaders):
        self.rot = rot
        self.loaders = loaders
        self.depth = len(rot.bufs)
        self.issued = []
        self.i = 0

    def _issue(self):
        k = len(self.issued)
        if k < len(self.loaders):
            b = self.rot.next()
            self.issued.append((b, self.loaders[k](b)))

    def next(self):
        while len(self.issued) < min(len(self.loaders), self.i + self.depth - 1) or len(self.issued) <= self.i:
            self._issue()
        r = self.issued[self.i]
        self.issued[self.i] = None
        self.i += 1
        return r


class Builder:
    def __init__(self, plan, TB=512, debug_out=()):
        self.plan = plan
        self.TB = TB
        self.stop = None
        self.nc = bass.Bass("TRN2", target_bir_lowering=False)
        self.fw = FW(self.nc)
        self.din = {}
        self.dout = {}
        self.stack = []
        self.debug_out = debug_out

    def inp(self, name, shape):
        ap = self.nc.dram_tensor(name, list(shape), F32, kind="ExternalInput").ap()
        self.din[name] = Buf(ap, name)
        return self.din[name]

    def outp(self, name, shape):
        ap = self.nc.dram_tensor(name, list(shape), F32, kind="ExternalOutput").ap()
        self.dout[name] = Buf(ap, name)
        return self.dout[name]

    def scratch(self, name, shape, dtype=F32):
        kind = "ExternalOutput" if name in self.debug_out else "Internal"
        ap = self.nc.dram_tensor(name, list(shape), dtype, kind=kind).ap()
        return Buf(ap, name)

    def phase_begin(self):
        self.fw.barrier()
        self.stack.append([])

    def phase_end(self):
        self.fw.barrier()
        guards = self.stack.pop()
        for g in reversed(guards):
            g.__exit__(None, None, None)

    def sb(self, name, shape, dtype=F32):
        self.uid = getattr(self, "uid", 0) + 1
        name = "%s_u%d" % (name, self.uid)
        g = self.nc.sbuf_tensor(name, list(shape), dtype)
        t = g.__enter__()
        self.stack[-1].append(g)
        return Buf(t, name)

    def ps(self, name, shape, dtype=F32):
        self.uid = getattr(self, "uid", 0) + 1
        name = "%s_u%d" % (name, self.uid)
        g = self.nc.psum_tensor(name, list(shape), dtype)
        t = g.__enter__()
        self.stack[-1].append(g)
        return Buf(t, name, psum=True)

    def sbrot(self, name, n, shape, dtype=F32):
        return Rot([self.sb("%s%d" % (name, i), shape, dtype) for i in range(n)])

    def psrot(self, name, n, shape, dtype=F32):
        return Rot([self.ps("%s%d" % (name, i), shape, dtype) for i in range(n)])

    def colvecs(self, rows, out, ps, stage, ident):
        fw = self.fw
        n = len(rows)
        assert n * 16 <= 128
        for i, (b, ap) in enumerate(rows):
            fw.dma("sp", stage.ap[i * 16:(i + 1) * 16, :], ap.rearrange("(kc p) -> kc p", p=128),
                   reads=[b], writes=[stage])
        fw.op("pe", lambda e: e.transpose(ps.ap[:, 0:n * 16], stage.ap[0:n * 16, :], ident.ap[0:n * 16, 0:n * 16]),
              reads=[stage, ident], writes=[ps])
        fw.op("dve", lambda e: e.tensor_copy(out=out.ap[:, 0:n * 16], in_=ps.ap[:, 0:n * 16]), reads=[ps], writes=[out])

    def group_of(self, row):
        return 0 if row < NS else 1

    def blocks(self, TB=None):
        TB = TB or self.TB
        out = []
        t = 0
        while t < NS:
            out.append((t, min(TB, NS - t)))
            t += TB
        t = NS
        while t < R:
            out.append((t, min(TB, R - t)))
            t += TB
        return out

    def consts_begin(self):
        nc, fw = self.nc, self.fw
        self.stack.append([])
        self.identf = self.sb("identf", [128, 128], F32)
        self.identb = self.sb("identb", [128, 128], BF16)
        for t in (self.identf, self.identb):
            fw.op("pool", lambda e: e.memset(t.ap[:], 1.0), writes=[t])
            fw.op("pool", lambda e: e.affine_select(out=t.ap[:], in_=t.ap[:], pattern=[[-1, 128]],
                                                    compare_op=ALU.is_equal, fill=0.0, base=0, channel_multiplier=1),
                  reads=[t], writes=[t])

    def adaln(self):
        fw = self.fw
        self.phase_begin()
        cond = self.din["cond"]
        adaw, adab = self.din["ada_w"], self.din["ada_b"]
        stage = self.sb("ad_stage", [32, 128])
        pst = self.ps("ad_pst", [128, 32])
        scT = self.sb("ad_scT", [128, 32], BF16)
        fw.dma("sp", stage.ap[:], cond.ap.rearrange("j (kc p) -> (j kc) p", p=128), reads=[cond], writes=[stage])
        fw.op("pe", lambda e: e.transpose(pst.ap[:], stage.ap[:], self.identf.ap[0:32, 0:32]),
              reads=[stage, self.identf], writes=[pst])
        fw.op("act", lambda e: e.activation(out=scT.ap[:], in_=pst.ap[:], func=AF.Silu), reads=[pst], writes=[scT])
        scv = scT.ap[:].rearrange("p (j kc) -> p j kc", j=2)
        wrot = self.sbrot("ad_w", 3, [128, KC, 512], BF16)
        prot = self.psrot("ad_ps", 2, [2, 512])
        brow = self.sb("ad_b", [2, 6 * D])
        mout = self.sbrot("ad_m", 2, [2, 6 * D])
        for l in range(DEPTH):
            fw.dma("sp", brow.ap[:], adab.ap[l:l + 1, :].partition_broadcast(2), reads=[adab], writes=[brow])
            mo = mout.next()
            for cb in range(6 * D // 512):
                w = wrot.next()
                fw.dma("pool", w.ap[:], adaw.ap[l, :, cb * 512:(cb + 1) * 512].rearrange("(kc p) n -> p kc n", p=128),
                       reads=[adaw], writes=[w])
                p = prot.next()
                for kc in range(KC):
                    fw.op("pe", lambda e: e.matmul(p.ap[:], lhsT=scv[:, :, kc], rhs=w.ap[:, kc, :],
                                                   start=(kc == 0), stop=(kc == KC - 1)),
                          reads=[scT, w], writes=[p])
                fw.op("dve", lambda e: e.tensor_tensor(out=mo.ap[:, cb * 512:(cb + 1) * 512], in0=p.ap[:],
                                                       in1=brow.ap[:, cb * 512:(cb + 1) * 512], op=ALU.add),
                      reads=[p, brow], writes=[mo])
            fw.dma("sp", self.mrow.ap[l], mo.ap[:], reads=[mo], writes=[self.mrow])
        self.phase_end()

    def prep_setup(self, l, which):
        fw = self.fw
        nw = self.din["norm1_w" if which == 0 else "norm2_w"]
        sh_s, sc_s = (0, 1) if which == 0 else (3, 4)
        m = self.mrow
        stage = self.sb("pp_stage", [128, 128])
        pst = self.ps("pp_pst", [128, 128])
        raw = self.sb("pp_raw", [128, 128])
        rows = [(nw, nw.ap[l]),
                (m, m.ap[l, 0, sc_s * D:(sc_s + 1) * D]), (m, m.ap[l, 1, sc_s * D:(sc_s + 1) * D]),
                (m, m.ap[l, 0, sh_s * D:(sh_s + 1) * D]), (m, m.ap[l, 1, sh_s * D:(sh_s + 1) * D])]
        self.colvecs(rows, raw, pst, stage, self.identf)
        modc = self.sb("pp_modc", [128, 64])
        for g in range(2):
            fw.op("dve", lambda e: e.scalar_tensor_tensor(out=modc.ap[:, g * 16:(g + 1) * 16],
                                                          in0=raw.ap[:, (1 + g) * 16:(2 + g) * 16], scalar=1.0,
                                                          in1=raw.ap[:, 0:16], op0=ALU.add, op1=ALU.mult),
                  reads=[raw], writes=[modc])
            fw.op("dve", lambda e: e.tensor_copy(out=modc.ap[:, (2 + g) * 16:(3 + g) * 16],
                                                 in_=raw.ap[:, (3 + g) * 16:(4 + g) * 16]),
                  reads=[raw], writes=[modc])
        self.modc = modc
        self.pp_x = self.sbrot("pp_x", 2, [128, D])
        self.pp_xn = self.sbrot("pp_xn", 2, [128, D], BF16)
        self.pp_junk = self.sb("pp_junk", [128, D], BF16)
        self.pp_st = self.sbrot("pp_st", 4, [128, 4])
        self.pp_ps = self.psrot("pp_ps", 2, [128, 8 * 128], BF16)

    def rstd_of(self, x, st, width=D, eps=EPS):
        fw = self.fw
        fw.op("act", lambda e: e.activation(out=self.pp_junk.ap[:, 0:width], in_=x.ap[:, 0:width], func=AF.Square,
                                            accum_out=st.ap[:, 0:1]),
              reads=[x], writes=[self.pp_junk, st])
        fw.op("act", lambda e: e.activation(out=st.ap[:, 1:2], in_=st.ap[:, 0:1], func=AF.Ln, scale=1.0 / width,
                                            bias=self.epsc.ap[:, 0:1] if eps == EPS else self.epsc.ap[:, 1:2]),
              reads=[st, self.epsc], writes=[st])
        fw.op("act", lambda e: e.activation(out=st.ap[:, 1:2], in_=st.ap[:, 1:2], func=AF.Exp, scale=-0.5),
              reads=[st], writes=[st])
        return st.ap[:, 1:2]

    def prep_block(self, t0, tb, hT, col0=0):
        fw = self.fw
        g = self.group_of(t0)
        modc = self.modc
        for s in range(tb // 128):
            x = self.pp_x.next()
            r0 = t0 + s * 128
            X = self.Xt[r0 // 128]
            fw.dma("sp", x.ap[:], X.ap, reads=[X], writes=[x])
            st = self.pp_st.next()
            rs = self.rstd_of(x, st)
            xn = self.pp_xn.next()
            fw.op("dve", lambda e: e.tensor_scalar(out=xn.ap[:], in0=x.ap[:], scalar1=rs, scalar2=None, op0=ALU.mult),
                  reads=[x, st], writes=[xn])
            for half in range(2):
                p = self.pp_ps.next()
                for q in range(8):
                    kc = half * 8 + q
                    fw.op("pe", lambda e: e.transpose(p.ap[:, q * 128:(q + 1) * 128], xn.ap[:, kc * 128:(kc + 1) * 128],
                                                      self.identb.ap[:]),
                          reads=[xn, self.identb], writes=[p])
                for q in range(8):
                    kc = half * 8 + q
                    dst = hT.ap[:, kc, col0 + s * 128: col0 + (s + 1) * 128]
                    src = p.ap[:, q * 128:(q + 1) * 128]
                    a_ap = modc.ap[:, g * 16 + kc: g * 16 + kc + 1]
                    b_ap = modc.ap[:, (2 + g) * 16 + kc: (2 + g) * 16 + kc + 1]
                    if half == 0:
                        fw.op("act", lambda e: e.activation(out=dst, in_=src, func=AF.Identity, scale=a_ap, bias=b_ap),
                              reads=[p, modc], writes=[hT])
                    else:
                        fw.op("dve", lambda e: e.tensor_scalar(out=dst, in0=src, scalar1=a_ap, scalar2=b_ap,
                                                               op0=ALU.mult, op1=ALU.add),
                              reads=[p, modc], writes=[hT])

    def gate_rows(self, l, sect, name):
        fw = self.fw
        g = self.sb(name, [128, 2 * D])
        for j in range(2):
            fw.dma("sp", g.ap[:, j * D:(j + 1) * D],
                   self.mrow.ap[l, j:j + 1, sect * D:(sect + 1) * D].partition_broadcast(128),
                   reads=[self.mrow], writes=[g])
        return g

    def load_w(self, wbuf, off, wsrc, src_ap, kcn, ncols):
        dst = wbuf.ap[:, off: off + kcn * ncols].rearrange("p (kc n) -> p kc n", kc=kcn)
        self.fw.dma("pool", dst, src_ap.rearrange("(kc p) n -> p kc n", p=128), reads=[wsrc], writes=[wbuf])
        return dst

    def ffn(self, l):
        fw = self.fw
        self.phase_begin()
        TB = self.TB
        w_in, w_out = self.din["ffn_w_in"], self.din["ffn_w_out"]
        self.prep_setup(l, 1)
        grow = self.gate_rows(l, 5, "ffn_g")
        hT = self.sb("ffn_hT", [128, KC, TB], BF16)
        uT = self.sb("ffn_uT", [128, DFF // 128, TB], BF16)
        CW = 256
        NK = DFF // 128
        WSZ = NK * CW
        wrot = self.sbrot("ffn_w", 3, [128, WSZ], BF16)
        psa = self.psrot("ffn_pa", 2, [128, 512])
        psb = self.psrot("ffn_pb", 2, [128, 512])
        sil = self.sbrot("ffn_sil", 2, [128, 512])
        xo = self.sbrot("ffn_xo", 3, [128, CW])
        xt = self.sbrot("ffn_xt", 3, [128, CW])
        blocks = self.blocks()
        w_in_v = w_in.ap[l].rearrange("(kc p) (two f) -> p two kc f", p=128, two=2)
        w_out_v = w_out.ap[l].rearrange("(kc p) n -> p kc n", p=128)

        def ld_in(cb):
            def f(buf):
                dst = buf.ap[:, 0:2 * KC * CW].rearrange("p (two kc n) -> p two kc n", two=2, kc=KC)
                fw.dma("pool", dst, w_in_v[:, :, :, cb * CW:(cb + 1) * CW], reads=[w_in], writes=[buf])
                return dst
            return f

        def ld_out(cb):
            def f(buf):
                dst = buf.ap[:, 0:NK * CW].rearrange("p (kc n) -> p kc n", kc=NK)
                fw.dma("pool", dst, w_out_v[:, :, cb * CW:(cb + 1) * CW], reads=[w_out], writes=[buf])
                return dst
            return f

        loaders = []
        for _ in blocks:
            loaders += [ld_in(cb) for cb in range(DFF // CW)]
            loaders += [ld_out(cb) for cb in range(D // CW)]
        ws = WStream(wrot, loaders)
        for (t0, tb) in blocks:
            g = self.group_of(t0)
            self.prep_block(t0, tb, hT)
            for cb in range(DFF // CW):
                w, wv = ws.next()
                for j in range(CW // 128):
                    fc = cb * (CW // 128) + j
                    for ts in range(0, tb, 512):
                        n = min(512, tb - ts)
                        pa, pb = psa.next(), psb.next()
                        for (pp, two) in ((pa, 0), (pb, 1)):
                            for kc in range(KC):
                                fw.op("pe", lambda e: e.matmul(pp.ap[:, 0:n], lhsT=wv[:, two, kc, j * 128:(j + 1) * 128],
                                                               rhs=hT.ap[:, kc, ts:ts + n], start=(kc == 0), stop=(kc == KC - 1)),
                                      reads=[w, hT], writes=[pp])
                        s_ = sil.next()
                        fw.op("act", lambda e: e.activation(out=s_.ap[:, 0:n], in_=pa.ap[:, 0:n], func=AF.Silu),
                              reads=[pa], writes=[s_])
                        fw.op("dve", lambda e: e.tensor_tensor(out=uT.ap[:, fc, ts:ts + n], in0=pb.ap[:, 0:n],
                                                               in1=s_.ap[:, 0:n], op=ALU.mult),
                              reads=[pb, s_], writes=[uT])
            for cb in range(D // CW):
                w, w2 = ws.next()
                for s in range(tb // 128):
                    X = self.Xt[(t0 + s * 128) // 128]
                    p = psa.next()
                    xold = xo.next()
                    fw.dma("sp", xold.ap[:], X.ap[:, cb * CW:(cb + 1) * CW], reads=[X], writes=[xold])
                    for kc in range(NK):
                        fw.op("pe", lambda e: e.matmul(p.ap[:, 0:CW], lhsT=uT.ap[:, kc, s * 128:(s + 1) * 128],
                                                       rhs=w2[:, kc, :], start=(kc == 0), stop=(kc == NK - 1)),
                              reads=[w, uT], writes=[p])
                    xn = xt.next()
                    fw.op("dve", lambda e: e.tensor_tensor(out=xn.ap[:], in0=p.ap[:, 0:CW],
                                                           in1=grow.ap[:, g * D + cb * CW: g * D + (cb + 1) * CW], op=ALU.mult),
                          reads=[p, grow], writes=[xn])
                    fw.op("dve", lambda e: e.tensor_tensor(out=xn.ap[:], in0=xn.ap[:], in1=xold.ap[:], op=ALU.add),
                          reads=[xn, xold], writes=[xn])
                    fw.dma("sp", X.ap[:, cb * CW:(cb + 1) * CW], xn.ap[:], reads=[xn], writes=[X])
        self.phase_end()

    def out_proj(self, l, AT, W_buf, W_ap, gate_sect):
        fw = self.fw
        self.phase_begin()
        TB = self.TB
        CW = 512
        grow = self.gate_rows(l, gate_sect, "op_g")
        aT = self.sbrot("op_aT", 2, [128, KC, TB], BF16)
        wrot = self.sbrot("op_w", 3, [128, KC * CW], BF16)
        psr = self.psrot("op_ps", 4, [128, 512])
        xo = self.sbrot("op_xo", 3, [128, CW])
        xt = self.sbrot("op_xt", 3, [128, CW])
        blocks = self.blocks()
        Wv = W_ap.rearrange("(kc p) n -> p kc n", p=128)

        def ld(cb):
            def f(buf):
                dst = buf.ap[:].rearrange("p (kc n) -> p kc n", kc=KC)
                fw.dma("pool", dst, Wv[:, :, cb * CW:(cb + 1) * CW], reads=[W_buf], writes=[buf])
                return dst
            return f
        loaders = []
        for _ in blocks:
            loaders += [ld(cb) for cb in range(D // CW)]
        ws = WStream(wrot, loaders)
        ATv = AT.ap.rearrange("(kc p) r -> p kc r", p=128)
        for (t0, tb) in blocks:
            g = self.group_of(t0)
            a = aT.next()
            fw.dma("sp", a.ap[:, :, 0:tb], ATv[:, :, t0:t0 + tb], reads=[AT], writes=[a])
            for cb in range(D // CW):
                w, wv = ws.next()
                for s_ in range(tb // 128):
                    X = self.Xt[(t0 + s_ * 128) // 128]
                    p = psr.next()
                    xold = xo.next()
                    fw.dma("sp", xold.ap[:], X.ap[:, cb * CW:(cb + 1) * CW], reads=[X], writes=[xold])
                    for kc in range(KC):
                        fw.op("pe", lambda e: e.matmul(p.ap[:, 0:CW], lhsT=a.ap[:, kc, s_ * 128:(s_ + 1) * 128],
                                                       rhs=wv[:, kc, :], start=(kc == 0), stop=(kc == KC - 1)),
                              reads=[w, a], writes=[p])
                    xn = xt.next()
                    fw.op("dve", lambda e: e.tensor_tensor(out=xn.ap[:], in0=p.ap[:, 0:CW],
                                                           in1=grow.ap[:, g * D + cb * CW: g * D + (cb + 1) * CW], op=ALU.mult),
                          reads=[p, grow], writes=[xn])
                    fw.op("dve", lambda e: e.tensor_tensor(out=xn.ap[:], in0=xn.ap[:], in1=xold.ap[:], op=ALU.add),
                          reads=[xn, xold], writes=[xn])
                    fw.dma("sp", X.ap[:, cb * CW:(cb + 1) * CW], xn.ap[:], reads=[xn], writes=[X])
        self.phase_end()

    def mla(self, l):
        fw = self.fw
        j = l // 3
        NKT = NS + 256 + 2 * NP
        H, DN, DR, DV = 16, 128, 64, 128
        QnT = self.scratch("ml_QnT", [H, 128, R], BF16)
        QrT = self.scratch("ml_QrT", [H, 64, R], BF16)
        KnT = self.scratch("ml_KnT", [H, 128, NKT], BF16)
        KrT = self.scratch("ml_KrT", [64, NKT], BF16)
        Vd = self.scratch("ml_V", [NKT, H * DV], BF16)
        OT = self.scratch("ml_OT", [D, R], BF16)
        w_down, w_uq, w_ukv = self.din["ml_w_down"], self.din["ml_w_uq"], self.din["ml_w_ukv"]
        ckv_out, kr_out = self.dout["ckv"], self.dout["krope"]

        self.phase_begin()
        TB = 256
        self.prep_setup(l, 0)
        Wd = self.sb("ml_Wd", [128, KC, 1088], BF16)
        Wq = self.sb("ml_Wq", [128, 4, 3072], BF16)
        Wkv = self.sb("ml_Wkv", [128, 4, 4096], BF16)
        fw.dma("pool", Wd.ap[:], w_down.ap[j].rearrange("(kc p) n -> p kc n", p=128), reads=[w_down], writes=[Wd])
        fw.dma("pool", Wq.ap[:], w_uq.ap[j].rearrange("(kc p) n -> p kc n", p=128), reads=[w_uq], writes=[Wq])
        fw.dma("pool", Wkv.ap[:], w_ukv.ap[j].rearrange("(kc p) n -> p kc n", p=128), reads=[w_ukv], writes=[Wkv])
        qnw = self.sb("ml_qnw", [128, 512])
        kvnw = self.sb("ml_kvnw", [128, 512])
        fw.dma("sp", qnw.ap[:], self.din["ml_qnorm_w"].ap[j:j + 1, :].partition_broadcast(128),
               reads=[self.din["ml_qnorm_w"]], writes=[qnw])
        fw.dma("sp", kvnw.ap[:], self.din["ml_kvnorm_w"].ap[j:j + 1, :].partition_broadcast(128),
               reads=[self.din["ml_kvnorm_w"]], writes=[kvnw])
        hT = self.sb("ml_hT", [128, KC, TB], BF16)
        cqnT = self.sb("ml_cqnT", [128, 4, TB], BF16)
        ckvnT = self.sb("ml_ckvnT", [128, 4, TB], BF16)
        krTt = self.sb("ml_krT", [64, TB], BF16)
        QnTb = self.sb("ml_QnTb", [128, H, TB], BF16)
        QrTb = self.sb("ml_QrTb", [64, H, TB], BF16)
        KnTb = self.sb("ml_KnTb", [128, H, TB], BF16)
        pA = self.psrot("ml_pA", 2, [128, 512])
        pkr = self.ps("ml_pkr", [128, 64])
        ptb = self.ps("ml_ptb", [128, 1024], BF16)
        ptf = self.ps("ml_ptf", [128, 512])
        cqn = self.sbrot("ml_cqn", 2, [128, 512], BF16)
        ckvn = self.sbrot("ml_ckvn", 2, [128, 512])
        krt = self.sbrot("ml_kr", 2, [128, 64])
        krr = self.sbrot("ml_krr", 2, [128, 64])
        tmp1 = self.sbrot("ml_t1", 2, [128, 2, 64])
        tmp2 = self.sbrot("ml_t2", 2, [128, 2, 64])
        cosr = self.sbrot("ml_cos", 2, [128, 64])
        sinr = self.sbrot("ml_sin", 2, [128, 64])
        qb16 = self.sbrot("ml_qb", 2, [128, 384], BF16)
        vtm = self.sbrot("ml_vtm", 2, [128, H * DV], BF16)
        sts = self.sbrot("ml_st", 4, [128, 4])
        rope_cos, rope_sin = self.din["rope_cos"], self.din["rope_sin"]

        def rope(dst3, src3, cs, sn, nh, t1, t2):
            csb = cs.ap[:].unsqueeze(1).to_broadcast([128, nh, 64])
            fw.op("dve", lambda e: e.tensor_tensor(out=t1.ap[:, 0:nh, :], in0=src3, in1=csb, op=ALU.mult),
                  reads=rd_src + [cs], writes=[t1])
            for rc in range(2):
                for hf in range(2):
                    o0 = rc * 32 + hf * 16
                    o1 = rc * 32 + (1 - hf) * 16
                    snb = sn.ap[:, o0:o0 + 16].unsqueeze(1).to_broadcast([128, nh, 16])
                    fw.op("dve", lambda e: e.tensor_tensor(out=t2.ap[:, 0:nh, o0:o0 + 16], in0=src3[:, :, o1:o1 + 16],
                                                           in1=snb, op=ALU.mult),
                          reads=rd_src + [sn], writes=[t2])
            fw.op("dve", lambda e: e.tensor_tensor(out=dst3, in0=t1.ap[:, 0:nh, :], in1=t2.ap[:, 0:nh, :], op=ALU.add),
                  reads=[t1, t2], writes=wr_dst)

        def kv_from_ckvnT(ncols, kcol0):
            for h in range(H):
                p = pA.next()
                for kc in range(4):
                    fw.op("pe", lambda e: e.matmul(p.ap[:, 0:ncols], lhsT=Wkv.ap[:, kc, h * 256:h * 256 + 128],
                                                   rhs=ckvnT.ap[:, kc, 0:ncols], start=(kc == 0), stop=(kc == 3)),
                          reads=[Wkv, ckvnT], writes=[p])
                eng = "act" if h % 2 == 0 else "dve"
                if eng == "act":
                    fw.op("act", lambda e: e.copy(out=KnTb.ap[:, h, 0:ncols], in_=p.ap[:, 0:ncols]), reads=[p], writes=[KnTb])
                else:
                    fw.op("dve", lambda e: e.tensor_copy(out=KnTb.ap[:, h, 0:ncols], in_=p.ap[:, 0:ncols]), reads=[p], writes=[KnTb])
            fw.dma("sp", KnT.ap[:, :, kcol0:kcol0 + ncols].rearrange("h p c -> p h c"), KnTb.ap[:, :, 0:ncols],
                   reads=[KnTb], writes=[KnT])
            Wv4 = Wkv.ap[:].rearrange("p kc (h two d) -> p kc h two d", h=H, two=2)
            for s_ in range(ncols // 128):
                v = vtm.next()
                for hg in range(4):
                    p = pA.next()
                    for kc in range(4):
                        fw.op("pe", lambda e: e.matmul(p.ap[:].rearrange("p (h d) -> p h d", h=4),
                                                       lhsT=ckvnT.ap[:, kc, s_ * 128:(s_ + 1) * 128],
                                                       rhs=Wv4[:, kc, hg * 4:(hg + 1) * 4, 1, :], start=(kc == 0), stop=(kc == 3)),
                              reads=[Wkv, ckvnT], writes=[p])
                    if hg % 2 == 0:
                        fw.op("act", lambda e: e.copy(out=v.ap[:, hg * 512:(hg + 1) * 512], in_=p.ap[:]), reads=[p], writes=[v])
                    else:
                        fw.op("dve", lambda e: e.tensor_copy(out=v.ap[:, hg * 512:(hg + 1) * 512], in_=p.ap[:]), reads=[p], writes=[v])
                fw.dma("sp", Vd.ap[kcol0 + s_ * 128: kcol0 + (s_ + 1) * 128, :], v.ap[:], reads=[v], writes=[Vd])

        def ckvn_to_T(src, s_):
            for kc in range(4):
                fw.op("pe", lambda e: e.transpose(ptf.ap[:, kc * 128:(kc + 1) * 128], src.ap[:, kc * 128:(kc + 1) * 128],
                                                  self.identf.ap[:]), reads=[src, self.identf], writes=[ptf])
            fw.op("act", lambda e: e.copy(out=ckvnT.ap[:, :, s_ * 128:(s_ + 1) * 128],
                                          in_=ptf.ap[:].rearrange("p (kc t) -> p kc t", kc=4)), reads=[ptf], writes=[ckvnT])

        def kr_to_T(src, s_):
            fw.op("pe", lambda e: e.transpose(ptf.ap[0:64, 0:128], src.ap[:, 0:64], self.identf.ap[:]),
                  reads=[src, self.identf], writes=[ptf])
            fw.op("dve", lambda e: e.tensor_copy(out=krTt.ap[:, s_ * 128:(s_ + 1) * 128], in_=ptf.ap[0:64, 0:128]),
                  reads=[ptf], writes=[krTt])

        if self.stop == "mla0":
            self.phase_end()
            return
        for (t0, tb) in self.blocks(TB):
            g = self.group_of(t0)
            kcol0 = t0 if g == 0 else t0 + 256
            self.prep_block(t0, tb, hT)
            if self.stop == "mla0b":
                break
            for s_ in range(tb // 128):
                r0 = t0 + s_ * 128
                pq, pkv = pA.next(), pA.next()
                for kc in range(KC):
                    lhs = hT.ap[:, kc, s_ * 128:(s_ + 1) * 128]
                    fw.op("pe", lambda e: e.matmul(pq.ap[:], lhsT=lhs, rhs=Wd.ap[:, kc, 0:512], start=(kc == 0), stop=(kc == KC - 1)),
                          reads=[hT, Wd], writes=[pq])
                    fw.op("pe", lambda e: e.matmul(pkv.ap[:], lhsT=lhs, rhs=Wd.ap[:, kc, 512:1024], start=(kc == 0), stop=(kc == KC - 1)),
                          reads=[hT, Wd], writes=[pkv])
                    fw.op("pe", lambda e: e.matmul(pkr.ap[:], lhsT=lhs, rhs=Wd.ap[:, kc, 1024:1088], start=(kc == 0), stop=(kc == KC - 1)),
                          reads=[hT, Wd], writes=[pkr])
                st = sts.next()
                rq = self.rstd_of(pq, st, width=512)
                cq = cqn.next()
                fw.op("dve", lambda e: e.scalar_tensor_tensor(out=cq.ap[:], in0=pq.ap[:], scalar=rq, in1=qnw.ap[:],
                                                              op0=ALU.mult, op1=ALU.mult), reads=[pq, st, qnw], writes=[cq])
                st2 = sts.next()
                rkv = self.rstd_of(pkv, st2, width=512)
                ck = ckvn.next()
                fw.op("dve", lambda e: e.scalar_tensor_tensor(out=ck.ap[:], in0=pkv.ap[:], scalar=rkv, in1=kvnw.ap[:],
                                                              op0=ALU.mult, op1=ALU.mult), reads=[pkv, st2, kvnw], writes=[ck])
                kr = krt.next()
                fw.op("act", lambda e: e.copy(out=kr.ap[:], in_=pkr.ap[:]), reads=[pkr], writes=[kr])
                if g == 1:
                    pr = r0 - NS
                    fw.dma("sp", ckv_out.ap[pr:pr + 128, :], ck.ap[:], reads=[ck], writes=[ckv_out])
                    fw.dma("sp", kr_out.ap[pr:pr + 128, :], kr.ap[:], reads=[kr], writes=[kr_out])
                    krsrc = kr
                else:
                    cs, sn = cosr.next(), sinr.next()
                    fw.dma("sp", cs.ap[:], rope_cos.ap[r0:r0 + 128, :], reads=[rope_cos], writes=[cs])
                    fw.dma("sp", sn.ap[:], rope_sin.ap[r0:r0 + 128, :], reads=[rope_sin], writes=[sn])
                    krsrc = krr.next()
                    rd_src, wr_dst = [kr], [krsrc]
                    rope(krsrc.ap[:].rearrange("p (o d) -> p o d", o=1), kr.ap[:].rearrange("p (o d) -> p o d", o=1),
                         cs, sn, 1, tmp1.next(), tmp2.next())
                if self.stop == "mla0c":
                    continue
                kr_to_T(krsrc, s_)
                ckvn_to_T(ck, s_)
                if self.stop == "mla0d":
                    continue
                for kc in range(4):
                    fw.op("pe", lambda e: e.transpose(ptb.ap[:, kc * 128:(kc + 1) * 128], cq.ap[:, kc * 128:(kc + 1) * 128],
                                                      self.identb.ap[:]), reads=[cq, self.identb], writes=[ptb])
                fw.op("act", lambda e: e.copy(out=cqnT.ap[:, :, s_ * 128:(s_ + 1) * 128],
                                              in_=ptb.ap[:, 0:512].rearrange("p (kc t) -> p kc t", kc=4)),
                      reads=[ptb], writes=[cqnT])
                if self.stop == "mla0e1":
                    continue
                for hp in range(H // 2):
                    p = pA.next()
                    for kc in range(4):
                        fw.op("pe", lambda e: e.matmul(p.ap[:, 0:384], lhsT=cqnT.ap[:, kc, s_ * 128:(s_ + 1) * 128],
                                                       rhs=Wq.ap[:, kc, hp * 384:(hp + 1) * 384], start=(kc == 0), stop=(kc == 3)),
                              reads=[cqnT, Wq], writes=[p])
                    q16 = qb16.next()
                    pv = p.ap[:, 0:384].rearrange("p (h d) -> p h d", h=2)
                    qv = q16.ap[:].rearrange("p (h d) -> p h d", h=2)
                    fw.op("act", lambda e: e.copy(out=qv[:, :, 0:128], in_=pv[:, :, 0:128]), reads=[p], writes=[q16])
                    if self.stop == "mla0e2":
                        continue
                    if g == 1 or self.stop == "mla0e4":
                        fw.op("act", lambda e: e.copy(out=qv[:, :, 128:192], in_=pv[:, :, 128:192]), reads=[p], writes=[q16])
                    else:
                        rd_src, wr_dst = [p, q16], [q16]
                        rope(qv[:, :, 128:192], pv[:, :, 128:192], cs, sn, 2, tmp1.next(), tmp2.next())
                    if self.stop == "mla0e3":
                        continue
                    for hh in range(2):
                        h = hp * 2 + hh
                        fw.op("pe", lambda e: e.transpose(ptb.ap[:, 512 + hh * 256: 512 + hh * 256 + 128], qv[:, hh, 0:128],
                                                          self.identb.ap[:]), reads=[q16, self.identb], writes=[ptb])
                        fw.op("pe", lambda e: e.transpose(ptb.ap[0:64, 512 + hh * 256 + 128: 512 + hh * 256 + 256], qv[:, hh, 128:192],
                                                          self.identb.ap[:]), reads=[q16, self.identb], writes=[ptb])
                    tv = ptb.ap[:, 512:1024].rearrange("p (h x) -> p h x", h=2)
                    fw.op("act", lambda e: e.copy(out=QnTb.ap[:, hp * 2:hp * 2 + 2, s_ * 128:(s_ + 1) * 128], in_=tv[:, :, 0:128]),
                          reads=[ptb], writes=[QnTb])
                    fw.op("act", lambda e: e.copy(out=QrTb.ap[:, hp * 2:hp * 2 + 2, s_ * 128:(s_ + 1) * 128], in_=tv[0:64, :, 128:256]),
                          reads=[ptb], writes=[QrTb])
            if self.stop in ("mla0c", "mla0d"):
                continue
            if self.stop in ("mla0e", "mla0e1", "mla0e2", "mla0e3", "mla0e4"):
                break
            fw.dma("sp", QnT.ap[:, :, t0:t0 + tb].rearrange("h p c -> p h c"), QnTb.ap[:, :, 0:tb], reads=[QnTb], writes=[QnT])
            fw.dma("sp", QrT.ap[:, :, t0:t0 + tb].rearrange("h p c -> p h c"), QrTb.ap[:, :, 0:tb], reads=[QrTb], writes=[QrT])
            fw.dma("sp", KrT.ap[:, kcol0:kcol0 + tb], krTt.ap[:, 0:tb], reads=[krTt], writes=[KrT])
            kv_from_ckvnT(tb, kcol0)
        if self.stop in ("mla0b", "mla0c", "mla0d", "mla0e", "mla0e1", "mla0e2", "mla0e3", "mla0e4"):
            self.phase_end()
            return
        cck, ckr = self.din["cache_ckv"], self.din["cache_krope"]
        for s_ in range(2):
            ck = ckvn.next()
            fw.dma("sp", ck.ap[:], cck.ap[j, s_ * 128:(s_ + 1) * 128, :], reads=[cck], writes=[ck])
            ckvn_to_T(ck, s_)
            kr = krt.next()
            fw.dma("sp", kr.ap[:], ckr.ap[j, s_ * 128:(s_ + 1) * 128, :], reads=[ckr], writes=[kr])
            kr_to_T(kr, s_)
        fw.dma("sp", KrT.ap[:, NS:NS + 256], krTt.ap[:, 0:256], reads=[krTt], writes=[KrT])
        kv_from_ckvnT(256, NS)
        self.phase_end()

        if self.stop == "mla1":
            return
        self.phase_begin()
        SCALE = 1.0 / math.sqrt(192.0)
        ones = self.sb("at_ones", [128, 128], BF16)
        fw.op("pool", lambda e: e.memset(ones.ap[:], 1.0), writes=[ones])
        NKmax = NS + 256
        Kn_r = self.sbrot("at_Kn", 2, [128, NKmax], BF16)
        V_r = self.sbrot("at_V", 2, [128, NKmax // 128, 128], BF16)
        Qn_r = self.sbrot("at_Qn", 2, [128, NS], BF16)
        Qr_r = self.sbrot("at_Qr", 2, [64, NS], BF16)
        Kr_t = self.sb("at_Kr", [64, NKmax], BF16)
        P_r = self.sbrot("at_P", 3, [128, 512], BF16)
        rd_r = self.sbrot("at_rd", 2, [128, 512])
        o_r = self.sbrot("at_o", 2, [128, 512], BF16)
        ps_s = self.psrot("at_pss", 3, [128, 512])
        ps_o = self.psrot("at_pso", 2, [128, 512])
        ps_d = self.psrot("at_psd", 2, [128, 512])
        seqs = [(0, NS, 0, NS + 256), (NS, NP, NS + 256, NP), (NS + NP, NP, NS + 256 + NP, NP)]
        for (q0, nq, k0, nk) in seqs:
            fw.dma("sp", Kr_t.ap[:, 0:nk], KrT.ap[:, k0:k0 + nk], reads=[KrT], writes=[Kr_t])
            nkt = nk // 128
            for h in range(H):
                Kn, V, Qn, Qr = Kn_r.next(), V_r.next(), Qn_r.next(), Qr_r.next()
                fw.dma("sp", Kn.ap[:, 0:nk], KnT.ap[h, :, k0:k0 + nk], reads=[KnT], writes=[Kn])
                fw.dma("sp", V.ap[:, 0:nkt, :], Vd.ap[k0:k0 + nk, h * 128:(h + 1) * 128].rearrange("(kt p) d -> p kt d", p=128),
                       reads=[Vd], writes=[V])
                fw.dma("sp", Qn.ap[:, 0:nq], QnT.ap[h, :, q0:q0 + nq], reads=[QnT], writes=[Qn])
                fw.dma("sp", Qr.ap[:, 0:nq], QrT.ap[h, :, q0:q0 + nq], reads=[QrT], writes=[Qr])
                for qb in range(0, nq, 512):
                    n = min(512, nq - qb)
                    po, pd = ps_o.next(), ps_d.next()
                    for kt in range(nkt):
                        pss = ps_s.next()
                        fw.op("pe", lambda e: e.matmul(pss.ap[:, 0:n], lhsT=Kn.ap[:, kt * 128:(kt + 1) * 128], rhs=Qn.ap[:, qb:qb + n],
                                                       start=True, stop=False), reads=[Kn, Qn], writes=[pss])
                        fw.op("pe", lambda e: e.matmul(pss.ap[:, 0:n], lhsT=Kr_t.ap[:, kt * 128:(kt + 1) * 128], rhs=Qr.ap[:, qb:qb + n],
                                                       start=False, stop=True), reads=[Kr_t, Qr], writes=[pss])
                        P = P_r.next()
                        fw.op("act", lambda e: e.activation(out=P.ap[:, 0:n], in_=pss.ap[:, 0:n], func=AF.Exp, scale=SCALE),
                              reads=[pss], writes=[P])
                        fw.op("pe", lambda e: e.matmul(po.ap[:, 0:n], lhsT=V.ap[:, kt, :], rhs=P.ap[:, 0:n],
                                                       start=(kt == 0), stop=(kt == nkt - 1)), reads=[V, P], writes=[po])
                        fw.op("pe", lambda e: e.matmul(pd.ap[:, 0:n], lhsT=ones.ap[:], rhs=P.ap[:, 0:n],
                                                       start=(kt == 0), stop=(kt == nkt - 1)), reads=[ones, P], writes=[pd])
                    rd = rd_r.next()
                    fw.op("dve", lambda e: e.reciprocal(out=rd.ap[:, 0:n], in_=pd.ap[:, 0:n]), reads=[pd], writes=[rd])
                    o = o_r.next()
                    fw.op("dve", lambda e: e.tensor_tensor(out=o.ap[:, 0:n], in0=po.ap[:, 0:n], in1=rd.ap[:, 0:n], op=ALU.mult),
                          reads=[po, rd], writes=[o])
                    fw.dma("sp", OT.ap[h * 128:(h + 1) * 128, q0 + qb:q0 + qb + n], o.ap[:, 0:n], reads=[o], writes=[OT])
        self.phase_end()
        if self.stop == "mla2":
            return
        self.out_proj(l, OT, self.din["ml_wo"], self.din["ml_wo"].ap[j], 2)

    def hgrn(self, l):
        fw = self.fw
        j = l // 3
        H, C = 16, 32
        w_in = self.din["hg_w_in"]
        Qd = [self.scratch("hg_Q%d" % d, [D, R], BF16) for d in range(2)]
        Kd = [self.scratch("hg_K%d" % d, [D, R], BF16) for d in range(2)]
        Khd = [self.scratch("hg_Kh%d" % d, [R, D], BF16) for d in range(2)]
        WCd = [self.scratch("hg_WC%d" % d, [128, H, R // C]) for d in range(2)]
        IVd = self.scratch("hg_IV", [R, D], BF16)
        GSd = self.scratch("hg_GS", [R, D], BF16)
        Od = [self.scratch("hg_O%d" % d, [R, D]) for d in range(2)]
        OT = self.scratch("hg_OT", [D, R], BF16)

        self.phase_begin()
        TB = 256
        NCH = TB // C
        self.prep_setup(l, 0)
        hlb = self.din["hg_lb"]
        stage = self.sb("hg_stage", [128, 128])
        pst = self.ps("hg_pst", [128, 128])
        lbr = self.sb("hg_lbraw", [128, 128])
        self.colvecs([(hlb, hlb.ap[i]) for i in range(DEPTH)], lbr, pst, stage, self.identf)
        fw.op("act", lambda e: e.activation(out=lbr.ap[:, 0:64], in_=lbr.ap[:, 0:64], func=AF.Exp), reads=[lbr], writes=[lbr])
        lbc = self.sb("hg_lbc", [128, 64])
        fw.op("dve", lambda e: e.tensor_tensor(out=lbc.ap[:, 32:48], in0=lbr.ap[:, 0:16], in1=lbr.ap[:, 16:32], op=ALU.add), reads=[lbr], writes=[lbc])
        fw.op("dve", lambda e: e.tensor_tensor(out=lbc.ap[:, 48:64], in0=lbr.ap[:, 32:48], in1=lbr.ap[:, 48:64], op=ALU.add), reads=[lbr], writes=[lbc])
        fw.op("dve", lambda e: e.tensor_tensor(out=lbc.ap[:, 32:48], in0=lbc.ap[:, 32:48], in1=lbc.ap[:, 48:64], op=ALU.add), reads=[lbc], writes=[lbc])
        fw.op("dve", lambda e: e.reciprocal(out=lbc.ap[:, 32:48], in_=lbc.ap[:, 32:48]), reads=[lbc], writes=[lbc])
        fw.op("dve", lambda e: e.memset(lbc.ap[:, 48:64], 0.0), reads=[lbc], writes=[lbc])
        for i in range(1, l + 1):
            fw.op("dve", lambda e: e.tensor_tensor(out=lbc.ap[:, 48:64], in0=lbc.ap[:, 48:64], in1=lbr.ap[:, i * 16:(i + 1) * 16], op=ALU.add),
                  reads=[lbc, lbr], writes=[lbc])
        fw.op("dve", lambda e: e.tensor_tensor(out=lbc.ap[:, 0:16], in0=lbc.ap[:, 48:64], in1=lbc.ap[:, 32:48], op=ALU.mult), reads=[lbc], writes=[lbc])
        fw.op("dve", lambda e: e.tensor_scalar(out=lbc.ap[:, 16:32], in0=lbc.ap[:, 0:16], scalar1=-1.0, scalar2=1.0, op0=ALU.mult, op1=ALU.add),
              reads=[lbc], writes=[lbc])
        msk = self.sb("hg_rst", [128, TB])
        fw.op("pool", lambda e: e.memset(msk.ap[:], 1.0), writes=[msk])
        fw.op("pool", lambda e: e.memset(msk.ap[:].rearrange("p (c t) -> p c t", t=C)[:, :, 0:1], 0.0), reads=[msk], writes=[msk])
        hT = self.sb("hg_hT", [128, KC, TB], BF16)
        wrot = self.sbrot("hg_w", 3, [128, KC * 512], BF16)
        hrot = Rot([[self.sb("hg_wh%d_%d" % (i, sec), [128, KC, 128], BF16) for sec in range(3)] for i in range(3)])
        p3 = [self.psrot("hg_p%d" % i, 1, [128, 512]) for i in range(3)]
        ptb = self.ps("hg_ptb", [128, 1024], BF16)
        qs_r = self.sbrot("hg_qs", 3, [128, TB])
        f_r = self.sbrot("hg_f", 4, [128, TB])
        gl_r = self.sbrot("hg_gl", 4, [128, TB])
        kk_r = self.sbrot("hg_kk", 4, [128, TB])
        F_r = self.sbrot("hg_F", 4, [128, TB])
        G_r = self.sbrot("hg_G", 4, [128, TB])
        Gr_r = self.sbrot("hg_Gr", 4, [128, TB])
        e_r = self.sbrot("hg_e", 9, [128, TB])
        o16 = self.sbrot("hg_o16", 10, [128, TB], BF16)
        kh_r = self.sbrot("hg_kh", 4, [128, TB], BF16)
        Kht = self.sb("hg_Kht", [128, TB // 128, 2, D], BF16)
        WCt = self.sb("hg_WCt", [128, 2, H, NCH])
        tmo = self.sbrot("hg_tmo", 2, [128, D], BF16)
        blocks = self.blocks(TB)
        w3 = w_in.ap[j].rearrange("(kc p) (sec f) -> p sec kc f", p=128, sec=5)
        wtm = w_in.ap[j].rearrange("(kc p) n -> p kc n", p=128)

        def ld_head(h):
            def f(grp):
                for sec in range(3):
                    fw.dma("pool", grp[sec].ap[:], w3[:, sec, :, h * 128:(h + 1) * 128], reads=[w_in], writes=[grp[sec]])
                return grp
            return f

        def ld_tm(c0):
            def f(buf):
                dst = buf.ap[:].rearrange("p (kc n) -> p kc n", kc=KC)
                fw.dma("pool", dst, wtm[:, :, c0:c0 + 512], reads=[w_in], writes=[buf])
                return dst
            return f
        lh, lt = [], []
        for _ in blocks:
            lh += [ld_head(h) for h in range(H)]
            lt += [ld_tm(3 * D + cb * 512) for cb in range(8)]
        wsH = WStream(hrot, lh)
        ws = WStream(wrot, lt)
        for (t0, tb) in blocks:
            self.prep_block(t0, tb, hT)
            nch = tb // C
            ch0 = t0 // C
            for h in range(H):
                grp, _ = wsH.next()
                pz = [p3[i].next() for i in range(3)]
                for sec in range(3):
                    for kc in range(KC):
                        fw.op("pe", lambda e: e.matmul(pz[sec].ap[:, 0:tb], lhsT=grp[sec].ap[:, kc, :], rhs=hT.ap[:, kc, 0:tb],
                                                       start=(kc == 0), stop=(kc == KC - 1)), reads=[grp[sec], hT], writes=[pz[sec]])
                qs = qs_r.next()
                fw.op("act", lambda e: e.activation(out=qs.ap[:, 0:tb], in_=pz[0].ap[:, 0:tb], func=AF.Silu), reads=[pz[0]], writes=[qs])
                for d in range(2):
                    f_, gl, kk, F, G, Gr = f_r.next(), gl_r.next(), kk_r.next(), F_r.next(), G_r.next(), Gr_r.next()
                    fw.op("act", lambda e: e.activation(out=f_.ap[:, 0:tb], in_=pz[1 + d].ap[:, 0:tb], func=AF.Sigmoid), reads=[pz[1 + d]], writes=[f_])
                    fw.op("dve", lambda e: e.tensor_scalar(out=f_.ap[:, 0:tb], in0=f_.ap[:, 0:tb], scalar1=lbc.ap[:, 16 + h:17 + h],
                                                           scalar2=lbc.ap[:, h:h + 1], op0=ALU.mult, op1=ALU.add), reads=[f_, lbc], writes=[f_])
                    fw.op("act", lambda e: e.activation(out=gl.ap[:, 0:tb], in_=f_.ap[:, 0:tb], func=AF.Ln), reads=[f_], writes=[gl])
                    fw.op("dve", lambda e: e.tensor_scalar(out=kk.ap[:, 0:tb], in0=f_.ap[:, 0:tb], scalar1=-1.0, scalar2=1.0,
                                                           op0=ALU.mult, op1=ALU.add), reads=[f_], writes=[kk])
                    fw.op("dve", lambda e: e.tensor_tensor_scan(out=F.ap[:, 0:tb], data0=msk.ap[:, 0:tb], data1=gl.ap[:, 0:tb],
                                                                initial=0.0, op0=ALU.mult, op1=ALU.add), reads=[msk, gl], writes=[F])
                    F3 = F.ap[:, 0:tb].rearrange("p (c t) -> p c t", t=C)
                    tot = F3[:, :, C - 1:C]
                    if d == 0:
                        Gt = F
                        fw.op("dve", lambda e: e.tensor_tensor(out=Gr.ap[:, 0:tb].rearrange("p (c t) -> p c t", t=C),
                                                               in0=tot.to_broadcast([128, nch, C]), in1=F3, op=ALU.subtract),
                              reads=[F], writes=[Gr])
                    else:
                        fw.op("dve", lambda e: e.tensor_tensor(out=Gr.ap[:, 0:tb], in0=F.ap[:, 0:tb], in1=gl.ap[:, 0:tb], op=ALU.subtract),
                              reads=[F, gl], writes=[Gr])
                        fw.op("dve", lambda e: e.tensor_tensor(out=G.ap[:, 0:tb].rearrange("p (c t) -> p c t", t=C),
                                                               in0=tot.to_broadcast([128, nch, C]),
                                                               in1=Gr.ap[:, 0:tb].rearrange("p (c t) -> p c t", t=C), op=ALU.subtract),
                              reads=[F, Gr], writes=[G])
                        Gt = G
                    fw.op("act", lambda e: e.activation(out=WCt.ap[:, d, h, 0:nch].unsqueeze(2), in_=tot, func=AF.Exp), reads=[F], writes=[WCt])
                    eG, enG, eR = e_r.next(), e_r.next(), e_r.next()
                    fw.op("act", lambda e: e.activation(out=eG.ap[:, 0:tb], in_=Gt.ap[:, 0:tb], func=AF.Exp), reads=[Gt], writes=[eG])
                    fw.op("act", lambda e: e.activation(out=enG.ap[:, 0:tb], in_=Gt.ap[:, 0:tb], func=AF.Exp, scale=-1.0), reads=[Gt], writes=[enG])
                    fw.op("act", lambda e: e.activation(out=eR.ap[:, 0:tb], in_=Gr.ap[:, 0:tb], func=AF.Exp), reads=[Gr], writes=[eR])
                    qt, kt, kh = o16.next(), o16.next(), kh_r.next()
                    fw.op("dve", lambda e: e.tensor_tensor(out=qt.ap[:, 0:tb], in0=qs.ap[:, 0:tb], in1=eG.ap[:, 0:tb], op=ALU.mult), reads=[qs, eG], writes=[qt])
                    fw.op("dve", lambda e: e.tensor_tensor(out=kt.ap[:, 0:tb], in0=kk.ap[:, 0:tb], in1=enG.ap[:, 0:tb], op=ALU.mult), reads=[kk, enG], writes=[kt])
                    fw.op("dve", lambda e: e.tensor_tensor(out=kh.ap[:, 0:tb], in0=kk.ap[:, 0:tb], in1=eR.ap[:, 0:tb], op=ALU.mult), reads=[kk, eR], writes=[kh])
                    fw.dma("sp", Qd[d].ap[h * 128:(h + 1) * 128, t0:t0 + tb], qt.ap[:, 0:tb], reads=[qt], writes=[Qd[d]])
                    fw.dma("sp", Kd[d].ap[h * 128:(h + 1) * 128, t0:t0 + tb], kt.ap[:, 0:tb], reads=[kt], writes=[Kd[d]])
                    for s_ in range(tb // 128):
                        fw.op("pe", lambda e: e.transpose(ptb.ap[:, (d * 2 + s_) * 128:(d * 2 + s_ + 1) * 128], kh.ap[:, s_ * 128:(s_ + 1) * 128],
                                                          self.identb.ap[:]), reads=[kh, self.identb], writes=[ptb])
                    fw.op("act", lambda e: e.copy(out=Kht.ap[:, 0:tb // 128, d, h * 128:(h + 1) * 128],
                                                  in_=ptb.ap[:, d * 256: d * 256 + tb].rearrange("p (s k) -> p s k", k=128)),
                          reads=[ptb], writes=[Kht])
            for d in range(2):
                for s_ in range(tb // 128):
                    fw.dma("sp", Khd[d].ap[t0 + s_ * 128: t0 + (s_ + 1) * 128, :], Kht.ap[:, s_, d, :], reads=[Kht], writes=[Khd[d]])
                fw.dma("sp", WCd[d].ap[:, :, ch0:ch0 + nch], WCt.ap[:, d, :, 0:nch], reads=[WCt], writes=[WCd[d]])
            for sec in range(2):
                dst_d = IVd if sec == 0 else GSd
                tiles = [tmo.next() for _ in range(tb // 128)]
                for cb in range(4):
                    w, wv = ws.next()
                    for s_ in range(tb // 128):
                        p = p3[s_ % 3].next()
                        for kc in range(KC):
                            fw.op("pe", lambda e: e.matmul(p.ap[:], lhsT=hT.ap[:, kc, s_ * 128:(s_ + 1) * 128], rhs=wv[:, kc, :],
                                                           start=(kc == 0), stop=(kc == KC - 1)), reads=[w, hT], writes=[p])
                        fw.op("act", lambda e: e.activation(out=tiles[s_].ap[:, cb * 512:(cb + 1) * 512], in_=p.ap[:],
                                                            func=(AF.Copy if sec == 0 else AF.Silu)), reads=[p], writes=[tiles[s_]])
                for s_ in range(tb // 128):
                    fw.dma("sp", dst_d.ap[t0 + s_ * 128: t0 + (s_ + 1) * 128, :], tiles[s_].ap[:], reads=[tiles[s_]], writes=[dst_d])
        self.phase_end()
        if self.stop == "hg1":
            return

        self.phase_begin()
        mk = self.sb("hg_mask", [64, 2, 64])
        fw.dma("sp", mk.ap[:], self.din["hg_mask"].ap.rearrange("d s t -> s d t"), reads=[self.din["hg_mask"]], writes=[mk])
        S = [[self.sb("hg_S%d_%d" % (d, g), [128, 4, 128]) for g in range(4)] for d in range(2)]
        Sb = [[self.sb("hg_Sb%d_%d" % (d, g), [128, 4, 128], BF16) for g in range(4)] for d in range(2)]
        qt_r = self.sbrot("hs_q", 2, [128, H, 128], BF16)
        kt_r = self.sbrot("hs_k", 2, [128, H, 128], BF16)
        kh_r2 = self.sbrot("hs_kh", 2, [64, 2, D], BF16)
        iv_r = self.sbrot("hs_iv", 2, [64, 2, D], BF16)
        wc_r = self.sbrot("hs_wc", 2, [128, H, 4])
        AT_r = self.sbrot("hs_AT", 2, [64, 16 * 64], BF16)
        o_r = self.sbrot("hs_o", 2, [64, 2, D])
        pa_r = self.psrot("hs_pa", 2, [128, 512])
        pu_r = self.psrot("hs_pu", 2, [128, 512])
        po = [self.ps("hs_po%d" % g, [128, 512]) for g in range(4)]
        sth, st_out = self.din["state_hgrn"], self.dout["st_hgrn"]

        def tile_step(d, r0, first_chunk_global):
            qt, kt, kh, iv, wc = qt_r.next(), kt_r.next(), kh_r2.next(), iv_r.next(), wc_r.next()
            fw.dma("sp", qt.ap[:], Qd[d].ap[:, r0:r0 + 128].rearrange("(h k) t -> k h t", k=128), reads=[Qd[d]], writes=[qt])
            fw.dma("sp", kt.ap[:], Kd[d].ap[:, r0:r0 + 128].rearrange("(h k) t -> k h t", k=128), reads=[Kd[d]], writes=[kt])
            fw.dma("sp", kh.ap[:], Khd[d].ap[r0:r0 + 128, :].rearrange("(hf p) f -> p hf f", p=64), reads=[Khd[d]], writes=[kh])
            fw.dma("sp", iv.ap[:], IVd.ap[r0:r0 + 128, :].rearrange("(hf p) f -> p hf f", p=64), reads=[IVd], writes=[iv])
            c0 = r0 // C
            fw.dma("sp", wc.ap[:], WCd[d].ap[:, :, c0:c0 + 4], reads=[WCd[d]], writes=[wc])
            o = o_r.next()
            halves = [0, 1] if d == 0 else [1, 0]
            for hf in halves:
                AT = AT_r.next()
                for g2 in range(2):
                    pa = pa_r.next()
                    for hh in range(8):
                        h = g2 * 8 + hh
                        fw.op("pe", lambda e: e.matmul(pa.ap[0:64, hh * 64:(hh + 1) * 64], lhsT=kt.ap[:, h, hf * 64:(hf + 1) * 64],
                                                       rhs=qt.ap[:, h, hf * 64:(hf + 1) * 64], start=True, stop=True),
                              reads=[kt, qt], writes=[pa])
                    fw.op("dve", lambda e: e.tensor_tensor(out=AT.ap[:, g2 * 512:(g2 + 1) * 512].rearrange("p (h t) -> p h t", h=8),
                                                           in0=pa.ap[0:64, :].rearrange("p (h t) -> p h t", h=8),
                                                           in1=mk.ap[:, d, :].unsqueeze(1).to_broadcast([64, 8, 64]), op=ALU.mult),
                          reads=[pa, mk], writes=[AT])
                for g in range(4):
                    for hh in range(4):
                        h = g * 4 + hh
                        fw.op("pe", lambda e: e.matmul(po[g].ap[0:64, hh * 128:(hh + 1) * 128], lhsT=AT.ap[:, h * 64:(h + 1) * 64],
                                                       rhs=iv.ap[:, hf, h * 128:(h + 1) * 128], start=(hh == 0), stop=False, skip_group_check=True),
                              reads=[AT, iv], writes=[po[g]])
                order = [0, 1] if d == 0 else [1, 0]
                for ci, c in enumerate(order):
                    cg = hf * 2 + c
                    for g in range(4):
                        for hh in range(4):
                            h = g * 4 + hh
                            fw.op("pe", lambda e: e.matmul(po[g].ap[c * 32:(c + 1) * 32, hh * 128:(hh + 1) * 128],
                                                           lhsT=qt.ap[:, h, hf * 64 + c * 32: hf * 64 + (c + 1) * 32],
                                                           rhs=Sb[d][g].ap[:, hh, :], start=False, stop=True, skip_group_check=True), reads=[qt, Sb[d][g]], writes=[po[g]])
                        pu = pu_r.next()
                        for hh in range(4):
                            h = g * 4 + hh
                            fw.op("pe", lambda e: e.matmul(pu.ap[:, hh * 128:(hh + 1) * 128], lhsT=kh.ap[c * 32:(c + 1) * 32, hf, h * 128:(h + 1) * 128],
                                                           rhs=iv.ap[c * 32:(c + 1) * 32, hf, h * 128:(h + 1) * 128], start=True, stop=True),
                                  reads=[kh, iv], writes=[pu])
                        for hh in range(4):
                            h = g * 4 + hh
                            fw.op("dve", lambda e: e.scalar_tensor_tensor(out=S[d][g].ap[:, hh, :], in0=S[d][g].ap[:, hh, :], scalar=wc.ap[:, h, cg:cg + 1],
                                                                          in1=pu.ap[:, hh * 128:(hh + 1) * 128], op0=ALU.mult, op1=ALU.add),
                                  reads=[S[d][g], wc, pu], writes=[S[d][g]])
                        fw.op("act", lambda e: e.copy(out=Sb[d][g].ap[:], in_=S[d][g].ap[:]), reads=[S[d][g]], writes=[Sb[d][g]])
                for g in range(4):
                    if g % 2 == 0:
                        fw.op("act", lambda e: e.copy(out=o.ap[:, hf, g * 512:(g + 1) * 512], in_=po[g].ap[0:64, :]), reads=[po[g]], writes=[o])
                    else:
                        fw.op("dve", lambda e: e.tensor_copy(out=o.ap[:, hf, g * 512:(g + 1) * 512], in_=po[g].ap[0:64, :]), reads=[po[g]], writes=[o])
            fw.dma("sp", Od[d].ap[r0:r0 + 128, :].rearrange("(hf p) f -> p hf f", p=64), o.ap[:], reads=[o], writes=[Od[d]])

        seqs = [(0, NS, None), (NS, NP, 0), (NS + NP, NP, 1)]
        for (row0, T, pidx) in seqs:
            for d in range(2):
                for g in range(4):
                    if pidx is None:
                        fw.dma("sp", S[d][g].ap[:], sth.ap[j, d, g * 4:(g + 1) * 4].rearrange("h k v -> k h v"), reads=[sth], writes=[S[d][g]])
                    else:
                        fw.op("pool", lambda e: e.memset(S[d][g].ap[:], 0.0), writes=[S[d][g]])
                    fw.op("act", lambda e: e.copy(out=Sb[d][g].ap[:], in_=S[d][g].ap[:]), reads=[S[d][g]], writes=[Sb[d][g]])
            nt = T // 128
            for i in range(nt):
                tile_step(0, row0 + i * 128, None)
                tile_step(1, row0 + (nt - 1 - i) * 128, None)
            if pidx is not None:
                for d in range(2):
                    for g in range(4):
                        fw.dma("sp", st_out.ap[pidx, d, g * 4:(g + 1) * 4].rearrange("h k v -> k h v"), S[d][g].ap[:], reads=[S[d][g]], writes=[st_out])
        self.phase_end()
        if self.stop == "hg2":
            return

        self.phase_begin()
        nwt = self.sb("hc_nw", [128, 128])
        fw.dma("sp", nwt.ap[:], self.din["hg_norm_w"].ap[j:j + 1, :].partition_broadcast(128), reads=[self.din["hg_norm_w"]], writes=[nwt])
        of_r = self.sbrot("hc_of", 2, [128, D])
        ob_r = self.sbrot("hc_ob", 2, [128, D])
        gs_r = self.sbrot("hc_gs", 2, [128, D], BF16)
        sq_r = self.sbrot("hc_sq", 1, [128, D])
        y_r = self.sbrot("hc_y", 2, [128, D], BF16)
        st_r = self.sbrot("hc_st", 2, [128, 16])
        OTt = self.sbrot("hc_OTt", 2, [128, KC, 512], BF16)
        ptr = self.psrot("hc_pt", 2, [128, 1024], BF16)
        eps128 = self.sb("hc_eps", [128, 1])
        fw.op("pool", lambda e: e.memset(eps128.ap[:], EPS), writes=[eps128])
        for b0 in range(0, R, 512):
            ot = OTt.next()
            for s_ in range(4):
                r0 = b0 + s_ * 128
                of, ob, gs = of_r.next(), ob_r.next(), gs_r.next()
                fw.dma("sp", of.ap[:], Od[0].ap[r0:r0 + 128, :], reads=[Od[0]], writes=[of])
                fw.dma("sp", ob.ap[:], Od[1].ap[r0:r0 + 128, :], reads=[Od[1]], writes=[ob])
                fw.dma("sp", gs.ap[:], GSd.ap[r0:r0 + 128, :], reads=[GSd], writes=[gs])
                fw.op("pool", lambda e: e.tensor_tensor(out=of.ap[:], in0=of.ap[:], in1=ob.ap[:], op=ALU.add), reads=[of, ob], writes=[of])
                sq = sq_r.next()
                fw.op("act", lambda e: e.activation(out=sq.ap[:], in_=of.ap[:], func=AF.Square), reads=[of], writes=[sq])
                st = st_r.next()
                fw.op("dve", lambda e: e.tensor_reduce(out=st.ap[:], in_=sq.ap[:].rearrange("p (h v) -> p h v", h=16), axis=AX.X, op=ALU.add),
                      reads=[sq], writes=[st])
                fw.op("act", lambda e: e.activation(out=st.ap[:], in_=st.ap[:], func=AF.Ln, scale=1.0 / 128.0, bias=eps128.ap[:, 0:1]), reads=[st, eps128], writes=[st])
                fw.op("act", lambda e: e.activation(out=st.ap[:], in_=st.ap[:], func=AF.Exp, scale=-0.5), reads=[st], writes=[st])
                of3 = of.ap[:].rearrange("p (h v) -> p h v", h=16)
                fw.op("dve", lambda e: e.tensor_tensor(out=of3, in0=of3, in1=st.ap[:].unsqueeze(2).to_broadcast([128, 16, 128]), op=ALU.mult),
                      reads=[of, st], writes=[of])
                fw.op("pool", lambda e: e.tensor_tensor(out=of3, in0=of3, in1=nwt.ap[:].unsqueeze(1).to_broadcast([128, 16, 128]), op=ALU.mult),
                      reads=[of, nwt], writes=[of])
                y = y_r.next()
                fw.op("dve", lambda e: e.tensor_tensor(out=y.ap[:], in0=of.ap[:], in1=gs.ap[:], op=ALU.mult), reads=[of, gs], writes=[y])
                for half in range(2):
                    pt = ptr.next()
                    for q in range(8):
                        kc = half * 8 + q
                        fw.op("pe", lambda e: e.transpose(pt.ap[:, q * 128:(q + 1) * 128], y.ap[:, kc * 128:(kc + 1) * 128], self.identb.ap[:]),
                              reads=[y, self.identb], writes=[pt])
                    fw.op("act", lambda e: e.copy(out=ot.ap[:, half * 8:(half + 1) * 8, s_ * 128:(s_ + 1) * 128],
                                                  in_=pt.ap[:].rearrange("p (q t) -> p q t", q=8)), reads=[pt], writes=[ot])
            fw.dma("sp", OT.ap[:, b0:b0 + 512].rearrange("(kc p) t -> p kc t", p=128), ot.ap[:], reads=[ot], writes=[OT])
        self.phase_end()
        if self.stop == "hg3":
            return
        self.out_proj(l, OT, self.din["hg_wo"], self.din["hg_wo"].ap[j], 2)

    def rwkv(self, l):
        fw = self.fw
        ja = l // 3
        H, C, LD, LG = 32, 64, 96, 256
        din = self.din
        HTd = self.scratch("rw_HT%d" % l, [D, R + 2], BF16)
        Atd = [self.scratch("rw_At%d_%d" % (l, d), [D, R], BF16) for d in range(2)]
        Ktd = [self.scratch("rw_Kt%d_%d" % (l, d), [D, R], BF16) for d in range(2)]
        Btd = [self.scratch("rw_Bt%d_%d" % (l, d), [D, R], BF16) for d in range(2)]
        Rtd = [self.scratch("rw_Rt%d_%d" % (l, d), [D, R], BF16) for d in range(2)]
        Kmd = [self.scratch("rw_Km%d_%d" % (l, d), [R, D], BF16) for d in range(2)]
        Bmd = [self.scratch("rw_Bm%d_%d" % (l, d), [R, D], BF16) for d in range(2)]
        WCd = [self.scratch("rw_WC%d_%d" % (l, d), [128, KC, R // C]) for d in range(2)]
        Vd = self.scratch("rw_V%d" % l, [R, D], BF16)
        Gd = self.scratch("rw_G%d" % l, [R, D], BF16)
        CSd = self.scratch("rw_CS%d" % l, [R, 64])
        Yd = [self.scratch("rw_Y%d_%d" % (l, d), [R, D]) for d in range(2)]
        OT = self.scratch("rw_OT%d" % l, [D, R], BF16)

        self.phase_begin()
        TB = 256
        self.prep_setup(l, 0)
        hTr = self.sbrot("r0_hT", 2, [128, KC, TB], BF16)
        for (t0, tb) in self.blocks(TB):
            hT = hTr.next()
            self.prep_block(t0, tb, hT)
            fw.dma("sp", HTd.ap[:, 1 + t0:1 + t0 + tb].rearrange("(kc p) t -> p kc t", p=128), hT.ap[:, :, 0:tb], reads=[hT], writes=[HTd])
        self.phase_end()

        self.phase_begin()
        stage = self.sb("r1_stage", [128, 128])
        pst = self.ps("r1_pst", [128, 128])
        cva = self.sb("r1_cva", [128, 128])
        cvb = self.sb("r1_cvb", [128, 128])
        self.colvecs([(din["rw_mu"], din["rw_mu"].ap[ja, i]) for i in range(6)] + [(din["rw_kk"], din["rw_kk"].ap[ja]), (din["rw_ka"], din["rw_ka"].ap[ja])],
                     cva, pst, stage, self.identf)
        rkf = din["rw_rk"].ap.rearrange("j d h n -> j d (h n)")
        self.colvecs([(din["rw_w0"], din["rw_w0"].ap[ja, 0]), (din["rw_w0"], din["rw_w0"].ap[ja, 1]),
                      (din["rw_a0"], din["rw_a0"].ap[ja, 0]), (din["rw_a0"], din["rw_a0"].ap[ja, 1]),
                      (din["rw_rk"], rkf[ja, 0]), (din["rw_rk"], rkf[ja, 1])], cvb, pst, stage, self.identf)
        fw.op("dve", lambda e: e.tensor_scalar(out=cvb.ap[:, 96:112], in0=cva.ap[:, 112:128], scalar1=-1.0, scalar2=1.0, op0=ALU.mult, op1=ALU.add),
              reads=[cva], writes=[cvb])
        cvn = self.sb("r1_cvn", [128, 64])
        fw.op("dve", lambda e: e.tensor_scalar(out=cvn.ap[:], in0=cvb.ap[:, 0:64], scalar1=-1.0, scalar2=None, op0=ALU.mult), reads=[cvb], writes=[cvn])
        NW0 = lambda d, kc: cvn.ap[:, d * 16 + kc:d * 16 + kc + 1]
        NA0 = lambda d, kc: cvn.ap[:, 32 + d * 16 + kc:33 + d * 16 + kc]
        CDEC = math.exp(-0.5)
        MU = lambda i, kc: cva.ap[:, i * 16 + kc:i * 16 + kc + 1]
        KKW = lambda kc: cva.ap[:, 96 + kc:97 + kc]
        KA = lambda kc: cva.ap[:, 112 + kc:113 + kc]
        W0 = lambda d, kc: cvb.ap[:, d * 16 + kc:d * 16 + kc + 1]
        A0 = lambda d, kc: cvb.ap[:, 32 + d * 16 + kc:33 + d * 16 + kc]
        RK = lambda d, kc: cvb.ap[:, 64 + d * 16 + kc:65 + d * 16 + kc]
        OMKA = lambda kc: cvb.ap[:, 96 + kc:97 + kc]
        W1 = self.sb("r1_W1", [128, 2, KC, LD], BF16)
        A1 = self.sb("r1_A1", [128, 2, KC, LD], BF16)
        W2 = self.sb("r1_W2", [LD, 2, D], BF16)
        A2 = self.sb("r1_A2", [LD, 2, D], BF16)
        G1 = self.sb("r1_G1", [128, KC, LG], BF16)
        G2 = self.sb("r1_G2", [128, 2, D], BF16)
        for d in range(2):
            fw.dma("pool", W1.ap[:, d], din["rw_w1"].ap[ja, d].rearrange("(kc p) n -> p kc n", p=128), reads=[din["rw_w1"]], writes=[W1])
            fw.dma("pool", A1.ap[:, d], din["rw_a1"].ap[ja, d].rearrange("(kc p) n -> p kc n", p=128), reads=[din["rw_a1"]], writes=[A1])
            fw.dma("pool", W2.ap[:, d], din["rw_w2"].ap[ja, d], reads=[din["rw_w2"]], writes=[W2])
            fw.dma("pool", A2.ap[:, d], din["rw_a2"].ap[ja, d], reads=[din["rw_a2"]], writes=[A2])
        fw.dma("pool", G1.ap[:], din["rw_g1"].ap[ja].rearrange("(kc p) n -> p kc n", p=128), reads=[din["rw_g1"]], writes=[G1])
        fw.dma("pool", G2.ap[:], din["rw_g2"].ap[ja].rearrange("(kc p) n -> p kc n", p=128), reads=[din["rw_g2"]], writes=[G2])
        bones = self.sb("r1_bones", [128, 128], BF16)
        fw.op("pool", lambda e: e.memset(bones.ap[:], 0.0), writes=[bones])
        fw.op("pool", lambda e: e.memset(bones.ap[0:64, 0:64], 1.0), reads=[bones], writes=[bones])
        fw.op("pool", lambda e: e.memset(bones.ap[64:128, 64:128], 1.0), reads=[bones], writes=[bones])
        Eh = self.sb("r1_E", [128, KC, 32], BF16)
        fw.op("pool", lambda e: e.memset(Eh.ap[:], 0.0), writes=[Eh])
        for kc in range(KC):
            fw.op("pool", lambda e: e.memset(Eh.ap[0:64, kc, 2 * kc:2 * kc + 1], 1.0), reads=[Eh], writes=[Eh])
            fw.op("pool", lambda e: e.memset(Eh.ap[64:128, kc, 2 * kc + 1:2 * kc + 2], 1.0), reads=[Eh], writes=[Eh])
        msk = self.sb("r1_rst", [128, TB])
        fw.op("pool", lambda e: e.memset(msk.ap[:], 1.0), writes=[msk])
        fw.op("pool", lambda e: e.memset(msk.ap[:].rearrange("p (c t) -> p c t", t=C)[:, :, 0:1], 0.0), reads=[msk], writes=[msk])
        e12 = self.sb("r1_e12", [128, 1])
        fw.op("pool", lambda e: e.memset(e12.ap[:], 1e-12), writes=[e12])

        hTh = self.sb("r1_hTh", [128, KC, TB + 2], BF16)
        xx = self.sb("r1_xx", [128, KC, TB], BF16)
        xmr = self.sbrot("r1_xm", 2, [128, KC, TB], BF16)
        wrot = self.sbrot("r1_w", 2, [128, KC * 512], BF16)
        NCH = TB // C
        r_f = self.sb("r1_r", [128, KC, TB], BF16)
        k_f = self.sb("r1_k", [128, KC, TB], BF16)
        kkn = self.sb("r1_kkn", [128, KC, TB], BF16)
        a_r = self.sbrot("r1_a", 4, [128, TB])
        sg_r = self.sbrot("r1_sg", 4, [128, TB])
        whT = self.sb("r1_wh", [LD, 4, TB], BF16)
        ghT = self.sb("r1_gh", [128, 2, TB], BF16)
        vtm = self.sbrot("r1_vtm", 2, [128, D], BF16)
        pp = self.psrot("r1_pp", 4, [128, 512])
        ptb = self.psrot("r1_ptb", 2, [128, 1024], BF16)
        tr = self.sbrot("r1_t", 24, [128, TB])
        ob = self.sbrot("r1_ob", 20, [128, TB], BF16)
        Kmr = self.sbrot("r1_Kmt", 4, [128, 2, TB // 128, 128], BF16)
        WCt = self.sb("r1_WCt", [128, 2, KC, NCH])
        Zr = self.sbrot("r1_Z", 4, [128, TB], BF16)
        pcs = self.ps("r1_pcs", [128, 2 * 64])
        cst = self.sbrot("r1_cs", 2, [128, 2 * 64])
        blocks = self.blocks(TB)

        def ld_sq(Wb, cb):
            def f(buf):
                dst = buf.ap[:].rearrange("p (kc n) -> p kc n", kc=KC)
                fw.dma("pool", dst, Wb.ap[ja].rearrange("(kc p) n -> p kc n", p=128)[:, :, cb * 512:(cb + 1) * 512], reads=[Wb], writes=[buf])
                return dst
            return f
        loaders = []
        for _ in blocks:
            for nm in ("rw_wr", "rw_wk", "rw_wv"):
                loaders += [ld_sq(din[nm], cb) for cb in range(4)]
        ws = WStream(wrot, loaders)

        def mix(i):
            xm = xmr.next()
            for kc in range(KC):
                fw.op("dve", lambda e: e.scalar_tensor_tensor(out=xm.ap[:, kc, 0:tb], in0=xx.ap[:, kc, 0:tb], scalar=MU(i, kc),
                                                            in1=hTh.ap[:, kc, 1:1 + tb], op0=ALU.mult, op1=ALU.add),
                      reads=[xx, hTh, cva], writes=[xm])
            return xm

        for (t0, tb) in blocks:
            seq0 = 0 if t0 < NS else (NS if t0 < NS + NP else NS + NP)
            seq1 = NS if t0 < NS else (NS + NP if t0 < NS + NP else R)
            lo = t0 - 1 if t0 > seq0 else t0
            hi = t0 + tb + 1 if t0 + tb < seq1 else t0 + tb
            fw.dma("sp", hTh.ap[:, :, (lo - t0 + 1):(hi - t0 + 1)], HTd.ap[:, lo + 1:hi + 1].rearrange("(kc p) t -> p kc t", p=128),
                   reads=[HTd], writes=[hTh])
            if lo == t0:
                fw.op("pool", lambda e: e.memset(hTh.ap[:, :, 0:1], 0.0), reads=[hTh], writes=[hTh])
            if hi == t0 + tb:
                fw.op("pool", lambda e: e.memset(hTh.ap[:, :, tb + 1:tb + 2], 0.0), reads=[hTh], writes=[hTh])
            fw.op("dve", lambda e: e.tensor_tensor(out=xx.ap[:, :, 0:tb], in0=hTh.ap[:, :, 0:tb], in1=hTh.ap[:, :, 2:2 + tb], op=ALU.add),
                  reads=[hTh], writes=[xx])
            fw.op("dve", lambda e: e.scalar_tensor_tensor(out=xx.ap[:, :, 0:tb], in0=xx.ap[:, :, 0:tb], scalar=0.5, in1=hTh.ap[:, :, 1:1 + tb],
                                                          op0=ALU.mult, op1=ALU.subtract), reads=[xx, hTh], writes=[xx])
            nch = tb // C
            ch0 = t0 // C
            for (mi, dstf) in ((0, r_f), (2, k_f)):
                xm = mix(mi)
                for cb in range(4):
                    w, wv = ws.next()
                    for jj in range(4):
                        fc = cb * 4 + jj
                        p = pp.next()
                        for kc in range(KC):
                            fw.op("pe", lambda e: e.matmul(p.ap[:, 0:tb], lhsT=wv[:, kc, jj * 128:(jj + 1) * 128], rhs=xm.ap[:, kc, 0:tb],
                                                           start=(kc == 0), stop=(kc == KC - 1)), reads=[w, xm], writes=[p])
                        fw.op("act", lambda e: e.copy(out=dstf.ap[:, fc, 0:tb], in_=p.ap[:, 0:tb]), reads=[p], writes=[dstf])
            xm = mix(3)
            vts = [vtm.next() for _ in range(tb // 128)]
            for cb in range(4):
                w, wv = ws.next()
                for s_ in range(tb // 128):
                    p = pp.next()
                    for kc in range(KC):
                        fw.op("pe", lambda e: e.matmul(p.ap[:], lhsT=xm.ap[:, kc, s_ * 128:(s_ + 1) * 128], rhs=wv[:, kc, :],
                                                       start=(kc == 0), stop=(kc == KC - 1)), reads=[w, xm], writes=[p])
                    fw.op("act", lambda e: e.copy(out=vts[s_].ap[:, cb * 512:(cb + 1) * 512], in_=p.ap[:]), reads=[p], writes=[vts[s_]])
            for s_ in range(tb // 128):
                fw.dma("sp", Vd.ap[t0 + s_ * 128:t0 + (s_ + 1) * 128, :], vts[s_].ap[:], reads=[vts[s_]], writes=[Vd])
            xm = mix(5)
            for c2 in range(2):
                p = pp.next()
                for kc in range(KC):
                    fw.op("pe", lambda e: e.matmul(p.ap[:, 0:tb], lhsT=G1.ap[:, kc, c2 * 128:(c2 + 1) * 128], rhs=xm.ap[:, kc, 0:tb],
                                                   start=(kc == 0), stop=(kc == KC - 1)), reads=[G1, xm], writes=[p])
                fw.op("act", lambda e: e.activation(out=ghT.ap[:, c2, 0:tb], in_=p.ap[:, 0:tb], func=AF.Sigmoid), reads=[p], writes=[ghT])
            gts = [vtm.next() for _ in range(tb // 128)]
            for s_ in range(tb // 128):
                for cb in range(4):
                    p = pp.next()
                    for c2 in range(2):
                        fw.op("pe", lambda e: e.matmul(p.ap[:], lhsT=ghT.ap[:, c2, s_ * 128:(s_ + 1) * 128], rhs=G2.ap[:, c2, cb * 512:(cb + 1) * 512],
                                                       start=(c2 == 0), stop=(c2 == 1)), reads=[ghT, G2], writes=[p])
                    fw.op("act", lambda e: e.copy(out=gts[s_].ap[:, cb * 512:(cb + 1) * 512], in_=p.ap[:]), reads=[p], writes=[gts[s_]])
                fw.dma("sp", Gd.ap[t0 + s_ * 128:t0 + (s_ + 1) * 128, :], gts[s_].ap[:], reads=[gts[s_]], writes=[Gd])
            for (mi, Wl, off, fn) in ((1, W1, 0, AF.Tanh), (4, A1, 2, AF.Copy)):
                xm = mix(mi)
                for d in range(2):
                    p = pp.next()
                    for kc in range(KC):
                        fw.op("pe", lambda e: e.matmul(p.ap[0:LD, 0:tb], lhsT=Wl.ap[:, d, kc, :], rhs=xm.ap[:, kc, 0:tb],
                                                       start=(kc == 0), stop=(kc == KC - 1)), reads=[Wl, xm], writes=[p])
                    fw.op("act", lambda e: e.activation(out=whT.ap[:, off + d, 0:tb], in_=p.ap[0:LD, 0:tb], func=fn), reads=[p], writes=[whT])
            for kc in range(KC):
                t_sq = ob.next()
                fw.op("dve", lambda e: e.tensor_scalar(out=kkn.ap[:, kc, 0:tb], in0=k_f.ap[:, kc, 0:tb], scalar1=KKW(kc), scalar2=None, op0=ALU.mult),
                      reads=[k_f, cva], writes=[kkn])
                fw.op("pool", lambda e: e.tensor_tensor(out=t_sq.ap[:, 0:tb], in0=kkn.ap[:, kc, 0:tb], in1=kkn.ap[:, kc, 0:tb], op=ALU.mult),
                      reads=[kkn], writes=[t_sq])
                p = pp.next()
                fw.op("pe", lambda e: e.matmul(p.ap[:, 0:tb], lhsT=bones.ap[:], rhs=t_sq.ap[:, 0:tb], start=True, stop=True), reads=[bones, t_sq], writes=[p])
                rn = tr.next()
                fw.op("act", lambda e: e.activation(out=rn.ap[:, 0:tb], in_=p.ap[:, 0:tb], func=AF.Ln, bias=e12.ap[:, 0:1]), reads=[p, e12], writes=[rn])
                fw.op("act", lambda e: e.activation(out=rn.ap[:, 0:tb], in_=rn.ap[:, 0:tb], func=AF.Exp, scale=-0.5), reads=[rn], writes=[rn])
                fw.op("dve", lambda e: e.tensor_tensor(out=kkn.ap[:, kc, 0:tb], in0=kkn.ap[:, kc, 0:tb], in1=rn.ap[:, 0:tb], op=ALU.mult),
                      reads=[kkn, rn], writes=[kkn])
            for d in range(2):
                for kc in range(KC):
                    sgt, at_ = sg_r.next(), a_r.next()
                    p = pp.next()
                    fw.op("pe", lambda e: e.matmul(p.ap[:, 0:tb], lhsT=W2.ap[:, d, kc * 128:(kc + 1) * 128], rhs=whT.ap[:, d, 0:tb], start=True, stop=True),
                          reads=[W2, whT], writes=[p])
                    fw.op("act", lambda e: e.activation(out=sgt.ap[:, 0:tb], in_=p.ap[:, 0:tb], func=AF.Exp, scale=-1.0, bias=NW0(d, kc)),
                          reads=[p, cvn], writes=[sgt])
                    fw.op("dve", lambda e: e.tensor_scalar(out=sgt.ap[:, 0:tb], in0=sgt.ap[:, 0:tb], scalar1=1.0, scalar2=None, op0=ALU.add), reads=[sgt], writes=[sgt])
                    fw.op("dve", lambda e: e.reciprocal(out=sgt.ap[:, 0:tb], in_=sgt.ap[:, 0:tb]), reads=[sgt], writes=[sgt])
                    p = pp.next()
                    fw.op("pe", lambda e: e.matmul(p.ap[:, 0:tb], lhsT=A2.ap[:, d, kc * 128:(kc + 1) * 128], rhs=whT.ap[:, 2 + d, 0:tb], start=True, stop=True),
                          reads=[A2, whT], writes=[p])
                    fw.op("act", lambda e: e.activation(out=at_.ap[:, 0:tb], in_=p.ap[:, 0:tb], func=AF.Exp, scale=-1.0, bias=NA0(d, kc)),
                          reads=[p, cvn], writes=[at_])
                    fw.op("dve", lambda e: e.tensor_scalar(out=at_.ap[:, 0:tb], in0=at_.ap[:, 0:tb], scalar1=1.0, scalar2=None, op0=ALU.add), reads=[at_], writes=[at_])
                    fw.op("dve", lambda e: e.reciprocal(out=at_.ap[:, 0:tb], in_=at_.ap[:, 0:tb]), reads=[at_], writes=[at_])
                    a_ = at_.ap[:, 0:tb]
                    a_f = at_
                    F, L, Lex = tr.next(), tr.next(), tr.next()
                    lw = sgt
                    fw.op("dve", lambda e: e.tensor_tensor_scan(out=F.ap[:, 0:tb], data0=msk.ap[:, 0:tb], data1=lw.ap[:, 0:tb], initial=0.0,
                                                                op0=ALU.mult, op1=ALU.add), reads=[msk, lw], writes=[F])
                    F3 = F.ap[:, 0:tb].rearrange("p (c t) -> p c t", t=C)
                    tot = F3[:, :, C - 1:C]
                    if d == 0:
                        Lt = F
                        fw.op("pool", lambda e: e.tensor_tensor(out=Lex.ap[:, 0:tb], in0=F.ap[:, 0:tb], in1=lw.ap[:, 0:tb], op=ALU.subtract),
                              reads=[F, lw], writes=[Lex])
                    else:
                        fw.op("dve", lambda e: e.tensor_tensor(out=Lex.ap[:, 0:tb].rearrange("p (c t) -> p c t", t=C), in0=tot.to_broadcast([128, nch, C]),
                                                               in1=F3, op=ALU.subtract), reads=[F], writes=[Lex])
                        fw.op("pool", lambda e: e.tensor_tensor(out=L.ap[:, 0:tb], in0=Lex.ap[:, 0:tb], in1=lw.ap[:, 0:tb], op=ALU.add),
                              reads=[Lex, lw], writes=[L])
                        Lt = L
                    fw.op("act", lambda e: e.activation(out=WCt.ap[:, d, kc, 0:nch].unsqueeze(2), in_=tot, func=AF.Exp, scale=-CDEC), reads=[F], writes=[WCt])
                    eL, enL, eX = tr.next(), tr.next(), tr.next()
                    fw.op("act", lambda e: e.activation(out=eL.ap[:, 0:tb], in_=Lt.ap[:, 0:tb], func=AF.Exp, scale=-CDEC), reads=[Lt], writes=[eL])
                    fw.op("act", lambda e: e.activation(out=enL.ap[:, 0:tb], in_=Lt.ap[:, 0:tb], func=AF.Exp, scale=CDEC), reads=[Lt], writes=[enL])
                    fw.op("act", lambda e: e.activation(out=eX.ap[:, 0:tb], in_=Lex.ap[:, 0:tb], func=AF.Exp, scale=-CDEC), reads=[Lex], writes=[eX])
                    kd, bp = tr.next(), tr.next()
                    fw.op("dve", lambda e: e.tensor_scalar(out=kd.ap[:, 0:tb], in0=a_, scalar1=KA(kc), scalar2=OMKA(kc), op0=ALU.mult, op1=ALU.add),
                          reads=[a_f, cva, cvb], writes=[kd])
                    fw.op("dve", lambda e: e.tensor_tensor(out=kd.ap[:, 0:tb], in0=kd.ap[:, 0:tb], in1=k_f.ap[:, kc, 0:tb], op=ALU.mult),
                          reads=[kd, k_f], writes=[kd])
                    fw.op("dve", lambda e: e.tensor_tensor(out=bp.ap[:, 0:tb], in0=kkn.ap[:, kc, 0:tb], in1=a_, op=ALU.mult), reads=[kkn, a_f], writes=[bp])
                    at, kt, bt, rt = ob.next(), ob.next(), ob.next(), ob.next()
                    fw.op("dve", lambda e: e.tensor_tensor(out=at.ap[:, 0:tb], in0=kkn.ap[:, kc, 0:tb], in1=eX.ap[:, 0:tb], op=ALU.mult), reads=[kkn, eX], writes=[at])
                    fw.op("dve", lambda e: e.tensor_tensor(out=kt.ap[:, 0:tb], in0=kd.ap[:, 0:tb], in1=enL.ap[:, 0:tb], op=ALU.mult), reads=[kd, enL], writes=[kt])
                    fw.op("pool", lambda e: e.tensor_tensor(out=bt.ap[:, 0:tb], in0=bp.ap[:, 0:tb], in1=enL.ap[:, 0:tb], op=ALU.mult), reads=[bp, enL], writes=[bt])
                    fw.op("dve", lambda e: e.tensor_tensor(out=rt.ap[:, 0:tb], in0=r_f.ap[:, kc, 0:tb], in1=eL.ap[:, 0:tb], op=ALU.mult), reads=[r_f, eL], writes=[rt])
                    Zt = Zr.next()
                    fw.op("dve", lambda e: e.scalar_tensor_tensor(out=Zt.ap[:, 0:tb], in0=kd.ap[:, 0:tb], scalar=RK(d, kc), in1=r_f.ap[:, kc, 0:tb],
                                                                  op0=ALU.mult, op1=ALU.mult), reads=[kd, cvb, r_f], writes=[Zt])
                    for s_ in range(tb // 128):
                        fw.op("pe", lambda e: e.matmul(pcs.ap[:, s_ * 64 + d * 32: s_ * 64 + (d + 1) * 32], lhsT=Zt.ap[:, s_ * 128:(s_ + 1) * 128], rhs=Eh.ap[:, kc, :],
                                                       start=(kc == 0 and d == 0 and s_ == 0), stop=(kc == KC - 1 and d == 1 and s_ == tb // 128 - 1),
                                                       skip_group_check=True), reads=[Zt, Eh], writes=[pcs])
                    rows = slice(kc * 128, (kc + 1) * 128)
                    fw.dma("sp", Atd[d].ap[rows, t0:t0 + tb], at.ap[:, 0:tb], reads=[at], writes=[Atd[d]])
                    fw.dma("sp", Ktd[d].ap[rows, t0:t0 + tb], kt.ap[:, 0:tb], reads=[kt], writes=[Ktd[d]])
                    fw.dma("sp", Btd[d].ap[rows, t0:t0 + tb], bt.ap[:, 0:tb], reads=[bt], writes=[Btd[d]])
                    fw.dma("sp", Rtd[d].ap[rows, t0:t0 + tb], rt.ap[:, 0:tb], reads=[rt], writes=[Rtd[d]])
                    pt = ptb.next()
                    for qi, src in enumerate((kt, bt)):
                        for s_ in range(tb // 128):
                            fw.op("pe", lambda e: e.transpose(pt.ap[:, (qi * 2 + s_) * 128:(qi * 2 + s_ + 1) * 128], src.ap[:, s_ * 128:(s_ + 1) * 128],
                                                              self.identb.ap[:]), reads=[src, self.identb], writes=[pt])
                    Kmt = Kmr.next()
                    ptv = pt.ap[:, 0:512].rearrange("p (q s k) -> p q s k", q=2, k=128)
                    fw.op("act", lambda e: e.copy(out=Kmt.ap[:, 0, 0:tb // 128, :], in_=ptv[:, 0, 0:tb // 128, :]), reads=[pt], writes=[Kmt])
                    fw.op("act", lambda e: e.mul(out=Kmt.ap[:, 1, 0:tb // 128, :], in_=ptv[:, 1, 0:tb // 128, :], mul=-1.0), reads=[pt], writes=[Kmt])
                    for qi, dst in enumerate((Kmd[d], Bmd[d])):
                        fw.dma("sp", dst.ap[t0:t0 + tb, kc * 128:(kc + 1) * 128].rearrange("(s p) f -> p s f", p=128), Kmt.ap[:, qi, 0:tb // 128, :],
                               reads=[Kmt], writes=[dst])
            for d in range(2):
                fw.dma("sp", WCd[d].ap[:, :, ch0:ch0 + nch], WCt.ap[:, d, :, 0:nch], reads=[WCt], writes=[WCd[d]])
            cs = cst.next()
            fw.op("act", lambda e: e.copy(out=cs.ap[:], in_=pcs.ap[:]), reads=[pcs], writes=[cs])
            for s_ in range(tb // 128):
                fw.dma("sp", CSd.ap[t0 + s_ * 128:t0 + (s_ + 1) * 128, :], cs.ap[:, s_ * 64:(s_ + 1) * 64], reads=[cs], writes=[CSd])
        self.phase_end()
        if self.stop == "rw1":
            return
        self.rwkv_scan(l, ja, Atd, Ktd, Btd, Rtd, Kmd, Bmd, WCd, Vd, Yd)
        if self.stop in ("rw2", "rs_a", "rs_b", "rs_0"):
            return
        self.rwkv_out(l, ja, Yd, Vd, Gd, CSd, OT)

    def rwkv_scan(self, l, ja, Atd, Ktd, Btd, Rtd, Kmd, Bmd, WCd, Vd, Yd):
        fw = self.fw
        din = self.din
        C = 64
        self.phase_begin()
        mk = self.sb("rs_mask", [128, 8, 128])
        fw.dma("sp", mk.ap[:], din["rw_mask"].ap.rearrange("m i t -> i m t"), reads=[din["rw_mask"]], writes=[mk])
        S = [self.sb("rs_S%d" % d, [128, KC, 64]) for d in range(2)]
        Sb = [self.sb("rs_Sb%d" % d, [128, KC, 64], BF16) for d in range(2)]
        fm_r = [self.sbrot("rs_fm%d" % i, 2, [128, 8, 128], BF16) for i in range(4)]
        tm_r = [self.sbrot("rs_tm%d" % i, 2, [128, 1024], BF16) for i in range(3)]
        wc_r = self.sbrot("rs_wc", 2, [128, 8, 2])
        gr = [self.sbrot("rs_g%d" % i, 2, [128, 16, 128], BF16) for i in range(5)]
        Pr = [self.sbrot("rs_P%d" % i, 2, [128, 16, 128], BF16) for i in range(2)]
        QTr = self.sbrot("rs_QT", 2, [128, 16, 128], BF16)
        Tt = self.sbrot("rs_T", 2, [128, 16, 128], BF16)
        Xb = self.sb("rs_Xb", [128, 1024], BF16)
        Ub = self.sb("rs_Ub", [128, 1024], BF16)
        tmp = self.sb("rs_tmp", [128, 4, 64])
        yt_r = self.sbrot("rs_y", 2, [128, 1024])
        stg = self.sb("rs_stg", [64, 2, 64])
        stg2 = self.sb("rs_stg2", [64, 128])
        pg = self.psrot("rs_pg", 3, [128, 512])
        px = self.ps("rs_px", [128, 512])
        pu = self.ps("rs_pu", [128, 512])
        pss = self.ps("rs_ps", [128, 512])
        py = [self.ps("rs_py%d" % i, [128, 512]) for i in range(2)]
        idb4 = self.identb.ap[:].unsqueeze(1).to_broadcast([128, 4, 128])
        sti, st_out = din["state_rwkv"], self.dout["st_rwkv"]

        def v4(t, g):
            return t.ap[:, g * 4:(g + 1) * 4, :]

        def p4(p):
            return p.ap[:].rearrange("p (h t) -> p h t", h=4)

        def tile_step(d, r0, half):
            At, Kt, Bt, Rt = [r.next() for r in fm_r]
            for t, src in ((At, Atd[d]), (Kt, Ktd[d]), (Bt, Btd[d]), (Rt, Rtd[d])):
                fw.dma("sp", t.ap[:], src.ap[half * 1024:(half + 1) * 1024, r0:r0 + 128].rearrange("(q p) t -> p q t", p=128), reads=[src], writes=[t])
            Km, Bm, V = [r.next() for r in tm_r]
            for t, src in ((Km, Kmd[d]), (Bm, Bmd[d]), (V, Vd)):
                fw.dma("sp", t.ap[:], src.ap[r0:r0 + 128, half * 1024:(half + 1) * 1024], reads=[src], writes=[t])
            wc = wc_r.next()
            c0 = r0 // C
            fw.dma("sp", wc.ap[:], WCd[d].ap[:, half * 8:(half + 1) * 8, c0:c0 + 2], reads=[WCd[d]], writes=[wc])
            N_, NT_, Nak, Mrk, MrbN = [r.next() for r in gr]
            T = Tt.next()
            for g in range(4):
                specs = ((Bt, At, N_, 0), (At, Bt, NT_, 1), (Kt, At, Nak, 0), (Kt, Rt, Mrk, 2), (Bt, Rt, MrbN, 3))
                for (L_, R_, dst, mi) in specs:
                    p = pg.next()
                    for hh in (0, 2, 1, 3):
                        h16 = g * 4 + hh
                        q, hp = h16 // 2, (h16 % 2) * 64
                        fw.op("pe", lambda e: e.matmul(p.ap[:, hh * 128:(hh + 1) * 128], lhsT=L_.ap[hp:hp + 64, q, :], rhs=R_.ap[hp:hp + 64, q, :],
                                                       start=True, stop=True), reads=[L_, R_], writes=[p], rt=hp)
                    fw.op("dve", lambda e: e.tensor_tensor(out=v4(dst, g), in0=p4(p), in1=mk.ap[:, d * 4 + mi, :].unsqueeze(1).to_broadcast([128, 4, 128]),
                                                           op=ALU.mult), reads=[p, mk], writes=[dst])
                fw.op("pool", lambda e: e.tensor_tensor(out=v4(T, g), in0=idb4, in1=v4(N_, g), op=ALU.subtract), reads=[self.identb, N_], writes=[T])
            if self.stop == "rs_a":
                return
            Pp, PTp = N_, NT_
            for lev in range(1, 6):
                Pc, PTc = Pr[0].next(), Pr[1].next()
                QT = QTr.next()
                for g in range(4):
                    if lev < 5:
                        p = pg.next()
                        for hh in range(4):
                            h16 = g * 4 + hh
                            fw.op("pe", lambda e: e.matmul(p.ap[:, hh * 128:(hh + 1) * 128], lhsT=PTp.ap[:, h16, :], rhs=Pp.ap[:, h16, :], start=True, stop=True),
                                  reads=[PTp, Pp], writes=[p])
                        fw.op("act", lambda e: e.copy(out=v4(Pc, g), in_=p4(p)), reads=[p], writes=[Pc])
                    p = pg.next()
                    for hh in range(4):
                        h16 = g * 4 + hh
                        fw.op("pe", lambda e: e.matmul(p.ap[:, hh * 128:(hh + 1) * 128], lhsT=Pp.ap[:, h16, :], rhs=PTp.ap[:, h16, :], start=True, stop=True),
                              reads=[PTp, Pp], writes=[p])
                    fw.op("dve", lambda e: e.tensor_copy(out=v4(PTc, g), in_=p4(p)), reads=[p], writes=[PTc])
                    fw.op("pool", lambda e: e.tensor_tensor(out=v4(QT, g), in0=v4(PTc, g), in1=idb4, op=ALU.add), reads=[PTc, self.identb], writes=[QT])
                Tn = Tt.next()
                for g in range(4):
                    p = pg.next()
                    for hh in range(4):
                        h16 = g * 4 + hh
                        fw.op("pe", lambda e: e.matmul(p.ap[:, hh * 128:(hh + 1) * 128], lhsT=QT.ap[:, h16, :], rhs=T.ap[:, h16, :], start=True, stop=True),
                              reads=[QT, T], writes=[p])
                    fw.op("act", lambda e: e.copy(out=v4(Tn, g), in_=p4(p)), reads=[p], writes=[Tn])
                T = Tn
                Pp, PTp = Pc, PTc
            if self.stop == "rs_b":
                return
            yt = yt_r.next()
            order = [0, 1] if d == 0 else [1, 0]
            for ci, c in enumerate(order):
                cr = slice(c * 64, (c + 1) * 64)
                for g8 in range(2):
                    hs = [(g8 * 8 + hh, (g8 * 8 + hh) // 2, ((g8 * 8 + hh) % 2) * 64) for hh in range(8)]
                    hso = [x for x in enumerate(hs) if x[1][2] != c * 64] + [x for x in enumerate(hs) if x[1][2] == c * 64]
                    for i_, (hh, (h16, q, hp)) in enumerate(hso):
                        fw.op("pe", lambda e: e.matmul(px.ap[cr, hh * 64:(hh + 1) * 64], lhsT=At.ap[hp:hp + 64, q, cr], rhs=Sb[d].ap[hp:hp + 64, half * 8 + q, :],
                                                       start=(i_ == 0), stop=False, skip_group_check=True), reads=[At, Sb[d]], writes=[px], rt=hp)
                    for hh, (h16, q, hp) in enumerate(hs):
                        fw.op("pe", lambda e: e.matmul(px.ap[cr, hh * 64:(hh + 1) * 64], lhsT=Nak.ap[cr, h16, cr], rhs=V.ap[cr, h16 * 64:(h16 + 1) * 64],
                                                       start=False, stop=True, skip_group_check=True), reads=[Nak, V], writes=[px], rt=c * 64)
                    fw.op("act", lambda e: e.copy(out=Xb.ap[cr, g8 * 512:(g8 + 1) * 512], in_=px.ap[cr, :]), reads=[px], writes=[Xb])
                    for hh, (h16, q, hp) in enumerate(hs):
                        fw.op("pe", lambda e: e.matmul(pu.ap[cr, hh * 64:(hh + 1) * 64], lhsT=T.ap[cr, h16, cr], rhs=Xb.ap[cr, h16 * 64:(h16 + 1) * 64],
                                                       start=(hh == 0), stop=True, skip_group_check=True), reads=[T, Xb], writes=[pu], rt=c * 64)
                    fw.op("dve", lambda e: e.tensor_copy(out=Ub.ap[cr, g8 * 512:(g8 + 1) * 512], in_=pu.ap[cr, :]), reads=[pu], writes=[Ub])
                    for i_, (hh, (h16, q, hp)) in enumerate(hso):
                        fw.op("pe", lambda e: e.matmul(py[g8].ap[cr, hh * 64:(hh + 1) * 64], lhsT=Rt.ap[hp:hp + 64, q, cr], rhs=Sb[d].ap[hp:hp + 64, half * 8 + q, :],
                                                       start=(i_ == 0), stop=False, skip_group_check=True), reads=[Rt, Sb[d]], writes=[py[g8]], rt=hp)
                    for hh, (h16, q, hp) in enumerate(hs):
                        fw.op("pe", lambda e: e.matmul(py[g8].ap[cr, hh * 64:(hh + 1) * 64], lhsT=Mrk.ap[cr, h16, cr], rhs=V.ap[cr, h16 * 64:(h16 + 1) * 64],
                                                       start=False, stop=False, skip_group_check=True), reads=[Mrk, V], writes=[py[g8]], rt=c * 64)
                        fw.op("pe", lambda e: e.matmul(py[g8].ap[cr, hh * 64:(hh + 1) * 64], lhsT=MrbN.ap[cr, h16, cr], rhs=Ub.ap[cr, h16 * 64:(h16 + 1) * 64],
                                                       start=False, stop=True, skip_group_check=True), reads=[MrbN, Ub], writes=[py[g8]], rt=c * 64)
                    for q4 in range(4):
                        q = g8 * 4 + q4
                        fw.op("pe", lambda e: e.matmul(pss.ap[:, q4 * 128:(q4 + 1) * 128], lhsT=Km.ap[cr, q * 128:(q + 1) * 128], rhs=V.ap[cr, q * 128:(q + 1) * 128],
                                                       start=(q4 == 0), stop=False, skip_group_check=True), reads=[Km, V], writes=[pss], rt=c * 64)
                        fw.op("pe", lambda e: e.matmul(pss.ap[:, q4 * 128:(q4 + 1) * 128], lhsT=Bm.ap[cr, q * 128:(q + 1) * 128], rhs=Ub.ap[cr, q * 128:(q + 1) * 128],
                                                       start=False, stop=True, skip_group_check=True), reads=[Bm, Ub], writes=[pss], rt=c * 64)
                    for hp in (0, 64):
                        hr = slice(hp, hp + 64)
                        pdiag = pss.ap[hr, :].rearrange("p (q x) -> p q x", q=4)[:, :, hp:hp + 64]
                        wcb = wc.ap[hr, g8 * 4:(g8 + 1) * 4, c:c + 1].to_broadcast([64, 4, 64])
                        Sv = S[d].ap[hr, half * 8 + g8 * 4: half * 8 + (g8 + 1) * 4, :]
                        fw.op("dve", lambda e: e.tensor_tensor(out=tmp.ap[hr], in0=pdiag, in1=wcb, op=ALU.mult), reads=[pss, wc], writes=[tmp])
                        fw.op("pool", lambda e: e.tensor_tensor(out=Sv, in0=Sv, in1=wcb, op=ALU.mult), reads=[S[d], wc], writes=[S[d]])
                        fw.op("pool", lambda e: e.tensor_tensor(out=Sv, in0=Sv, in1=tmp.ap[hr], op=ALU.add), reads=[S[d], tmp], writes=[S[d]])
                    fw.op("act", lambda e: e.copy(out=Sb[d].ap[:, half * 8 + g8 * 4: half * 8 + (g8 + 1) * 4, :],
                                                  in_=S[d].ap[:, half * 8 + g8 * 4: half * 8 + (g8 + 1) * 4, :]), reads=[S[d]], writes=[Sb[d]])
            for g8 in range(2):
                if g8 == 0:
                    fw.op("act", lambda e: e.copy(out=yt.ap[:, g8 * 512:(g8 + 1) * 512], in_=py[g8].ap[:]), reads=[py[g8]], writes=[yt])
                else:
                    fw.op("dve", lambda e: e.tensor_copy(out=yt.ap[:, g8 * 512:(g8 + 1) * 512], in_=py[g8].ap[:]), reads=[py[g8]], writes=[yt])
            fw.dma("sp", Yd[d].ap[r0:r0 + 128, half * 1024:(half + 1) * 1024], yt.ap[:], reads=[yt], writes=[Yd[d]])

        seqs = [(0, NS, None), (NS, NP, 0), (NS + NP, NP, 1)]
        if self.stop in ("rs_a", "rs_b", "rs_0", "rs_c"):
            seqs = seqs[0:2]
        for (row0, T_, pidx) in seqs:
            if self.stop in ("rs_a", "rs_b", "rs_0", "rs_c"):
                T_ = 256
            for d in range(2):
                if pidx is None:
                    for q in range(KC):
                        fw.dma("sp", stg.ap[:], sti.ap[ja, d, 2 * q:2 * q + 2].rearrange("hh v k -> v hh k"), reads=[sti], writes=[stg])
                        p = pg.next()
                        fw.op("pe", lambda e: e.transpose(p.ap[:, 0:64], stg.ap[:].rearrange("v hh k -> v (hh k)"), self.identf.ap[0:64, 0:64]),
                              reads=[stg, self.identf], writes=[p])
                        fw.op("act", lambda e: e.copy(out=S[d].ap[:, q, :], in_=p.ap[:, 0:64]), reads=[p], writes=[S[d]])
                else:
                    fw.op("pool", lambda e: e.memset(S[d].ap[:], 0.0), writes=[S[d]])
                fw.op("act", lambda e: e.copy(out=Sb[d].ap[:], in_=S[d].ap[:]), reads=[S[d]], writes=[Sb[d]])
            nt = T_ // 128
            for i in range(nt):
                if self.stop == "rs_0":
                    break
                for half in range(2):
                    tile_step(0, row0 + i * 128, half)
                    tile_step(1, row0 + (nt - 1 - i) * 128, half)
            if pidx is not None:
                for d in range(2):
                    for q in range(KC):
                        p = pg.next()
                        fw.op("pe", lambda e: e.transpose(p.ap[0:64, 0:128], S[d].ap[:, q, :], self.identf.ap[:]), reads=[S[d], self.identf], writes=[p])
                        fw.op("act", lambda e: e.copy(out=stg2.ap[:], in_=p.ap[0:64, 0:128]), reads=[p], writes=[stg2])
                        fw.dma("sp", st_out.ap[pidx, ja, d, 2 * q:2 * q + 2].rearrange("hh v k -> v hh k"),
                               stg2.ap[:].rearrange("v (hh k) -> v hh k", hh=2), reads=[stg2], writes=[st_out])
        self.phase_end()

    def rwkv_out(self, l, ja, Yd, Vd, Gd, CSd, OT):
        fw = self.fw
        din = self.din
        self.phase_begin()
        lw_row = self.sb("ro_lw", [128, D])
        lb_row = self.sb("ro_lb", [128, D])
        fw.dma("sp", lw_row.ap[:], din["rw_lnx_w"].ap[ja:ja + 1, :].partition_broadcast(128), reads=[din["rw_lnx_w"]], writes=[lw_row])
        fw.dma("sp", lb_row.ap[:], din["rw_lnx_b"].ap[ja:ja + 1, :].partition_broadcast(128), reads=[din["rw_lnx_b"]], writes=[lb_row])
        y0_r = self.sbrot("ro_y0", 2, [128, D])
        y1_r = self.sbrot("ro_y1", 2, [128, D])
        v_r = self.sbrot("ro_v", 2, [128, D], BF16)
        g_r = self.sbrot("ro_g", 2, [128, D], BF16)
        cs_r = self.sbrot("ro_cs", 2, [128, 64])
        st_r = self.sbrot("ro_st", 2, [128, 64])
        o_r = self.sbrot("ro_o", 2, [128, D], BF16)
        OTt = self.sbrot("ro_OTt", 2, [128, KC, 512], BF16)
        ptr = self.psrot("ro_pt", 2, [128, 1024], BF16)
        epsl = self.sb("ro_eps", [128, 1])
        fw.op("pool", lambda e: e.memset(epsl.ap[:], 64e-5), writes=[epsl])
        h3 = lambda ap_: ap_.rearrange("p (h n) -> p h n", h=32)
        for b0 in range(0, R, 512):
            ot = OTt.next()
            for s_ in range(4):
                r0 = b0 + s_ * 128
                y0, y1, v, g, cs, st = y0_r.next(), y1_r.next(), v_r.next(), g_r.next(), cs_r.next(), st_r.next()
                fw.dma("sp", y0.ap[:], Yd[0].ap[r0:r0 + 128, :], reads=[Yd[0]], writes=[y0])
                fw.dma("sp", y1.ap[:], Yd[1].ap[r0:r0 + 128, :], reads=[Yd[1]], writes=[y1])
                fw.dma("sp", v.ap[:], Vd.ap[r0:r0 + 128, :], reads=[Vd], writes=[v])
                fw.dma("sp", g.ap[:], Gd.ap[r0:r0 + 128, :], reads=[Gd], writes=[g])
                fw.dma("sp", cs.ap[:], CSd.ap[r0:r0 + 128, :], reads=[CSd], writes=[cs])
                fw.op("pool", lambda e: e.tensor_tensor(out=y0.ap[:], in0=y0.ap[:], in1=y1.ap[:], op=ALU.add), reads=[y0, y1], writes=[y0])
                fw.op("dve", lambda e: e.tensor_reduce(out=st.ap[:, 0:32], in_=h3(y0.ap[:]), axis=AX.X, op=ALU.add), reads=[y0], writes=[st])
                fw.op("dve", lambda e: e.tensor_scalar(out=st.ap[:, 0:32], in0=st.ap[:, 0:32], scalar1=1.0 / 64.0, scalar2=None, op0=ALU.mult), reads=[st], writes=[st])
                fw.op("dve", lambda e: e.tensor_tensor(out=h3(y0.ap[:]), in0=h3(y0.ap[:]), in1=st.ap[:, 0:32].unsqueeze(2).to_broadcast([128, 32, 64]),
                                                       op=ALU.subtract), reads=[y0, st], writes=[y0])
                fw.op("act", lambda e: e.activation(out=y1.ap[:], in_=y0.ap[:], func=AF.Square), reads=[y0], writes=[y1])
                fw.op("dve", lambda e: e.tensor_reduce(out=st.ap[:, 32:64], in_=h3(y1.ap[:]), axis=AX.X, op=ALU.add), reads=[y1], writes=[st])
                fw.op("act", lambda e: e.activation(out=st.ap[:, 32:64], in_=st.ap[:, 32:64], func=AF.Ln, scale=1.0 / 64.0, bias=epsl.ap[:, 0:1]), reads=[st, epsl], writes=[st])
                fw.op("act", lambda e: e.activation(out=st.ap[:, 32:64], in_=st.ap[:, 32:64], func=AF.Exp, scale=-0.5), reads=[st], writes=[st])
                fw.op("dve", lambda e: e.tensor_tensor(out=h3(y0.ap[:]), in0=h3(y0.ap[:]), in1=st.ap[:, 32:64].unsqueeze(2).to_broadcast([128, 32, 64]),
                                                       op=ALU.mult), reads=[y0, st], writes=[y0])
                fw.op("pool", lambda e: e.tensor_tensor(out=y0.ap[:], in0=y0.ap[:], in1=lw_row.ap[:], op=ALU.mult), reads=[y0, lw_row], writes=[y0])
                fw.op("pool", lambda e: e.tensor_tensor(out=y0.ap[:], in0=y0.ap[:], in1=lb_row.ap[:], op=ALU.add), reads=[y0, lb_row], writes=[y0])
                fw.op("dve", lambda e: e.tensor_tensor(out=cs.ap[:, 0:32], in0=cs.ap[:, 0:32], in1=cs.ap[:, 32:64], op=ALU.add), reads=[cs], writes=[cs])
                fw.op("dve", lambda e: e.tensor_tensor(out=h3(y1.ap[:]), in0=h3(v.ap[:]), in1=cs.ap[:, 0:32].unsqueeze(2).to_broadcast([128, 32, 64]),
                                                       op=ALU.mult), reads=[v, cs], writes=[y1])
                fw.op("pool", lambda e: e.tensor_tensor(out=y0.ap[:], in0=y0.ap[:], in1=y1.ap[:], op=ALU.add), reads=[y0, y1], writes=[y0])
                o = o_r.next()
                fw.op("dve", lambda e: e.tensor_tensor(out=o.ap[:], in0=y0.ap[:], in1=g.ap[:], op=ALU.mult), reads=[y0, g], writes=[o])
                for hf in range(2):
                    pt = ptr.next()
                    for q in range(8):
                        kc = hf * 8 + q
                        fw.op("pe", lambda e: e.transpose(pt.ap[:, q * 128:(q + 1) * 128], o.ap[:, kc * 128:(kc + 1) * 128], self.identb.ap[:]),
                              reads=[o, self.identb], writes=[pt])
                    fw.op("act", lambda e: e.copy(out=ot.ap[:, hf * 8:(hf + 1) * 8, s_ * 128:(s_ + 1) * 128],
                                                  in_=pt.ap[:].rearrange("p (q t) -> p q t", q=8)), reads=[pt], writes=[ot])
            fw.dma("sp", OT.ap[:, b0:b0 + 512].rearrange("(kc p) t -> p kc t", p=128), ot.ap[:], reads=[ot], writes=[OT])
        self.phase_end()
        if self.stop == "rw3":
            return
        self.out_proj(l, OT, din["rw_wo"], din["rw_wo"].ap[ja], 2)

    def final_norm(self):
        fw = self.fw
        self.phase_begin()
        Y = self.dout["y"]
        fnw = self.din["final_norm_w"]
        wrow = self.sb("fn_w", [128, D])
        fw.dma("sp", wrow.ap[:], fnw.ap.rearrange("(o d) -> o d", o=1).partition_broadcast(128), reads=[fnw], writes=[wrow])
        self.pp_junk = self.sb("pp_junk", [128, D], BF16)
        xr = self.sbrot("fn_x", 3, [128, D])
        yr = self.sbrot("fn_y", 3, [128, D])
        sts = self.sbrot("fn_st", 4, [128, 4])
        for s in range(R // 128):
            x = xr.next()
            fw.dma("sp", x.ap[:], self.Xt[s].ap, reads=[self.Xt[s]], writes=[x])
            st = sts.next()
            rs = self.rstd_of(x, st)
            y = yr.next()
            fw.op("dve", lambda e: e.scalar_tensor_tensor(out=y.ap[:], in0=x.ap[:], scalar=rs, in1=wrow.ap[:],
                                                          op0=ALU.mult, op1=ALU.mult),
                  reads=[x, st, wrow], writes=[y])
            fw.dma("sp", Y.ap[s * 128:(s + 1) * 128, :], y.ap[:], reads=[y], writes=[Y])
        self.phase_end()

    def declare(self):
        self.inp("xin", [R, D])
        self.inp("cond", [2, D])
        self.inp("ada_w", [DEPTH, D, 6 * D])
        self.inp("ada_b", [DEPTH, 6 * D])
        self.inp("norm1_w", [DEPTH, D])
        self.inp("norm2_w", [DEPTH, D])
        self.inp("ffn_w_in", [DEPTH, D, 2 * DFF])
        self.inp("ffn_w_out", [DEPTH, DFF, D])
        self.inp("final_norm_w", [D])
        self.inp("ml_w_down", [1, D, 1088])
        self.inp("ml_qnorm_w", [1, 512])
        self.inp("ml_kvnorm_w", [1, 512])
        self.inp("ml_w_uq", [1, 512, 3072])
        self.inp("ml_w_ukv", [1, 512, 4096])
        self.inp("ml_wo", [1, D, D])
        self.inp("cache_ckv", [1, 256, 512])
        self.inp("cache_krope", [1, 256, 64])
        self.inp("hg_w_in", [1, D, 5 * D])
        self.inp("hg_lb", [DEPTH, D])
        self.inp("hg_norm_w", [1, 128])
        self.inp("hg_wo", [1, D, D])
        self.inp("state_hgrn", [1, 2, 16, 128, 128])
        self.inp("hg_mask", [2, 64, 64])
        self.outp("st_hgrn", [2, 2, 16, 128, 128])
        for nm, shp in (("rw_mu", [2, 6, D]), ("rw_wr", [2, D, D]), ("rw_wk", [2, D, D]), ("rw_wv", [2, D, D]), ("rw_wo", [2, D, D]),
                        ("rw_w0", [2, 2, D]), ("rw_w1", [2, 2, D, 96]), ("rw_w2", [2, 2, 96, D]), ("rw_a0", [2, 2, D]),
                        ("rw_a1", [2, 2, D, 96]), ("rw_a2", [2, 2, 96, D]), ("rw_g1", [2, D, 256]), ("rw_g2", [2, 256, D]),
                        ("rw_kk", [2, D]), ("rw_ka", [2, D]), ("rw_rk", [2, 2, 32, 64]), ("rw_lnx_w", [2, D]), ("rw_lnx_b", [2, D]),
                        ("state_rwkv", [2, 2, 32, 64, 64]), ("rw_mask", [8, 128, 128])):
            self.inp(nm, shp)
        self.outp("st_rwkv", [2, 2, 2, 32, 64, 64])
        self.inp("rope_cos", [NS, 64])
        self.inp("rope_sin", [NS, 64])
        self.outp("y", [R, D])
        self.outp("ckv", [2 * NP, 512])
        self.outp("krope", [2 * NP, 64])
        Xs = self.scratch("X", [R, D])
        self.Xt = [Buf(Xs.ap[i * 128:(i + 1) * 128, :], "X%d" % i) for i in range(R // 128)]
        self.mrow = self.scratch("mrow", [DEPTH, 2, 6 * D])

    def copy_in(self):
        fw = self.fw
        xin = self.din["xin"]
        for i in range(R // 128):
            fw.dma("sp", self.Xt[i].ap, xin.ap[i * 128:(i + 1) * 128, :], reads=[xin], writes=[self.Xt[i]])

    def build(self):
        fw = self.fw
        self.declare()
        self.consts_begin()
        self.epsc = self.sb("epsc", [128, 2])
        fw.op("pool", lambda e: e.memset(self.epsc.ap[:, 0:1], EPS), writes=[self.epsc])
        fw.op("pool", lambda e: e.memset(self.epsc.ap[:, 1:2], 64e-5), writes=[self.epsc])
        self.copy_in()
        self.adaln()
        for (l, what) in self.plan:
            if what == "ffn":
                self.ffn(l)
            elif what == "mix":
                self.mixer(l)
        self.final_norm()
        fw.barrier()
        return self.nc

    def mixer(self, l):
        kind = l % 3
        if kind == 2:
            self.mla(l)
        elif kind == 1:
            self.hgrn(l)
        else:
            self.rwkv(l)


FULL_PLAN = [(l, w) for l in range(DEPTH) for w in ("mix", "ffn")]


def rope_tables():
    t = np.arange(NS)
    row = (t // 64).astype(np.float32)
    col = (t % 64).astype(np.float32)
    nf = 16
    inv = (np.float32(10000.0) ** (-np.arange(nf, dtype=np.float32) / np.float32(nf))).astype(np.float32)
    ar = row[:, None] * inv[None, :]
    ac = col[:, None] * inv[None, :]
    cs = np.concatenate([np.cos(ar), np.cos(ar), np.cos(ac), np.cos(ac)], axis=1).astype(np.float32)
    sn = np.concatenate([-np.sin(ar), np.sin(ar), -np.sin(ac), np.sin(ac)], axis=1).astype(np.float32)
    return np.ascontiguousarray(cs), np.ascontiguousarray(sn)


def hg_masks():
    i = np.arange(64)
    same = (i[:, None] // 32) == (i[None, :] // 32)
    mf = (same & (i[:, None] <= i[None, :])).astype(np.float32)
    mb = (same & (i[:, None] >= i[None, :])).astype(np.float32)
    return np.ascontiguousarray(np.stack([mf, mb], 0))


def rw_masks():
    i = np.arange(128)
    same = (i[:, None] // 64) == (i[None, :] // 64)
    out = []
    for d in range(2):
        lt = (i[:, None] < i[None, :]) if d == 0 else (i[:, None] > i[None, :])
        le = (i[:, None] <= i[None, :]) if d == 0 else (i[:, None] >= i[None, :])
        ms = (same & lt).astype(np.float32)
        mi = (same & le).astype(np.float32)
        out += [ms, ms.T.copy(), mi, -mi]
    return np.ascontiguousarray(np.stack(out, 0))


def make_in_maps(inputs, cores):
    maps = []
    for i in cores:
        m = {
            "xin": np.ascontiguousarray(np.concatenate(
                [inputs["x_sample"][i], inputs["x_prompt"][2 * i], inputs["x_prompt"][2 * i + 1]], axis=0)),
            "cond": np.ascontiguousarray(np.stack([inputs["c"][i], inputs["c_ctx"]], axis=0)),
        }
        for k in ("ada_w", "ada_b", "norm1_w", "norm2_w", "ffn_w_in", "ffn_w_out", "final_norm_w",
                  "ml_w_down", "ml_qnorm_w", "ml_kvnorm_w", "ml_w_uq", "ml_w_ukv", "ml_wo",
                  "hg_w_in", "hg_lb", "hg_norm_w", "hg_wo",
                  "rw_mu", "rw_wr", "rw_wk", "rw_wv", "rw_wo", "rw_w0", "rw_w1", "rw_w2", "rw_a0", "rw_a1", "rw_a2",
                  "rw_g1", "rw_g2", "rw_kk", "rw_ka", "rw_rk", "rw_lnx_w", "rw_lnx_b"):
            m[k] = np.ascontiguousarray(inputs[k])
        m["cache_ckv"] = np.ascontiguousarray(inputs["cache_ckv"][i])
        m["cache_krope"] = np.ascontiguousarray(inputs["cache_krope"][i])
        m["state_hgrn"] = np.ascontiguousarray(inputs["state_hgrn"][i])
        m["hg_mask"] = hg_masks()
        m["state_rwkv"] = np.ascontiguousarray(inputs["state_rwkv"][i])
        m["rw_mask"] = rw_masks()
        cs, sn = rope_tables()
        m["rope_cos"], m["rope_sin"] = cs, sn
        maps.append(m)
    return maps


def kernel(**inputs):
    inputs = {k: np.asarray(v) for k, v in inputs.items()}
    b = Builder(FULL_PLAN)
    nc = b.build()
    cores = list(range(8))
    res = run_bass_kernel_spmd(nc, make_in_maps(inputs, cores), core_ids=cores)
    rs = res.results
    f32 = np.float32
    y_sample = np.stack([np.asarray(rs[i]["y"])[0:NS] for i in cores], 0).astype(f32)
    y_prompt = np.concatenate([np.asarray(rs[i]["y"])[NS:R].reshape(2, NP, D) for i in cores], 0).astype(f32)
    st_rwkv = np.concatenate([np.asarray(rs[i]["st_rwkv"]).reshape(2, 2, 2, 32, 64, 64) for i in cores], 0).astype(f32)
    st_hgrn = np.concatenate([np.asarray(rs[i]["st_hgrn"]).reshape(2, 1, 2, 16, 128, 128) for i in cores], 0).astype(f32)
    ckv = np.concatenate([np.asarray(rs[i]["ckv"]).reshape(2, 1, NP, 512) for i in cores], 0).astype(f32)
    krope = np.concatenate([np.asarray(rs[i]["krope"]).reshape(2, 1, NP, 64) for i in cores], 0).astype(f32)
    return (y_prompt, y_sample, st_rwkv, st_hgrn, ckv, krope)
```

```python
import math
import numpy as np
import concourse.bass as bass
import concourse.mybir as mybir
from concourse.bass_utils import run_bass_kernel_spmd

F32 = mybir.dt.float32
BF16 = mybir.dt.bfloat16
AF = mybir.ActivationFunctionType
ALU = mybir.AluOpType
AX = mybir.AxisListType

D = 2048
KC = 16
DFF = 5632
NS = 4096
NP = 256
R = NS + 2 * NP
DEPTH = 4
EPS = 1e-6


class Buf:
    __slots__ = ("ap", "lw", "rd", "name", "psum")

    def __init__(self, ap, name="", psum=False):
        self.ap = ap
        self.lw = None
        self.rd = []
        self.name = name
        self.psum = psum


class FW:
    def __init__(self, nc, ndma=64):
        self.nc = nc
        self.eng = {"pe": nc.tensor, "act": nc.scalar, "dve": nc.vector, "pool": nc.gpsimd, "sp": nc.sync}
        self.sem = {}
        self.cnt = {}
        for k in self.eng:
            self.sem[k] = nc.alloc_semaphore("s_" + k)
            self.cnt[k] = 0
        self.ndma = ndma
        self.dsem = [nc.alloc_semaphore("d%d" % i) for i in range(ndma)]
        self.dcnt = [0] * ndma
        self.dslots = {"sp": list(range(0, ndma // 2)), "pool": list(range(ndma // 2, 3 * ndma // 4)),
                       "act": list(range(3 * ndma // 4, ndma))}
        self.dnext = {"sp": 0, "pool": 0, "act": 0}
        self.waited = {k: {} for k in self.eng}
        self.n_inst = 0
        self.n_wait = 0

    def _semobj(self, key):
        return self.sem[key] if isinstance(key, str) else self.dsem[key]

    def _wait(self, e, deps, skip_self):
        w = self.waited[e]
        need = {}
        for d in deps:
            if d is None:
                continue
            k, v = d
            if skip_self and k == e:
                continue
            if w.get(k, 0) >= v:
                continue
            if need.get(k, 0) < v:
                need[k] = v
        for k, v in need.items():
            self.eng[e].wait_ge(self._semobj(k), v)
            w[k] = v
            self.n_wait += 1

    @staticmethod
    def _deps(reads, writes, e=None):
        deps = []
        for b in reads:
            deps.append(b.lw)
            if b.psum:
                deps.extend(ev for ev in b.rd if ev[0] != e)
        for b in writes:
            deps.append(b.lw)
            deps.extend(b.rd)
        return deps

    @staticmethod
    def _commit(ev, reads, writes):
        for b in reads:
            b.rd.append(ev)
            if len(b.rd) > 48:
                m = {}
                for k, v in b.rd:
                    if m.get(k, 0) < v:
                        m[k] = v
                b.rd = list(m.items())
        for b in writes:
            b.lw = ev
            b.rd = []

    def op(self, e, fn, reads=(), writes=(), rt=None):
        self._wait(e, self._deps(reads, writes, e), skip_self=(e == "pe"))
        if e == "pe":
            last = getattr(self, "last_rt", None)
            if rt is not None and last is not None and rt != last and self.cnt["pe"] > 0:
                self.eng[e].wait_ge(self.sem["pe"], self.cnt["pe"])
                self.n_wait += 1
            self.last_rt = rt
        ins = fn(self.eng[e])
        self.cnt[e] += 1
        ins.then_inc(self.sem[e], 1)
        self._commit((e, self.cnt[e]), reads, writes)
        self.n_inst += 1
        return ins

    def dma(self, e, out, in_, reads=(), writes=(), **kw):
        self._wait(e, self._deps(reads, writes), skip_self=False)
        sl = self.dslots[e]
        i = sl[self.dnext[e]]
        self.dnext[e] = (self.dnext[e] + 1) % len(sl)
        if self.dcnt[i] > 0:
            self._wait(e, [(i, self.dcnt[i])], skip_self=False)
        ins = self.eng[e].dma_start(out=out, in_=in_, **kw)
        self.dcnt[i] += 16
        ins.then_inc(self.dsem[i], 16)
        self._commit((i, self.dcnt[i]), reads, writes)
        self.n_inst += 1
        return ins

    def barrier(self):
        evs = [(k, c) for k, c in self.cnt.items() if c > 0]
        evs += [(i, c) for i, c in enumerate(self.dcnt) if c > 0]
        for e in self.eng:
            self._wait(e, evs, skip_self=True)


class Rot:
    def __init__(self, bufs):
        self.bufs = bufs
        self.i = 0

    def next(self):
        b = self.bufs[self.i]
        self.i = (self.i + 1) % len(self.bufs)
        return b


class WStream:
    def __init__(self, rot, loaders):
        self.rot = rot
        self.loaders = loaders
        self.depth = len(rot.bufs)
        self.issued = []
        self.i = 0

    def _issue(self):
        k = len(self.issued)
        if k < len(self.loaders):
            b = self.rot.next()
            self.issued.append((b, self.loaders[k](b)))

    def next(self):
        while len(self.issued) < min(len(self.loaders), self.i + self.depth - 1) or len(self.issued) <= self.i:
            self._issue()
        r = self.issued[self.i]
        self.issued[self.i] = None
        self.i += 1
        return r


class Builder:
    def __init__(self, plan, TB=512, debug_out=()):
        self.plan = plan
        self.TB = TB
        self.stop = None
        self.nc = bass.Bass("TRN2", target_bir_lowering=False)
        self.fw = FW(self.nc)
        self.din = {}
        self.dout = {}
        self.stack = []
        self.debug_out = debug_out

    def inp(self, name, shape):
        ap = self.nc.dram_tensor(name, list(shape), F32, kind="ExternalInput").ap()
        self.din[name] = Buf(ap, name)
        return self.din[name]

    def outp(self, name, shape):
        ap = self.nc.dram_tensor(name, list(shape), F32, kind="ExternalOutput").ap()
        self.dout[name] = Buf(ap, name)
        return self.dout[name]

    def scratch(self, name, shape, dtype=F32):
        kind = "ExternalOutput" if name in self.debug_out else "Internal"
        ap = self.nc.dram_tensor(name, list(shape), dtype, kind=kind).ap()
        return Buf(ap, name)

    def phase_begin(self):
        self.fw.barrier()
        self.stack.append([])

    def phase_end(self):
        self.fw.barrier()
        guards = self.stack.pop()
        for g in reversed(guards):
            g.__exit__(None, None, None)

    def sb(self, name, shape, dtype=F32):
        self.uid = getattr(self, "uid", 0) + 1
        name = "%s_u%d" % (name, self.uid)
        g = self.nc.sbuf_tensor(name, list(shape), dtype)
        t = g.__enter__()
        self.stack[-1].append(g)
        return Buf(t, name)

    def ps(self, name, shape, dtype=F32):
        self.uid = getattr(self, "uid", 0) + 1
        name = "%s_u%d" % (name, self.uid)
        g = self.nc.psum_tensor(name, list(shape), dtype)
        t = g.__enter__()
        self.stack[-1].append(g)
        return Buf(t, name, psum=True)

    def sbrot(self, name, n, shape, dtype=F32):
        return Rot([self.sb("%s%d" % (name, i), shape, dtype) for i in range(n)])

    def psrot(self, name, n, shape, dtype=F32):
        return Rot([self.ps("%s%d" % (name, i), shape, dtype) for i in range(n)])

    def colvecs(self, rows, out, ps, stage, ident):
        fw = self.fw
        n = len(rows)
        assert n * 16 <= 128
        for i, (b, ap) in enumerate(rows):
            fw.dma("sp", stage.ap[i * 16:(i + 1) * 16, :], ap.rearrange("(kc p) -> kc p", p=128),
                   reads=[b], writes=[stage])
        fw.op("pe", lambda e: e.transpose(ps.ap[:, 0:n * 16], stage.ap[0:n * 16, :], ident.ap[0:n * 16, 0:n * 16]),
              reads=[stage, ident], writes=[ps])
        fw.op("dve", lambda e: e.tensor_copy(out=out.ap[:, 0:n * 16], in_=ps.ap[:, 0:n * 16]), reads=[ps], writes=[out])

    def group_of(self, row):
        return 0 if row < NS else 1

    def blocks(self, TB=None):
        TB = TB or self.TB
        out = []
        t = 0
        while t < NS:
            out.append((t, min(TB, NS - t)))
            t += TB
        t = NS
        while t < R:
            out.append((t, min(TB, R - t)))
            t += TB
        return out

    def consts_begin(self):
        nc, fw = self.nc, self.fw
        self.stack.append([])
        self.identf = self.sb("identf", [128, 128], F32)
        self.identb = self.sb("identb", [128, 128], BF16)
        for t in (self.identf, self.identb):
            fw.op("pool", lambda e: e.memset(t.ap[:], 1.0), writes=[t])
            fw.op("pool", lambda e: e.affine_select(out=t.ap[:], in_=t.ap[:], pattern=[[-1, 128]],
                                                    compare_op=ALU.is_equal, fill=0.0, base=0, channel_multiplier=1),
                  reads=[t], writes=[t])

    def adaln(self):
        fw = self.fw
        self.phase_begin()
        cond = self.din["cond"]
        adaw, adab = self.din["ada_w"], self.din["ada_b"]
        stage = self.sb("ad_stage", [32, 128])
        pst = self.ps("ad_pst", [128, 32])
        scT = self.sb("ad_scT", [128, 32], BF16)
        fw.dma("sp", stage.ap[:], cond.ap.rearrange("j (kc p) -> (j kc) p", p=128), reads=[cond], writes=[stage])
        fw.op("pe", lambda e: e.transpose(pst.ap[:], stage.ap[:], self.identf.ap[0:32, 0:32]),
              reads=[stage, self.identf], writes=[pst])
        fw.op("act", lambda e: e.activation(out=scT.ap[:], in_=pst.ap[:], func=AF.Silu), reads=[pst], writes=[scT])
        scv = scT.ap[:].rearrange("p (j kc) -> p j kc", j=2)
        wrot = self.sbrot("ad_w", 3, [128, KC, 512], BF16)
        prot = self.psrot("ad_ps", 2, [2, 512])
        brow = self.sb("ad_b", [2, 6 * D])
        mout = self.sbrot("ad_m", 2, [2, 6 * D])
        for l in range(DEPTH):
            fw.dma("sp", brow.ap[:], adab.ap[l:l + 1, :].partition_broadcast(2), reads=[adab], writes=[brow])
            mo = mout.next()
            for cb in range(6 * D // 512):
                w = wrot.next()
                fw.dma("pool", w.ap[:], adaw.ap[l, :, cb * 512:(cb + 1) * 512].rearrange("(kc p) n -> p kc n", p=128),
                       reads=[adaw], writes=[w])
                p = prot.next()
                for kc in range(KC):
                    fw.op("pe", lambda e: e.matmul(p.ap[:], lhsT=scv[:, :, kc], rhs=w.ap[:, kc, :],
                                                   start=(kc == 0), stop=(kc == KC - 1)),
                          reads=[scT, w], writes=[p])
                fw.op("dve", lambda e: e.tensor_tensor(out=mo.ap[:, cb * 512:(cb + 1) * 512], in0=p.ap[:],
                                                       in1=brow.ap[:, cb * 512:(cb + 1) * 512], op=ALU.add),
                      reads=[p, brow], writes=[mo])
            fw.dma("sp", self.mrow.ap[l], mo.ap[:], reads=[mo], writes=[self.mrow])
        self.phase_end()

    def prep_setup(self, l, which):
        fw = self.fw
        nw = self.din["norm1_w" if which == 0 else "norm2_w"]
        sh_s, sc_s = (0, 1) if which == 0 else (3, 4)
        m = self.mrow
        stage = self.sb("pp_stage", [128, 128])
        pst = self.ps("pp_pst", [128, 128])
        raw = self.sb("pp_raw", [128, 128])
        rows = [(nw, nw.ap[l]),
                (m, m.ap[l, 0, sc_s * D:(sc_s + 1) * D]), (m, m.ap[l, 1, sc_s * D:(sc_s + 1) * D]),
                (m, m.ap[l, 0, sh_s * D:(sh_s + 1) * D]), (m, m.ap[l, 1, sh_s * D:(sh_s + 1) * D])]
        self.colvecs(rows, raw, pst, stage, self.identf)
        modc = self.sb("pp_modc", [128, 64])
        for g in range(2):
            fw.op("dve", lambda e: e.scalar_tensor_tensor(out=modc.ap[:, g * 16:(g + 1) * 16],
                                                          in0=raw.ap[:, (1 + g) * 16:(2 + g) * 16], scalar=1.0,
                                                          in1=raw.ap[:, 0:16], op0=ALU.add, op1=ALU.mult),
                  reads=[raw], writes=[modc])
            fw.op("dve", lambda e: e.tensor_copy(out=modc.ap[:, (2 + g) * 16:(3 + g) * 16],
                                                 in_=raw.ap[:, (3 + g) * 16:(4 + g) * 16]),
                  reads=[raw], writes=[modc])
        self.modc = modc
        self.pp_x = self.sbrot("pp_x", 2, [128, D])
        self.pp_xn = self.sbrot("pp_xn", 2, [128, D], BF16)
        self.pp_junk = self.sb("pp_junk", [128, D], BF16)
        self.pp_st = self.sbrot("pp_st", 4, [128, 4])
        self.pp_ps = self.psrot("pp_ps", 2, [128, 8 * 128], BF16)

    def rstd_of(self, x, st, width=D, eps=EPS):
        fw = self.fw
        fw.op("act", lambda e: e.activation(out=self.pp_junk.ap[:, 0:width], in_=x.ap[:, 0:width], func=AF.Square,
                                            accum_out=st.ap[:, 0:1]),
              reads=[x], writes=[self.pp_junk, st])
        fw.op("act", lambda e: e.activation(out=st.ap[:, 1:2], in_=st.ap[:, 0:1], func=AF.Ln, scale=1.0 / width,
                                            bias=self.epsc.ap[:, 0:1] if eps == EPS else self.epsc.ap[:, 1:2]),
              reads=[st, self.epsc], writes=[st])
        fw.op("act", lambda e: e.activation(out=st.ap[:, 1:2], in_=st.ap[:, 1:2], func=AF.Exp, scale=-0.5),
              reads=[st], writes=[st])
        return st.ap[:, 1:2]

    def prep_block(self, t0, tb, hT, col0=0):
        fw = self.fw
        g = self.group_of(t0)
        modc = self.modc
        for s in range(tb // 128):
            x = self.pp_x.next()
            r0 = t0 + s * 128
            X = self.Xt[r0 // 128]
            fw.dma("sp", x.ap[:], X.ap, reads=[X], writes=[x])
            st = self.pp_st.next()
            rs = self.rstd_of(x, st)
            xn = self.pp_xn.next()
            fw.op("dve", lambda e: e.tensor_scalar(out=xn.ap[:], in0=x.ap[:], scalar1=rs, scalar2=None, op0=ALU.mult),
                  reads=[x, st], writes=[xn])
            for half in range(2):
                p = self.pp_ps.next()
                for q in range(8):
                    kc = half * 8 + q
                    fw.op("pe", lambda e: e.transpose(p.ap[:, q * 128:(q + 1) * 128], xn.ap[:, kc * 128:(kc + 1) * 128],
                                                      self.identb.ap[:]),
                          reads=[xn, self.identb], writes=[p])
                for q in range(8):
                    kc = half * 8 + q
                    dst = hT.ap[:, kc, col0 + s * 128: col0 + (s + 1) * 128]
                    src = p.ap[:, q * 128:(q + 1) * 128]
                    a_ap = modc.ap[:, g * 16 + kc: g * 16 + kc + 1]
                    b_ap = modc.ap[:, (2 + g) * 16 + kc: (2 + g) * 16 + kc + 1]
                    if half == 0:
                        fw.op("act", lambda e: e.activation(out=dst, in_=src, func=AF.Identity, scale=a_ap, bias=b_ap),
                              reads=[p, modc], writes=[hT])
                    else:
                        fw.op("dve", lambda e: e.tensor_scalar(out=dst, in0=src, scalar1=a_ap, scalar2=b_ap,
                                                               op0=ALU.mult, op1=ALU.add),
                              reads=[p, modc], writes=[hT])

    def gate_rows(self, l, sect, name):
        fw = self.fw
        g = self.sb(name, [128, 2 * D])
        for j in range(2):
            fw.dma("sp", g.ap[:, j * D:(j + 1) * D],
                   self.mrow.ap[l, j:j + 1, sect * D:(sect + 1) * D].partition_broadcast(128),
                   reads=[self.mrow], writes=[g])
        return g

    def load_w(self, wbuf, off, wsrc, src_ap, kcn, ncols):
        dst = wbuf.ap[:, off: off + kcn * ncols].rearrange("p (kc n) -> p kc n", kc=kcn)
        self.fw.dma("pool", dst, src_ap.rearrange("(kc p) n -> p kc n", p=128), reads=[wsrc], writes=[wbuf])
        return dst

    def ffn(self, l):
        fw = self.fw
        self.phase_begin()
        TB = self.TB
        w_in, w_out = self.din["ffn_w_in"], self.din["ffn_w_out"]
        self.prep_setup(l, 1)
        grow = self.gate_rows(l, 5, "ffn_g")
        hT = self.sb("ffn_hT", [128, KC, TB], BF16)
        uT = self.sb("ffn_uT", [128, DFF // 128, TB], BF16)
        CW = 256
        NK = DFF // 128
        WSZ = NK * CW
        wrot = self.sbrot("ffn_w", 3, [128, WSZ], BF16)
        psa = self.psrot("ffn_pa", 2, [128, 512])
        psb = self.psrot("ffn_pb", 2, [128, 512])
        sil = self.sbrot("ffn_sil", 2, [128, 512])
        xo = self.sbrot("ffn_xo", 3, [128, CW])
        xt = self.sbrot("ffn_xt", 3, [128, CW])
        blocks = self.blocks()
        w_in_v = w_in.ap[l].rearrange("(kc p) (two f) -> p two kc f", p=128, two=2)
        w_out_v = w_out.ap[l].rearrange("(kc p) n -> p kc n", p=128)

        def ld_in(cb):
            def f(buf):
                dst = buf.ap[:, 0:2 * KC * CW].rearrange("p (two kc n) -> p two kc n", two=2, kc=KC)
                fw.dma("pool", dst, w_in_v[:, :, :, cb * CW:(cb + 1) * CW], reads=[w_in], writes=[buf])
                return dst
            return f

        def ld_out(cb):
            def f(buf):
                dst = buf.ap[:, 0:NK * CW].rearrange("p (kc n) -> p kc n", kc=NK)
                fw.dma("pool", dst, w_out_v[:, :, cb * CW:(cb + 1) * CW], reads=[w_out], writes=[buf])
                return dst
            return f

        loaders = []
        for _ in blocks:
            loaders += [ld_in(cb) for cb in range(DFF // CW)]
            loaders += [ld_out(cb) for cb in range(D // CW)]
        ws = WStream(wrot, loaders)
        for (t0, tb) in blocks:
            g = self.group_of(t0)
            self.prep_block(t0, tb, hT)
            for cb in range(DFF // CW):
                w, wv = ws.next()
                for j in range(CW // 128):
                    fc = cb * (CW // 128) + j
                    for ts in range(0, tb, 512):
                        n = min(512, tb - ts)
                        pa, pb = psa.next(), psb.next()
                        for (pp, two) in ((pa, 0), (pb, 1)):
                            for kc in range(KC):
                                fw.op("pe", lambda e: e.matmul(pp.ap[:, 0:n], lhsT=wv[:, two, kc, j * 128:(j + 1) * 128],
                                                               rhs=hT.ap[:, kc, ts:ts + n], start=(kc == 0), stop=(kc == KC - 1)),
                                      reads=[w, hT], writes=[pp])
                        s_ = sil.next()
                        fw.op("act", lambda e: e.activation(out=s_.ap[:, 0:n], in_=pa.ap[:, 0:n], func=AF.Silu),
                              reads=[pa], writes=[s_])
                        fw.op("dve", lambda e: e.tensor_tensor(out=uT.ap[:, fc, ts:ts + n], in0=pb.ap[:, 0:n],
                                                               in1=s_.ap[:, 0:n], op=ALU.mult),
                              reads=[pb, s_], writes=[uT])
            for cb in range(D // CW):
                w, w2 = ws.next()
                for s in range(tb // 128):
                    X = self.Xt[(t0 + s * 128) // 128]
                    p = psa.next()
                    xold = xo.next()
                    fw.dma("sp", xold.ap[:], X.ap[:, cb * CW:(cb + 1) * CW], reads=[X], writes=[xold])
                    for kc in range(NK):
                        fw.op("pe", lambda e: e.matmul(p.ap[:, 0:CW], lhsT=uT.ap[:, kc, s * 128:(s + 1) * 128],
                                                       rhs=w2[:, kc, :], start=(kc == 0), stop=(kc == NK - 1)),
                              reads=[w, uT], writes=[p])
                    xn = xt.next()
                    fw.op("dve", lambda e: e.tensor_tensor(out=xn.ap[:], in0=p.ap[:, 0:CW],
                                                           in1=grow.ap[:, g * D + cb * CW: g * D + (cb + 1) * CW], op=ALU.mult),
                          reads=[p, grow], writes=[xn])
                    fw.op("dve", lambda e: e.tensor_tensor(out=xn.ap[:], in0=xn.ap[:], in1=xold.ap[:], op=ALU.add),
                          reads=[xn, xold], writes=[xn])
                    fw.dma("sp", X.ap[:, cb * CW:(cb + 1) * CW], xn.ap[:], reads=[xn], writes=[X])
        self.phase_end()

    def out_proj(self, l, AT, W_buf, W_ap, gate_sect):
        fw = self.fw
        self.phase_begin()
        TB = self.TB
        CW = 512
        grow = self.gate_rows(l, gate_sect, "op_g")
        aT = self.sbrot("op_aT", 2, [128, KC, TB], BF16)
        wrot = self.sbrot("op_w", 3, [128, KC * CW], BF16)
        psr = self.psrot("op_ps", 4, [128, 512])
        xo = self.sbrot("op_xo", 3, [128, CW])
        xt = self.sbrot("op_xt", 3, [128, CW])
        blocks = self.blocks()
        Wv = W_ap.rearrange("(kc p) n -> p kc n", p=128)

        def ld(cb):
            def f(buf):
                dst = buf.ap[:].rearrange("p (kc n) -> p kc n", kc=KC)
                fw.dma("pool", dst, Wv[:, :, cb * CW:(cb + 1) * CW], reads=[W_buf], writes=[buf])
                return dst
            return f
        loaders = []
        for _ in blocks:
            loaders += [ld(cb) for cb in range(D // CW)]
        ws = WStream(wrot, loaders)
        ATv = AT.ap.rearrange("(kc p) r -> p kc r", p=128)
        for (t0, tb) in blocks:
            g = self.group_of(t0)
            a = aT.next()
            fw.dma("sp", a.ap[:, :, 0:tb], ATv[:, :, t0:t0 + tb], reads=[AT], writes=[a])
            for cb in range(D // CW):
                w, wv = ws.next()
                for s_ in range(tb // 128):
                    X = self.Xt[(t0 + s_ * 128) // 128]
                    p = psr.next()
                    xold = xo.next()
                    fw.dma("sp", xold.ap[:], X.ap[:, cb * CW:(cb + 1) * CW], reads=[X], writes=[xold])
                    for kc in range(KC):
                        fw.op("pe", lambda e: e.matmul(p.ap[:, 0:CW], lhsT=a.ap[:, kc, s_ * 128:(s_ + 1) * 128],
                                                       rhs=wv[:, kc, :], start=(kc == 0), stop=(kc == KC - 1)),
                              reads=[w, a], writes=[p])
                    xn = xt.next()
                    fw.op("dve", lambda e: e.tensor_tensor(out=xn.ap[:], in0=p.ap[:, 0:CW],
                                                           in1=grow.ap[:, g * D + cb * CW: g * D + (cb + 1) * CW], op=ALU.mult),
                          reads=[p, grow], writes=[xn])
                    fw.op("dve", lambda e: e.tensor_tensor(out=xn.ap[:], in0=xn.ap[:], in1=xold.ap[:], op=ALU.add),
                          reads=[xn, xold], writes=[xn])
                    fw.dma("sp", X.ap[:, cb * CW:(cb + 1) * CW], xn.ap[:], reads=[xn], writes=[X])
        self.phase_end()

    def mla(self, l):
        fw = self.fw
        j = l // 3
        NKT = NS + 256 + 2 * NP
        H, DN, DR, DV = 16, 128, 64, 128
        QnT = self.scratch("ml_QnT", [H, 128, R], BF16)
        QrT = self.scratch("ml_QrT", [H, 64, R], BF16)
        KnT = self.scratch("ml_KnT", [H, 128, NKT], BF16)
        KrT = self.scratch("ml_KrT", [64, NKT], BF16)
        Vd = self.scratch("ml_V", [NKT, H * DV], BF16)
        OT = self.scratch("ml_OT", [D, R], BF16)
        w_down, w_uq, w_ukv = self.din["ml_w_down"], self.din["ml_w_uq"], self.din["ml_w_ukv"]
        ckv_out, kr_out = self.dout["ckv"], self.dout["krope"]

        self.phase_begin()
        TB = 256
        self.prep_setup(l, 0)
        Wd = self.sb("ml_Wd", [128, KC, 1088], BF16)
        Wq = self.sb("ml_Wq", [128, 4, 3072], BF16)
        Wkv = self.sb("ml_Wkv", [128, 4, 4096], BF16)
        fw.dma("pool", Wd.ap[:], w_down.ap[j].rearrange("(kc p) n -> p kc n", p=128), reads=[w_down], writes=[Wd])
        fw.dma("pool", Wq.ap[:], w_uq.ap[j].rearrange("(kc p) n -> p kc n", p=128), reads=[w_uq], writes=[Wq])
        fw.dma("pool", Wkv.ap[:], w_ukv.ap[j].rearrange("(kc p) n -> p kc n", p=128), reads=[w_ukv], writes=[Wkv])
        qnw = self.sb("ml_qnw", [128, 512])
        kvnw = self.sb("ml_kvnw", [128, 512])
        fw.dma("sp", qnw.ap[:], self.din["ml_qnorm_w"].ap[j:j + 1, :].partition_broadcast(128),
               reads=[self.din["ml_qnorm_w"]], writes=[qnw])
        fw.dma("sp", kvnw.ap[:], self.din["ml_kvnorm_w"].ap[j:j + 1, :].partition_broadcast(128),
               reads=[self.din["ml_kvnorm_w"]], writes=[kvnw])
        hT = self.sb("ml_hT", [128, KC, TB], BF16)
        cqnT = self.sb("ml_cqnT", [128, 4, TB], BF16)
        ckvnT = self.sb("ml_ckvnT", [128, 4, TB], BF16)
        krTt = self.sb("ml_krT", [64, TB], BF16)
        QnTb = self.sb("ml_QnTb", [128, H, TB], BF16)
        QrTb = self.sb("ml_QrTb", [64, H, TB], BF16)
        KnTb = self.sb("ml_KnTb", [128, H, TB], BF16)
        pA = self.psrot("ml_pA", 2, [128, 512])
        pkr = self.ps("ml_pkr", [128, 64])
        ptb = self.ps("ml_ptb", [128, 1024], BF16)
        ptf = self.ps("ml_ptf", [128, 512])
        cqn = self.sbrot("ml_cqn", 2, [128, 512], BF16)
        ckvn = self.sbrot("ml_ckvn", 2, [128, 512])
        krt = self.sbrot("ml_kr", 2, [128, 64])
        krr = self.sbrot("ml_krr", 2, [128, 64])
        tmp1 = self.sbrot("ml_t1", 2, [128, 2, 64])
        tmp2 = self.sbrot("ml_t2", 2, [128, 2, 64])
        cosr = self.sbrot("ml_cos", 2, [128, 64])
        sinr = self.sbrot("ml_sin", 2, [128, 64])
        qb16 = self.sbrot("ml_qb", 2, [128, 384], BF16)
        vtm = self.sbrot("ml_vtm", 2, [128, H * DV], BF16)
        sts = self.sbrot("ml_st", 4, [128, 4])
        rope_cos, rope_sin = self.din["rope_cos"], self.din["rope_sin"]

        def rope(dst3, src3, cs, sn, nh, t1, t2):
            csb = cs.ap[:].unsqueeze(1).to_broadcast([128, nh, 64])
            fw.op("dve", lambda e: e.tensor_tensor(out=t1.ap[:, 0:nh, :], in0=src3, in1=csb, op=ALU.mult),
                  reads=rd_src + [cs], writes=[t1])
            for rc in range(2):
                for hf in range(2):
                    o0 = rc * 32 + hf * 16
                    o1 = rc * 32 + (1 - hf) * 16
                    snb = sn.ap[:, o0:o0 + 16].unsqueeze(1).to_broadcast([128, nh, 16])
                    fw.op("dve", lambda e: e.tensor_tensor(out=t2.ap[:, 0:nh, o0:o0 + 16], in0=src3[:, :, o1:o1 + 16],
                                                           in1=snb, op=ALU.mult),
                          reads=rd_src + [sn], writes=[t2])
            fw.op("dve", lambda e: e.tensor_tensor(out=dst3, in0=t1.ap[:, 0:nh, :], in1=t2.ap[:, 0:nh, :], op=ALU.add),
                  reads=[t1, t2], writes=wr_dst)

        def kv_from_ckvnT(ncols, kcol0):
            for h in range(H):
                p = pA.next()
                for kc in range(4):
                    fw.op("pe", lambda e: e.matmul(p.ap[:, 0:ncols], lhsT=Wkv.ap[:, kc, h * 256:h * 256 + 128],
                                                   rhs=ckvnT.ap[:, kc, 0:ncols], start=(kc == 0), stop=(kc == 3)),
                          reads=[Wkv, ckvnT], writes=[p])
                eng = "act" if h % 2 == 0 else "dve"
                if eng == "act":
                    fw.op("act", lambda e: e.copy(out=KnTb.ap[:, h, 0:ncols], in_=p.ap[:, 0:ncols]), reads=[p], writes=[KnTb])
                else:
                    fw.op("dve", lambda e: e.tensor_copy(out=KnTb.ap[:, h, 0:ncols], in_=p.ap[:, 0:ncols]), reads=[p], writes=[KnTb])
            fw.dma("sp", KnT.ap[:, :, kcol0:kcol0 + ncols].rearrange("h p c -> p h c"), KnTb.ap[:, :, 0:ncols],
                   reads=[KnTb], writes=[KnT])
            Wv4 = Wkv.ap[:].rearrange("p kc (h two d) -> p kc h two d", h=H, two=2)
            for s_ in range(ncols // 128):
                v = vtm.next()
                for hg in range(4):
                    p = pA.next()
                    for kc in range(4):
                        fw.op("pe", lambda e: e.matmul(p.ap[:].rearrange("p (h d) -> p h d", h=4),
                                                       lhsT=ckvnT.ap[:, kc, s_ * 128:(s_ + 1) * 128],
                                                       rhs=Wv4[:, kc, hg * 4:(hg + 1) * 4, 1, :], start=(kc == 0), stop=(kc == 3)),
                              reads=[Wkv, ckvnT], writes=[p])
                    if hg % 2 == 0:
                        fw.op("act", lambda e: e.copy(out=v.ap[:, hg * 512:(hg + 1) * 512], in_=p.ap[:]), reads=[p], writes=[v])
                    else:
                        fw.op("dve", lambda e: e.tensor_copy(out=v.ap[:, hg * 512:(hg + 1) * 512], in_=p.ap[:]), reads=[p], writes=[v])
                fw.dma("sp", Vd.ap[kcol0 + s_ * 128: kcol0 + (s_ + 1) * 128, :], v.ap[:], reads=[v], writes=[Vd])

        def ckvn_to_T(src, s_):
            for kc in range(4):
                fw.op("pe", lambda e: e.transpose(ptf.ap[:, kc * 128:(kc + 1) * 128], src.ap[:, kc * 128:(kc + 1) * 128],
                                                  self.identf.ap[:]), reads=[src, self.identf], writes=[ptf])
            fw.op("act", lambda e: e.copy(out=ckvnT.ap[:, :, s_ * 128:(s_ + 1) * 128],
                                          in_=ptf.ap[:].rearrange("p (kc t) -> p kc t", kc=4)), reads=[ptf], writes=[ckvnT])

        def kr_to_T(src, s_):
            fw.op("pe", lambda e: e.transpose(ptf.ap[0:64, 0:128], src.ap[:, 0:64], self.identf.ap[:]),
                  reads=[src, self.identf], writes=[ptf])
            fw.op("dve", lambda e: e.tensor_copy(out=krTt.ap[:, s_ * 128:(s_ + 1) * 128], in_=ptf.ap[0:64, 0:128]),
                  reads=[ptf], writes=[krTt])

        if self.stop == "mla0":
            self.phase_end()
            return
        for (t0, tb) in self.blocks(TB):
            g = self.group_of(t0)
            kcol0 = t0 if g == 0 else t0 + 256
            self.prep_block(t0, tb, hT)
            if self.stop == "mla0b":
                break
            for s_ in range(tb // 128):
                r0 = t0 + s_ * 128
                pq, pkv = pA.next(), pA.next()
                for kc in range(KC):
                    lhs = hT.ap[:, kc, s_ * 128:(s_ + 1) * 128]
                    fw.op("pe", lambda e: e.matmul(pq.ap[:], lhsT=lhs, rhs=Wd.ap[:, kc, 0:512], start=(kc == 0), stop=(kc == KC - 1)),
                          reads=[hT, Wd], writes=[pq])
                    fw.op("pe", lambda e: e.matmul(pkv.ap[:], lhsT=lhs, rhs=Wd.ap[:, kc, 512:1024], start=(kc == 0), stop=(kc == KC - 1)),
                          reads=[hT, Wd], writes=[pkv])
                    fw.op("pe", lambda e: e.matmul(pkr.ap[:], lhsT=lhs, rhs=Wd.ap[:, kc, 1024:1088], start=(kc == 0), stop=(kc == KC - 1)),
                          reads=[hT, Wd], writes=[pkr])
                st = sts.next()
                rq = self.rstd_of(pq, st, width=512)
                cq = cqn.next()
                fw.op("dve", lambda e: e.scalar_tensor_tensor(out=cq.ap[:], in0=pq.ap[:], scalar=rq, in1=qnw.ap[:],
                                                              op0=ALU.mult, op1=ALU.mult), reads=[pq, st, qnw], writes=[cq])
                st2 = sts.next()
                rkv = self.rstd_of(pkv, st2, width=512)
                ck = ckvn.next()
                fw.op("dve", lambda e: e.scalar_tensor_tensor(out=ck.ap[:], in0=pkv.ap[:], scalar=rkv, in1=kvnw.ap[:],
                                                              op0=ALU.mult, op1=ALU.mult), reads=[pkv, st2, kvnw], writes=[ck])
                kr = krt.next()
                fw.op("act", lambda e: e.copy(out=kr.ap[:], in_=pkr.ap[:]), reads=[pkr], writes=[kr])
                if g == 1:
                    pr = r0 - NS
                    fw.dma("sp", ckv_out.ap[pr:pr + 128, :], ck.ap[:], reads=[ck], writes=[ckv_out])
                    fw.dma("sp", kr_out.ap[pr:pr + 128, :], kr.ap[:], reads=[kr], writes=[kr_out])
                    krsrc = kr
                else:
                    cs, sn = cosr.next(), sinr.next()
                    fw.dma("sp", cs.ap[:], rope_cos.ap[r0:r0 + 128, :], reads=[rope_cos], writes=[cs])
                    fw.dma("sp", sn.ap[:], rope_sin.ap[r0:r0 + 128, :], reads=[rope_sin], writes=[sn])
                    krsrc = krr.next()
                    rd_src, wr_dst = [kr], [krsrc]
                    rope(krsrc.ap[:].rearrange("p (o d) -> p o d", o=1), kr.ap[:].rearrange("p (o d) -> p o d", o=1),
                         cs, sn, 1, tmp1.next(), tmp2.next())
                if self.stop == "mla0c":
                    continue
                kr_to_T(krsrc, s_)
                ckvn_to_T(ck, s_)
                if self.stop == "mla0d":
                    continue
                for kc in range(4):
                    fw.op("pe", lambda e: e.transpose(ptb.ap[:, kc * 128:(kc + 1) * 128], cq.ap[:, kc * 128:(kc + 1) * 128],
                                                      self.identb.ap[:]), reads=[cq, self.identb], writes=[ptb])
                fw.op("act", lambda e: e.copy(out=cqnT.ap[:, :, s_ * 128:(s_ + 1) * 128],
                                              in_=ptb.ap[:, 0:512].rearrange("p (kc t) -> p kc t", kc=4)),
                      reads=[ptb], writes=[cqnT])
                if self.stop == "mla0e1":
                    continue
                for hp in range(H // 2):
                    p = pA.next()
                    for kc in range(4):
                        fw.op("pe", lambda e: e.matmul(p.ap[:, 0:384], lhsT=cqnT.ap[:, kc, s_ * 128:(s_ + 1) * 128],
                                                       rhs=Wq.ap[:, kc, hp * 384:(hp + 1) * 384], start=(kc == 0), stop=(kc == 3)),
                              reads=[cqnT, Wq], writes=[p])
                    q16 = qb16.next()
                    pv = p.ap[:, 0:384].rearrange("p (h d) -> p h d", h=2)
                    qv = q16.ap[:].rearrange("p (h d) -> p h d", h=2)
                    fw.op("act", lambda e: e.copy(out=qv[:, :, 0:128], in_=pv[:, :, 0:128]), reads=[p], writes=[q16])
                    if self.stop == "mla0e2":
                        continue
                    if g == 1 or self.stop == "mla0e4":
                        fw.op("act", lambda e: e.copy(out=qv[:, :, 128:192], in_=pv[:, :, 128:192]), reads=[p], writes=[q16])
                    else:
                        rd_src, wr_dst = [p, q16], [q16]
                        rope(qv[:, :, 128:192], pv[:, :, 128:192], cs, sn, 2, tmp1.next(), tmp2.next())
                    if self.stop == "mla0e3":
                        continue
                    for hh in range(2):
                        h = hp * 2 + hh
                        fw.op("pe", lambda e: e.transpose(ptb.ap[:, 512 + hh * 256: 512 + hh * 256 + 128], qv[:, hh, 0:128],
                                                          self.identb.ap[:]), reads=[q16, self.identb], writes=[ptb])
                        fw.op("pe", lambda e: e.transpose(ptb.ap[0:64, 512 + hh * 256 + 128: 512 + hh * 256 + 256], qv[:, hh, 128:192],
                                                          self.identb.ap[:]), reads=[q16, self.identb], writes=[ptb])
                    tv = ptb.ap[:, 512:1024].rearrange("p (h x) -> p h x", h=2)
                    fw.op("act", lambda e: e.copy(out=QnTb.ap[:, hp * 2:hp * 2 + 2, s_ * 128:(s_ + 1) * 128], in_=tv[:, :, 0:128]),
                          reads=[ptb], writes=[QnTb])
                    fw.op("act", lambda e: e.copy(out=QrTb.ap[:, hp * 2:hp * 2 + 2, s_ * 128:(s_ + 1) * 128], in_=tv[0:64, :, 128:256]),
                          reads=[ptb], writes=[QrTb])
            if self.stop in ("mla0c", "mla0d"):
                continue
            if self.stop in ("mla0e", "mla0e1", "mla0e2", "mla0e3", "mla0e4"):
                break
            fw.dma("sp", QnT.ap[:, :, t0:t0 + tb].rearrange("h p c -> p h c"), QnTb.ap[:, :, 0:tb], reads=[QnTb], writes=[QnT])
            fw.dma("sp", QrT.ap[:, :, t0:t0 + tb].rearrange("h p c -> p h c"), QrTb.ap[:, :, 0:tb], reads=[QrTb], writes=[QrT])
            fw.dma("sp", KrT.ap[:, kcol0:kcol0 + tb], krTt.ap[:, 0:tb], reads=[krTt], writes=[KrT])
            kv_from_ckvnT(tb, kcol0)
        if self.stop in ("mla0b", "mla0c", "mla0d", "mla0e", "mla0e1", "mla0e2", "mla0e3", "mla0e4"):
            self.phase_end()
            return
        cck, ckr = self.din["cache_ckv"], self.din["cache_krope"]
        for s_ in range(2):
            ck = ckvn.next()
            fw.dma("sp", ck.ap[:], cck.ap[j, s_ * 128:(s_ + 1) * 128, :], reads=[cck], writes=[ck])
            ckvn_to_T(ck, s_)
            kr = krt.next()
            fw.dma("sp", kr.ap[:], ckr.ap[j, s_ * 128:(s_ + 1) * 128, :], reads=[ckr], writes=[kr])
            kr_to_T(kr, s_)
        fw.dma("sp", KrT.ap[:, NS:NS + 256], krTt.ap[:, 0:256], reads=[krTt], writes=[KrT])
        kv_from_ckvnT(256, NS)
        self.phase_end()

        if self.stop == "mla1":
            return
        self.phase_begin()
        SCALE = 1.0 / math.sqrt(192.0)
        ones = self.sb("at_ones", [128, 128], BF16)
        fw.op("pool", lambda e: e.memset(ones.ap[:], 1.0), writes=[ones])
        NKmax = NS + 256
        Kn_r = self.sbrot("at_Kn", 2, [128, NKmax], BF16)
        V_r = self.sbrot("at_V", 2, [128, NKmax // 128, 128], BF16)
        Qn_r = self.sbrot("at_Qn", 2, [128, NS], BF16)
        Qr_r = self.sbrot("at_Qr", 2, [64, NS], BF16)
        Kr_t = self.sb("at_Kr", [64, NKmax], BF16)
        P_r = self.sbrot("at_P", 3, [128, 512], BF16)
        rd_r = self.sbrot("at_rd", 2, [128, 512])
        o_r = self.sbrot("at_o", 2, [128, 512], BF16)
        ps_s = self.psrot("at_pss", 3, [128, 512])
        ps_o = self.psrot("at_pso", 2, [128, 512])
        ps_d = self.psrot("at_psd", 2, [128, 512])
        seqs = [(0, NS, 0, NS + 256), (NS, NP, NS + 256, NP), (NS + NP, NP, NS + 256 + NP, NP)]
        for (q0, nq, k0, nk) in seqs:
            fw.dma("sp", Kr_t.ap[:, 0:nk], KrT.ap[:, k0:k0 + nk], reads=[KrT], writes=[Kr_t])
            nkt = nk // 128
            for h in range(H):
                Kn, V, Qn, Qr = Kn_r.next(), V_r.next(), Qn_r.next(), Qr_r.next()
                fw.dma("sp", Kn.ap[:, 0:nk], KnT.ap[h, :, k0:k0 + nk], reads=[KnT], writes=[Kn])
                fw.dma("sp", V.ap[:, 0:nkt, :], Vd.ap[k0:k0 + nk, h * 128:(h + 1) * 128].rearrange("(kt p) d -> p kt d", p=128),
                       reads=[Vd], writes=[V])
                fw.dma("sp", Qn.ap[:, 0:nq], QnT.ap[h, :, q0:q0 + nq], reads=[QnT], writes=[Qn])
                fw.dma("sp", Qr.ap[:, 0:nq], QrT.ap[h, :, q0:q0 + nq], reads=[QrT], writes=[Qr])
                for qb in range(0, nq, 512):
                    n = min(512, nq - qb)
                    po, pd = ps_o.next(), ps_d.next()
                    for kt in range(nkt):
                        pss = ps_s.next()
                        fw.op("pe", lambda e: e.matmul(pss.ap[:, 0:n], lhsT=Kn.ap[:, kt * 128:(kt + 1) * 128], rhs=Qn.ap[:, qb:qb + n],
                                                       start=True, stop=False), reads=[Kn, Qn], writes=[pss])
                        fw.op("pe", lambda e: e.matmul(pss.ap[:, 0:n], lhsT=Kr_t.ap[:, kt * 128:(kt + 1) * 128], rhs=Qr.ap[:, qb:qb + n],
                                                       start=False, stop=True), reads=[Kr_t, Qr], writes=[pss])
                        P = P_r.next()
                        fw.op("act", lambda e: e.activation(out=P.ap[:, 0:n], in_=pss.ap[:, 0:n], func=AF.Exp, scale=SCALE),
                              reads=[pss], writes=[P])
                        fw.op("pe", lambda e: e.matmul(po.ap[:, 0:n], lhsT=V.ap[:, kt, :], rhs=P.ap[:, 0:n],
                                                       start=(kt == 0), stop=(kt == nkt - 1)), reads=[V, P], writes=[po])
                        fw.op("pe", lambda e: e.matmul(pd.ap[:, 0:n], lhsT=ones.ap[:], rhs=P.ap[:, 0:n],
                                                       start=(kt == 0), stop=(kt == nkt - 1)), reads=[ones, P], writes=[pd])
                    rd = rd_r.next()
                    fw.op("dve", lambda e: e.reciprocal(out=rd.ap[:, 0:n], in_=pd.ap[:, 0:n]), reads=[pd], writes=[rd])
                    o = o_r.next()
                    fw.op("dve", lambda e: e.tensor_tensor(out=o.ap[:, 0:n], in0=po.ap[:, 0:n], in1=rd.ap[:, 0:n], op=ALU.mult),
                          reads=[po, rd], writes=[o])
                    fw.dma("sp", OT.ap[h * 128:(h + 1) * 128, q0 + qb:q0 + qb + n], o.ap[:, 0:n], reads=[o], writes=[OT])
        self.phase_end()
        if self.stop == "mla2":
            return
        self.out_proj(l, OT, self.din["ml_wo"], self.din["ml_wo"].ap[j], 2)

    def hgrn(self, l):
        fw = self.fw
        j = l // 3
        H, C = 16, 32
        w_in = self.din["hg_w_in"]
        Qd = [self.scratch("hg_Q%d" % d, [D, R], BF16) for d in range(2)]
        Kd = [self.scratch("hg_K%d" % d, [D, R], BF16) for d in range(2)]
        Khd = [self.scratch("hg_Kh%d" % d, [R, D], BF16) for d in range(2)]
        WCd = [self.scratch("hg_WC%d" % d, [128, H, R // C]) for d in range(2)]
        IVd = self.scratch("hg_IV", [R, D], BF16)
        GSd = self.scratch("hg_GS", [R, D], BF16)
        Od = [self.scratch("hg_O%d" % d, [R, D]) for d in range(2)]
        OT = self.scratch("hg_OT", [D, R], BF16)

        self.phase_begin()
        TB = 256
        NCH = TB // C
        self.prep_setup(l, 0)
        hlb = self.din["hg_lb"]
        stage = self.sb("hg_stage", [128, 128])
        pst = self.ps("hg_pst", [128, 128])
        lbr = self.sb("hg_lbraw", [128, 128])
        self.colvecs([(hlb, hlb.ap[i]) for i in range(DEPTH)], lbr, pst, stage, self.identf)
        fw.op("act", lambda e: e.activation(out=lbr.ap[:, 0:64], in_=lbr.ap[:, 0:64], func=AF.Exp), reads=[lbr], writes=[lbr])
        lbc = self.sb("hg_lbc", [128, 64])
        fw.op("dve", lambda e: e.tensor_tensor(out=lbc.ap[:, 32:48], in0=lbr.ap[:, 0:16], in1=lbr.ap[:, 16:32], op=ALU.add), reads=[lbr], writes=[lbc])
        fw.op("dve", lambda e: e.tensor_tensor(out=lbc.ap[:, 48:64], in0=lbr.ap[:, 32:48], in1=lbr.ap[:, 48:64], op=ALU.add), reads=[lbr], writes=[lbc])
        fw.op("dve", lambda e: e.tensor_tensor(out=lbc.ap[:, 32:48], in0=lbc.ap[:, 32:48], in1=lbc.ap[:, 48:64], op=ALU.add), reads=[lbc], writes=[lbc])
        fw.op("dve", lambda e: e.reciprocal(out=lbc.ap[:, 32:48], in_=lbc.ap[:, 32:48]), reads=[lbc], writes=[lbc])
        fw.op("dve", lambda e: e.memset(lbc.ap[:, 48:64], 0.0), reads=[lbc], writes=[lbc])
        for i in range(1, l + 1):
            fw.op("dve", lambda e: e.tensor_tensor(out=lbc.ap[:, 48:64], in0=lbc.ap[:, 48:64], in1=lbr.ap[:, i * 16:(i + 1) * 16], op=ALU.add),
                  reads=[lbc, lbr], writes=[lbc])
        fw.op("dve", lambda e: e.tensor_tensor(out=lbc.ap[:, 0:16], in0=lbc.ap[:, 48:64], in1=lbc.ap[:, 32:48], op=ALU.mult), reads=[lbc], writes=[lbc])
        fw.op("dve", lambda e: e.tensor_scalar(out=lbc.ap[:, 16:32], in0=lbc.ap[:, 0:16], scalar1=-1.0, scalar2=1.0, op0=ALU.mult, op1=ALU.add),
              reads=[lbc], writes=[lbc])
        msk = self.sb("hg_rst", [128, TB])
        fw.op("pool", lambda e: e.memset(msk.ap[:], 1.0), writes=[msk])
        fw.op("pool", lambda e: e.memset(msk.ap[:].rearrange("p (c t) -> p c t", t=C)[:, :, 0:1], 0.0), reads=[msk], writes=[msk])
        hT = self.sb("hg_hT", [128, KC, TB], BF16)
        wrot = self.sbrot("hg_w", 3, [128, KC * 512], BF16)
        hrot = Rot([[self.sb("hg_wh%d_%d" % (i, sec), [128, KC, 128], BF16) for sec in range(3)] for i in range(3)])
        p3 = [self.psrot("hg_p%d" % i, 1, [128, 512]) for i in range(3)]
        ptb = self.ps("hg_ptb", [128, 1024], BF16)
        qs_r = self.sbrot("hg_qs", 3, [128, TB])
        f_r = self.sbrot("hg_f", 4, [128, TB])
        gl_r = self.sbrot("hg_gl", 4, [128, TB])
        kk_r = self.sbrot("hg_kk", 4, [128, TB])
        F_r = self.sbrot("hg_F", 4, [128, TB])
        G_r = self.sbrot("hg_G", 4, [128, TB])
        Gr_r = self.sbrot("hg_Gr", 4, [128, TB])
        e_r = self.sbrot("hg_e", 9, [128, TB])
        o16 = self.sbrot("hg_o16", 10, [128, TB], BF16)
        kh_r = self.sbrot("hg_kh", 4, [128, TB], BF16)
        Kht = self.sb("hg_Kht", [128, TB // 128, 2, D], BF16)
        WCt = self.sb("hg_WCt", [128, 2, H, NCH])
        tmo = self.sbrot("hg_tmo", 2, [128, D], BF16)
        blocks = self.blocks(TB)
        w3 = w_in.ap[j].rearrange("(kc p) (sec f) -> p sec kc f", p=128, sec=5)
        wtm = w_in.ap[j].rearrange("(kc p) n -> p kc n", p=128)

        def ld_head(h):
            def f(grp):
                for sec in range(3):
                    fw.dma("pool", grp[sec].ap[:], w3[:, sec, :, h * 128:(h + 1) * 128], reads=[w_in], writes=[grp[sec]])
                return grp
            return f

        def ld_tm(c0):
            def f(buf):
                dst = buf.ap[:].rearrange("p (kc n) -> p kc n", kc=KC)
                fw.dma("pool", dst, wtm[:, :, c0:c0 + 512], reads=[w_in], writes=[buf])
                return dst
            return f
        lh, lt = [], []
        for _ in blocks:
            lh += [ld_head(h) for h in range(H)]
            lt += [ld_tm(3 * D + cb * 512) for cb in range(8)]
        wsH = WStream(hrot, lh)
        ws = WStream(wrot, lt)
        for (t0, tb) in blocks:
            self.prep_block(t0, tb, hT)
            nch = tb // C
            ch0 = t0 // C
            for h in range(H):
                grp, _ = wsH.next()
                pz = [p3[i].next() for i in range(3)]
                for sec in range(3):
                    for kc in range(KC):
                        fw.op("pe", lambda e: e.matmul(pz[sec].ap[:, 0:tb], lhsT=grp[sec].ap[:, kc, :], rhs=hT.ap[:, kc, 0:tb],
                                                       start=(kc == 0), stop=(kc == KC - 1)), reads=[grp[sec], hT], writes=[pz[sec]])
                qs = qs_r.next()
                fw.op("act", lambda e: e.activation(out=qs.ap[:, 0:tb], in_=pz[0].ap[:, 0:tb], func=AF.Silu), reads=[pz[0]], writes=[qs])
                for d in range(2):
                    f_, gl, kk, F, G, Gr = f_r.next(), gl_r.next(), kk_r.next(), F_r.next(), G_r.next(), Gr_r.next()
                    fw.op("act", lambda e: e.activation(out=f_.ap[:, 0:tb], in_=pz[1 + d].ap[:, 0:tb], func=AF.Sigmoid), reads=[pz[1 + d]], writes=[f_])
                    fw.op("dve", lambda e: e.tensor_scalar(out=f_.ap[:, 0:tb], in0=f_.ap[:, 0:tb], scalar1=lbc.ap[:, 16 + h:17 + h],
                                                           scalar2=lbc.ap[:, h:h + 1], op0=ALU.mult, op1=ALU.add), reads=[f_, lbc], writes=[f_])
                    fw.op("act", lambda e: e.activation(out=gl.ap[:, 0:tb], in_=f_.ap[:, 0:tb], func=AF.Ln), reads=[f_], writes=[gl])
                    fw.op("dve", lambda e: e.tensor_scalar(out=kk.ap[:, 0:tb], in0=f_.ap[:, 0:tb], scalar1=-1.0, scalar2=1.0,
                                                           op0=ALU.mult, op1=ALU.add), reads=[f_], writes=[kk])
                    fw.op("dve", lambda e: e.tensor_tensor_scan(out=F.ap[:, 0:tb], data0=msk.ap[:, 0:tb], data1=gl.ap[:, 0:tb],
                                                                initial=0.0, op0=ALU.mult, op1=ALU.add), reads=[msk, gl], writes=[F])
                    F3 = F.ap[:, 0:tb].rearrange("p (c t) -> p c t", t=C)
                    tot = F3[:, :, C - 1:C]
                    if d == 0:
                        Gt = F
                        fw.op("dve", lambda e: e.tensor_tensor(out=Gr.ap[:, 0:tb].rearrange("p (c t) -> p c t", t=C),
                                                               in0=tot.to_broadcast([128, nch, C]), in1=F3, op=ALU.subtract),
                              reads=[F], writes=[Gr])
                    else:
                        fw.op("dve", lambda e: e.tensor_tensor(out=Gr.ap[:, 0:tb], in0=F.ap[:, 0:tb], in1=gl.ap[:, 0:tb], op=ALU.subtract),
                              reads=[F, gl], writes=[Gr])
                        fw.op("dve", lambda e: e.tensor_tensor(out=G.ap[:, 0:tb].rearrange("p (c t) -> p c t", t=C),
                                                               in0=tot.to_broadcast([128, nch, C]),
                                                               in1=Gr.ap[:, 0:tb].rearrange("p (c t) -> p c t", t=C), op=ALU.subtract),
                              reads=[F, Gr], writes=[G])
                        Gt = G
                    fw.op("act", lambda e: e.activation(out=WCt.ap[:, d, h, 0:nch].unsqueeze(2), in_=tot, func=AF.Exp), reads=[F], writes=[WCt])
                    eG, enG, eR = e_r.next(), e_r.next(), e_r.next()
                    fw.op("act", lambda e: e.activation(out=eG.ap[:, 0:tb], in_=Gt.ap[:, 0:tb], func=AF.Exp), reads=[Gt], writes=[eG])
                    fw.op("act", lambda e: e.activation(out=enG.ap[:, 0:tb], in_=Gt.ap[:, 0:tb], func=AF.Exp, scale=-1.0), reads=[Gt], writes=[enG])
                    fw.op("act", lambda e: e.activation(out=eR.ap[:, 0:tb], in_=Gr.ap[:, 0:tb], func=AF.Exp), reads=[Gr], writes=[eR])
                    qt, kt, kh = o16.next(), o16.next(), kh_r.next()
                    fw.op("dve", lambda e: e.tensor_tensor(out=qt.ap[:, 0:tb], in0=qs.ap[:, 0:tb], in1=eG.ap[:, 0:tb], op=ALU.mult), reads=[qs, eG], writes=[qt])
                    fw.op("dve", lambda e: e.tensor_tensor(out=kt.ap[:, 0:tb], in0=kk.ap[:, 0:tb], in1=enG.ap[:, 0:tb], op=ALU.mult), reads=[kk, enG], writes=[kt])
                    fw.op("dve", lambda e: e.tensor_tensor(out=kh.ap[:, 0:tb], in0=kk.ap[:, 0:tb], in1=eR.ap[:, 0:tb], op=ALU.mult), reads=[kk, eR], writes=[kh])
                    fw.dma("sp", Qd[d].ap[h * 128:(h + 1) * 128, t0:t0 + tb], qt.ap[:, 0:tb], reads=[qt], writes=[Qd[d]])
                    fw.dma("sp", Kd[d].ap[h * 128:(h + 1) * 128, t0:t0 + tb], kt.ap[:, 0:tb], reads=[kt], writes=[Kd[d]])
                    for s_ in range(tb // 128):
                        fw.op("pe", lambda e: e.transpose(ptb.ap[:, (d * 2 + s_) * 128:(d * 2 + s_ + 1) * 128], kh.ap[:, s_ * 128:(s_ + 1) * 128],
                                                          self.identb.ap[:]), reads=[kh, self.identb], writes=[ptb])
                    fw.op("act", lambda e: e.copy(out=Kht.ap[:, 0:tb // 128, d, h * 128:(h + 1) * 128],
                                                  in_=ptb.ap[:, d * 256: d * 256 + tb].rearrange("p (s k) -> p s k", k=128)),
                          reads=[ptb], writes=[Kht])
            for d in range(2):
                for s_ in range(tb // 128):
                    fw.dma("sp", Khd[d].ap[t0 + s_ * 128: t0 + (s_ + 1) * 128, :], Kht.ap[:, s_, d, :], reads=[Kht], writes=[Khd[d]])
                fw.dma("sp", WCd[d].ap[:, :, ch0:ch0 + nch], WCt.ap[:, d, :, 0:nch], reads=[WCt], writes=[WCd[d]])
            for sec in range(2):
                dst_d = IVd if sec == 0 else GSd
                tiles = [tmo.next() for _ in range(tb // 128)]
                for cb in range(4):
                    w, wv = ws.next()
                    for s_ in range(tb // 128):
                        p = p3[s_ % 3].next()
                        for kc in range(KC):
                            fw.op("pe", lambda e: e.matmul(p.ap[:], lhsT=hT.ap[:, kc, s_ * 128:(s_ + 1) * 128], rhs=wv[:, kc, :],
                                                           start=(kc == 0), stop=(kc == KC - 1)), reads=[w, hT], writes=[p])
                        fw.op("act", lambda e: e.activation(out=tiles[s_].ap[:, cb * 512:(cb + 1) * 512], in_=p.ap[:],
                                                            func=(AF.Copy if sec == 0 else AF.Silu)), reads=[p], writes=[tiles[s_]])
                for s_ in range(tb // 128):
                    fw.dma("sp", dst_d.ap[t0 + s_ * 128: t0 + (s_ + 1) * 128, :], tiles[s_].ap[:], reads=[tiles[s_]], writes=[dst_d])
        self.phase_end()
        if self.stop == "hg1":
            return

        self.phase_begin()
        mk = self.sb("hg_mask", [64, 2, 64])
        fw.dma("sp", mk.ap[:], self.din["hg_mask"].ap.rearrange("d s t -> s d t"), reads=[self.din["hg_mask"]], writes=[mk])
        S = [[self.sb("hg_S%d_%d" % (d, g), [128, 4, 128]) for g in range(4)] for d in range(2)]
        Sb = [[self.sb("hg_Sb%d_%d" % (d, g), [128, 4, 128], BF16) for g in range(4)] for d in range(2)]
        qt_r = self.sbrot("hs_q", 2, [128, H, 128], BF16)
        kt_r = self.sbrot("hs_k", 2, [128, H, 128], BF16)
        kh_r2 = self.sbrot("hs_kh", 2, [64, 2, D], BF16)
        iv_r = self.sbrot("hs_iv", 2, [64, 2, D], BF16)
        wc_r = self.sbrot("hs_wc", 2, [128, H, 4])
        AT_r = self.sbrot("hs_AT", 2, [64, 16 * 64], BF16)
        o_r = self.sbrot("hs_o", 2, [64, 2, D])
        pa_r = self.psrot("hs_pa", 2, [128, 512])
        pu_r = self.psrot("hs_pu", 2, [128, 512])
        po = [self.ps("hs_po%d" % g, [128, 512]) for g in range(4)]
        sth, st_out = self.din["state_hgrn"], self.dout["st_hgrn"]

        def tile_step(d, r0, first_chunk_global):
            qt, kt, kh, iv, wc = qt_r.next(), kt_r.next(), kh_r2.next(), iv_r.next(), wc_r.next()
            fw.dma("sp", qt.ap[:], Qd[d].ap[:, r0:r0 + 128].rearrange("(h k) t -> k h t", k=128), reads=[Qd[d]], writes=[qt])
            fw.dma("sp", kt.ap[:], Kd[d].ap[:, r0:r0 + 128].rearrange("(h k) t -> k h t", k=128), reads=[Kd[d]], writes=[kt])
            fw.dma("sp", kh.ap[:], Khd[d].ap[r0:r0 + 128, :].rearrange("(hf p) f -> p hf f", p=64), reads=[Khd[d]], writes=[kh])
            fw.dma("sp", iv.ap[:], IVd.ap[r0:r0 + 128, :].rearrange("(hf p) f -> p hf f", p=64), reads=[IVd], writes=[iv])
            c0 = r0 // C
            fw.dma("sp", wc.ap[:], WCd[d].ap[:, :, c0:c0 + 4], reads=[WCd[d]], writes=[wc])
            o = o_r.next()
            halves = [0, 1] if d == 0 else [1, 0]
            for hf in halves:
                AT = AT_r.next()
                for g2 in range(2):
                    pa = pa_r.next()
                    for hh in range(8):
                        h = g2 * 8 + hh
                        fw.op("pe", lambda e: e.matmul(pa.ap[0:64, hh * 64:(hh + 1) * 64], lhsT=kt.ap[:, h, hf * 64:(hf + 1) * 64],
                                                       rhs=qt.ap[:, h, hf * 64:(hf + 1) * 64], start=True, stop=True),
                              reads=[kt, qt], writes=[pa])
                    fw.op("dve", lambda e: e.tensor_tensor(out=AT.ap[:, g2 * 512:(g2 + 1) * 512].rearrange("p (h t) -> p h t", h=8),
                                                           in0=pa.ap[0:64, :].rearrange("p (h t) -> p h t", h=8),
                                                           in1=mk.ap[:, d, :].unsqueeze(1).to_broadcast([64, 8, 64]), op=ALU.mult),
                          reads=[pa, mk], writes=[AT])
                for g in range(4):
                    for hh in range(4):
                        h = g * 4 + hh
                        fw.op("pe", lambda e: e.matmul(po[g].ap[0:64, hh * 128:(hh + 1) * 128], lhsT=AT.ap[:, h * 64:(h + 1) * 64],
                                                       rhs=iv.ap[:, hf, h * 128:(h + 1) * 128], start=(hh == 0), stop=False, skip_group_check=True),
                              reads=[AT, iv], writes=[po[g]])
                order = [0, 1] if d == 0 else [1, 0]
                for ci, c in enumerate(order):
                    cg = hf * 2 + c
                    for g in range(4):
                        for hh in range(4):
                            h = g * 4 + hh
                            fw.op("pe", lambda e: e.matmul(po[g].ap[c * 32:(c + 1) * 32, hh * 128:(hh + 1) * 128],
                                                           lhsT=qt.ap[:, h, hf * 64 + c * 32: hf * 64 + (c + 1) * 32],
                                                           rhs=Sb[d][g].ap[:, hh, :], start=False, stop=True, skip_group_check=True), reads=[qt, Sb[d][g]], writes=[po[g]])
                        pu = pu_r.next()
                        for hh in range(4):
                            h = g * 4 + hh
                            fw.op("pe", lambda e: e.matmul(pu.ap[:, hh * 128:(hh + 1) * 128], lhsT=kh.ap[c * 32:(c + 1) * 32, hf, h * 128:(h + 1) * 128],
                                                           rhs=iv.ap[c * 32:(c + 1) * 32, hf, h * 128:(h + 1) * 128], start=True, stop=True),
                                  reads=[kh, iv], writes=[pu])
                        for hh in range(4):
                            h = g * 4 + hh
                            fw.op("dve", lambda e: e.scalar_tensor_tensor(out=S[d][g].ap[:, hh, :], in0=S[d][g].ap[:, hh, :], scalar=wc.ap[:, h, cg:cg + 1],
                                                                          in1=pu.ap[:, hh * 128:(hh + 1) * 128], op0=ALU.mult, op1=ALU.add),
                                  reads=[S[d][g], wc, pu], writes=[S[d][g]])
                        fw.op("act", lambda e: e.copy(out=Sb[d][g].ap[:], in_=S[d][g].ap[:]), reads=[S[d][g]], writes=[Sb[d][g]])
                for g in range(4):
                    if g % 2 == 0:
                        fw.op("act", lambda e: e.copy(out=o.ap[:, hf, g * 512:(g + 1) * 512], in_=po[g].ap[0:64, :]), reads=[po[g]], writes=[o])
                    else:
                        fw.op("dve", lambda e: e.tensor_copy(out=o.ap[:, hf, g * 512:(g + 1) * 512], in_=po[g].ap[0:64, :]), reads=[po[g]], writes=[o])
            fw.dma("sp", Od[d].ap[r0:r0 + 128, :].rearrange("(hf p) f -> p hf f", p=64), o.ap[:], reads=[o], writes=[Od[d]])

        seqs = [(0, NS, None), (NS, NP, 0), (NS + NP, NP, 1)]
        for (row0, T, pidx) in seqs:
            for d in range(2):
                for g in range(4):
                    if pidx is None:
                        fw.dma("sp", S[d][g].ap[:], sth.ap[j, d, g * 4:(g + 1) * 4].rearrange("h k v -> k h v"), reads=[sth], writes=[S[d][g]])
                    else:
                        fw.op("pool", lambda e: e.memset(S[d][g].ap[:], 0.0), writes=[S[d][g]])
                    fw.op("act", lambda e: e.copy(out=Sb[d][g].ap[:], in_=S[d][g].ap[:]), reads=[S[d][g]], writes=[Sb[d][g]])
            nt = T // 128
            for i in range(nt):
                tile_step(0, row0 + i * 128, None)
                tile_step(1, row0 + (nt - 1 - i) * 128, None)
            if pidx is not None:
                for d in range(2):
                    for g in range(4):
                        fw.dma("sp", st_out.ap[pidx, d, g * 4:(g + 1) * 4].rearrange("h k v -> k h v"), S[d][g].ap[:], reads=[S[d][g]], writes=[st_out])
        self.phase_end()
        if self.stop == "hg2":
            return

        self.phase_begin()
        nwt = self.sb("hc_nw", [128, 128])
        fw.dma("sp", nwt.ap[:], self.din["hg_norm_w"].ap[j:j + 1, :].partition_broadcast(128), reads=[self.din["hg_norm_w"]], writes=[nwt])
        of_r = self.sbrot("hc_of", 2, [128, D])
        ob_r = self.sbrot("hc_ob", 2, [128, D])
        gs_r = self.sbrot("hc_gs", 2, [128, D], BF16)
        sq_r = self.sbrot("hc_sq", 1, [128, D])
        y_r = self.sbrot("hc_y", 2, [128, D], BF16)
        st_r = self.sbrot("hc_st", 2, [128, 16])
        OTt = self.sbrot("hc_OTt", 2, [128, KC, 512], BF16)
        ptr = self.psrot("hc_pt", 2, [128, 1024], BF16)
        eps128 = self.sb("hc_eps", [128, 1])
        fw.op("pool", lambda e: e.memset(eps128.ap[:], EPS), writes=[eps128])
        for b0 in range(0, R, 512):
            ot = OTt.next()
            for s_ in range(4):
                r0 = b0 + s_ * 128
                of, ob, gs = of_r.next(), ob_r.next(), gs_r.next()
                fw.dma("sp", of.ap[:], Od[0].ap[r0:r0 + 128, :], reads=[Od[0]], writes=[of])
                fw.dma("sp", ob.ap[:], Od[1].ap[r0:r0 + 128, :], reads=[Od[1]], writes=[ob])
                fw.dma("sp", gs.ap[:], GSd.ap[r0:r0 + 128, :], reads=[GSd], writes=[gs])
                fw.op("pool", lambda e: e.tensor_tensor(out=of.ap[:], in0=of.ap[:], in1=ob.ap[:], op=ALU.add), reads=[of, ob], writes=[of])
                sq = sq_r.next()
                fw.op("act", lambda e: e.activation(out=sq.ap[:], in_=of.ap[:], func=AF.Square), reads=[of], writes=[sq])
                st = st_r.next()
                fw.op("dve", lambda e: e.tensor_reduce(out=st.ap[:], in_=sq.ap[:].rearrange("p (h v) -> p h v", h=16), axis=AX.X, op=ALU.add),
                      reads=[sq], writes=[st])
                fw.op("act", lambda e: e.activation(out=st.ap[:], in_=st.ap[:], func=AF.Ln, scale=1.0 / 128.0, bias=eps128.ap[:, 0:1]), reads=[st, eps128], writes=[st])
                fw.op("act", lambda e: e.activation(out=st.ap[:], in_=st.ap[:], func=AF.Exp, scale=-0.5), reads=[st], writes=[st])
                of3 = of.ap[:].rearrange("p (h v) -> p h v", h=16)
                fw.op("dve", lambda e: e.tensor_tensor(out=of3, in0=of3, in1=st.ap[:].unsqueeze(2).to_broadcast([128, 16, 128]), op=ALU.mult),
                      reads=[of, st], writes=[of])
                fw.op("pool", lambda e: e.tensor_tensor(out=of3, in0=of3, in1=nwt.ap[:].unsqueeze(1).to_broadcast([128, 16, 128]), op=ALU.mult),
                      reads=[of, nwt], writes=[of])
                y = y_r.next()
                fw.op("dve", lambda e: e.tensor_tensor(out=y.ap[:], in0=of.ap[:], in1=gs.ap[:], op=ALU.mult), reads=[of, gs], writes=[y])
                for half in range(2):
                    pt = ptr.next()
                    for q in range(8):
                        kc = half * 8 + q
                        fw.op("pe", lambda e: e.transpose(pt.ap[:, q * 128:(q + 1) * 128], y.ap[:, kc * 128:(kc + 1) * 128], self.identb.ap[:]),
                              reads=[y, self.identb], writes=[pt])
                    fw.op("act", lambda e: e.copy(out=ot.ap[:, half * 8:(half + 1) * 8, s_ * 128:(s_ + 1) * 128],
                                                  in_=pt.ap[:].rearrange("p (q t) -> p q t", q=8)), reads=[pt], writes=[ot])
            fw.dma("sp", OT.ap[:, b0:b0 + 512].rearrange("(kc p) t -> p kc t", p=128), ot.ap[:], reads=[ot], writes=[OT])
        self.phase_end()
        if self.stop == "hg3":
            return
        self.out_proj(l, OT, self.din["hg_wo"], self.din["hg_wo"].ap[j], 2)

    def rwkv(self, l):
        fw = self.fw
        ja = l // 3
        H, C, LD, LG = 32, 64, 96, 256
        din = self.din
        HTd = self.scratch("rw_HT%d" % l, [D, R + 2], BF16)
        Atd = [self.scratch("rw_At%d_%d" % (l, d), [D, R], BF16) for d in range(2)]
        Ktd = [self.scratch("rw_Kt%d_%d" % (l, d), [D, R], BF16) for d in range(2)]
        Btd = [self.scratch("rw_Bt%d_%d" % (l, d), [D, R], BF16) for d in range(2)]
        Rtd = [self.scratch("rw_Rt%d_%d" % (l, d), [D, R], BF16) for d in range(2)]
        Kmd = [self.scratch("rw_Km%d_%d" % (l, d), [R, D], BF16) for d in range(2)]
        Bmd = [self.scratch("rw_Bm%d_%d" % (l, d), [R, D], BF16) for d in range(2)]
        WCd = [self.scratch("rw_WC%d_%d" % (l, d), [128, KC, R // C]) for d in range(2)]
        Vd = self.scratch("rw_V%d" % l, [R, D], BF16)
        Gd = self.scratch("rw_G%d" % l, [R, D], BF16)
        CSd = self.scratch("rw_CS%d" % l, [R, 64])
        Yd = [self.scratch("rw_Y%d_%d" % (l, d), [R, D]) for d in range(2)]
        OT = self.scratch("rw_OT%d" % l, [D, R], BF16)

        self.phase_begin()
        TB = 256
        self.prep_setup(l, 0)
        hTr = self.sbrot("r0_hT", 2, [128, KC, TB], BF16)
        for (t0, tb) in self.blocks(TB):
            hT = hTr.next()
            self.prep_block(t0, tb, hT)
            fw.dma("sp", HTd.ap[:, 1 + t0:1 + t0 + tb].rearrange("(kc p) t -> p kc t", p=128), hT.ap[:, :, 0:tb], reads=[hT], writes=[HTd])
        self.phase_end()

        self.phase_begin()
        stage = self.sb("r1_stage", [128, 128])
        pst = self.ps("r1_pst", [128, 128])
        cva = self.sb("r1_cva", [128, 128])
        cvb = self.sb("r1_cvb", [128, 128])
        self.colvecs([(din["rw_mu"], din["rw_mu"].ap[ja, i]) for i in range(6)] + [(din["rw_kk"], din["rw_kk"].ap[ja]), (din["rw_ka"], din["rw_ka"].ap[ja])],
                     cva, pst, stage, self.identf)
        rkf = din["rw_rk"].ap.rearrange("j d h n -> j d (h n)")
        self.colvecs([(din["rw_w0"], din["rw_w0"].ap[ja, 0]), (din["rw_w0"], din["rw_w0"].ap[ja, 1]),
                      (din["rw_a0"], din["rw_a0"].ap[ja, 0]), (din["rw_a0"], din["rw_a0"].ap[ja, 1]),
                      (din["rw_rk"], rkf[ja, 0]), (din["rw_rk"], rkf[ja, 1])], cvb, pst, stage, self.identf)
        fw.op("dve", lambda e: e.tensor_scalar(out=cvb.ap[:, 96:112], in0=cva.ap[:, 112:128], scalar1=-1.0, scalar2=1.0, op0=ALU.mult, op1=ALU.add),
              reads=[cva], writes=[cvb])
        cvn = self.sb("r1_cvn", [128, 64])
        fw.op("dve", lambda e: e.tensor_scalar(out=cvn.ap[:], in0=cvb.ap[:, 0:64], scalar1=-1.0, scalar2=None, op0=ALU.mult), reads=[cvb], writes=[cvn])
        NW0 = lambda d, kc: cvn.ap[:, d * 16 + kc:d * 16 + kc + 1]
        NA0 = lambda d, kc: cvn.ap[:, 32 + d * 16 + kc:33 + d * 16 + kc]
        CDEC = math.exp(-0.5)
        MU = lambda i, kc: cva.ap[:, i * 16 + kc:i * 16 + kc + 1]
        KKW = lambda kc: cva.ap[:, 96 + kc:97 + kc]
        KA = lambda kc: cva.ap[:, 112 + kc:113 + kc]
        W0 = lambda d, kc: cvb.ap[:, d * 16 + kc:d * 16 + kc + 1]
        A0 = lambda d, kc: cvb.ap[:, 32 + d * 16 + kc:33 + d * 16 + kc]
        RK = lambda d, kc: cvb.ap[:, 64 + d * 16 + kc:65 + d * 16 + kc]
        OMKA = lambda kc: cvb.ap[:, 96 + kc:97 + kc]
        W1 = self.sb("r1_W1", [128, 2, KC, LD], BF16)
        A1 = self.sb("r1_A1", [128, 2, KC, LD], BF16)
        W2 = self.sb("r1_W2", [LD, 2, D], BF16)
        A2 = self.sb("r1_A2", [LD, 2, D], BF16)
        G1 = self.sb("r1_G1", [128, KC, LG], BF16)
        G2 = self.sb("r1_G2", [128, 2, D], BF16)
        for d in range(2):
            fw.dma("pool", W1.ap[:, d], din["rw_w1"].ap[ja, d].rearrange("(kc p) n -> p kc n", p=128), reads=[din["rw_w1"]], writes=[W1])
            fw.dma("pool", A1.ap[:, d], din["rw_a1"].ap[ja, d].rearrange("(kc p) n -> p kc n", p=128), reads=[din["rw_a1"]], writes=[A1])
            fw.dma("pool", W2.ap[:, d], din["rw_w2"].ap[ja, d], reads=[din["rw_w2"]], writes=[W2])
            fw.dma("pool", A2.ap[:, d], din["rw_a2"].ap[ja, d], reads=[din["rw_a2"]], writes=[A2])
        fw.dma("pool", G1.ap[:], din["rw_g1"].ap[ja].rearrange("(kc p) n -> p kc n", p=128), reads=[din["rw_g1"]], writes=[G1])
        fw.dma("pool", G2.ap[:], din["rw_g2"].ap[ja].rearrange("(kc p) n -> p kc n", p=128), reads=[din["rw_g2"]], writes=[G2])
        bones = self.sb("r1_bones", [128, 128], BF16)
        fw.op("pool", lambda e: e.memset(bones.ap[:], 0.0), writes=[bones])
        fw.op("pool", lambda e: e.memset(bones.ap[0:64, 0:64], 1.0), reads=[bones], writes=[bones])
        fw.op("pool", lambda e: e.memset(bones.ap[64:128, 64:128], 1.0), reads=[bones], writes=[bones])
        Eh = self.sb("r1_E", [128, KC, 32], BF16)
        fw.op("pool", lambda e: e.memset(Eh.ap[:], 0.0), writes=[Eh])
        for kc in range(KC):
            fw.op("pool", lambda e: e.memset(Eh.ap[0:64, kc, 2 * kc:2 * kc + 1], 1.0), reads=[Eh], writes=[Eh])
            fw.op("pool", lambda e: e.memset(Eh.ap[64:128, kc, 2 * kc + 1:2 * kc + 2], 1.0), reads=[Eh], writes=[Eh])
        msk = self.sb("r1_rst", [128, TB])
        fw.op("pool", lambda e: e.memset(msk.ap[:], 1.0), writes=[msk])
        fw.op("pool", lambda e: e.memset(msk.ap[:].rearrange("p (c t) -> p c t", t=C)[:, :, 0:1], 0.0), reads=[msk], writes=[msk])
        e12 = self.sb("r1_e12", [128, 1])
        fw.op("pool", lambda e: e.memset(e12.ap[:], 1e-12), writes=[e12])

        hTh = self.sb("r1_hTh", [128, KC, TB + 2], BF16)
        xx = self.sb("r1_xx", [128, KC, TB], BF16)
        xmr = self.sbrot("r1_xm", 2, [128, KC, TB], BF16)
        wrot = self.sbrot("r1_w", 2, [128, KC * 512], BF16)
        NCH = TB // C
        r_f = self.sb("r1_r", [128, KC, TB], BF16)
        k_f = self.sb("r1_k", [128, KC, TB], BF16)
        kkn = self.sb("r1_kkn", [128, KC, TB], BF16)
        a_r = self.sbrot("r1_a", 4, [128, TB])
        sg_r = self.sbrot("r1_sg", 4, [128, TB])
        whT = self.sb("r1_wh", [LD, 4, TB], BF16)
        ghT = self.sb("r1_gh", [128, 2, TB], BF16)
        vtm = self.sbrot("r1_vtm", 2, [128, D], BF16)
        pp = self.psrot("r1_pp", 4, [128, 512])
        ptb = self.psrot("r1_ptb", 2, [128, 1024], BF16)
        tr = self.sbrot("r1_t", 24, [128, TB])
        ob = self.sbrot("r1_ob", 20, [128, TB], BF16)
        Kmr = self.sbrot("r1_Kmt", 4, [128, 2, TB // 128, 128], BF16)
        WCt = self.sb("r1_WCt", [128, 2, KC, NCH])
        Zr = self.sbrot("r1_Z", 4, [128, TB], BF16)
        pcs = self.ps("r1_pcs", [128, 2 * 64])
        cst = self.sbrot("r1_cs", 2, [128, 2 * 64])
        blocks = self.blocks(TB)

        def ld_sq(Wb, cb):
            def f(buf):
                dst = buf.ap[:].rearrange("p (kc n) -> p kc n", kc=KC)
                fw.dma("pool", dst, Wb.ap[ja].rearrange("(kc p) n -> p kc n", p=128)[:, :, cb * 512:(cb + 1) * 512], reads=[Wb], writes=[buf])
                return dst
            return f
        loaders = []
        for _ in blocks:
            for nm in ("rw_wr", "rw_wk", "rw_wv"):
                loaders += [ld_sq(din[nm], cb) for cb in range(4)]
        ws = WStream(wrot, loaders)

        def mix(i):
            xm = xmr.next()
            for kc in range(KC):
                fw.op("dve", lambda e: e.scalar_tensor_tensor(out=xm.ap[:, kc, 0:tb], in0=xx.ap[:, kc, 0:tb], scalar=MU(i, kc),
                                                            in1=hTh.ap[:, kc, 1:1 + tb], op0=ALU.mult, op1=ALU.add),
                      reads=[xx, hTh, cva], writes=[xm])
            return xm

        for (t0, tb) in blocks:
            seq0 = 0 if t0 < NS else (NS if t0 < NS + NP else NS + NP)
            seq1 = NS if t0 < NS else (NS + NP if t0 < NS + NP else R)
            lo = t0 - 1 if t0 > seq0 else t0
            hi = t0 + tb + 1 if t0 + tb < seq1 else t0 + tb
            fw.dma("sp", hTh.ap[:, :, (lo - t0 + 1):(hi - t0 + 1)], HTd.ap[:, lo + 1:hi + 1].rearrange("(kc p) t -> p kc t", p=128),
                   reads=[HTd], writes=[hTh])
            if lo == t0:
                fw.op("pool", lambda e: e.memset(hTh.ap[:, :, 0:1], 0.0), reads=[hTh], writes=[hTh])
            if hi == t0 + tb:
                fw.op("pool", lambda e: e.memset(hTh.ap[:, :, tb + 1:tb + 2], 0.0), reads=[hTh], writes=[hTh])
            fw.op("dve", lambda e: e.tensor_tensor(out=xx.ap[:, :, 0:tb], in0=hTh.ap[:, :, 0:tb], in1=hTh.ap[:, :, 2:2 + tb], op=ALU.add),
                  reads=[hTh], writes=[xx])
            fw.op("dve", lambda e: e.scalar_tensor_tensor(out=xx.ap[:, :, 0:tb], in0=xx.ap[:, :, 0:tb], scalar=0.5, in1=hTh.ap[:, :, 1:1 + tb],
                                                          op0=ALU.mult, op1=ALU.subtract), reads=[xx, hTh], writes=[xx])
            nch = tb // C
            ch0 = t0 // C
            for (mi, dstf) in ((0, r_f), (2, k_f)):
                xm = mix(mi)
                for cb in range(4):
                    w, wv = ws.next()
                    for jj in range(4):
                        fc = cb * 4 + jj
                        p = pp.next()
                        for kc in range(KC):
                            fw.op("pe", lambda e: e.matmul(p.ap[:, 0:tb], lhsT=wv[:, kc, jj * 128:(jj + 1) * 128], rhs=xm.ap[:, kc, 0:tb],
                                                           start=(kc == 0), stop=(kc == KC - 1)), reads=[w, xm], writes=[p])
                        fw.op("act", lambda e: e.copy(out=dstf.ap[:, fc, 0:tb], in_=p.ap[:, 0:tb]), reads=[p], writes=[dstf])
            xm = mix(3)
            vts = [vtm.next() for _ in range(tb // 128)]
            for cb in range(4):
                w, wv = ws.next()
                for s_ in range(tb // 128):
                    p = pp.next()
                    for kc in range(KC):
                        fw.op("pe", lambda e: e.matmul(p.ap[:], lhsT=xm.ap[:, kc, s_ * 128:(s_ + 1) * 128], rhs=wv[:, kc, :],
                                                       start=(kc == 0), stop=(kc == KC - 1)), reads=[w, xm], writes=[p])
                    fw.op("act", lambda e: e.copy(out=vts[s_].ap[:, cb * 512:(cb + 1) * 512], in_=p.ap[:]), reads=[p], writes=[vts[s_]])
            for s_ in range(tb // 128):
                fw.dma("sp", Vd.ap[t0 + s_ * 128:t0 + (s_ + 1) * 128, :], vts[s_].ap[:], reads=[vts[s_]], writes=[Vd])
            xm = mix(5)
            for c2 in range(2):
                p = pp.next()
                for kc in range(KC):
                    fw.op("pe", lambda e: e.matmul(p.ap[:, 0:tb], lhsT=G1.ap[:, kc, c2 * 128:(c2 + 1) * 128], rhs=xm.ap[:, kc, 0:tb],
                                                   start=(kc == 0), stop=(kc == KC - 1)), reads=[G1, xm], writes=[p])
                fw.op("act", lambda e: e.activation(out=ghT.ap[:, c2, 0:tb], in_=p.ap[:, 0:tb], func=AF.Sigmoid), reads=[p], writes=[ghT])
            gts = [vtm.next() for _ in range(tb // 128)]
            for s_ in range(tb // 128):
                for cb in range(4):
                    p = pp.next()
                    for c2 in range(2):
                        fw.op("pe", lambda e: e.matmul(p.ap[:], lhsT=ghT.ap[:, c2, s_ * 128:(s_ + 1) * 128], rhs=G2.ap[:, c2, cb * 512:(cb + 1) * 512],
                                                       start=(c2 == 0), stop=(c2 == 1)), reads=[ghT, G2], writes=[p])
                    fw.op("act", lambda e: e.copy(out=gts[s_].ap[:, cb * 512:(cb + 1) * 512], in_=p.ap[:]), reads=[p], writes=[gts[s_]])
                fw.dma("sp", Gd.ap[t0 + s_ * 128:t0 + (s_ + 1) * 128, :], gts[s_].ap[:], reads=[gts[s_]], writes=[Gd])
            for (mi, Wl, off, fn) in ((1, W1, 0, AF.Tanh), (4, A1, 2, AF.Copy)):
                xm = mix(mi)
                for d in range(2):
                    p = pp.next()
                    for kc in range(KC):
                        fw.op("pe", lambda e: e.matmul(p.ap[0:LD, 0:tb], lhsT=Wl.ap[:, d, kc, :], rhs=xm.ap[:, kc, 0:tb],
                                                       start=(kc == 0), stop=(kc == KC - 1)), reads=[Wl, xm], writes=[p])
                    fw.op("act", lambda e: e.activation(out=whT.ap[:, off + d, 0:tb], in_=p.ap[0:LD, 0:tb], func=fn), reads=[p], writes=[whT])
            for kc in range(KC):
                t_sq = ob.next()
                fw.op("dve", lambda e: e.tensor_scalar(out=kkn.ap[:, kc, 0:tb], in0=k_f.ap[:, kc, 0:tb], scalar1=KKW(kc), scalar2=None, op0=ALU.mult),
                      reads=[k_f, cva], writes=[kkn])
                fw.op("pool", lambda e: e.tensor_tensor(out=t_sq.ap[:, 0:tb], in0=kkn.ap[:, kc, 0:tb], in1=kkn.ap[:, kc, 0:tb], op=ALU.mult),
                      reads=[kkn], writes=[t_sq])
                p = pp.next()
                fw.op("pe", lambda e: e.matmul(p.ap[:, 0:tb], lhsT=bones.ap[:], rhs=t_sq.ap[:, 0:tb], start=True, stop=True), reads=[bones, t_sq], writes=[p])
                rn = tr.next()
                fw.op("act", lambda e: e.activation(out=rn.ap[:, 0:tb], in_=p.ap[:, 0:tb], func=AF.Ln, bias=e12.ap[:, 0:1]), reads=[p, e12], writes=[rn])
                fw.op("act", lambda e: e.activation(out=rn.ap[:, 0:tb], in_=rn.ap[:, 0:tb], func=AF.Exp, scale=-0.5), reads=[rn], writes=[rn])
                fw.op("dve", lambda e: e.tensor_tensor(out=kkn.ap[:, kc, 0:tb], in0=kkn.ap[:, kc, 0:tb], in1=rn.ap[:, 0:tb], op=ALU.mult),
                      reads=[kkn, rn], writes=[kkn])
            def head(d, kc):
                sgt, at_ = sg_r.next(), a_r.next()
                p = pp.next()
                fw.op("pe", lambda e: e.matmul(p.ap[:, 0:tb], lhsT=W2.ap[:, d, kc * 128:(kc + 1) * 128], rhs=whT.ap[:, d, 0:tb], start=True, stop=True),
                      reads=[W2, whT], writes=[p])
                fw.op("act", lambda e: e.activation(out=sgt.ap[:, 0:tb], in_=p.ap[:, 0:tb], func=AF.Sigmoid, bias=W0(d, kc)),
                      reads=[p, cvb], writes=[sgt])
                p = pp.next()
                fw.op("pe", lambda e: e.matmul(p.ap[:, 0:tb], lhsT=A2.ap[:, d, kc * 128:(kc + 1) * 128], rhs=whT.ap[:, 2 + d, 0:tb], start=True, stop=True),
                      reads=[A2, whT], writes=[p])
                fw.op("act", lambda e: e.activation(out=at_.ap[:, 0:tb], in_=p.ap[:, 0:tb], func=AF.Sigmoid, bias=A0(d, kc)),
                      reads=[p, cvb], writes=[at_])
                return sgt, at_

            iters = [(d, kc) for d in range(2) for kc in range(KC)]
            pending = head(*iters[0])
            for it_i, (d, kc) in enumerate(iters):
                if True:
                    sgt, at_ = pending
                    if it_i + 1 < len(iters):
                        pending = head(*iters[it_i + 1])
                    a_ = at_.ap[:, 0:tb]
                    a_f = at_
                    F, L, Lex = tr.next(), tr.next(), tr.next()
                    lw = sgt
                    fw.op("dve", lambda e: e.tensor_tensor_scan(out=F.ap[:, 0:tb], data0=msk.ap[:, 0:tb], data1=lw.ap[:, 0:tb], initial=0.0,
                                                                op0=ALU.mult, op1=ALU.add), reads=[msk, lw], writes=[F])
                    F3 = F.ap[:, 0:tb].rearrange("p (c t) -> p c t", t=C)
                    tot = F3[:, :, C - 1:C]
                    if d == 0:
                        Lt = F
                        fw.op("pool", lambda e: e.tensor_tensor(out=Lex.ap[:, 0:tb], in0=F.ap[:, 0:tb], in1=lw.ap[:, 0:tb], op=ALU.subtract),
                              reads=[F, lw], writes=[Lex])
                    else:
                        fw.op("dve", lambda e: e.tensor_tensor(out=Lex.ap[:, 0:tb].rearrange("p (c t) -> p c t", t=C), in0=tot.to_broadcast([128, nch, C]),
                                                               in1=F3, op=ALU.subtract), reads=[F], writes=[Lex])
                        fw.op("pool", lambda e: e.tensor_tensor(out=L.ap[:, 0:tb], in0=Lex.ap[:, 0:tb], in1=lw.ap[:, 0:tb], op=ALU.add),
                              reads=[Lex, lw], writes=[L])
                        Lt = L
                    fw.op("act", lambda e: e.activation(out=WCt.ap[:, d, kc, 0:nch].unsqueeze(2), in_=tot, func=AF.Exp, scale=-CDEC), reads=[F], writes=[WCt])
                    eL, enL, eX = tr.next(), tr.next(), tr.next()
                    fw.op("act", lambda e: e.activation(out=eL.ap[:, 0:tb], in_=Lt.ap[:, 0:tb], func=AF.Exp, scale=-CDEC), reads=[Lt], writes=[eL])
                    fw.op("act", lambda e: e.activation(out=enL.ap[:, 0:tb], in_=Lt.ap[:, 0:tb], func=AF.Exp, scale=CDEC), reads=[Lt], writes=[enL])
                    fw.op("act", lambda e: e.activation(out=eX.ap[:, 0:tb], in_=Lex.ap[:, 0:tb], func=AF.Exp, scale=-CDEC), reads=[Lex], writes=[eX])
                    kd, bp = tr.next(), tr.next()
                    fw.op("dve", lambda e: e.tensor_scalar(out=kd.ap[:, 0:tb], in0=a_, scalar1=KA(kc), scalar2=OMKA(kc), op0=ALU.mult, op1=ALU.add),
                          reads=[a_f, cva, cvb], writes=[kd])
                    fw.op("dve", lambda e: e.tensor_tensor(out=kd.ap[:, 0:tb], in0=kd.ap[:, 0:tb], in1=k_f.ap[:, kc, 0:tb], op=ALU.mult),
                          reads=[kd, k_f], writes=[kd])
                    fw.op("dve", lambda e: e.tensor_tensor(out=bp.ap[:, 0:tb], in0=kkn.ap[:, kc, 0:tb], in1=a_, op=ALU.mult), reads=[kkn, a_f], writes=[bp])
                    at, kt, bt, rt = ob.next(), ob.next(), ob.next(), ob.next()
                    fw.op("dve", lambda e: e.tensor_tensor(out=at.ap[:, 0:tb], in0=kkn.ap[:, kc, 0:tb], in1=eX.ap[:, 0:tb], op=ALU.mult), reads=[kkn, eX], writes=[at])
                    fw.op("dve", lambda e: e.tensor_tensor(out=kt.ap[:, 0:tb], in0=kd.ap[:, 0:tb], in1=enL.ap[:, 0:tb], op=ALU.mult), reads=[kd, enL], writes=[kt])
                    fw.op("pool", lambda e: e.tensor_tensor(out=bt.ap[:, 0:tb], in0=bp.ap[:, 0:tb], in1=enL.ap[:, 0:tb], op=ALU.mult), reads=[bp, enL], writes=[bt])
                    fw.op("dve", lambda e: e.tensor_tensor(out=rt.ap[:, 0:tb], in0=r_f.ap[:, kc, 0:tb], in1=eL.ap[:, 0:tb], op=ALU.mult), reads=[r_f, eL], writes=[rt])
                    Zt = Zr.next()
                    fw.op("dve", lambda e: e.scalar_tensor_tensor(out=Zt.ap[:, 0:tb], in0=kd.ap[:, 0:tb], scalar=RK(d, kc), in1=r_f.ap[:, kc, 0:tb],
                                                                  op0=ALU.mult, op1=ALU.mult), reads=[kd, cvb, r_f], writes=[Zt])
                    for s_ in range(tb // 128):
                        fw.op("pe", lambda e: e.matmul(pcs.ap[:, s_ * 64 + d * 32: s_ * 64 + (d + 1) * 32], lhsT=Zt.ap[:, s_ * 128:(s_ + 1) * 128], rhs=Eh.ap[:, kc, :],
                                                       start=(kc == 0 and d == 0 and s_ == 0), stop=(kc == KC - 1 and d == 1 and s_ == tb // 128 - 1),
                                                       skip_group_check=True), reads=[Zt, Eh], writes=[pcs])
                    rows = slice(kc * 128, (kc + 1) * 128)
                    fw.dma("sp", Atd[d].ap[rows, t0:t0 + tb], at.ap[:, 0:tb], reads=[at], writes=[Atd[d]])
                    fw.dma("sp", Ktd[d].ap[rows, t0:t0 + tb], kt.ap[:, 0:tb], reads=[kt], writes=[Ktd[d]])
                    fw.dma("sp", Btd[d].ap[rows, t0:t0 + tb], bt.ap[:, 0:tb], reads=[bt], writes=[Btd[d]])
                    fw.dma("sp", Rtd[d].ap[rows, t0:t0 + tb], rt.ap[:, 0:tb], reads=[rt], writes=[Rtd[d]])
                    pt = ptb.next()
                    for qi, src in enumerate((kt, bt)):
                        for s_ in range(tb // 128):
                            fw.op("pe", lambda e: e.transpose(pt.ap[:, (qi * 2 + s_) * 128:(qi * 2 + s_ + 1) * 128], src.ap[:, s_ * 128:(s_ + 1) * 128],
                                                              self.identb.ap[:]), reads=[src, self.identb], writes=[pt])
                    Kmt = Kmr.next()
                    ptv = pt.ap[:, 0:512].rearrange("p (q s k) -> p q s k", q=2, k=128)
                    fw.op("act", lambda e: e.copy(out=Kmt.ap[:, 0, 0:tb // 128, :], in_=ptv[:, 0, 0:tb // 128, :]), reads=[pt], writes=[Kmt])
                    fw.op("act", lambda e: e.mul(out=Kmt.ap[:, 1, 0:tb // 128, :], in_=ptv[:, 1, 0:tb // 128, :], mul=-1.0), reads=[pt], writes=[Kmt])
                    for qi, dst in enumerate((Kmd[d], Bmd[d])):
                        fw.dma("sp", dst.ap[t0:t0 + tb, kc * 128:(kc + 1) * 128].rearrange("(s p) f -> p s f", p=128), Kmt.ap[:, qi, 0:tb // 128, :],
                               reads=[Kmt], writes=[dst])
            for d in range(2):
                fw.dma("sp", WCd[d].ap[:, :, ch0:ch0 + nch], WCt.ap[:, d, :, 0:nch], reads=[WCt], writes=[WCd[d]])
            cs = cst.next()
            fw.op("act", lambda e: e.copy(out=cs.ap[:], in_=pcs.ap[:]), reads=[pcs], writes=[cs])
            for s_ in range(tb // 128):
                fw.dma("sp", CSd.ap[t0 + s_ * 128:t0 + (s_ + 1) * 128, :], cs.ap[:, s_ * 64:(s_ + 1) * 64], reads=[cs], writes=[CSd])
        self.phase_end()
        if self.stop == "rw1":
            return
        self.rwkv_scan(l, ja, Atd, Ktd, Btd, Rtd, Kmd, Bmd, WCd, Vd, Yd)
        if self.stop in ("rw2", "rs_a", "rs_b", "rs_0"):
            return
        self.rwkv_out(l, ja, Yd, Vd, Gd, CSd, OT)

    def rwkv_scan(self, l, ja, Atd, Ktd, Btd, Rtd, Kmd, Bmd, WCd, Vd, Yd):
        fw = self.fw
        din = self.din
        C = 64
        self.phase_begin()
        mk = self.sb("rs_mask", [128, 8, 128])
        fw.dma("sp", mk.ap[:], din["rw_mask"].ap.rearrange("m i t -> i m t"), reads=[din["rw_mask"]], writes=[mk])
        S = [self.sb("rs_S%d" % d, [128, KC, 64]) for d in range(2)]
        Sb = [self.sb("rs_Sb%d" % d, [128, KC, 64], BF16) for d in range(2)]
        fm_r = [self.sbrot("rs_fm%d" % i, 2, [128, 8, 128], BF16) for i in range(4)]
        tm_r = [self.sbrot("rs_tm%d" % i, 2, [128, 1024], BF16) for i in range(3)]
        wc_r = self.sbrot("rs_wc", 2, [128, 8, 2])
        gr = [self.sbrot("rs_g%d" % i, 2, [128, 16, 128], BF16) for i in range(5)]
        Pr = [self.sbrot("rs_P%d" % i, 2, [128, 16, 128], BF16) for i in range(2)]
        QTr = self.sbrot("rs_QT", 2, [128, 16, 128], BF16)
        Tt = self.sbrot("rs_T", 2, [128, 16, 128], BF16)
        Xb = self.sb("rs_Xb", [128, 1024], BF16)
        Ub = self.sb("rs_Ub", [128, 1024], BF16)
        tmp = self.sb("rs_tmp", [128, 4, 64])
        yt_r = self.sbrot("rs_y", 2, [128, 1024])
        stg = self.sb("rs_stg", [64, 2, 64])
        stg2 = self.sb("rs_stg2", [64, 128])
        pg = self.psrot("rs_pg", 3, [128, 512])
        px = self.ps("rs_px", [128, 512])
        pu = self.ps("rs_pu", [128, 512])
        pss = self.ps("rs_ps", [128, 512])
        py = [self.ps("rs_py%d" % i, [128, 512]) for i in range(2)]
        idb4 = self.identb.ap[:].unsqueeze(1).to_broadcast([128, 4, 128])
        sti, st_out = din["state_rwkv"], self.dout["st_rwkv"]

        def v4(t, g):
            return t.ap[:, g * 4:(g + 1) * 4, :]

        def p4(p):
            return p.ap[:].rearrange("p (h t) -> p h t", h=4)

        def tile_step(d, r0, half):
            At, Kt, Bt, Rt = [r.next() for r in fm_r]
            for t, src in ((At, Atd[d]), (Kt, Ktd[d]), (Bt, Btd[d]), (Rt, Rtd[d])):
                fw.dma("sp", t.ap[:], src.ap[half * 1024:(half + 1) * 1024, r0:r0 + 128].rearrange("(q p) t -> p q t", p=128), reads=[src], writes=[t])
            Km, Bm, V = [r.next() for r in tm_r]
            for t, src in ((Km, Kmd[d]), (Bm, Bmd[d]), (V, Vd)):
                fw.dma("sp", t.ap[:], src.ap[r0:r0 + 128, half * 1024:(half + 1) * 1024], reads=[src], writes=[t])
            wc = wc_r.next()
            c0 = r0 // C
            fw.dma("sp", wc.ap[:], WCd[d].ap[:, half * 8:(half + 1) * 8, c0:c0 + 2], reads=[WCd[d]], writes=[wc])
            N_, NT_, Nak, Mrk, MrbN = [r.next() for r in gr]
            T = Tt.next()
            for g in range(4):
                specs = ((Bt, At, N_, 0), (At, Bt, NT_, 1), (Kt, At, Nak, 0), (Kt, Rt, Mrk, 2), (Bt, Rt, MrbN, 3))
                for (L_, R_, dst, mi) in specs:
                    p = pg.next()
                    for hh in (0, 2, 1, 3):
                        h16 = g * 4 + hh
                        q, hp = h16 // 2, (h16 % 2) * 64
                        fw.op("pe", lambda e: e.matmul(p.ap[:, hh * 128:(hh + 1) * 128], lhsT=L_.ap[hp:hp + 64, q, :], rhs=R_.ap[hp:hp + 64, q, :],
                                                       start=True, stop=True), reads=[L_, R_], writes=[p], rt=hp)
                    fw.op("dve", lambda e: e.tensor_tensor(out=v4(dst, g), in0=p4(p), in1=mk.ap[:, d * 4 + mi, :].unsqueeze(1).to_broadcast([128, 4, 128]),
                                                           op=ALU.mult), reads=[p, mk], writes=[dst])
                fw.op("pool", lambda e: e.tensor_tensor(out=v4(T, g), in0=idb4, in1=v4(N_, g), op=ALU.subtract), reads=[self.identb, N_], writes=[T])
            if self.stop == "rs_a":
                return
            Pp, PTp = N_, NT_
            for lev in range(1, 6):
                Pc, PTc = Pr[0].next(), Pr[1].next()
                QT = QTr.next()
                for g in range(4):
                    if lev < 5:
                        p = pg.next()
                        for hh in range(4):
                            h16 = g * 4 + hh
                            fw.op("pe", lambda e: e.matmul(p.ap[:, hh * 128:(hh + 1) * 128], lhsT=PTp.ap[:, h16, :], rhs=Pp.ap[:, h16, :], start=True, stop=True),
                                  reads=[PTp, Pp], writes=[p])
                        fw.op("act", lambda e: e.copy(out=v4(Pc, g), in_=p4(p)), reads=[p], writes=[Pc])
                    p = pg.next()
                    for hh in range(4):
                        h16 = g * 4 + hh
                        fw.op("pe", lambda e: e.matmul(p.ap[:, hh * 128:(hh + 1) * 128], lhsT=Pp.ap[:, h16, :], rhs=PTp.ap[:, h16, :], start=True, stop=True),
                              reads=[PTp, Pp], writes=[p])
                    fw.op("dve", lambda e: e.tensor_copy(out=v4(PTc, g), in_=p4(p)), reads=[p], writes=[PTc])
                    fw.op("pool", lambda e: e.tensor_tensor(out=v4(QT, g), in0=v4(PTc, g), in1=idb4, op=ALU.add), reads=[PTc, self.identb], writes=[QT])
                Tn = Tt.next()
                for g in range(4):
                    p = pg.next()
                    for hh in range(4):
                        h16 = g * 4 + hh
                        fw.op("pe", lambda e: e.matmul(p.ap[:, hh * 128:(hh + 1) * 128], lhsT=QT.ap[:, h16, :], rhs=T.ap[:, h16, :], start=True, stop=True),
                              reads=[QT, T], writes=[p])
                    fw.op("act", lambda e: e.copy(out=v4(Tn, g), in_=p4(p)), reads=[p], writes=[Tn])
                T = Tn
                Pp, PTp = Pc, PTc
            if self.stop == "rs_b":
                return
            yt = yt_r.next()
            order = [0, 1] if d == 0 else [1, 0]
            for ci, c in enumerate(order):
                cr = slice(c * 64, (c + 1) * 64)
                for g8 in range(2):
                    hs = [(g8 * 8 + hh, (g8 * 8 + hh) // 2, ((g8 * 8 + hh) % 2) * 64) for hh in range(8)]
                    hso = [x for x in enumerate(hs) if x[1][2] != c * 64] + [x for x in enumerate(hs) if x[1][2] == c * 64]
                    for i_, (hh, (h16, q, hp)) in enumerate(hso):
                        fw.op("pe", lambda e: e.matmul(px.ap[cr, hh * 64:(hh + 1) * 64], lhsT=At.ap[hp:hp + 64, q, cr], rhs=Sb[d].ap[hp:hp + 64, half * 8 + q, :],
                                                       start=(i_ == 0), stop=False, skip_group_check=True), reads=[At, Sb[d]], writes=[px], rt=hp)
                    for hh, (h16, q, hp) in enumerate(hs):
                        fw.op("pe", lambda e: e.matmul(px.ap[cr, hh * 64:(hh + 1) * 64], lhsT=Nak.ap[cr, h16, cr], rhs=V.ap[cr, h16 * 64:(h16 + 1) * 64],
                                                       start=False, stop=True, skip_group_check=True), reads=[Nak, V], writes=[px], rt=c * 64)
                    fw.op("act", lambda e: e.copy(out=Xb.ap[cr, g8 * 512:(g8 + 1) * 512], in_=px.ap[cr, :]), reads=[px], writes=[Xb])
                    for hh, (h16, q, hp) in enumerate(hs):
                        fw.op("pe", lambda e: e.matmul(pu.ap[cr, hh * 64:(hh + 1) * 64], lhsT=T.ap[cr, h16, cr], rhs=Xb.ap[cr, h16 * 64:(h16 + 1) * 64],
                                                       start=(hh == 0), stop=True, skip_group_check=True), reads=[T, Xb], writes=[pu], rt=c * 64)
                    fw.op("dve", lambda e: e.tensor_copy(out=Ub.ap[cr, g8 * 512:(g8 + 1) * 512], in_=pu.ap[cr, :]), reads=[pu], writes=[Ub])
                    for i_, (hh, (h16, q, hp)) in enumerate(hso):
                        fw.op("pe", lambda e: e.matmul(py[g8].ap[cr, hh * 64:(hh + 1) * 64], lhsT=Rt.ap[hp:hp + 64, q, cr], rhs=Sb[d].ap[hp:hp + 64, half * 8 + q, :],
                                                       start=(i_ == 0), stop=False, skip_group_check=True), reads=[Rt, Sb[d]], writes=[py[g8]], rt=hp)
                    for hh, (h16, q, hp) in enumerate(hs):
                        fw.op("pe", lambda e: e.matmul(py[g8].ap[cr, hh * 64:(hh + 1) * 64], lhsT=Mrk.ap[cr, h16, cr], rhs=V.ap[cr, h16 * 64:(h16 + 1) * 64],
                                                       start=False, stop=False, skip_group_check=True), reads=[Mrk, V], writes=[py[g8]], rt=c * 64)
                        fw.op("pe", lambda e: e.matmul(py[g8].ap[cr, hh * 64:(hh + 1) * 64], lhsT=MrbN.ap[cr, h16, cr], rhs=Ub.ap[cr, h16 * 64:(h16 + 1) * 64],
                                                       start=False, stop=True, skip_group_check=True), reads=[MrbN, Ub], writes=[py[g8]], rt=c * 64)
                    for q4 in range(4):
                        q = g8 * 4 + q4
                        fw.op("pe", lambda e: e.matmul(pss.ap[:, q4 * 128:(q4 + 1) * 128], lhsT=Km.ap[cr, q * 128:(q + 1) * 128], rhs=V.ap[cr, q * 128:(q + 1) * 128],
                                                       start=(q4 == 0), stop=False, skip_group_check=True), reads=[Km, V], writes=[pss], rt=c * 64)
                        fw.op("pe", lambda e: e.matmul(pss.ap[:, q4 * 128:(q4 + 1) * 128], lhsT=Bm.ap[cr, q * 128:(q + 1) * 128], rhs=Ub.ap[cr, q * 128:(q + 1) * 128],
                                                       start=False, stop=True, skip_group_check=True), reads=[Bm, Ub], writes=[pss], rt=c * 64)
                    for hp in (0, 64):
                        hr = slice(hp, hp + 64)
                        pdiag = pss.ap[hr, :].rearrange("p (q x) -> p q x", q=4)[:, :, hp:hp + 64]
                        wcb = wc.ap[hr, g8 * 4:(g8 + 1) * 4, c:c + 1].to_broadcast([64, 4, 64])
                        Sv = S[d].ap[hr, half * 8 + g8 * 4: half * 8 + (g8 + 1) * 4, :]
                        fw.op("dve", lambda e: e.tensor_tensor(out=tmp.ap[hr], in0=pdiag, in1=wcb, op=ALU.mult), reads=[pss, wc], writes=[tmp])
                        fw.op("pool", lambda e: e.tensor_tensor(out=Sv, in0=Sv, in1=wcb, op=ALU.mult), reads=[S[d], wc], writes=[S[d]])
                        fw.op("pool", lambda e: e.tensor_tensor(out=Sv, in0=Sv, in1=tmp.ap[hr], op=ALU.add), reads=[S[d], tmp], writes=[S[d]])
                    fw.op("act", lambda e: e.copy(out=Sb[d].ap[:, half * 8 + g8 * 4: half * 8 + (g8 + 1) * 4, :],
                                                  in_=S[d].ap[:, half * 8 + g8 * 4: half * 8 + (g8 + 1) * 4, :]), reads=[S[d]], writes=[Sb[d]])
            for g8 in range(2):
                if g8 == 0:
                    fw.op("act", lambda e: e.copy(out=yt.ap[:, g8 * 512:(g8 + 1) * 512], in_=py[g8].ap[:]), reads=[py[g8]], writes=[yt])
                else:
                    fw.op("dve", lambda e: e.tensor_copy(out=yt.ap[:, g8 * 512:(g8 + 1) * 512], in_=py[g8].ap[:]), reads=[py[g8]], writes=[yt])
            fw.dma("sp", Yd[d].ap[r0:r0 + 128, half * 1024:(half + 1) * 1024], yt.ap[:], reads=[yt], writes=[Yd[d]])

        seqs = [(0, NS, None), (NS, NP, 0), (NS + NP, NP, 1)]
        if self.stop in ("rs_a", "rs_b", "rs_0", "rs_c"):
            seqs = seqs[0:2]
        for (row0, T_, pidx) in seqs:
            if self.stop in ("rs_a", "rs_b", "rs_0", "rs_c"):
                T_ = 256
            for d in range(2):
                if pidx is None:
                    for q in range(KC):
                        fw.dma("sp", stg.ap[:], sti.ap[ja, d, 2 * q:2 * q + 2].rearrange("hh v k -> v hh k"), reads=[sti], writes=[stg])
                        p = pg.next()
                        fw.op("pe", lambda e: e.transpose(p.ap[:, 0:64], stg.ap[:].rearrange("v hh k -> v (hh k)"), self.identf.ap[0:64, 0:64]),
                              reads=[stg, self.identf], writes=[p])
                        fw.op("act", lambda e: e.copy(out=S[d].ap[:, q, :], in_=p.ap[:, 0:64]), reads=[p], writes=[S[d]])
                else:
                    fw.op("pool", lambda e: e.memset(S[d].ap[:], 0.0), writes=[S[d]])
                fw.op("act", lambda e: e.copy(out=Sb[d].ap[:], in_=S[d].ap[:]), reads=[S[d]], writes=[Sb[d]])
            nt = T_ // 128
            for i in range(nt):
                if self.stop == "rs_0":
                    break
                for half in range(2):
                    tile_step(0, row0 + i * 128, half)
                    tile_step(1, row0 + (nt - 1 - i) * 128, half)
            if pidx is not None:
                for d in range(2):
                    for q in range(KC):
                        p = pg.next()
                        fw.op("pe", lambda e: e.transpose(p.ap[0:64, 0:128], S[d].ap[:, q, :], self.identf.ap[:]), reads=[S[d], self.identf], writes=[p])
                        fw.op("act", lambda e: e.copy(out=stg2.ap[:], in_=p.ap[0:64, 0:128]), reads=[p], writes=[stg2])
                        fw.dma("sp", st_out.ap[pidx, ja, d, 2 * q:2 * q + 2].rearrange("hh v k -> v hh k"),
                               stg2.ap[:].rearrange("v (hh k) -> v hh k", hh=2), reads=[stg2], writes=[st_out])
        self.phase_end()

    def rwkv_out(self, l, ja, Yd, Vd, Gd, CSd, OT):
        fw = self.fw
        din = self.din
        self.phase_begin()
        lw_row = self.sb("ro_lw", [128, D])
        lb_row = self.sb("ro_lb", [128, D])
        fw.dma("sp", lw_row.ap[:], din["rw_lnx_w"].ap[ja:ja + 1, :].partition_broadcast(128), reads=[din["rw_lnx_w"]], writes=[lw_row])
        fw.dma("sp", lb_row.ap[:], din["rw_lnx_b"].ap[ja:ja + 1, :].partition_broadcast(128), reads=[din["rw_lnx_b"]], writes=[lb_row])
        y0_r = self.sbrot("ro_y0", 2, [128, D])
        y1_r = self.sbrot("ro_y1", 2, [128, D])
        v_r = self.sbrot("ro_v", 2, [128, D], BF16)
        g_r = self.sbrot("ro_g", 2, [128, D], BF16)
        cs_r = self.sbrot("ro_cs", 2, [128, 64])
        st_r = self.sbrot("ro_st", 2, [128, 64])
        o_r = self.sbrot("ro_o", 2, [128, D], BF16)
        OTt = self.sbrot("ro_OTt", 2, [128, KC, 512], BF16)
        ptr = self.psrot("ro_pt", 2, [128, 1024], BF16)
        epsl = self.sb("ro_eps", [128, 1])
        fw.op("pool", lambda e: e.memset(epsl.ap[:], 64e-5), writes=[epsl])
        h3 = lambda ap_: ap_.rearrange("p (h n) -> p h n", h=32)
        for b0 in range(0, R, 512):
            ot = OTt.next()
            for s_ in range(4):
                r0 = b0 + s_ * 128
                y0, y1, v, g, cs, st = y0_r.next(), y1_r.next(), v_r.next(), g_r.next(), cs_r.next(), st_r.next()
                fw.dma("sp", y0.ap[:], Yd[0].ap[r0:r0 + 128, :], reads=[Yd[0]], writes=[y0])
                fw.dma("sp", y1.ap[:], Yd[1].ap[r0:r0 + 128, :], reads=[Yd[1]], writes=[y1])
                fw.dma("sp", v.ap[:], Vd.ap[r0:r0 + 128, :], reads=[Vd], writes=[v])
                fw.dma("sp", g.ap[:], Gd.ap[r0:r0 + 128, :], reads=[Gd], writes=[g])
                fw.dma("sp", cs.ap[:], CSd.ap[r0:r0 + 128, :], reads=[CSd], writes=[cs])
                fw.op("pool", lambda e: e.tensor_tensor(out=y0.ap[:], in0=y0.ap[:], in1=y1.ap[:], op=ALU.add), reads=[y0, y1], writes=[y0])
                fw.op("dve", lambda e: e.tensor_reduce(out=st.ap[:, 0:32], in_=h3(y0.ap[:]), axis=AX.X, op=ALU.add), reads=[y0], writes=[st])
                fw.op("dve", lambda e: e.tensor_scalar(out=st.ap[:, 0:32], in0=st.ap[:, 0:32], scalar1=1.0 / 64.0, scalar2=None, op0=ALU.mult), reads=[st], writes=[st])
                fw.op("dve", lambda e: e.tensor_tensor(out=h3(y0.ap[:]), in0=h3(y0.ap[:]), in1=st.ap[:, 0:32].unsqueeze(2).to_broadcast([128, 32, 64]),
                                                       op=ALU.subtract), reads=[y0, st], writes=[y0])
                fw.op("act", lambda e: e.activation(out=y1.ap[:], in_=y0.ap[:], func=AF.Square), reads=[y0], writes=[y1])
                fw.op("dve", lambda e: e.tensor_reduce(out=st.ap[:, 32:64], in_=h3(y1.ap[:]), axis=AX.X, op=ALU.add), reads=[y1], writes=[st])
                fw.op("act", lambda e: e.activation(out=st.ap[:, 32:64], in_=st.ap[:, 32:64], func=AF.Ln, scale=1.0 / 64.0, bias=epsl.ap[:, 0:1]), reads=[st, epsl], writes=[st])
                fw.op("act", lambda e: e.activation(out=st.ap[:, 32:64], in_=st.ap[:, 32:64], func=AF.Exp, scale=-0.5), reads=[st], writes=[st])
                fw.op("dve", lambda e: e.tensor_tensor(out=h3(y0.ap[:]), in0=h3(y0.ap[:]), in1=st.ap[:, 32:64].unsqueeze(2).to_broadcast([128, 32, 64]),
                                                       op=ALU.mult), reads=[y0, st], writes=[y0])
                fw.op("pool", lambda e: e.tensor_tensor(out=y0.ap[:], in0=y0.ap[:], in1=lw_row.ap[:], op=ALU.mult), reads=[y0, lw_row], writes=[y0])
                fw.op("pool", lambda e: e.tensor_tensor(out=y0.ap[:], in0=y0.ap[:], in1=lb_row.ap[:], op=ALU.add), reads=[y0, lb_row], writes=[y0])
                fw.op("dve", lambda e: e.tensor_tensor(out=cs.ap[:, 0:32], in0=cs.ap[:, 0:32], in1=cs.ap[:, 32:64], op=ALU.add), reads=[cs], writes=[cs])
                fw.op("dve", lambda e: e.tensor_tensor(out=h3(y1.ap[:]), in0=h3(v.ap[:]), in1=cs.ap[:, 0:32].unsqueeze(2).to_broadcast([128, 32, 64]),
                                                       op=ALU.mult), reads=[v, cs], writes=[y1])
                fw.op("pool", lambda e: e.tensor_tensor(out=y0.ap[:], in0=y0.ap[:], in1=y1.ap[:], op=ALU.add), reads=[y0, y1], writes=[y0])
                o = o_r.next()
                fw.op("dve", lambda e: e.tensor_tensor(out=o.ap[:], in0=y0.ap[:], in1=g.ap[:], op=ALU.mult), reads=[y0, g], writes=[o])
                for hf in range(2):
                    pt = ptr.next()
                    for q in range(8):
                        kc = hf * 8 + q
                        fw.op("pe", lambda e: e.transpose(pt.ap[:, q * 128:(q + 1) * 128], o.ap[:, kc * 128:(kc + 1) * 128], self.identb.ap[:]),
                              reads=[o, self.identb], writes=[pt])
                    fw.op("act", lambda e: e.copy(out=ot.ap[:, hf * 8:(hf + 1) * 8, s_ * 128:(s_ + 1) * 128],
                                                  in_=pt.ap[:].rearrange("p (q t) -> p q t", q=8)), reads=[pt], writes=[ot])
            fw.dma("sp", OT.ap[:, b0:b0 + 512].rearrange("(kc p) t -> p kc t", p=128), ot.ap[:], reads=[ot], writes=[OT])
        self.phase_end()
        if self.stop == "rw3":
            return
        self.out_proj(l, OT, din["rw_wo"], din["rw_wo"].ap[ja], 2)

    def final_norm(self):
        fw = self.fw
        self.phase_begin()
        Y = self.dout["y"]
        fnw = self.din["final_norm_w"]
        wrow = self.sb("fn_w", [128, D])
        fw.dma("sp", wrow.ap[:], fnw.ap.rearrange("(o d) -> o d", o=1).partition_broadcast(128), reads=[fnw], writes=[wrow])
        self.pp_junk = self.sb("pp_junk", [128, D], BF16)
        xr = self.sbrot("fn_x", 3, [128, D])
        yr = self.sbrot("fn_y", 3, [128, D])
        sts = self.sbrot("fn_st", 4, [128, 4])
        for s in range(R // 128):
            x = xr.next()
            fw.dma("sp", x.ap[:], self.Xt[s].ap, reads=[self.Xt[s]], writes=[x])
            st = sts.next()
            rs = self.rstd_of(x, st)
            y = yr.next()
            fw.op("dve", lambda e: e.scalar_tensor_tensor(out=y.ap[:], in0=x.ap[:], scalar=rs, in1=wrow.ap[:],
                                                          op0=ALU.mult, op1=ALU.mult),
                  reads=[x, st, wrow], writes=[y])
            fw.dma("sp", Y.ap[s * 128:(s + 1) * 128, :], y.ap[:], reads=[y], writes=[Y])
        self.phase_end()

    def declare(self):
        self.inp("xin", [R, D])
        self.inp("cond", [2, D])
        self.inp("ada_w", [DEPTH, D, 6 * D])
        self.inp("ada_b", [DEPTH, 6 * D])
        self.inp("norm1_w", [DEPTH, D])
        self.inp("norm2_w", [DEPTH, D])
        self.inp("ffn_w_in", [DEPTH, D, 2 * DFF])
        self.inp("ffn_w_out", [DEPTH, DFF, D])
        self.inp("final_norm_w", [D])
        self.inp("ml_w_down", [1, D, 1088])
        self.inp("ml_qnorm_w", [1, 512])
        self.inp("ml_kvnorm_w", [1, 512])
        self.inp("ml_w_uq", [1, 512, 3072])
        self.inp("ml_w_ukv", [1, 512, 4096])
        self.inp("ml_wo", [1, D, D])
        self.inp("cache_ckv", [1, 256, 512])
        self.inp("cache_krope", [1, 256, 64])
        self.inp("hg_w_in", [1, D, 5 * D])
        self.inp("hg_lb", [DEPTH, D])
        self.inp("hg_norm_w", [1, 128])
        self.inp("hg_wo", [1, D, D])
        self.inp("state_hgrn", [1, 2, 16, 128, 128])
        self.inp("hg_mask", [2, 64, 64])
        self.outp("st_hgrn", [2, 2, 16, 128, 128])
        for nm, shp in (("rw_mu", [2, 6, D]), ("rw_wr", [2, D, D]), ("rw_wk", [2, D, D]), ("rw_wv", [2, D, D]), ("rw_wo", [2, D, D]),
                        ("rw_w0", [2, 2, D]), ("rw_w1", [2, 2, D, 96]), ("rw_w2", [2, 2, 96, D]), ("rw_a0", [2, 2, D]),
                        ("rw_a1", [2, 2, D, 96]), ("rw_a2", [2, 2, 96, D]), ("rw_g1", [2, D, 256]), ("rw_g2", [2, 256, D]),
                        ("rw_kk", [2, D]), ("rw_ka", [2, D]), ("rw_rk", [2, 2, 32, 64]), ("rw_lnx_w", [2, D]), ("rw_lnx_b", [2, D]),
                        ("state_rwkv", [2, 2, 32, 64, 64]), ("rw_mask", [8, 128, 128])):
            self.inp(nm, shp)
        self.outp("st_rwkv", [2, 2, 2, 32, 64, 64])
        self.inp("rope_cos", [NS, 64])
        self.inp("rope_sin", [NS, 64])
        self.outp("y", [R, D])
        self.outp("ckv", [2 * NP, 512])
        self.outp("krope", [2 * NP, 64])
        Xs = self.scratch("X", [R, D])
        self.Xt = [Buf(Xs.ap[i * 128:(i + 1) * 128, :], "X%d" % i) for i in range(R // 128)]
        self.mrow = self.scratch("mrow", [DEPTH, 2, 6 * D])

    def copy_in(self):
        fw = self.fw
        xin = self.din["xin"]
        for i in range(R // 128):
            fw.dma("sp", self.Xt[i].ap, xin.ap[i * 128:(i + 1) * 128, :], reads=[xin], writes=[self.Xt[i]])

    def build(self):
        fw = self.fw
        self.declare()
        self.consts_begin()
        self.epsc = self.sb("epsc", [128, 2])
        fw.op("pool", lambda e: e.memset(self.epsc.ap[:, 0:1], EPS), writes=[self.epsc])
        fw.op("pool", lambda e: e.memset(self.epsc.ap[:, 1:2], 64e-5), writes=[self.epsc])
        self.copy_in()
        self.adaln()
        for (l, what) in self.plan:
            if what == "ffn":
                self.ffn(l)
            elif what == "mix":
                self.mixer(l)
        self.final_norm()
        fw.barrier()
        return self.nc

    def mixer(self, l):
        kind = l % 3
        if kind == 2:
            self.mla(l)
        elif kind == 1:
            self.hgrn(l)
        else:
            self.rwkv(l)


FULL_PLAN = [(l, w) for l in range(DEPTH) for w in ("mix", "ffn")]


def rope_tables():
    t = np.arange(NS)
    row = (t // 64).astype(np.float32)
    col = (t % 64).astype(np.float32)
    nf = 16
    inv = (np.float32(10000.0) ** (-np.arange(nf, dtype=np.float32) / np.float32(nf))).astype(np.float32)
    ar = row[:, None] * inv[None, :]
    ac = col[:, None] * inv[None, :]
    cs = np.concatenate([np.cos(ar), np.cos(ar), np.cos(ac), np.cos(ac)], axis=1).astype(np.float32)
    sn = np.concatenate([-np.sin(ar), np.sin(ar), -np.sin(ac), np.sin(ac)], axis=1).astype(np.float32)
    return np.ascontiguousarray(cs), np.ascontiguousarray(sn)


def hg_masks():
    i = np.arange(64)
    same = (i[:, None] // 32) == (i[None, :] // 32)
    mf = (same & (i[:, None] <= i[None, :])).astype(np.float32)
    mb = (same & (i[:, None] >= i[None, :])).astype(np.float32)
    return np.ascontiguousarray(np.stack([mf, mb], 0))


def rw_masks():
    i = np.arange(128)
    same = (i[:, None] // 64) == (i[None, :] // 64)
    out = []
    for d in range(2):
        lt = (i[:, None] < i[None, :]) if d == 0 else (i[:, None] > i[None, :])
        le = (i[:, None] <= i[None, :]) if d == 0 else (i[:, None] >= i[None, :])
        ms = (same & lt).astype(np.float32)
        mi = (same & le).astype(np.float32)
        out += [ms, ms.T.copy(), mi, -mi]
    return np.ascontiguousarray(np.stack(out, 0))


def make_in_maps(inputs, cores):
    maps = []
    for i in cores:
        m = {
            "xin": np.ascontiguousarray(np.concatenate(
                [inputs["x_sample"][i], inputs["x_prompt"][2 * i], inputs["x_prompt"][2 * i + 1]], axis=0)),
            "cond": np.ascontiguousarray(np.stack([inputs["c"][i], inputs["c_ctx"]], axis=0)),
        }
        for k in ("ada_w", "ada_b", "norm1_w", "norm2_w", "ffn_w_in", "ffn_w_out", "final_norm_w",
                  "ml_w_down", "ml_qnorm_w", "ml_kvnorm_w", "ml_w_uq", "ml_w_ukv", "ml_wo",
                  "hg_w_in", "hg_lb", "hg_norm_w", "hg_wo",
                  "rw_mu", "rw_wr", "rw_wk", "rw_wv", "rw_wo", "rw_w0", "rw_w1", "rw_w2", "rw_a0", "rw_a1", "rw_a2",
                  "rw_g1", "rw_g2", "rw_kk", "rw_ka", "rw_rk", "rw_lnx_w", "rw_lnx_b"):
            m[k] = np.ascontiguousarray(inputs[k])
        m["cache_ckv"] = np.ascontiguousarray(inputs["cache_ckv"][i])
        m["cache_krope"] = np.ascontiguousarray(inputs["cache_krope"][i])
        m["state_hgrn"] = np.ascontiguousarray(inputs["state_hgrn"][i])
        m["hg_mask"] = hg_masks()
        m["state_rwkv"] = np.ascontiguousarray(inputs["state_rwkv"][i])
        m["rw_mask"] = rw_masks()
        cs, sn = rope_tables()
        m["rope_cos"], m["rope_sin"] = cs, sn
        maps.append(m)
    return maps


def kernel(**inputs):
    inputs = {k: np.asarray(v) for k, v in inputs.items()}
    b = Builder(FULL_PLAN)
    nc = b.build()
    cores = list(range(8))
    res = run_bass_kernel_spmd(nc, make_in_maps(inputs, cores), core_ids=cores)
    rs = res.results
    f32 = np.float32
    y_sample = np.stack([np.asarray(rs[i]["y"])[0:NS] for i in cores], 0).astype(f32)
    y_prompt = np.concatenate([np.asarray(rs[i]["y"])[NS:R].reshape(2, NP, D) for i in cores], 0).astype(f32)
    st_rwkv = np.concatenate([np.asarray(rs[i]["st_rwkv"]).reshape(2, 2, 2, 32, 64, 64) for i in cores], 0).astype(f32)
    st_hgrn = np.concatenate([np.asarray(rs[i]["st_hgrn"]).reshape(2, 1, 2, 16, 128, 128) for i in cores], 0).astype(f32)
    ckv = np.concatenate([np.asarray(rs[i]["ckv"]).reshape(2, 1, NP, 512) for i in cores], 0).astype(f32)
    krope = np.concatenate([np.asarray(rs[i]["krope"]).reshape(2, 1, NP, 64) for i in cores], 0).astype(f32)
    return (y_prompt, y_sample, st_rwkv, st_hgrn, ckv, krope)
```
